# Optimizing a Trainium2 kernel written in Bass

```python
import math
import jax, jax.numpy as jnp
from jax import lax
import numpy as np

D_MODEL = 1024
BATCH = 2
SEQ = 8192
DEPTH = 2
DEC_BATCH = 8
DEC_SEQ = 4096
PAST_LEN = 128

PLE_DIM = 256
MIX_WIDTH = 2 * D_MODEL
S5_WIDTH = D_MODEL
S5_GROUP = 16
S5_GROUPS = S5_WIDTH // S5_GROUP
S5_STATE = 64
S5_DT_MIN = 1e-3
S5_DT_MAX = 1e-1
MLA_HEADS = 8
MLA_NOPE = 128
MLA_ROPE = 64
MLA_QK = MLA_NOPE + MLA_ROPE
MLA_V = 128
MLA_Q_RANK = 384
MLA_KV_RANK = 256
MLA_WIDTH = MLA_HEADS * MLA_V
ROPE_THETA = 10000.0
Q_BLOCK = 128
AB_SPLITS = [S5_WIDTH, S5_WIDTH + MLA_Q_RANK, S5_WIDTH + MLA_Q_RANK + MLA_KV_RANK, S5_WIDTH + MLA_Q_RANK + MLA_KV_RANK + MLA_ROPE]
AB_IN = AB_SPLITS[-1] + MIX_WIDTH
HY_WIDTH = MIX_WIDTH
HY_IN = 4 * HY_WIDTH
HY_EMB = 33
HY_BANDS = (HY_EMB - 1) // 2
HY_ORDER = 64
HY_DECAY_TARGET = 1e-2
HY_FAST_PCT = 0.3
HY_SLOW_PCT = 1.5
N_AB = (DEPTH + 1) // 2
N_HY = DEPTH // 2
EPS = 1e-6

kernel_name = 'hybrid_s5_mla_hyena_encoder'


def rms_norm(x, g):
    xf = x.astype(jnp.float32)
    y = xf * lax.rsqrt(jnp.mean(xf * xf, axis=-1, keepdims=True) + EPS)
    return (y * g.astype(jnp.float32)).astype(x.dtype)


def _ssm_combine(left, right):
    a_l, b_l = left
    a_r, b_r = right
    return a_l * a_r, a_r * b_l + b_r


def s5_direction(u_groups, a_re, a_im, log_dt, b_re, b_im, c_re, c_im, reverse):
    f32 = jnp.float32
    lam = lax.complex(a_re.astype(f32), a_im.astype(f32))
    dt = jnp.exp(log_dt.astype(f32))[:, None]
    a_bar = jnp.exp(lam * dt)
    b = lax.complex(b_re.astype(f32), b_im.astype(f32))
    b_bar = ((a_bar - 1.0) / lam)[:, :, None] * b
    c = lax.complex(c_re.astype(f32), c_im.astype(f32))
    bu = jnp.einsum('blgc,gnc->blgn', u_groups.astype(jnp.complex64), b_bar)
    a_seq = jnp.broadcast_to(a_bar, (1, u_groups.shape[1]) + a_bar.shape)
    _, states = lax.associative_scan(_ssm_combine, (a_seq, bu), reverse=reverse, axis=1)
    return jnp.real(jnp.einsum('blgn,gcn->blgc', states, c))


def s5_mixer(u, a_re, a_im, log_dt, b_re, b_im, c_re, c_im, d, glu_w, glu_b):
    bsz, seq_len, _ = u.shape
    uf = u.astype(jnp.float32)
    ug = uf.reshape(bsz, seq_len, S5_GROUPS, S5_GROUP)
    y_fwd = s5_direction(ug, a_re[0], a_im[0], log_dt[0], b_re[0], b_im[0], c_re[0], c_im[0], False)
    y_bwd = s5_direction(ug, a_re[1], a_im[1], log_dt[1], b_re[1], b_im[1], c_re[1], c_im[1], True)
    y = (y_fwd + y_bwd).reshape(bsz, seq_len, S5_WIDTH) + d.astype(jnp.float32) * uf
    y = jax.nn.gelu(y).astype(u.dtype)
    return y * jax.nn.sigmoid(y @ glu_w + glu_b)


def rope_tables(seq_len):
    inv = 1.0 / (ROPE_THETA ** (jnp.arange(0, MLA_ROPE, 2, dtype=jnp.float32) / MLA_ROPE))
    ang = jnp.arange(seq_len, dtype=jnp.float32)[:, None] * inv[None, :]
    return jnp.cos(ang), jnp.sin(ang)


def apply_rope(x, cos, sin):
    half = MLA_ROPE // 2
    xf = x.astype(jnp.float32)
    x1, x2 = xf[..., :half], xf[..., half:]
    return jnp.concatenate([x1 * cos - x2 * sin, x1 * sin + x2 * cos], axis=-1).astype(x.dtype)


def mla_mixer(q_lat, kv_lat, k_rope, q_norm, w_q_up, kv_norm, w_kv_up):
    bsz, seq_len, _ = q_lat.shape
    q = (rms_norm(q_lat, q_norm) @ w_q_up).reshape(bsz, seq_len, MLA_HEADS, MLA_QK)
    kv = (rms_norm(kv_lat, kv_norm) @ w_kv_up).reshape(bsz, seq_len, MLA_HEADS, MLA_NOPE + MLA_V)
    q_nope, q_rope = q[..., :MLA_NOPE], q[..., MLA_NOPE:]
    k_nope, v = kv[..., :MLA_NOPE], kv[..., MLA_NOPE:]
    cos, sin = rope_tables(seq_len)
    q_rope = apply_rope(q_rope, cos[:, None, :], sin[:, None, :])
    k_rope = apply_rope(k_rope, cos, sin)
    n_blocks = seq_len // Q_BLOCK
    scale = MLA_QK ** -0.5

    def to_blocks(t):
        return t.reshape((bsz, n_blocks, Q_BLOCK) + t.shape[2:]).swapaxes(0, 1)

    def attend(blk):
        qn, qr = blk
        s = jnp.einsum('bqhd,bkhd->bhqk', qn, k_nope) + jnp.einsum('bqhr,bkr->bhqk', qr, k_rope)
        p = jax.nn.softmax(s.astype(jnp.float32) * scale, axis=-1).astype(v.dtype)
        return jnp.einsum('bhqk,bkhd->bqhd', p, v)

    o = lax.map(attend, (to_blocks(q_nope), to_blocks(q_rope)))
    return o.swapaxes(0, 1).reshape(bsz, seq_len, MLA_WIDTH)


def hyena_filter(seq_len, w1, b1, f1, w2, b2, f2, w3):
    f32 = jnp.float32
    t = jnp.linspace(0.0, 1.0, seq_len, dtype=f32)[:, None]
    w = 2.0 * math.pi * jnp.arange(seq_len, dtype=f32)[:, None] / seq_len
    bands = jnp.linspace(1e-4, HY_BANDS - 1, HY_BANDS, dtype=f32)[None, :]
    z = jnp.concatenate([t, jnp.cos(bands * w), -jnp.sin(bands * w)], axis=-1)
    h = jnp.sin(f1.astype(f32) * (z @ w1.astype(f32) + b1.astype(f32)))
    h = jnp.sin(f2.astype(f32) * (h @ w2.astype(f32) + b2.astype(f32)))
    h = h @ w3.astype(f32)
    min_decay = math.log(HY_DECAY_TARGET) / HY_SLOW_PCT
    max_decay = math.log(HY_DECAY_TARGET) / HY_FAST_PCT
    deltas = jnp.abs(jnp.linspace(min_decay, max_decay, HY_WIDTH, dtype=f32))
    return h * jnp.exp(-t * deltas[None, :])


def hyena_mixer(xv, conv_w, conv_b, f_w1, f_b1, f_freq1, f_w2, f_b2, f_freq2, f_w3, bias):
    bsz, seq_len, _ = xv.shape
    xp = jnp.pad(xv, ((0, 0), (1, 1), (0, 0)))
    uc = xp[:, :-2] * conv_w[0] + xp[:, 1:-1] * conv_w[1] + xp[:, 2:] * conv_w[2] + conv_b
    x0, x1, v = jnp.split(uc, 3, axis=-1)
    k_fwd = hyena_filter(seq_len, f_w1[0], f_b1[0], f_freq1[0], f_w2[0], f_b2[0], f_freq2[0], f_w3[0])
    k_bwd = hyena_filter(seq_len, f_w1[1], f_b1[1], f_freq1[1], f_w2[1], f_b2[1], f_freq2[1], f_w3[1])
    k_two = jnp.concatenate([k_fwd, jnp.zeros((1, HY_WIDTH), jnp.float32), k_bwd[:0:-1]], axis=0)
    k_two = k_two * lax.rsqrt(jnp.sum(k_two * k_two, axis=0, keepdims=True) + EPS)
    vx = (v * x1).astype(jnp.float32)
    n_fft = 2 * seq_len
    conv = jnp.fft.irfft(jnp.fft.rfft(vx, n=n_fft, axis=1) * jnp.fft.rfft(k_two, n=n_fft, axis=0)[None], n=n_fft, axis=1)[:, :seq_len]
    y = (conv + vx * bias.astype(jnp.float32)) * x0.astype(jnp.float32)
    return y.astype(xv.dtype)


def setup_inputs(seed: int = 0) -> dict:
    key = jax.random.key(seed)
    ks = iter(jax.random.split(key, 48))
    f32 = jnp.float32

    def nrm(shape, scale=1.0):
        return scale * jax.random.normal(next(ks), shape, f32)

    def gain(shape):
        return 1.0 + 0.05 * jax.random.normal(next(ks), shape, f32)

    s5_ssm = (N_AB, 2, S5_GROUPS, S5_STATE)
    n_idx = jnp.arange(S5_STATE, dtype=f32)
    return {
        'x_prompt': nrm((BATCH, SEQ, D_MODEL)),
        'x_sample': nrm((DEC_BATCH, DEC_SEQ, D_MODEL)),
        'p_prompt': nrm((DEPTH, BATCH, SEQ, PLE_DIM)),
        'p_sample': nrm((DEPTH, DEC_BATCH, DEC_SEQ, PLE_DIM)),
        'norm_g': gain((DEPTH, D_MODEL)),
        'final_g': gain((D_MODEL,)),
        'ple_w': nrm((DEPTH, PLE_DIM, D_MODEL), PLE_DIM ** -0.5),
        'ple_gate_w': nrm((DEPTH, D_MODEL, D_MODEL), D_MODEL ** -0.5),
        'ab_w_in': nrm((N_AB, D_MODEL, AB_IN), D_MODEL ** -0.5),
        'ab_w_out': nrm((N_AB, MIX_WIDTH, D_MODEL), MIX_WIDTH ** -0.5),
        's5_a_re': -0.5 * jnp.exp(nrm(s5_ssm, 0.05)),
        's5_a_im': math.pi * n_idx + nrm(s5_ssm, 0.01),
        's5_log_dt': jax.random.uniform(next(ks), (N_AB, 2, S5_GROUPS), f32, math.log(S5_DT_MIN), math.log(S5_DT_MAX)),
        's5_b_re': nrm((N_AB, 2, S5_GROUPS, S5_STATE, S5_GROUP), (2 * S5_GROUP) ** -0.5),
        's5_b_im': nrm((N_AB, 2, S5_GROUPS, S5_STATE, S5_GROUP), (2 * S5_GROUP) ** -0.5),
        's5_c_re': nrm((N_AB, 2, S5_GROUPS, S5_GROUP, S5_STATE), 0.5),
        's5_c_im': nrm((N_AB, 2, S5_GROUPS, S5_GROUP, S5_STATE), 0.5),
        's5_d': nrm((N_AB, S5_WIDTH)),
        's5_glu_w': nrm((N_AB, S5_WIDTH, S5_WIDTH), S5_WIDTH ** -0.5),
        's5_glu_b': nrm((N_AB, S5_WIDTH), 0.01),
        'mla_q_norm': gain((N_AB, MLA_Q_RANK)),
        'mla_w_q_up': nrm((N_AB, MLA_Q_RANK, MLA_HEADS * MLA_QK), MLA_Q_RANK ** -0.5),
        'mla_kv_norm': gain((N_AB, MLA_KV_RANK)),
        'mla_w_kv_up': nrm((N_AB, MLA_KV_RANK, MLA_HEADS * (MLA_NOPE + MLA_V)), MLA_KV_RANK ** -0.5),
        'hy_w_in': nrm((N_HY, D_MODEL, HY_IN), D_MODEL ** -0.5),
        'hy_w_out': nrm((N_HY, HY_WIDTH, D_MODEL), HY_WIDTH ** -0.5),
        'hy_conv_w': nrm((N_HY, 3, 3 * HY_WIDTH), 3 ** -0.5),
        'hy_conv_b': nrm((N_HY, 3 * HY_WIDTH), 0.01),
        'hy_f_w1': nrm((N_HY, 2, HY_EMB, HY_ORDER), HY_EMB ** -0.5),
        'hy_f_b1': nrm((N_HY, 2, HY_ORDER), 0.01),
        'hy_f_freq1': gain((N_HY, 2, HY_ORDER)),
        'hy_f_w2': nrm((N_HY, 2, HY_ORDER, HY_ORDER), HY_ORDER ** -0.5),
        'hy_f_b2': nrm((N_HY, 2, HY_ORDER), 0.01),
        'hy_f_freq2': gain((N_HY, 2, HY_ORDER)),
        'hy_f_w3': nrm((N_HY, 2, HY_ORDER, HY_WIDTH), HY_ORDER ** -0.5),
        'hy_bias': nrm((N_HY, HY_WIDTH)),
    }


def reference(x_prompt, x_sample, p_prompt, p_sample, norm_g, final_g, ple_w, ple_gate_w,
              ab_w_in, ab_w_out, s5_a_re, s5_a_im, s5_log_dt, s5_b_re, s5_b_im, s5_c_re, s5_c_im,
              s5_d, s5_glu_w, s5_glu_b, mla_q_norm, mla_w_q_up, mla_kv_norm, mla_w_kv_up,
              hy_w_in, hy_w_out, hy_conv_w, hy_conv_b, hy_f_w1, hy_f_b1, hy_f_freq1,
              hy_f_w2, hy_f_b2, hy_f_freq2, hy_f_w3, hy_bias):
    def trunk(x, p):
        h = x
        for i in range(DEPTH):
            j = i // 2
            hn = rms_norm(h, norm_g[i])
            if i % 2 == 0:
                u_s5, q_lat, kv_lat, k_rope, gate = jnp.split(hn @ ab_w_in[j], AB_SPLITS, axis=-1)
                y_a = s5_mixer(u_s5, s5_a_re[j], s5_a_im[j], s5_log_dt[j], s5_b_re[j], s5_b_im[j],
                               s5_c_re[j], s5_c_im[j], s5_d[j], s5_glu_w[j], s5_glu_b[j])
                y_b = mla_mixer(q_lat, kv_lat, k_rope, mla_q_norm[j], mla_w_q_up[j], mla_kv_norm[j], mla_w_kv_up[j])
                y = jnp.concatenate([y_a, y_b], axis=-1)
                h = h + (y * jax.nn.silu(gate)) @ ab_w_out[j]
            else:
                z = hn @ hy_w_in[j]
                xv, gate = z[..., :3 * HY_WIDTH], z[..., 3 * HY_WIDTH:]
                y = hyena_mixer(xv, hy_conv_w[j], hy_conv_b[j], hy_f_w1[j], hy_f_b1[j], hy_f_freq1[j],
                                hy_f_w2[j], hy_f_b2[j], hy_f_freq2[j], hy_f_w3[j], hy_bias[j])
                h = h + (y * jax.nn.silu(gate)) @ hy_w_out[j]
            h = h + jax.nn.sigmoid(h @ ple_gate_w[i]) * (p[i] @ ple_w[i])
        return rms_norm(h, final_g)

    y_prompt = trunk(x_prompt, p_prompt)
    y_sample = trunk(x_sample, p_sample)
    return (y_prompt, y_sample)
```

```python
import math
from contextlib import ExitStack
import numpy as np
import concourse.bass as bass
import concourse.mybir as mybir
from concourse.bass_utils import run_bass_kernel_spmd

F32 = mybir.dt.float32
BF16 = mybir.dt.bfloat16
ALU = mybir.AluOpType
AF = mybir.ActivationFunctionType
AX = mybir.AxisListType

D = 1024
EPS = 1e-6
import os
CUT = float(os.environ.get('CUT', '99'))
CW = int(os.environ.get('CW', '640'))


class Buf:
    def __init__(self, name, accum=False):
        self.name = name
        self.w = {}
        self.r = {}
        self.accum = accum


class Prog:
    ENG = ["pe", "act", "dve", "pool", "sp"]

    def __init__(self, nc, es):
        self.nc = nc
        self.es = es
        self.q = {e: [] for e in self.ENG}
        self.sems = {}
        self.cnt = {}
        self.seen = {e: {} for e in self.ENG}
        ss = os.environ.get("SELF", "act,dve,pool").split(",")
        self.selfsync = {"pe": False, "act": "act" in ss, "dve": "dve" in ss, "pool": "pool" in ss, "sp": False}
        self.bufs = []
        self.ninstr = 0
        self.dma_map = {}
        for e in self.ENG:
            self.newsem(e)

    def newsem(self, key):
        self.sems[key] = self.es.enter_context(self.nc.semaphore("s_" + str(key)))
        self.cnt[key] = 0

    def buf(self, name, accum=False):
        b = Buf(name, accum)
        self.bufs.append(b)
        return b

    def op(self, eng, fn, reads=(), writes=(), dma=None):
        waits = {}

        def need(tok):
            for k, v in tok.items():
                if v > waits.get(k, 0):
                    waits[k] = v

        for b in reads:
            need(b.w)
        raw_self = waits.get(eng, 0)
        for b in writes:
            if not b.accum:
                need(b.w)
            need(b.r)
        wl = []
        for k, v in waits.items():
            if k == eng:
                if not self.selfsync[eng]:
                    continue
            if self.seen[eng].get(k, 0) >= v:
                continue
            self.seen[eng][k] = v
            wl.append((k, v))
        if dma is None:
            key, inc = eng, 1
        else:
            key, inc = dma, 16
            if key not in self.sems:
                self.newsem(key)
        self.cnt[key] += inc
        val = self.cnt[key]
        sems = self.sems

        def emit(e, wl=wl, fn=fn, key=key, inc=inc):
            for k, v in wl:
                e.wait_ge(sems[k], v)
            fn(e).then_inc(sems[key], inc)

        self.q[eng].append(emit)
        self.ninstr += 1
        if os.environ.get("KTRACE"):
            print("OP", self.ninstr, eng, "tok", key, val, "waits", wl, "R", [b.name for b in reads], "W", [b.name for b in writes])
        for b in reads:
            b.r[key] = max(b.r.get(key, 0), val)
        for b in writes:
            if b.accum:
                b.w[key] = max(b.w.get(key, 0), val)
            else:
                b.w = {key: val}
                b.r = {}

    def dma(self, eng, out, in_, reads, writes, key, slow=False):
        if isinstance(key, Buf):
            key = key.name
        key = (eng == "pool", key)
        if key not in self.dma_map:
            n = sum(1 for kk in self.dma_map if kk[0] == key[0])
            self.dma_map[key] = ("dmaG%d" if key[0] else "dmaS%d") % n
        if slow:
            self.op(eng, lambda e: e.dma_start(out=out, in_=in_, allow_slow_non_contiguous=True), reads, writes, dma=self.dma_map[key])
        else:
            self.op(eng, lambda e: e.dma_start(out=out, in_=in_), reads, writes, dma=self.dma_map[key])

    def sync_dram(self, b):
        return b

    def barrier(self):
        snap = dict(self.cnt)
        sems = self.sems
        for e in self.ENG:
            wl = []
            for k, v in snap.items():
                if v > 0 and self.seen[e].get(k, 0) < v:
                    self.seen[e][k] = v
                    wl.append((k, v))

            def emit(eh, wl=wl):
                for k, v in wl:
                    eh.wait_ge(sems[k], v)

            self.q[e].append(emit)
        self.dma_map = {}
        for b in self.bufs:
            b.w = {}
            b.r = {}
        self.bufs = [b for b in self.bufs if getattr(b, "persist", False)]

    def flush(self, block):
        q = self.q

        @block.tensor
        def _(e):
            for f in q["pe"]:
                f(e)

        @block.scalar
        def _(e):
            for f in q["act"]:
                f(e)

        @block.vector
        def _(e):
            for f in q["dve"]:
                f(e)

        @block.gpsimd
        def _(e):
            for f in q["pool"]:
                f(e)

        @block.sync
        def _(e):
            for f in q["sp"]:
                f(e)

        self.q = {e: [] for e in self.ENG}


class Rot:
    def __init__(self, P, alloc, name, shape, dtype, n):
        self.slots = []
        for i in range(n):
            t = alloc(f"{name}{i}", shape, dtype)
            self.slots.append((t, P.buf(f"{name}{i}")))
        self.i = 0

    def next(self):
        s = self.slots[self.i % len(self.slots)]
        self.i += 1
        return s


class Ctx:
    pass


_PH = [0]


def _phase(P, nc, body):
    _PH[0] += 1
    sfx = "_%d" % _PH[0]
    with ExitStack() as ph, nc.Block() as block:
        tot = [0]

        def sb(name, shape, dtype):
            n = 1
            for d in shape[1:]:
                n *= d
            tot[0] += ((n * (2 if dtype == BF16 else 4) + 31) // 32) * 32
            return ph.enter_context(nc.sbuf_tensor(name + sfx, shape, dtype))

        def ps(name, shape, dtype):
            return ph.enter_context(nc.psum_tensor(name + sfx, shape, dtype))

        body(sb, ps)
        if os.environ.get("KDEBUG"):
            print("phase", sfx, body.__qualname__, "sbuf bytes/partition", tot[0], "instr", P.ninstr)
        assert tot[0] <= 190 * 1024, tot[0]
        P.barrier()
        P.flush(block)


def load_weight_bf16(P, sb, wdst, wbuf, wsrc, K, N, stage_rot, rowscale=None, chunk=1024):
    i = 0
    for k in range(K):
        for c0 in range(0, N, chunk):
            c1 = min(N, c0 + chunk)
            st, stb = stage_rot.next()
            P.dma("sp", st[:, 0:c1 - c0], wsrc[k * 128:(k + 1) * 128, c0:c1], [], [stb], stb)
            eng = "dve" if i % 2 == 0 else "pool"
            if rowscale is None:
                P.op(eng, lambda e, st=st, k=k, c0=c0, c1=c1: e.tensor_copy(out=wdst[:, k, c0:c1], in_=st[:, 0:c1 - c0]), [stb], [wbuf])
            else:
                rs, rsb = rowscale
                P.op(eng, lambda e, st=st, k=k, c0=c0, c1=c1, rs=rs: e.tensor_scalar(out=wdst[:, k, c0:c1], in0=st[:, 0:c1 - c0], scalar1=rs[:, k:k + 1], scalar2=None, op0=ALU.mult), [stb, rsb], [wbuf])
            i += 1


def rstd_from_ssq(P, eng, rstd, ssq, n, R, W):
    P.op(eng, lambda e: e.tensor_scalar(out=rstd, in0=ssq, scalar1=1.0 / n, scalar2=EPS, op0=ALU.mult, op1=ALU.add), R, W)
    P.op("act", lambda e: e.activation(out=rstd, in_=rstd, func=AF.Sqrt), W, W)
    P.op(eng, lambda e: e.reciprocal(out=rstd, in_=rstd), W, W)


def phase_A(P, nc, C, L, x_ap):
    NT = L // 128
    scr = C.scr

    def body(sb, ps):
        ident = sb("identA", [128, 128], BF16)
        identf = sb("identAf", [128, 128], F32)
        bid = P.buf("ident")
        P.dma("sp", identf[:], C.ident[:, :], [], [bid], bid)
        P.op("dve", lambda e: e.tensor_copy(out=ident[:], in_=identf[:]), [bid], [bid])
        ng = sb("ngA", [128, 8], F32)
        qn = sb("qnA", [128, 3], F32)
        kvn = sb("kvnA", [128, 2], F32)
        bsm = P.buf("smallA")
        P.dma("sp", ng[:], C.norm_g[0].rearrange("(k p) -> p k", p=128), [], [bsm], bsm, slow=True)
        P.dma("sp", qn[:], C.mla_q_norm[0].rearrange("(k p) -> p k", p=128), [], [bsm], bsm, slow=True)
        P.dma("sp", kvn[:], C.mla_kv_norm[0].rearrange("(k p) -> p k", p=128), [], [bsm], bsm, slow=True)
        stage = Rot(P, sb, "wstA", [128, 1024], F32, 2)
        w_in = sb("w_inA", [128, 8, 3776], BF16)
        w_q = sb("w_qA", [128, 3, 1536], BF16)
        w_kv = sb("w_kvA", [128, 2, 2048], BF16)
        bw_in, bw_q, bw_kv = P.buf("w_in"), P.buf("w_q"), P.buf("w_kv")
        load_weight_bf16(P, sb, w_in, bw_in, C.ab_w_in[0], 8, 3776, stage, rowscale=(ng, bsm))
        load_weight_bf16(P, sb, w_q, bw_q, C.mla_w_q_up[0], 3, 1536, stage, rowscale=(qn, bsm))
        load_weight_bf16(P, sb, w_kv, bw_kv, C.mla_w_kv_up[0], 2, 2048, stage, rowscale=(kvn, bsm))

        xt = Rot(P, sb, "xtA", [128, 1024], F32, 2)
        cst = Rot(P, sb, "csA", [128, 64], F32, 2)
        junk = sb("junkA", [128, 1024], BF16)
        bjunk = P.buf("junkA")
        stat = Rot(P, sb, "statA", [128, 8], F32, 2)
        hn = Rot(P, sb, "hnA", [128, 1024], BF16, 2)
        hnT = Rot(P, sb, "hnTA", [128, 8, 128], BF16, 2)
        u_sb = Rot(P, sb, "uA", [128, 1024], BF16, 2)
        g_sb = Rot(P, sb, "gA", [128, 2048], BF16, 2)
        lat = Rot(P, sb, "latA", [128, 640], BF16, 2)
        latT = Rot(P, sb, "latTA", [128, int(os.environ.get("LATK", "5")), 128], BF16, 2)
        kr32 = Rot(P, sb, "kr32A", [128, 64], F32, 2)
        q_sb = Rot(P, sb, "qA", [128, 1536], BF16, 2)
        qtmp = Rot(P, sb, "qtmpA", [128, 4, 256], F32, 2)
        kn_sb = Rot(P, sb, "knA", [128, 1024], BF16, 2)
        v_sb = Rot(P, sb, "vA", [128, 1024], BF16, 2)
        kr_sb = Rot(P, sb, "krA", [128, 64], BF16, 2)
        qnT = Rot(P, sb, "qnTA", [128, 8, 256], BF16, 2)
        qrT = Rot(P, sb, "qrTA", [64, 8, 256], BF16, 2)
        knT = Rot(P, sb, "knTA", [128, 8, 256], BF16, 2)
        krT = Rot(P, sb, "krTA", [64, 256], BF16, 2)
        zp = Rot(P, ps, "zpA", [128, 512], F32, 2)
        tp = Rot(P, ps, "tpA", [128, 1024], BF16, 2)
        tq = Rot(P, ps, "tqA", [128, 1024], BF16, 2)
        qp = Rot(P, ps, "qpA", [128, 512], F32, 2)

        def in_proj_group(hT, hTb, c0, c1):
            z, zb = zp.next()
            for k in range(8):
                P.op("pe", lambda e, z=z, k=k: e.matmul(z[:, 0:c1 - c0], lhsT=hT[:, k, :], rhs=w_in[:, k, c0:c1], start=(k == 0), stop=(k == 7)), [hTb, bw_in], [zb])
            return z, zb

        cur = None
        for t in range(NT if CUT > 1 else 0):
            t4 = t % 2
            if t4 == 0:
                cur = (qnT.next(), qrT.next(), knT.next(), krT.next())
            (qnT_t, qnT_b), (qrT_t, qrT_b), (knT_t, knT_b), (krT_t, krT_b) = cur
            x, xb = xt.next()
            cs, csb = cst.next()
            P.dma("sp", x[:], x_ap[t * 128:(t + 1) * 128, :], [], [xb], xb)
            P.dma("sp", cs[:], C.rope_cs[t * 128:(t + 1) * 128, :], [], [csb], csb)
            st, stb = stat.next()
            P.op("act", lambda e, x=x, st=st: e.activation(out=junk[:], in_=x[:], func=AF.Square, accum_out=st[:, 0:1]), [xb], [bjunk, stb])
            rstd_from_ssq(P, "dve", st[:, 1:2], st[:, 0:1], D, [stb], [stb])
            h, hb = hn.next()
            P.op("act", lambda e, x=x, st=st, h=h: e.activation(out=h[:], in_=x[:], func=AF.Copy, scale=st[:, 1:2]), [xb, stb], [hb])
            tpp, tpb = tp.next()
            for k in range(8):
                P.op("pe", lambda e, k=k, tpp=tpp, h=h: e.transpose(out=tpp[:, k * 128:(k + 1) * 128], in_=h[:, k * 128:(k + 1) * 128], identity=ident[:]), [hb, bid], [tpb])
            hT, hTb = hnT.next()
            P.op("act", lambda e, hT=hT, tpp=tpp: e.activation(out=hT[:].rearrange("p k t -> p (k t)"), in_=tpp[:], func=AF.Copy), [tpb], [hTb])
            u, ub = u_sb.next()
            for gi in range(2):
                z, zb = in_proj_group(hT, hTb, gi * 512, gi * 512 + 512)
                P.op("act", lambda e, z=z, u=u, gi=gi: e.activation(out=u[:, gi * 512:(gi + 1) * 512], in_=z[:], func=AF.Copy), [zb], [ub])
            P.dma("pool", scr.U[t * 128:(t + 1) * 128, :], u[:], [ub], [scr.bU], ub)
            if CUT <= 2:
                continue
            la, lab = lat.next()
            z, zb = in_proj_group(hT, hTb, 1024, 1408)
            P.op("act", lambda e, z=z, st=st: e.activation(out=junk[:, 0:384], in_=z[:, 0:384], func=AF.Square, accum_out=st[:, 2:3]), [zb], [bjunk, stb])
            rstd_from_ssq(P, "dve", st[:, 3:4], st[:, 2:3], 384, [stb], [stb])
            P.op("act", lambda e, z=z, st=st, la=la: e.activation(out=la[:, 0:384], in_=z[:, 0:384], func=AF.Copy, scale=st[:, 3:4]), [zb, stb], [lab])
            z, zb = in_proj_group(hT, hTb, 1408, 1728)
            P.op("act", lambda e, z=z, st=st: e.activation(out=junk[:, 0:256], in_=z[:, 0:256], func=AF.Square, accum_out=st[:, 4:5]), [zb], [bjunk, stb])
            rstd_from_ssq(P, "dve", st[:, 5:6], st[:, 4:5], 256, [stb], [stb])
            P.op("act", lambda e, z=z, st=st, la=la: e.activation(out=la[:, 384:640], in_=z[:, 0:256], func=AF.Copy, scale=st[:, 5:6]), [zb, stb], [lab])
            k32, k32b = kr32.next()
            P.op("act", lambda e, z=z, k32=k32: e.activation(out=k32[:], in_=z[:, 256:320], func=AF.Copy), [zb], [k32b])
            kr, krb = kr_sb.next()
            qt, qtb = qtmp.next()
            P.op("pool", lambda e, k32=k32, cs=cs, qt=qt: e.tensor_tensor(out=qt[:, 0, 0:32], in0=k32[:, 0:32], in1=cs[:, 0:32], op=ALU.mult), [k32b, csb], [qtb])
            P.op("pool", lambda e, k32=k32, cs=cs, qt=qt: e.tensor_tensor(out=qt[:, 0, 32:64], in0=k32[:, 32:64], in1=cs[:, 32:64], op=ALU.mult), [k32b, csb], [qtb])
            P.op("pool", lambda e, kr=kr, qt=qt: e.tensor_tensor(out=kr[:, 0:32], in0=qt[:, 0, 0:32], in1=qt[:, 0, 32:64], op=ALU.subtract), [qtb], [krb])
            P.op("pool", lambda e, k32=k32, cs=cs, qt=qt: e.tensor_tensor(out=qt[:, 0, 0:32], in0=k32[:, 0:32], in1=cs[:, 32:64], op=ALU.mult), [k32b, csb, krb], [qtb])
            P.op("pool", lambda e, k32=k32, cs=cs, qt=qt: e.tensor_tensor(out=qt[:, 0, 32:64], in0=k32[:, 32:64], in1=cs[:, 0:32], op=ALU.mult), [k32b, csb], [qtb])
            P.op("pool", lambda e, kr=kr, qt=qt: e.tensor_tensor(out=kr[:, 32:64], in0=qt[:, 0, 0:32], in1=qt[:, 0, 32:64], op=ALU.add), [qtb], [krb])
            if CUT <= 3:
                continue
            g, gb = g_sb.next()
            for gi in range(4):
                z, zb = in_proj_group(hT, hTb, 1728 + gi * 512, 1728 + gi * 512 + 512)
                P.op("act", lambda e, z=z, g=g, gi=gi: e.activation(out=g[:, gi * 512:(gi + 1) * 512], in_=z[:], func=AF.Silu), [zb], [gb])
            P.dma("pool", scr.G[t * 128:(t + 1) * 128, :], g[:], [gb], [scr.bG], gb)
            if CUT <= 4:
                continue
            tqq, tqb = tq.next()
            for k in range(int(os.environ.get("NK", "5"))):
                P.op("pe", lambda e, k=k, tqq=tqq, la=la: e.transpose(out=tqq[:, k * 128:(k + 1) * 128], in_=(h if os.environ.get("SRCH") else la)[:, k * 128:(k + 1) * 128], identity=ident[:]), [lab, bid, hb], [tqb])
            lT, lTb = latT.next()
            if not os.environ.get("NOCOPY"):
                if True:
                    P.op("act", lambda e, lT=lT, tqq=tqq: e.activation(out=lT[:].rearrange("p k t -> p (k t)"), in_=tqq[:, 0:640], func=AF.Copy), [tqb], [lTb])
                else:
                    if os.environ.get("JUNKDST"):
                        P.op("dve", lambda e, lT=lT, tqq=tqq: e.tensor_copy(out=junk[:, 0:CW], in_=tqq[:, 0:CW]), [tqb], [bjunk])
                    elif os.environ.get("TSCOPY"):
                        P.op("dve", lambda e, lT=lT, tqq=tqq: e.tensor_scalar(out=lT[:, 0:CW // 128, :].rearrange("p k t -> p (k t)"), in0=tqq[:, 0:CW], scalar1=1.0, scalar2=None, op0=ALU.mult), [tqb], [lTb])
                    else:
                        P.op("dve", lambda e, lT=lT, tqq=tqq: e.tensor_copy(out=lT[:, 0:CW // 128, :].rearrange("p k t -> p (k t)"), in_=tqq[:, 0:CW]), [tqb], [lTb])
            if CUT <= 4.1:
                continue
            q, qb = q_sb.next()
            for gi in range(3):
                pq, pqb = qp.next()
                for k in range(3):
                    P.op("pe", lambda e, pq=pq, k=k, gi=gi, lT=lT: e.matmul(pq[:], lhsT=lT[:, k, :], rhs=w_q[:, k, gi * 512:(gi + 1) * 512], start=(k == 0), stop=(k == 2)), [lTb, bw_q], [pqb])
                P.op("act", lambda e, pq=pq, q=q, gi=gi: e.activation(out=q[:, gi * 512:(gi + 1) * 512], in_=pq[:], func=AF.Copy), [pqb], [qb])
            if CUT <= 4.3:
                continue
            qv = q[:].rearrange("p (h d) -> p h d", h=8)
            x1 = qv[:, :, 128:160]
            x2 = qv[:, :, 160:192]
            cosb = cs[:, 0:32].unsqueeze(1).to_broadcast([128, 8, 32])
            sinb = cs[:, 32:64].unsqueeze(1).to_broadcast([128, 8, 32])
            qt, qtb = qtmp.next()
            a_ = qt[:, 0, :].rearrange("p (h d) -> p h d", h=8)
            b_ = qt[:, 1, :].rearrange("p (h d) -> p h d", h=8)
            c_ = qt[:, 2, :].rearrange("p (h d) -> p h d", h=8)
            d_ = qt[:, 3, :].rearrange("p (h d) -> p h d", h=8)
            P.op("pool", lambda e, a_=a_, x1=x1, cosb=cosb: e.tensor_tensor(out=a_, in0=x1, in1=cosb, op=ALU.mult), [qb, csb], [qtb])
            P.op("pool", lambda e, b_=b_, x2=x2, sinb=sinb: e.tensor_tensor(out=b_, in0=x2, in1=sinb, op=ALU.mult), [qb, csb], [qtb])
            P.op("pool", lambda e, c_=c_, x1=x1, sinb=sinb: e.tensor_tensor(out=c_, in0=x1, in1=sinb, op=ALU.mult), [qb, csb], [qtb])
            P.op("pool", lambda e, d_=d_, x2=x2, cosb=cosb: e.tensor_tensor(out=d_, in0=x2, in1=cosb, op=ALU.mult), [qb, csb], [qtb])
            P.op("pool", lambda e, a_=a_, b_=b_, x1=x1: e.tensor_tensor(out=x1, in0=a_, in1=b_, op=ALU.subtract), [qtb], [qb])
            P.op("pool", lambda e, c_=c_, d_=d_, x2=x2: e.tensor_tensor(out=x2, in0=c_, in1=d_, op=ALU.add), [qtb], [qb])
            if CUT <= 4.6:
                continue
            kn, knb = kn_sb.next()
            v, vb = v_sb.next()
            for gi in range(4):
                pq, pqb = qp.next()
                for k in range(2):
                    P.op("pe", lambda e, pq=pq, k=k, gi=gi, lT=lT: e.matmul(pq[:], lhsT=lT[:, 3 + k, :], rhs=w_kv[:, k, gi * 512:(gi + 1) * 512], start=(k == 0), stop=(k == 1)), [lTb, bw_kv], [pqb])
                pv = pq[:].rearrange("p (h d) -> p h d", h=2)
                P.op("act", lambda e, pv=pv, kn=kn, gi=gi: e.activation(out=kn[:, gi * 256:(gi + 1) * 256].rearrange("p (h d) -> p h d", h=2), in_=pv[:, :, 0:128], func=AF.Copy), [pqb], [knb])
                if os.environ.get("VMODE", "act") == "dve":
                    P.op("dve", lambda e, pv=pv, v=v, gi=gi: e.tensor_copy(out=v[:, gi * 256:(gi + 1) * 256].rearrange("p (h d) -> p h d", h=2), in_=pv[:, :, 128:256]), [pqb], [vb])
                else:
                    P.op("act", lambda e, pv=pv, v=v, gi=gi: e.activation(out=v[:, gi * 256:(gi + 1) * 256].rearrange("p (h d) -> p h d", h=2), in_=pv[:, :, 128:256], func=AF.Copy), [pqb], [vb])
            if os.environ.get("VMODE", "act") != "none":
                P.dma("pool", scr.V[t * 128:(t + 1) * 128, :], v[:], [vb], [scr.bV], vb)
            if CUT <= 5:
                continue
            tqq, tqb = tq.next()
            for hh in range(8):
                P.op("pe", lambda e, hh=hh, tqq=tqq, q=q: e.transpose(out=tqq[:, hh * 128:(hh + 1) * 128], in_=q[:, hh * 192:hh * 192 + 128], identity=ident[:]), [qb, bid], [tqb])
            P.op("act", lambda e, tqq=tqq, qnT_t=qnT_t, t4=t4: e.activation(out=qnT_t[:, :, t4 * 128:(t4 + 1) * 128], in_=tqq[:].rearrange("p (h t) -> p h t", h=8), func=AF.Copy), [tqb], [qnT_b])
            tqq, tqb = tq.next()
            for hh in range(8):
                P.op("pe", lambda e, hh=hh, tqq=tqq, q=q: e.transpose(out=tqq[0:64, hh * 128:(hh + 1) * 128], in_=q[:, hh * 192 + 128:hh * 192 + 192], identity=ident[:]), [qb, bid], [tqb])
            P.op("act", lambda e, tqq=tqq, qrT_t=qrT_t, t4=t4: e.activation(out=qrT_t[:, :, t4 * 128:(t4 + 1) * 128], in_=tqq[0:64, :].rearrange("p (h t) -> p h t", h=8), func=AF.Copy), [tqb], [qrT_b])
            tqq, tqb = tq.next()
            for hh in range(8):
                P.op("pe", lambda e, hh=hh, tqq=tqq, kn=kn: e.transpose(out=tqq[:, hh * 128:(hh + 1) * 128], in_=kn[:, hh * 128:(hh + 1) * 128], identity=ident[:]), [knb, bid], [tqb])
            P.op("act", lambda e, tqq=tqq, knT_t=knT_t, t4=t4: e.activation(out=knT_t[:, :, t4 * 128:(t4 + 1) * 128], in_=tqq[:].rearrange("p (h t) -> p h t", h=8), func=AF.Copy), [tqb], [knT_b])
            tqq, tqb = tq.next()
            P.op("pe", lambda e, tqq=tqq, kr=kr: e.transpose(out=tqq[0:64, 0:128], in_=kr[:, 0:64], identity=ident[:]), [krb, bid], [tqb])
            P.op("act", lambda e, tqq=tqq, krT_t=krT_t, t4=t4: e.activation(out=krT_t[:, t4 * 128:(t4 + 1) * 128], in_=tqq[0:64, 0:128], func=AF.Copy), [tqb], [krT_b])
            if t4 == 1 or t == NT - 1:
                nt = (t4 + 1) * 128
                t0 = (t - t4) * 128
                P.dma("pool", scr.QN[:, :, t0:t0 + nt].rearrange("h d t -> d h t"), qnT_t[:, :, 0:nt], [qnT_b], [scr.bQ], qnT_b)
                P.dma("pool", scr.QR[:, :, t0:t0 + nt].rearrange("h d t -> d h t"), qrT_t[:, :, 0:nt], [qrT_b], [scr.bQ], qrT_b)
                P.dma("pool", scr.KN[:, :, t0:t0 + nt].rearrange("h d t -> d h t"), knT_t[:, :, 0:nt], [knT_b], [scr.bK], knT_b)
                P.dma("pool", scr.KR[:, t0:t0 + nt], krT_t[:, 0:nt], [krT_b], [scr.bK], krT_b)

    _phase(P, nc, body)


def phase_B(P, nc, C, L):
    NKB = L // 128
    NQB = L // 512
    scr = C.scr
    scale = 192.0 ** -0.5

    def body(sb, ps):
        krt = sb("krB", [64, L], BF16)
        bkr = P.buf("krB")
        P.dma("sp", krt[:], scr.KR[:, 0:L], [], [bkr], bkr)
        qn = Rot(P, sb, "qnB", [128, L], BF16, 2)
        qr = Rot(P, sb, "qrB", [64, L], BF16, 2)
        kn = Rot(P, sb, "knB", [128, L], BF16, 2)
        vt = Rot(P, sb, "vtB", [128, NKB, 129], BF16, 2)
        for (v, vb) in vt.slots:
            P.op("pool", lambda e, v=v: e.memset(v[:, :, 128:129], 1.0), [], [vb])
        pT = Rot(P, sb, "pTB", [128, 512], BF16, 3)
        osb = Rot(P, sb, "osbB", [128, 4, 128], BF16, 2)
        rc = Rot(P, sb, "rcB", [128, 4], F32, 2)
        sp_ = Rot(P, ps, "spB", [128, 512], F32, 2)
        oa = Rot(P, ps, "oaB", [128, 512], F32, 2)
        ob = Rot(P, ps, "obB", [128, 512], F32, 2)
        for h in range(8):
            qn_t, qn_b = qn.next()
            qr_t, qr_b = qr.next()
            kn_t, kn_b = kn.next()
            v_t, v_b = vt.next()
            P.dma("sp", qn_t[:], scr.QN[h, :, 0:L], [], [qn_b], qn_b)
            P.dma("sp", qr_t[:], scr.QR[h, :, 0:L], [], [qr_b], qr_b)
            P.dma("sp", kn_t[:], scr.KN[h, :, 0:L], [], [kn_b], kn_b)
            P.dma("sp", v_t[:, :, 0:128], scr.V[0:L, h * 128:(h + 1) * 128].rearrange("(kb p) d -> p kb d", p=128), [], [v_b], v_b)
            for qb in range(NQB):
                oa_t, oa_b = oa.next()
                ob_t, ob_b = ob.next()
                for kb in range(NKB):
                    s_t, s_b = sp_.next()
                    P.op("pe", lambda e, s_t=s_t, kn_t=kn_t, qn_t=qn_t, kb=kb, qb=qb: e.matmul(s_t[:], lhsT=kn_t[:, kb * 128:(kb + 1) * 128], rhs=qn_t[:, qb * 512:(qb + 1) * 512], start=True, stop=False), [kn_b, qn_b], [s_b])
                    P.op("pe", lambda e, s_t=s_t, qr_t=qr_t, kb=kb, qb=qb: e.matmul(s_t[:], lhsT=krt[:, kb * 128:(kb + 1) * 128], rhs=qr_t[:, qb * 512:(qb + 1) * 512], start=False, stop=True), [bkr, qr_b], [s_b])
                    p_t, p_b = pT.next()
                    P.op("act", lambda e, s_t=s_t, p_t=p_t: e.activation(out=p_t[:], in_=s_t[:], func=AF.Exp, scale=scale), [s_b], [p_b])
                    for sub in range(4):
                        acc_t, acc_b = (oa_t, oa_b) if sub < 2 else (ob_t, ob_b)
                        c0 = (sub % 2) * 129
                        P.op("pe", lambda e, acc_t=acc_t, c0=c0, p_t=p_t, sub=sub, v_t=v_t, kb=kb: e.matmul(acc_t[:, c0:c0 + 129], lhsT=p_t[:, sub * 128:(sub + 1) * 128], rhs=v_t[:, kb, :], start=(kb == 0), stop=(kb == NKB - 1), skip_group_check=True), [p_b, v_b], [acc_b])
                o_t, o_b = osb.next()
                r_t, r_b = rc.next()
                for sub in range(4):
                    acc_t, acc_b = (oa_t, oa_b) if sub < 2 else (ob_t, ob_b)
                    c0 = (sub % 2) * 129
                    P.op("act", lambda e, r_t=r_t, acc_t=acc_t, c0=c0, sub=sub: e.activation(out=r_t[:, sub:sub + 1], in_=acc_t[:, c0 + 128:c0 + 129], func=AF.Copy), [acc_b], [r_b])
                    P.op("dve", lambda e, r_t=r_t, sub=sub: e.reciprocal(out=r_t[:, sub:sub + 1], in_=r_t[:, sub:sub + 1]), [r_b], [r_b])
                    P.op("act", lambda e, r_t=r_t, acc_t=acc_t, c0=c0, sub=sub, o_t=o_t: e.activation(out=o_t[:, sub, :], in_=acc_t[:, c0:c0 + 128], func=AF.Copy, scale=r_t[:, sub:sub + 1]), [acc_b, r_b], [o_b])
                P.dma("pool", scr.O[qb * 512:(qb + 1) * 512, h * 128:(h + 1) * 128].rearrange("(s p) d -> p s d", p=128), o_t[:], [o_b], [scr.bO], o_b)

    _phase(P, nc, body)


def phase_C(P, nc, C, L, layer, h_in, p_ap, w_out_ap, h_out, final_out, use_s5, rs_src=None):
    NT = L // 128
    scr = C.scr

    def body(sb, ps):
        ident = sb("identC", [128, 128], BF16)
        identf = sb("identCf", [128, 128], F32)
        bid = P.buf("identC")
        P.dma("sp", identf[:], C.ident[:, :], [], [bid], bid)
        P.op("dve", lambda e: e.tensor_copy(out=ident[:], in_=identf[:]), [bid], [bid])
        stage = Rot(P, sb, "wstC", [128, 1024], F32, 2)
        w_out = sb("w_outC", [128, 16, 1024], BF16)
        w_pg = sb("w_pgC", [128, 8, 1024], BF16)
        w_pl = sb("w_plC", [128, 2, 1024], BF16)
        bw_out, bw_pg, bw_pl = P.buf("w_outC"), P.buf("w_pgC"), P.buf("w_plC")
        load_weight_bf16(P, sb, w_out, bw_out, w_out_ap, 16, 1024, stage)
        load_weight_bf16(P, sb, w_pg, bw_pg, C.ple_gate_w[layer], 8, 1024, stage)
        load_weight_bf16(P, sb, w_pl, bw_pl, C.ple_w[layer], 2, 1024, stage)
        if final_out is not None:
            fg = sb("fgC", [128, 1024], F32)
            bfg = P.buf("fgC")
            P.dma("sp", fg[:], C.final_g.partition_broadcast(128), [], [bfg], bfg, slow=True)
        ht = Rot(P, sb, "htC", [128, 1024], F32, 2)
        if layer == 0:
            gt = Rot(P, sb, "gtC", [128, 2048], BF16, 2)
            yt = Rot(P, sb, "ytC", [128, 2048], BF16, 2)
        else:
            cvt = Rot(P, sb, "cvtC", [128, 2048], BF16, 2)
            vxt = Rot(P, sb, "vxtC", [128, 2048], BF16, 2)
            xgt = Rot(P, sb, "xgtC", [128, 2048], BF16, 2)
            rsr = sb("rsrC", [128, 2048], F32)
            bir = sb("birC", [128, 2048], F32)
            brr = P.buf("rsrC")
            P.dma("sp", rsr[:], rs_src[0].partition_broadcast(128), [], [brr], brr, slow=True)
            P.dma("sp", bir[:], C.hy_bias[0].partition_broadcast(128), [], [brr], brr, slow=True)
            tm1 = sb("tm1C", [128, 2048], F32)
            tm2 = sb("tm2C", [128, 2048], F32)
            btm1, btm2 = P.buf("tm1C"), P.buf("tm2C")
        pt = Rot(P, sb, "ptC", [128, 256], F32, 2)
        pb16 = Rot(P, sb, "pb16C", [128, 256], BF16, 2)
        mt = Rot(P, sb, "mtC", [128, 2048], BF16, 2)
        mT = Rot(P, sb, "mTC", [128, 16, 128], BF16, 2)
        h2 = Rot(P, sb, "h2C", [128, 1024], F32, 2)
        h2b = Rot(P, sb, "h2bC", [128, 1024], BF16, 2)
        h2T = Rot(P, sb, "h2TC", [128, 8, 128], BF16, 2)
        pT = Rot(P, sb, "pTC", [128, 2, 128], BF16, 2)
        sg = Rot(P, sb, "sgC", [128, 1024], F32, 2)
        h3 = Rot(P, sb, "h3C", [128, 1024], F32, 2)
        stat = Rot(P, sb, "statC", [128, 4], F32, 2)
        junk = sb("junkC", [128, 1024], BF16)
        bjunk = P.buf("junkC")
        tp = Rot(P, ps, "tpC", [128, 1024], BF16, 2)
        zp = Rot(P, ps, "zpC", [128, 512], F32, 4)
        for t in range(NT):
            rows = slice(t * 128, (t + 1) * 128)
            h_t, h_b = ht.next()
            if layer == 0:
                g_t, g_b = gt.next()
                y_t, y_b = yt.next()
            p_t, p_b = pt.next()
            P.dma("sp", h_t[:], h_in[rows, :], [], [h_b], h_b)
            P.dma("sp", p_t[:], p_ap[rows, :], [], [p_b], p_b)
            if layer == 0:
                P.dma("sp", g_t[:], scr.G[rows, :], [], [g_b], g_b)
                if use_s5:
                    P.dma("sp", y_t[:, 0:1024], scr.YA[rows, :], [], [y_b], y_b)
                else:
                    P.op("pool", lambda e, y_t=y_t: e.memset(y_t[:, 0:1024], 0.0), [], [y_b])
                P.dma("sp", y_t[:, 1024:2048], scr.O[rows, :], [], [y_b], y_b)
            m_t, m_b = mt.next()
            mT_t, mT_b = mT.next()
            if layer == 0:
                P.op("dve", lambda e, m_t=m_t, y_t=y_t, g_t=g_t: e.tensor_tensor(out=m_t[:], in0=y_t[:], in1=g_t[:], op=ALU.mult), [y_b, g_b], [m_b])
            else:
                cv_t, cv_b = cvt.next()
                vx_t, vx_b = vxt.next()
                xg_t, xg_b = xgt.next()
                P.dma("sp", cv_t[:], scr.CV[rows, :], [], [cv_b], cv_b)
                P.dma("sp", vx_t[:], scr.VX[rows, :], [], [vx_b], vx_b)
                P.dma("sp", xg_t[:], scr.XG[rows, :], [], [xg_b], xg_b)
                P.op("pool", lambda e, cv_t=cv_t: e.tensor_tensor(out=tm1[:], in0=cv_t[:], in1=rsr[:], op=ALU.mult), [cv_b, brr], [btm1])
                P.op("dve", lambda e, vx_t=vx_t: e.tensor_tensor(out=tm2[:], in0=vx_t[:], in1=bir[:], op=ALU.mult), [vx_b, brr], [btm2])
                P.op("pool", lambda e: e.tensor_tensor(out=tm1[:], in0=tm1[:], in1=tm2[:], op=ALU.add), [btm1, btm2], [btm1])
                P.op("dve", lambda e, m_t=m_t, xg_t=xg_t: e.tensor_tensor(out=m_t[:], in0=tm1[:], in1=xg_t[:], op=ALU.mult), [btm1, xg_b], [m_b])
            for half in range(2):
                tpp, tpb = tp.next()
                for k in range(8):
                    kk = half * 8 + k
                    P.op("pe", lambda e, tpp=tpp, k=k, kk=kk, m_t=m_t: e.transpose(out=tpp[:, k * 128:(k + 1) * 128], in_=m_t[:, kk * 128:(kk + 1) * 128], identity=ident[:]), [m_b, bid], [tpb])
                P.op("act", lambda e, tpp=tpp, mT_t=mT_t, half=half: e.activation(out=mT_t[:, half * 8:(half + 1) * 8, :].rearrange("p k t -> p (k t)"), in_=tpp[:], func=AF.Copy), [tpb], [mT_b])
            h2_t, h2_b = h2.next()
            for gi in range(2):
                z, zb = zp.next()
                for k in range(16):
                    P.op("pe", lambda e, z=z, k=k, gi=gi, mT_t=mT_t: e.matmul(z[:], lhsT=mT_t[:, k, :], rhs=w_out[:, k, gi * 512:(gi + 1) * 512], start=(k == 0), stop=(k == 15)), [mT_b, bw_out], [zb])
                P.op("act", lambda e, z=z, gi=gi, h2_t=h2_t: e.activation(out=h2_t[:, gi * 512:(gi + 1) * 512], in_=z[:], func=AF.Copy), [zb], [h2_b])
                P.op("pool", lambda e, gi=gi, h2_t=h2_t, h_t=h_t: e.tensor_tensor(out=h2_t[:, gi * 512:(gi + 1) * 512], in0=h2_t[:, gi * 512:(gi + 1) * 512], in1=h_t[:, gi * 512:(gi + 1) * 512], op=ALU.add), [h2_b, h_b], [h2_b])
            hb_t, hb_b = h2b.next()
            P.op("act", lambda e, hb_t=hb_t, h2_t=h2_t: e.activation(out=hb_t[:], in_=h2_t[:], func=AF.Copy), [h2_b], [hb_b])
            tpp, tpb = tp.next()
            for k in range(8):
                P.op("pe", lambda e, tpp=tpp, k=k, hb_t=hb_t: e.transpose(out=tpp[:, k * 128:(k + 1) * 128], in_=hb_t[:, k * 128:(k + 1) * 128], identity=ident[:]), [hb_b, bid], [tpb])
            hT_t, hT_b = h2T.next()
            P.op("act", lambda e, tpp=tpp, hT_t=hT_t: e.activation(out=hT_t[:].rearrange("p k t -> p (k t)"), in_=tpp[:], func=AF.Copy), [tpb], [hT_b])
            pb_t, pb_b = pb16.next()
            P.op("act", lambda e, pb_t=pb_t, p_t=p_t: e.activation(out=pb_t[:], in_=p_t[:], func=AF.Copy), [p_b], [pb_b])
            tpp, tpb = tp.next()
            for k in range(2):
                P.op("pe", lambda e, tpp=tpp, k=k, pb_t=pb_t: e.transpose(out=tpp[:, k * 128:(k + 1) * 128], in_=pb_t[:, k * 128:(k + 1) * 128], identity=ident[:]), [pb_b, bid], [tpb])
            pT_t, pT_b = pT.next()
            P.op("act", lambda e, tpp=tpp, pT_t=pT_t: e.activation(out=pT_t[:].rearrange("p k t -> p (k t)"), in_=tpp[:, 0:256], func=AF.Copy), [tpb], [pT_b])
            sg_t, sg_b = sg.next()
            h3_t, h3_b = h3.next()
            for gi in range(2):
                z, zb = zp.next()
                for k in range(8):
                    P.op("pe", lambda e, z=z, k=k, gi=gi, hT_t=hT_t: e.matmul(z[:], lhsT=hT_t[:, k, :], rhs=w_pg[:, k, gi * 512:(gi + 1) * 512], start=(k == 0), stop=(k == 7)), [hT_b, bw_pg], [zb])
                P.op("act", lambda e, z=z, gi=gi, sg_t=sg_t: e.activation(out=sg_t[:, gi * 512:(gi + 1) * 512], in_=z[:], func=AF.Sigmoid), [zb], [sg_b])
                z2, z2b = zp.next()
                for k in range(2):
                    P.op("pe", lambda e, z2=z2, k=k, gi=gi, pT_t=pT_t: e.matmul(z2[:], lhsT=pT_t[:, k, :], rhs=w_pl[:, k, gi * 512:(gi + 1) * 512], start=(k == 0), stop=(k == 1)), [pT_b, bw_pl], [z2b])
                P.op("act", lambda e, z2=z2, gi=gi, h3_t=h3_t: e.activation(out=h3_t[:, gi * 512:(gi + 1) * 512], in_=z2[:], func=AF.Copy), [z2b], [h3_b])
                P.op("dve", lambda e, gi=gi, sg_t=sg_t, h3_t=h3_t: e.tensor_tensor(out=sg_t[:, gi * 512:(gi + 1) * 512], in0=sg_t[:, gi * 512:(gi + 1) * 512], in1=h3_t[:, gi * 512:(gi + 1) * 512], op=ALU.mult), [h3_b, sg_b], [sg_b])
            P.op("pool", lambda e, h3_t=h3_t, sg_t=sg_t, h2_t=h2_t: e.tensor_tensor(out=h3_t[:], in0=sg_t[:], in1=h2_t[:], op=ALU.add), [sg_b, h2_b], [h3_b])
            if final_out is None:
                P.dma("pool", h_out[rows, :], h3_t[:], [h3_b], [scr.bH], h3_b)
            else:
                st, stb = stat.next()
                P.op("act", lambda e, h3_t=h3_t, st=st: e.activation(out=junk[:], in_=h3_t[:], func=AF.Square, accum_out=st[:, 0:1]), [h3_b], [bjunk, stb])
                rstd_from_ssq(P, "dve", st[:, 1:2], st[:, 0:1], D, [stb], [stb])
                P.op("dve", lambda e, h3_t=h3_t, st=st, sg_t=sg_t: e.scalar_tensor_tensor(out=sg_t[:], in0=h3_t[:], scalar=st[:, 1:2], in1=fg[:], op0=ALU.mult, op1=ALU.mult), [h3_b, stb, bfg], [sg_b])
                P.dma("pool", final_out[rows, :], sg_t[:], [sg_b], [scr.bH], sg_b)

    _phase(P, nc, body)


def phase_D1(P, nc, C, L):
    NT = L // 128
    scr = C.scr

    def body(sb, ps):
        ident = sb("identD", [128, 128], BF16)
        identf = sb("identDf", [128, 128], F32)
        bid = P.buf("identD")
        P.dma("sp", identf[:], C.ident[:, :], [], [bid], bid)
        P.op("dve", lambda e: e.tensor_copy(out=ident[:], in_=identf[:]), [bid], [bid])
        zc = sb("zcD", [128, 8, 2], BF16)
        bzc = P.buf("zcD")
        P.op("pool", lambda e: e.memset(zc[:], 0.0), [], [bzc])
        P.dma("pool", scr.HT[:, :, 0:1].rearrange("k f t -> f k t"), zc[:, :, 0:1], [bzc], [scr.bH], bzc, slow=True)
        P.dma("pool", scr.HT[:, :, L + 1:L + 2].rearrange("k f t -> f k t"), zc[:, :, 1:2], [bzc], [scr.bH], bzc, slow=True)
        xt = Rot(P, sb, "xtD", [128, 1024], F32, 2)
        junk = sb("junkD", [128, 1024], BF16)
        bjunk = P.buf("junkD")
        stat = Rot(P, sb, "statD", [128, 4], F32, 2)
        hn = Rot(P, sb, "hnD", [128, 1024], BF16, 2)
        hT4 = Rot(P, sb, "hT4D", [128, 8, 512], BF16, 2)
        tp = Rot(P, ps, "tpD", [128, 1024], BF16, 2)
        cur = None
        for t in range(NT):
            t4 = t % 4
            if t4 == 0:
                cur = hT4.next()
            h4, h4b = cur
            x, xb = xt.next()
            P.dma("sp", x[:], scr.H1[t * 128:(t + 1) * 128, :], [], [xb], xb)
            st, stb = stat.next()
            P.op("act", lambda e, x=x, st=st: e.activation(out=junk[:], in_=x[:], func=AF.Square, accum_out=st[:, 0:1]), [xb], [bjunk, stb])
            rstd_from_ssq(P, "dve", st[:, 1:2], st[:, 0:1], D, [stb], [stb])
            h, hb = hn.next()
            P.op("act", lambda e, x=x, st=st, h=h: e.activation(out=h[:], in_=x[:], func=AF.Copy, scale=st[:, 1:2]), [xb, stb], [hb])
            tpp, tpb = tp.next()
            for k in range(8):
                P.op("pe", lambda e, k=k, tpp=tpp, h=h: e.transpose(out=tpp[:, k * 128:(k + 1) * 128], in_=h[:, k * 128:(k + 1) * 128], identity=ident[:]), [hb, bid], [tpb])
            P.op("act", lambda e, tpp=tpp, h4=h4, t4=t4: e.activation(out=h4[:, :, t4 * 128:(t4 + 1) * 128], in_=tpp[:].rearrange("p (k t) -> p k t", k=8), func=AF.Copy), [tpb], [h4b])
            if t4 == 3 or t == NT - 1:
                nt = (t4 + 1) * 128
                t0 = (t - t4) * 128
                P.dma("pool", scr.HT[:, :, 1 + t0:1 + t0 + nt].rearrange("k f t -> f k t"), h4[:, :, 0:nt], [h4b], [scr.bH], h4b)

    _phase(P, nc, body)


def phase_D2(P, nc, C, L):
    scr = C.scr
    TB = 256
    NB = L // TB

    def body(sb, ps):
        ng = sb("ngD", [128, 8], F32)
        bsm = P.buf("smallD")
        P.dma("sp", ng[:], C.norm_g[1].rearrange("(k p) -> p k", p=128), [], [bsm], bsm, slow=True)
        cw = sb("cwD", [128, 3, 48], F32)
        cb = sb("cbD", [128, 48], F32)
        hb_ = sb("hbD", [128, 16], F32)
        P.dma("sp", cw[:], C.hy_conv_w[0].rearrange("j (i p) -> p j i", p=128), [], [bsm], bsm, slow=True)
        P.dma("sp", cb[:], C.hy_conv_b[0].rearrange("(i p) -> p i", p=128), [], [bsm], bsm, slow=True)
        P.dma("sp", hb_[:], C.hy_bias[0].rearrange("(i p) -> p i", p=128), [], [bsm], bsm, slow=True)
        stage = Rot(P, sb, "wstD", [128, 1024], F32, 2)
        w_in = sb("w_inD", [128, 8, 8192], BF16)
        bw_in = P.buf("w_inD")
        load_weight_bf16(P, sb, w_in, bw_in, C.hy_w_in[0], 8, 8192, stage, rowscale=(ng, bsm))
        hT = Rot(P, sb, "hTD", [128, 8, 258], BF16, 2)
        zs = Rot(P, sb, "zsD", [128, 258], F32, 4)
        uc = Rot(P, sb, "ucD", [128, 3, 256], F32, 2)
        sg = Rot(P, sb, "sgD", [128, 256], F32, 2)
        ident = sb("identD2", [128, 128], BF16)
        identf = sb("identD2f", [128, 128], F32)
        bid = P.buf("identD2")
        P.dma("sp", identf[:], C.ident[:, :], [], [bid], bid)
        P.op("dve", lambda e: e.tensor_copy(out=ident[:], in_=identf[:]), [bid], [bid])
        mo = Rot(P, sb, "moD", [128, 2, 256], BF16, 3)
        stg2 = Rot(P, sb, "stg2D", [128, 2, 2, 2048], BF16, 1)
        zp = Rot(P, ps, "zpD", [128, 512], F32, 4)
        tp = Rot(P, ps, "tpD2", [128, 1024], BF16, 2)
        def do_block(b):
            s0 = b * TB
            n_out = min(TB, L - s0)
            n_in = n_out + 2
            h_t, h_b = hT.next()
            st2, st2b = stg2.next()
            P.dma("sp", h_t[:, :, 0:n_in], scr.HT[:, :, s0:s0 + n_in].rearrange("k f t -> f k t"), [], [h_b], h_b)
            for i in range(16):
                u_t, u_b = uc.next()
                for part in range(4):
                    ch = part * 16 + i
                    z, zb = zp.next()
                    for k in range(8):
                        P.op("pe", lambda e, z=z, k=k, ch=ch, h_t=h_t: e.matmul(z[:, 0:n_in], lhsT=w_in[:, k, ch * 128:(ch + 1) * 128], rhs=h_t[:, k, 0:n_in], start=(k == 0), stop=(k == 7)), [h_b, bw_in], [zb])
                    if part < 3:
                        zs_t, zs_b = zs.next()
                        P.op("act", lambda e, z=z, zs_t=zs_t: e.activation(out=zs_t[:, 0:n_in], in_=z[:, 0:n_in], func=AF.Copy), [zb], [zs_b])
                        eng = "pool" if part != 1 else "dve"
                        P.op(eng, lambda e, zs_t=zs_t, u_t=u_t, part=part, ch=ch: e.tensor_scalar(out=u_t[:, part, 0:n_out], in0=zs_t[:, 0:n_out], scalar1=cw[:, 0, ch:ch + 1], scalar2=cb[:, ch:ch + 1], op0=ALU.mult, op1=ALU.add), [zs_b, bsm], [u_b])
                        P.op("dve", lambda e, zs_t=zs_t, u_t=u_t, part=part, ch=ch: e.scalar_tensor_tensor(out=u_t[:, part, 0:n_out], in0=zs_t[:, 1:1 + n_out], scalar=cw[:, 1, ch:ch + 1], in1=u_t[:, part, 0:n_out], op0=ALU.mult, op1=ALU.add), [zs_b, bsm, u_b], [u_b])
                        P.op("dve", lambda e, zs_t=zs_t, u_t=u_t, part=part, ch=ch: e.scalar_tensor_tensor(out=u_t[:, part, 0:n_out], in0=zs_t[:, 2:2 + n_out], scalar=cw[:, 2, ch:ch + 1], in1=u_t[:, part, 0:n_out], op0=ALU.mult, op1=ALU.add), [zs_b, bsm, u_b], [u_b])
                    else:
                        sg_t, sg_b = sg.next()
                        P.op("act", lambda e, z=z, sg_t=sg_t: e.activation(out=sg_t[:, 0:n_out], in_=z[:, 1:1 + n_out], func=AF.Silu), [zb], [sg_b])
                m_t, m_b = mo.next()
                P.op("pool", lambda e, u_t=u_t, m_t=m_t: e.tensor_tensor(out=m_t[:, 0, :], in0=u_t[:, 2, 0:n_out], in1=u_t[:, 1, 0:n_out], op=ALU.mult), [u_b], [m_b])
                P.op("pool", lambda e, u_t=u_t, sg_t=sg_t, m_t=m_t: e.tensor_tensor(out=m_t[:, 1, :], in0=u_t[:, 0, 0:n_out], in1=sg_t[:, 0:n_out], op=ALU.mult), [u_b, sg_b], [m_b])
                tpp, tpb = tp.next()
                for q in range(2):
                    for tl_ in range(2):
                        P.op("pe", lambda e, tpp=tpp, q=q, tl_=tl_, m_t=m_t: e.transpose(out=tpp[:, (q * 2 + tl_) * 128:(q * 2 + tl_ + 1) * 128], in_=m_t[:, q, tl_ * 128:(tl_ + 1) * 128], identity=ident[:]), [m_b, bid], [tpb])
                P.op("act", lambda e, tpp=tpp, st2=st2, i=i: e.activation(out=st2[:, :, :, i * 128:(i + 1) * 128], in_=tpp[:, 0:512].rearrange("p (q t c) -> p q t c", q=2, t=2), func=AF.Copy), [tpb], [st2b])
            for tl_ in range(2):
                r0 = s0 + tl_ * 128
                P.dma("sp", scr.VX[r0:r0 + 128, :], st2[:, 0, tl_, :], [st2b], [scr.bV], st2b)
                P.dma("sp", scr.XG[r0:r0 + 128, :], st2[:, 1, tl_, :], [st2b], [scr.bG], st2b)

        for b in range(NB):
            do_block(b)

    _phase(P, nc, body)


def fft_tables(L):
    N = 2 * L
    N1 = N // 128
    KL = L // 128
    k = np.arange(N1)[:, None].astype(np.float64)
    f1 = np.arange(N1)[None, :].astype(np.float64)
    ang1 = 2 * np.pi * k * f1 / N1
    p = np.arange(128).astype(np.float64)
    f2 = np.arange(128).astype(np.float64)
    f1v = np.arange(N1).astype(np.float64)
    angE = 2 * np.pi * p[:, None, None] * (f1v[None, :, None] + N1 * f2[None, None, :]) / N
    angE2 = 2 * np.pi * p[None, None, :] * (f1v[None, :, None] + N1 * f2[:, None, None]) / N
    t = {}
    t["s1c"] = np.cos(ang1)
    t["s1s"] = -np.sin(ang1)
    t["ec"] = np.cos(angE).reshape(128, N1 * 128)
    t["es"] = np.sin(angE).reshape(128, N1 * 128)
    t["e2c"] = np.cos(angE2).reshape(128, N1 * 128)
    t["e2s"] = np.sin(angE2).reshape(128, N1 * 128)
    t["i1c"] = np.cos(ang1).T[:, :KL] / N
    t["i1s"] = -np.sin(ang1).T[:, :KL] / N
    return {k_: np.ascontiguousarray(v.astype(np.float32)) for k_, v in t.items()}, N1, KL


def fft_tables_shapes(L):
    N1 = 2 * L // 128
    KL = L // 128
    return {"s1c": (N1, N1), "s1s": (N1, N1), "ec": (128, N1 * 128), "es": (128, N1 * 128),
            "e2c": (128, N1 * 128), "e2s": (128, N1 * 128), "i1c": (N1, KL), "i1s": (N1, KL)}, N1, KL


def load_table_bf16(P, sb, name, src, rows, cols, stage):
    t = sb(name, [rows, cols], BF16)
    b = P.buf(name)
    for c0 in range(0, cols, 1024):
        c1 = min(cols, c0 + 1024)
        st, stb = stage.next()
        P.dma("sp", st[0:rows, 0:c1 - c0], src[0:rows, c0:c1], [], [stb], stb)
        P.op("pool", lambda e, st=st, c0=c0, c1=c1: e.tensor_copy(out=t[:, c0:c1], in_=st[0:rows, 0:c1 - c0]), [stb], [b])
    return t, b


def phase_F(P, nc, C, L, src, KS, tb, khat_dst=None, khat_src=None, yhat_dst=None):
    N1 = 2 * L // 128
    scr = C.scr
    FC = min(4, N1)

    def body(sb, ps):
        stage = Rot(P, sb, "wstF", [128, 1024], F32, 2)
        s1c, bs1c = load_table_bf16(P, sb, "s1cF", tb["s1c"], KS, N1, stage)
        s1s, bs1s = load_table_bf16(P, sb, "s1sF", tb["s1s"], KS, N1, stage)
        ec, bec = load_table_bf16(P, sb, "ecF", tb["ec"], 128, N1 * 128, stage)
        es, bes = load_table_bf16(P, sb, "esF", tb["es"], 128, N1 * 128, stage)
        X = Rot(P, sb, "XF", [KS, 128 * 128], BF16, 2)
        Ast = Rot(P, sb, "AstF", [N1, 2, 512], BF16, 3)
        Bt = Rot(P, sb, "BtF", [128, 3, FC, 128], BF16, 2)
        Xs = Rot(P, sb, "XsF", [128, 2, FC * 128], F32, 2)
        Kh = Rot(P, sb, "KhF", [128, 2, FC * 128], F32, 2)
        Tm = Rot(P, sb, "TmF", [128, 4, FC * 128], F32, 2)
        Yo = Rot(P, sb, "YoF", [128, 2, FC * 128], BF16, 2)
        pa = Rot(P, ps, "paF", [128, 512], F32, 4)
        px = Rot(P, ps, "pxF", [128, 512], F32, 4)
        for s in range(16):
            x_t, x_b = X.next()
            P.dma("sp", x_t[:].rearrange("k (p c) -> k p c", c=128), src[0:KS * 128, s * 128:(s + 1) * 128].rearrange("(k p) c -> k p c", p=128), [], [x_b], x_b)
            for cb in range(32):
                a_t, a_b = Ast.next()
                for ri, (tab, tabb) in enumerate(((s1c, bs1c), (s1s, bs1s))):
                    z, zb = pa.next()
                    P.op("pe", lambda e, z=z, tab=tab, x_t=x_t, cb=cb: e.matmul(z[0:N1, :], lhsT=tab[:, :], rhs=x_t[:, cb * 512:(cb + 1) * 512], start=True, stop=True), [x_b, tabb], [zb])
                    P.op("act", lambda e, z=z, a_t=a_t, ri=ri: e.activation(out=a_t[:, ri, :], in_=z[0:N1, :], func=AF.Copy), [zb], [a_b])
                P.dma("pool", scr.AT[:, 0:N1, cb * 4:(cb + 1) * 4, :].rearrange("r f p c -> f r p c"), a_t[:].rearrange("f r (p c) -> f r p c", c=128), [a_b], [scr.bA], a_b)
            P.sync_dram(scr.bA)
            for fc in range(N1 // FC):
                b_t, b_b = Bt.next()
                for r_ in range(2):
                    P.dma("sp", b_t[:, r_, :, :], scr.AT[r_, fc * FC:(fc + 1) * FC, :, :].rearrange("f p c -> p f c"), [scr.bA], [b_b], b_b)
                P.op("pool", lambda e, b_t=b_t: e.tensor_scalar(out=b_t[:, 2, :, :], in0=b_t[:, 0, :, :], scalar1=-1.0, scalar2=None, op0=ALU.mult), [b_b], [b_b])
                zr, zrb = px.next()
                zi, zib = px.next()
                for j in range(FC):
                    f1 = fc * FC + j
                    P.op("pe", lambda e, zr=zr, j=j, f1=f1, b_t=b_t: e.matmul(zr[:, j * 128:(j + 1) * 128], lhsT=ec[:, f1 * 128:(f1 + 1) * 128], rhs=b_t[:, 0, j, :], start=True, stop=False, skip_group_check=True), [b_b, bec], [zrb])
                    P.op("pe", lambda e, zr=zr, j=j, f1=f1, b_t=b_t: e.matmul(zr[:, j * 128:(j + 1) * 128], lhsT=es[:, f1 * 128:(f1 + 1) * 128], rhs=b_t[:, 1, j, :], start=False, stop=True, skip_group_check=True), [b_b, bes], [zrb])
                    P.op("pe", lambda e, zi=zi, j=j, f1=f1, b_t=b_t: e.matmul(zi[:, j * 128:(j + 1) * 128], lhsT=ec[:, f1 * 128:(f1 + 1) * 128], rhs=b_t[:, 1, j, :], start=True, stop=False, skip_group_check=True), [b_b, bec], [zib])
                    P.op("pe", lambda e, zi=zi, j=j, f1=f1, b_t=b_t: e.matmul(zi[:, j * 128:(j + 1) * 128], lhsT=es[:, f1 * 128:(f1 + 1) * 128], rhs=b_t[:, 2, j, :], start=False, stop=True, skip_group_check=True), [b_b, bes], [zib])
                xs_t, xs_b = Xs.next()
                W = FC * 128
                P.op("act", lambda e, zr=zr, xs_t=xs_t: e.activation(out=xs_t[:, 0, :], in_=zr[:, 0:W], func=AF.Copy), [zrb], [xs_b])
                P.op("act", lambda e, zi=zi, xs_t=xs_t: e.activation(out=xs_t[:, 1, :], in_=zi[:, 0:W], func=AF.Copy), [zib], [xs_b])
                if khat_dst is not None:
                    P.dma("pool", khat_dst[s, :, :, fc * FC:(fc + 1) * FC, :].rearrange("r f g c -> f r g c"), xs_t[:].rearrange("f r (g c) -> f r g c", c=128), [xs_b], [scr.bK], xs_b)
                else:
                    kh_t, kh_b = Kh.next()
                    P.dma("sp", kh_t[:].rearrange("f r (g c) -> f r g c", c=128), khat_src[s, :, :, fc * FC:(fc + 1) * FC, :].rearrange("r f g c -> f r g c"), [], [kh_b], kh_b)
                    tm, tmb = Tm.next()
                    yo, yob = Yo.next()
                    P.op("pool", lambda e, tm=tm, xs_t=xs_t, kh_t=kh_t: e.tensor_tensor(out=tm[:, 0, :], in0=xs_t[:, 0, :], in1=kh_t[:, 0, :], op=ALU.mult), [xs_b, kh_b], [tmb])
                    P.op("dve", lambda e, tm=tm, xs_t=xs_t, kh_t=kh_t: e.tensor_tensor(out=tm[:, 1, :], in0=xs_t[:, 1, :], in1=kh_t[:, 1, :], op=ALU.mult), [xs_b, kh_b], [tmb])
                    P.op("pool", lambda e, tm=tm, xs_t=xs_t, kh_t=kh_t: e.tensor_tensor(out=tm[:, 2, :], in0=xs_t[:, 0, :], in1=kh_t[:, 1, :], op=ALU.mult), [xs_b, kh_b], [tmb])
                    P.op("dve", lambda e, tm=tm, xs_t=xs_t, kh_t=kh_t: e.tensor_tensor(out=tm[:, 3, :], in0=xs_t[:, 1, :], in1=kh_t[:, 0, :], op=ALU.mult), [xs_b, kh_b], [tmb])
                    P.op("pool", lambda e, tm=tm, yo=yo: e.tensor_tensor(out=yo[:, 0, :], in0=tm[:, 0, :], in1=tm[:, 1, :], op=ALU.subtract), [tmb], [yob])
                    P.op("pool", lambda e, tm=tm, yo=yo: e.tensor_tensor(out=yo[:, 1, :], in0=tm[:, 2, :], in1=tm[:, 3, :], op=ALU.add), [tmb], [yob])
                    P.dma("pool", yhat_dst[s, :, :, fc * FC:(fc + 1) * FC, :].rearrange("r f g c -> f r g c"), yo[:].rearrange("f r (g c) -> f r g c", c=128), [yob], [scr.bQ], yob)

    _phase(P, nc, body)


def phase_I(P, nc, C, L, tb, yhat_src):
    N1 = 2 * L // 128
    KL = L // 128
    scr = C.scr
    FC = min(4, N1)
    PC = 4

    def body(sb, ps):
        stage = Rot(P, sb, "wstI", [128, 1024], F32, 2)
        i1c, bi1c = load_table_bf16(P, sb, "i1cI", tb["i1c"], N1, KL, stage)
        i1s, bi1s = load_table_bf16(P, sb, "i1sI", tb["i1s"], N1, KL, stage)
        e2c, be2c = load_table_bf16(P, sb, "e2cI", tb["e2c"], 128, N1 * 128, stage)
        e2s, be2s = load_table_bf16(P, sb, "e2sI", tb["e2s"], 128, N1 * 128, stage)
        Yt = Rot(P, sb, "YtI", [128, 3, FC, 128], BF16, 2)
        Zst = Rot(P, sb, "ZstI", [128, 2, FC * 128], BF16, 3)
        Zt = Rot(P, sb, "ZtI", [N1, 2, PC * 128], BF16, 3)
        Ot = Rot(P, sb, "OtI", [KL, 16 * 512], BF16, 2)
        pz = Rot(P, ps, "pzI", [128, 512], F32, 4)
        po = Rot(P, ps, "poI", [128, 512], F32, 2)
        W = FC * 128
        for s in range(16):
            for fc in range(N1 // FC):
                y_t, y_b = Yt.next()
                P.dma("sp", y_t[:, 0:2, :, :], yhat_src[s, :, :, fc * FC:(fc + 1) * FC, :].rearrange("r f g c -> f r g c"), [], [y_b], y_b)
                P.op("pool", lambda e, y_t=y_t: e.tensor_scalar(out=y_t[:, 2, :, :], in0=y_t[:, 1, :, :], scalar1=-1.0, scalar2=None, op0=ALU.mult), [y_b], [y_b])
                zr, zrb = pz.next()
                zi, zib = pz.next()
                for j in range(FC):
                    f1 = fc * FC + j
                    P.op("pe", lambda e, zr=zr, j=j, f1=f1, y_t=y_t: e.matmul(zr[:, j * 128:(j + 1) * 128], lhsT=e2c[:, f1 * 128:(f1 + 1) * 128], rhs=y_t[:, 0, j, :], start=True, stop=False, skip_group_check=True), [y_b, be2c], [zrb])
                    P.op("pe", lambda e, zr=zr, j=j, f1=f1, y_t=y_t: e.matmul(zr[:, j * 128:(j + 1) * 128], lhsT=e2s[:, f1 * 128:(f1 + 1) * 128], rhs=y_t[:, 2, j, :], start=False, stop=True, skip_group_check=True), [y_b, be2s], [zrb])
                    P.op("pe", lambda e, zi=zi, j=j, f1=f1, y_t=y_t: e.matmul(zi[:, j * 128:(j + 1) * 128], lhsT=e2s[:, f1 * 128:(f1 + 1) * 128], rhs=y_t[:, 0, j, :], start=True, stop=False, skip_group_check=True), [y_b, be2s], [zib])
                    P.op("pe", lambda e, zi=zi, j=j, f1=f1, y_t=y_t: e.matmul(zi[:, j * 128:(j + 1) * 128], lhsT=e2c[:, f1 * 128:(f1 + 1) * 128], rhs=y_t[:, 1, j, :], start=False, stop=True, skip_group_check=True), [y_b, be2c], [zib])
                z_t, z_b = Zst.next()
                P.op("act", lambda e, zr=zr, z_t=z_t: e.activation(out=z_t[:, 0, :], in_=zr[:, 0:W], func=AF.Copy), [zrb], [z_b])
                P.op("act", lambda e, zi=zi, z_t=z_t: e.activation(out=z_t[:, 1, :], in_=zi[:, 0:W], func=AF.Copy), [zib], [z_b])
                P.dma("pool", scr.ZT[:, :, fc * FC:(fc + 1) * FC, :].rearrange("r p f c -> p r f c"), z_t[:].rearrange("p r (f c) -> p r f c", c=128), [z_b], [scr.bA], z_b)
            o_t, o_b = Ot.next()
            for pc in range(128 // PC):
                zz, zzb = Zt.next()
                for r_ in range(2):
                    P.dma("sp", zz[:, r_, :].rearrange("f (p c) -> f p c", c=128), scr.ZT[r_, pc * PC:(pc + 1) * PC, 0:N1, :].rearrange("p f c -> f p c"), [scr.bA], [zzb], zzb)
                o, ob = po.next()
                P.op("pe", lambda e, o=o, zz=zz: e.matmul(o[0:KL, :], lhsT=i1c[:, :], rhs=zz[:, 0, :], start=True, stop=False), [zzb, bi1c], [ob])
                P.op("pe", lambda e, o=o, zz=zz: e.matmul(o[0:KL, :], lhsT=i1s[:, :], rhs=zz[:, 1, :], start=False, stop=True), [zzb, bi1s], [ob])
                half = pc % 16
                P.op("act", lambda e, o=o, o_t=o_t, half=half: e.activation(out=o_t[:, half * 512:(half + 1) * 512], in_=o[0:KL, :], func=AF.Copy), [ob], [o_b])
                if half == 15:
                    p0 = (pc - 15) * PC
                    P.dma("pool", scr.CV[0:L, s * 128:(s + 1) * 128].rearrange("(k p) c -> k p c", p=128)[:, p0:p0 + 64, :], o_t[:].rearrange("k (p c) -> k p c", c=128), [o_b], [scr.bO], o_b)
                    if pc != 128 // PC - 1:
                        o_t, o_b = Ot.next()

    _phase(P, nc, body)


def phase_G(P, nc, C, L, zT, tl, ktwo_dst, rs_dst):
    NT2 = 2 * L // 128
    scr = C.scr
    PI = math.pi

    def body(sb, ps):
        w1 = sb("w1G", [33, 2, 64], BF16)
        w2 = sb("w2G", [64, 2, 64], BF16)
        w3 = sb("w3G", [64, 2, 2048], BF16)
        stg = sb("stgG", [64, 2, 2048], F32)
        sm = sb("smG", [64, 2, 8], F32)
        bw = P.buf("wG")
        for d in range(2):
            P.dma("sp", stg[0:33, d, 0:64], C.hy_f_w1[0, d], [], [bw], bw)
        P.op("pool", lambda e: e.tensor_copy(out=w1[:], in_=stg[0:33, :, 0:64]), [bw], [bw])
        for d in range(2):
            P.dma("sp", stg[0:64, d, 64:128], C.hy_f_w2[0, d], [], [bw], bw)
        P.op("pool", lambda e: e.tensor_copy(out=w2[:], in_=stg[0:64, :, 64:128]), [bw], [bw])
        for i, src in enumerate((C.hy_f_b1, C.hy_f_freq1, C.hy_f_b2, C.hy_f_freq2)):
            for d in range(2):
                P.dma("sp", sm[:, d, i:i + 1], src[0, d].rearrange("(o u) -> o u", u=1), [], [bw], bw, slow=True)
        for (bi, fi, oi) in ((0, 1, 4), (2, 3, 5)):
            P.op("pool", lambda e, bi=bi, fi=fi, oi=oi: e.tensor_tensor(out=sm[:, :, oi:oi + 1], in0=sm[:, :, bi:bi + 1], in1=sm[:, :, fi:fi + 1], op=ALU.mult), [bw], [bw])
        bw3 = P.buf("w3G")
        for d in range(2):
            P.dma("sp", stg[:, d, :], C.hy_f_w3[0, d], [bw], [bw3], bw3)
        P.op("pool", lambda e: e.tensor_copy(out=w3[:], in_=stg[:]), [bw3], [bw3])
        negpi = sb("negpiG", [128, 1], F32)
        P.op("pool", lambda e: e.memset(negpi[:], -PI), [], [bw])
        ones = sb("onesG", [128, 128], BF16)
        P.op("pool", lambda e: e.memset(ones[:], 1.0), [], [bw])
        dl = sb("dlG", [128, 2048], F32)
        bdl = P.buf("dlG")
        P.dma("sp", dl[:], C.hy_delta.partition_broadcast(128), [], [bdl], bdl, slow=True)
        tt = sb("ttG", [128, NT2], F32)
        P.dma("sp", tt[:], tl.rearrange("(k p) -> p k", p=128), [], [bdl], bdl, slow=True)
        P.op("pool", lambda e: e.tensor_scalar(out=tt[:], in0=tt[:], scalar1=-1.0, scalar2=None, op0=ALU.mult), [bdl], [bdl])
        zt = Rot(P, sb, "ztG", [33, 512], F32, 2)
        ztb = Rot(P, sb, "ztbG", [33, 512], BF16, 2)
        v1 = Rot(P, sb, "v1G", [64, 512], F32, 2)
        ni = Rot(P, sb, "niG", [64, 512], mybir.dt.int32, 2)
        nf = Rot(P, sb, "nfG", [64, 512], F32, 2)

        def range_reduce(P, v, vb, ni_s, nf_s):
            n_i, nib = ni_s
            n_f, nfb = nf_s
            P.op("dve", lambda e: e.tensor_scalar(out=n_f[:], in0=v[:], scalar1=1.0 / (2 * PI), scalar2=None, op0=ALU.mult), [vb], [nfb])
            P.op("dve", lambda e: e.tensor_copy(out=n_i[:], in_=n_f[:]), [nfb], [nib])
            P.op("dve", lambda e: e.tensor_copy(out=n_f[:], in_=n_i[:]), [nib], [nfb])
            P.op("dve", lambda e: e.scalar_tensor_tensor(out=v[:], in0=n_f[:], scalar=-2 * PI, in1=v[:], op0=ALU.mult, op1=ALU.add), [nfb, vb], [vb])
            P.op("dve", lambda e: e.tensor_scalar(out=n_f[:], in0=v[:], scalar1=PI, scalar2=None, op0=ALU.is_gt), [vb], [nfb])
            P.op("dve", lambda e: e.scalar_tensor_tensor(out=v[:], in0=n_f[:], scalar=-2 * PI, in1=v[:], op0=ALU.mult, op1=ALU.add), [nfb, vb], [vb])
            P.op("dve", lambda e: e.tensor_scalar(out=n_f[:], in0=v[:], scalar1=-PI, scalar2=None, op0=ALU.is_lt), [vb], [nfb])
            P.op("dve", lambda e: e.scalar_tensor_tensor(out=v[:], in0=n_f[:], scalar=2 * PI, in1=v[:], op0=ALU.mult, op1=ALU.add), [nfb, vb], [vb])
        h1 = Rot(P, sb, "h1G", [64, 512], BF16, 2)
        h2 = Rot(P, sb, "h2G", [64, 512], BF16, 2)
        dec = Rot(P, sb, "decG", [128, 2048], F32, 2)
        kf = Rot(P, sb, "kfG", [128, 2048], F32, 2)
        kb16 = Rot(P, sb, "kbG", [128, 2048], BF16, 2)
        sq = Rot(P, sb, "sqG", [128, 2048], BF16, 2)
        pm = Rot(P, ps, "pmG", [128, 512], F32, 2)
        pk = Rot(P, ps, "pkG", [128, 512], F32, 2)
        pss = [ps("pssG%d" % i, [128, 512], F32) for i in range(4)]
        bss = P.buf("pssG")
        for blk in range(2 * L // 512):
            d = 0 if blk * 512 < L else 1
            z_t, z_b = zt.next()
            P.dma("sp", z_t[:], zT[:, blk * 512:(blk + 1) * 512], [], [z_b], z_b)
            zb_t, zb_b = ztb.next()
            P.op("pool", lambda e, z_t=z_t, zb_t=zb_t: e.tensor_copy(out=zb_t[:], in_=z_t[:]), [z_b], [zb_b])
            m, mb = pm.next()
            P.op("pe", lambda e, m=m, zb_t=zb_t, d=d: e.matmul(m[0:64, :], lhsT=w1[:, d, :], rhs=zb_t[:], start=True, stop=True), [zb_b, bw], [mb])
            v, vb = v1.next()
            P.op("act", lambda e, m=m, v=v, d=d: e.activation(out=v[:], in_=m[0:64, :], func=AF.Identity, scale=sm[:, d, 1:2], bias=sm[:, d, 4:5]), [mb, bw], [vb])
            range_reduce(P, v, vb, ni.next(), nf.next())
            h_1, h1b = h1.next()
            P.op("act", lambda e, v=v, h_1=h_1: e.activation(out=h_1[:], in_=v[:], func=AF.Sin), [vb], [h1b])
            m, mb = pm.next()
            P.op("pe", lambda e, m=m, h_1=h_1, d=d: e.matmul(m[0:64, :], lhsT=w2[:, d, :], rhs=h_1[:], start=True, stop=True), [h1b, bw], [mb])
            v, vb = v1.next()
            P.op("act", lambda e, m=m, v=v, d=d: e.activation(out=v[:], in_=m[0:64, :], func=AF.Identity, scale=sm[:, d, 3:4], bias=sm[:, d, 5:6]), [mb, bw], [vb])
            range_reduce(P, v, vb, ni.next(), nf.next())
            h_2, h2b = h2.next()
            P.op("act", lambda e, v=v, h_2=h_2: e.activation(out=h_2[:], in_=v[:], func=AF.Sin), [vb], [h2b])
            for ti in range(4):
                tile_i = blk * 4 + ti
                dc, dcb = dec.next()
                P.op("act", lambda e, dc=dc, tile_i=tile_i: e.activation(out=dc[:], in_=dl[:], func=AF.Exp, scale=tt[:, tile_i:tile_i + 1]), [bdl], [dcb])
                k_t, k_b = kf.next()
                for gi in range(4):
                    kk, kkb = pk.next()
                    P.op("pe", lambda e, kk=kk, h_2=h_2, ti=ti, gi=gi, d=d: e.matmul(kk[:], lhsT=h_2[:, ti * 128:(ti + 1) * 128], rhs=w3[:, d, gi * 512:(gi + 1) * 512], start=True, stop=True), [h2b, bw3], [kkb])
                    P.op("act", lambda e, kk=kk, k_t=k_t, gi=gi: e.activation(out=k_t[:, gi * 512:(gi + 1) * 512], in_=kk[:], func=AF.Copy), [kkb], [k_b])
                P.op("dve", lambda e, k_t=k_t, dc=dc: e.tensor_tensor(out=k_t[:], in0=k_t[:], in1=dc[:], op=ALU.mult), [k_b, dcb], [k_b])
                kb_t, kb_b = kb16.next()
                P.op("pool", lambda e, k_t=k_t, kb_t=kb_t: e.tensor_copy(out=kb_t[:], in_=k_t[:]), [k_b], [kb_b])
                P.dma("sp", ktwo_dst[tile_i * 128:(tile_i + 1) * 128, :], kb_t[:], [kb_b], [scr.bK], kb_b)
                sq_t, sq_b = sq.next()
                P.op("dve", lambda e, k_t=k_t, sq_t=sq_t: e.tensor_tensor(out=sq_t[:], in0=k_t[:], in1=k_t[:], op=ALU.mult), [k_b], [sq_b])
                for gi in range(4):
                    P.op("pe", lambda e, gi=gi, sq_t=sq_t, tile_i=tile_i: e.matmul(pss[gi][:], lhsT=ones[:], rhs=sq_t[:, gi * 512:(gi + 1) * 512], start=(tile_i == 0), stop=(tile_i == NT2 - 1)), [sq_b, bw], [bss])
        rs = sb("rsG", [128, 2048], F32)
        brs = P.buf("rsG")
        for gi in range(4):
            P.op("act", lambda e, gi=gi: e.activation(out=rs[:, gi * 512:(gi + 1) * 512], in_=pss[gi][:], func=AF.Copy), [bss], [brs])
        P.op("pool", lambda e: e.tensor_scalar(out=rs[:], in0=rs[:], scalar1=EPS, scalar2=None, op0=ALU.add), [brs], [brs])
        P.op("act", lambda e: e.activation(out=rs[:], in_=rs[:], func=AF.Sqrt), [brs], [brs])
        P.op("dve", lambda e: e.reciprocal(out=rs[:], in_=rs[:]), [brs], [brs])
        P.dma("sp", rs_dst[0:1, :], rs[0:1, :], [brs], [scr.bK], brs)

    _phase(P, nc, body)


def s5_cmul(P, eng2, out_r, out_i, xr, xi, yr, yi, t1, t2, R, W, TB):
    P.op("pool", lambda e: e.tensor_tensor(out=t1, in0=xr, in1=yr, op=ALU.mult), R, [TB])
    P.op("dve", lambda e: e.tensor_tensor(out=t2, in0=xi, in1=yi, op=ALU.mult), R, [TB])
    P.op("pool", lambda e: e.tensor_tensor(out=out_r, in0=t1, in1=t2, op=ALU.subtract), [TB] + R, W)
    P.op("pool", lambda e: e.tensor_tensor(out=t1, in0=xr, in1=yi, op=ALU.mult), R + W, [TB])
    P.op("dve", lambda e: e.tensor_tensor(out=t2, in0=xi, in1=yr, op=ALU.mult), R + W, [TB])
    P.op("pool", lambda e: e.tensor_tensor(out=out_i, in0=t1, in1=t2, op=ALU.add), [TB] + R, W)


def s5_nstages(T):
    n, span = 0, 1
    while span < T:
        span *= 4
        n += 1
    return n


def phase_S5setup(P, nc, C, NS):
    scr = C.scr
    PI = math.pi
    G2 = 128

    def body(sb, ps):
        ident = sb("identS", [128, 128], BF16)
        identf = sb("identSf", [128, 128], F32)
        bid = P.buf("identS")
        P.dma("sp", identf[:], C.ident[:, :], [], [bid], bid)
        P.op("dve", lambda e: e.tensor_copy(out=ident[:], in_=identf[:]), [bid], [bid])
        msk = sb("mskS", [128, 4], F32)
        bm = P.buf("mskS")
        P.dma("sp", msk[:], C.s5_rowmask[:, :], [], [bm], bm)
        mfb = sb("mfbS", [128, 2, 128], F32)
        P.dma("sp", mfb[:, 0, :], C.s5_mf[:, :], [], [bm], bm)
        P.dma("sp", mfb[:, 1, :], C.s5_mb[:, :], [], [bm], bm)
        ar = sb("arS", [128, G2], F32)
        ai = sb("aiS", [128, G2], F32)
        dt = sb("dtS", [128, G2], F32)
        ba = P.buf("aS")
        for half in range(2):
            P.dma("sp", ar[half * 64:(half + 1) * 64, :].rearrange("n (d g) -> n d g", d=2), C.s5_a_re[0].rearrange("d g n -> n d g"), [], [ba], ba, slow=True)
            P.dma("sp", ai[half * 64:(half + 1) * 64, :].rearrange("n (d g) -> n d g", d=2), C.s5_a_im[0].rearrange("d g n -> n d g"), [], [ba], ba, slow=True)
        P.dma("sp", dt[:], C.s5_log_dt[0].rearrange("d g -> (d g)").partition_broadcast(128), [], [ba], ba, slow=True)
        P.op("act", lambda e: e.activation(out=dt[:], in_=dt[:], func=AF.Exp), [ba], [ba])
        NT_ = 12
        tmp = [sb("tmpS%d" % i, [128, G2], F32) for i in range(NT_)]
        btmp = [P.buf("tmpS%d" % i) for i in range(NT_)]
        ni = sb("niS", [128, G2], mybir.dt.int32)
        lr, li, mag, pr, pi_, cs_arg = tmp[0], tmp[1], tmp[2], tmp[3], tmp[4], tmp[5]
        bl = P.buf("lS")
        P.op("pool", lambda e: e.tensor_tensor(out=lr[:], in0=ar[:], in1=dt[:], op=ALU.mult), [ba], [bl])
        P.op("pool", lambda e: e.tensor_tensor(out=li[:], in0=ai[:], in1=dt[:], op=ALU.mult), [ba], [bl])
        P.op("act", lambda e: e.activation(out=mag[:], in_=lr[:], func=AF.Exp), [bl], [bl])

        def rr(v):
            nf = tmp[6]
            P.op("dve", lambda e: e.tensor_scalar(out=nf[:], in0=v[:], scalar1=1.0 / (2 * PI), scalar2=None, op0=ALU.mult), [bl], [bl])
            P.op("dve", lambda e: e.tensor_copy(out=ni[:], in_=nf[:]), [bl], [bl])
            P.op("dve", lambda e: e.tensor_copy(out=nf[:], in_=ni[:]), [bl], [bl])
            P.op("dve", lambda e: e.scalar_tensor_tensor(out=v[:], in0=nf[:], scalar=-2 * PI, in1=v[:], op0=ALU.mult, op1=ALU.add), [bl], [bl])
            P.op("dve", lambda e: e.tensor_scalar(out=nf[:], in0=v[:], scalar1=PI, scalar2=None, op0=ALU.is_gt), [bl], [bl])
            P.op("dve", lambda e: e.scalar_tensor_tensor(out=v[:], in0=nf[:], scalar=-2 * PI, in1=v[:], op0=ALU.mult, op1=ALU.add), [bl], [bl])
            P.op("dve", lambda e: e.tensor_scalar(out=nf[:], in0=v[:], scalar1=-PI, scalar2=None, op0=ALU.is_lt), [bl], [bl])
            P.op("dve", lambda e: e.scalar_tensor_tensor(out=v[:], in0=nf[:], scalar=2 * PI, in1=v[:], op0=ALU.mult, op1=ALU.add), [bl], [bl])

        P.op("pool", lambda e: e.tensor_scalar(out=cs_arg[:], in0=li[:], scalar1=PI / 2, scalar2=None, op0=ALU.add), [bl], [bl])
        rr(li)
        rr(cs_arg)
        P.op("act", lambda e: e.activation(out=pi_[:], in_=li[:], func=AF.Sin), [bl], [bl])
        P.op("act", lambda e: e.activation(out=pr[:], in_=cs_arg[:], func=AF.Sin), [bl], [bl])
        P.op("pool", lambda e: e.tensor_tensor(out=pr[:], in0=pr[:], in1=mag[:], op=ALU.mult), [bl], [bl])
        P.op("pool", lambda e: e.tensor_tensor(out=pi_[:], in0=pi_[:], in1=mag[:], op=ALU.mult), [bl], [bl])
        PW = sb("PWS", [128, 9, 2, G2], F32)
        bpw = P.buf("PWS")
        P.op("pool", lambda e: e.memset(PW[:, 0, 0, :], 1.0), [], [bpw])
        P.op("pool", lambda e: e.memset(PW[:, 0, 1, :], 0.0), [], [bpw])
        P.op("pool", lambda e: e.tensor_copy(out=PW[:, 1, 0, :], in_=pr[:]), [bl], [bpw])
        P.op("pool", lambda e: e.tensor_copy(out=PW[:, 1, 1, :], in_=pi_[:]), [bl], [bpw])
        t1, t2 = tmp[7], tmp[8]
        btt = P.buf("ttS")
        for e_ in range(2, 9):
            s5_cmul(P, None, PW[:, e_, 0, :], PW[:, e_, 1, :], PW[:, e_ - 1, 0, :], PW[:, e_ - 1, 1, :], PW[:, 1, 0, :], PW[:, 1, 1, :], t1[:], t2[:], [bpw], [bpw], btt)
        inv8 = sb("inv8S", [128, 2, G2], F32)
        binv = P.buf("inv8S")
        P.op("pool", lambda e: e.tensor_tensor(out=t1[:], in0=PW[:, 8, 0, :], in1=PW[:, 8, 0, :], op=ALU.mult), [bpw], [btt])
        P.op("pool", lambda e: e.tensor_tensor(out=t2[:], in0=PW[:, 8, 1, :], in1=PW[:, 8, 1, :], op=ALU.mult), [bpw], [btt])
        P.op("pool", lambda e: e.tensor_tensor(out=t1[:], in0=t1[:], in1=t2[:], op=ALU.add), [btt], [btt])
        P.op("dve", lambda e: e.reciprocal(out=t1[:], in_=t1[:]), [btt], [btt])
        P.op("pool", lambda e: e.tensor_tensor(out=inv8[:, 0, :], in0=PW[:, 8, 0, :], in1=t1[:], op=ALU.mult), [bpw, btt], [binv])
        P.op("pool", lambda e: e.tensor_tensor(out=inv8[:, 1, :], in0=PW[:, 8, 1, :], in1=t1[:], op=ALU.mult), [bpw, btt], [binv])
        P.op("pool", lambda e: e.tensor_scalar(out=inv8[:, 1, :], in0=inv8[:, 1, :], scalar1=-1.0, scalar2=None, op0=ALU.mult), [binv], [binv])
        SC = sb("SCS", [128, NS * 3, 2, G2], F32)
        bsc = P.buf("SCS")
        q = sb("qS", [128, 2, G2], F32)
        bq = P.buf("qS")
        P.op("pool", lambda e: e.tensor_copy(out=q[:], in_=PW[:, 8, :, :]), [bpw], [bq])
        for m in range(NS):
            P.op("pool", lambda e, m=m: e.tensor_copy(out=SC[:, m * 3, :, :], in_=q[:]), [bq], [bsc])
            s5_cmul(P, None, SC[:, m * 3 + 1, 0, :], SC[:, m * 3 + 1, 1, :], q[:, 0, :], q[:, 1, :], q[:, 0, :], q[:, 1, :], t1[:], t2[:], [bq, bsc], [bsc], btt)
            s5_cmul(P, None, SC[:, m * 3 + 2, 0, :], SC[:, m * 3 + 2, 1, :], SC[:, m * 3 + 1, 0, :], SC[:, m * 3 + 1, 1, :], q[:, 0, :], q[:, 1, :], t1[:], t2[:], [bq, bsc], [bsc], btt)
            if m < NS - 1:
                s5_cmul(P, None, q[:, 0, :], q[:, 1, :], SC[:, m * 3 + 1, 0, :], SC[:, m * 3 + 1, 1, :], SC[:, m * 3 + 1, 0, :], SC[:, m * 3 + 1, 1, :], t1[:], t2[:], [bsc], [bq], btt)
        P.op("pool", lambda e: e.tensor_scalar(out=SC[:, :, 1, :], in0=SC[:, :, 1, :], scalar1=msk[:, 2:3], scalar2=None, op0=ALU.mult), [bsc, bm], [bsc])
        P.dma("sp", scr.SSC[:, 0:NS * 3, :, :], SC[:], [bsc], [scr.bK], bsc)
        cr, ci, den = tmp[9], tmp[10], tmp[11]
        bc = P.buf("cS")
        xr = tmp[2]
        P.op("pool", lambda e: e.tensor_scalar(out=xr[:], in0=PW[:, 1, 0, :], scalar1=-1.0, scalar2=None, op0=ALU.add), [bpw, bl], [bl])
        P.op("pool", lambda e: e.tensor_tensor(out=den[:], in0=ar[:], in1=ar[:], op=ALU.mult), [ba], [bc])
        P.op("pool", lambda e: e.tensor_tensor(out=t1[:], in0=ai[:], in1=ai[:], op=ALU.mult), [ba, binv], [btt])
        P.op("pool", lambda e: e.tensor_tensor(out=den[:], in0=den[:], in1=t1[:], op=ALU.add), [btt], [bc])
        P.op("dve", lambda e: e.reciprocal(out=den[:], in_=den[:]), [bc], [bc])
        P.op("pool", lambda e: e.tensor_tensor(out=t1[:], in0=xr[:], in1=ar[:], op=ALU.mult), [bl, ba], [btt])
        P.op("pool", lambda e: e.tensor_tensor(out=t2[:], in0=PW[:, 1, 1, :], in1=ai[:], op=ALU.mult), [bpw, ba], [btt])
        P.op("pool", lambda e: e.tensor_tensor(out=cr[:], in0=t1[:], in1=t2[:], op=ALU.add), [btt], [bc])
        P.op("pool", lambda e: e.tensor_tensor(out=cr[:], in0=cr[:], in1=den[:], op=ALU.mult), [bc], [bc])
        P.op("pool", lambda e: e.tensor_tensor(out=t1[:], in0=PW[:, 1, 1, :], in1=ar[:], op=ALU.mult), [bpw, ba, bc], [btt])
        P.op("pool", lambda e: e.tensor_tensor(out=t2[:], in0=xr[:], in1=ai[:], op=ALU.mult), [bl, ba], [btt])
        P.op("pool", lambda e: e.tensor_tensor(out=ci[:], in0=t1[:], in1=t2[:], op=ALU.subtract), [btt], [bc])
        P.op("pool", lambda e: e.tensor_tensor(out=ci[:], in0=ci[:], in1=den[:], op=ALU.mult), [bc], [bc])
        Bri = sb("BriS", [128, 2, G2, 16], F32)
        bB = P.buf("BriS")
        for half in range(2):
            for d in range(2):
                P.dma("sp", Bri[half * 64:(half + 1) * 64, 0, d * 64:(d + 1) * 64, :], C.s5_b_re[0, d].rearrange("g n c -> n g c"), [], [bB], bB)
                P.dma("sp", Bri[half * 64:(half + 1) * 64, 1, d * 64:(d + 1) * 64, :], C.s5_b_im[0, d].rearrange("g n c -> n g c"), [], [bB], bB)
        Bb = sb("BbS", [128, 2, G2, 16], F32)
        bBb = P.buf("BbS")
        T1 = sb("T1S", [128, 64, 16], F32)
        T2 = sb("T2S", [128, 64, 16], F32)
        bT = P.buf("TS")
        for d in range(2):
            gs = slice(d * 64, (d + 1) * 64)
            crb = cr[:, gs].unsqueeze(2).to_broadcast([128, 64, 16])
            cib = ci[:, gs].unsqueeze(2).to_broadcast([128, 64, 16])
            s5_cmul(P, None, Bb[:, 0, gs, :], Bb[:, 1, gs, :], Bri[:, 0, gs, :], Bri[:, 1, gs, :], crb, cib, T1[:], T2[:], [bB, bc], [bBb], bT)
        Cri = sb("CriS", [128, 2, G2, 16], F32)
        bC = P.buf("CriS")
        cl = Rot(P, sb, "clS", [128, 128], F32, 2)
        tpc = Rot(P, ps, "tpcS", [128, 512], F32, 2)
        for ri, src in enumerate((C.s5_c_re, C.s5_c_im)):
            for d in range(2):
                for o in range(8):
                    c_t, c_b = cl.next()
                    for dup in range(2):
                        P.dma("sp", c_t[:, dup * 64:(dup + 1) * 64], src[0, d, o * 8:(o + 1) * 8].rearrange("g c n -> (g c) n"), [], [c_b], c_b)
                    tp_, tpb_ = tpc.next()
                    P.op("pe", lambda e, tp_=tp_, c_t=c_t: e.transpose(out=tp_[:, 0:128], in_=c_t[:], identity=identf[:]), [c_b, bid], [tpb_])
                    P.op("act", lambda e, tp_=tp_, ri=ri, d=d, o=o: e.activation(out=Cri[:, ri, d * 64 + o * 8:d * 64 + (o + 1) * 8, :], in_=tp_[:, 0:128].rearrange("p (g c) -> p g c", c=16), func=AF.Copy), [tpb_], [bC])
        GC = 32
        W8 = sb("W8S", [128, 2, GC, 8, 16], BF16)
        W8s = sb("W8sS", [128, GC, 8, 16], BF16)
        C8 = sb("C8S", [128, GC, 8, 16], BF16)
        bW8, bW8s, bC8 = P.buf("W8S"), P.buf("W8sS"), P.buf("C8S")
        O1 = sb("O1S", [128, GC, 16], F32)
        O2 = sb("O2S", [128, GC, 16], F32)
        O3 = sb("O3S", [128, GC, 16], F32)
        O4 = sb("O4S", [128, GC, 16], F32)
        bO = P.buf("OS")
        tpb16 = Rot(P, ps, "tpb16S", [128, 1024], BF16, 2)
        pd = Rot(P, ps, "pdS", [128, 512], F32, 2)
        b8st = Rot(P, sb, "b8stS", [128, 8, 128], BF16, 2)
        d8f = Rot(P, sb, "d8fS", [128, 4, 128], F32, 2)
        d8st = Rot(P, sb, "d8stS", [128, 4, 128], BF16, 2)
        TT1 = T1[:, 0:GC, :]
        TT2 = T2[:, 0:GC, :]
        for ch in range(G2 // GC):
            d = (ch * GC) // 64
            gs = slice(ch * GC, (ch + 1) * GC)
            for i in range(8):
                eb = (7 - i) if d == 0 else i
                prb = PW[:, eb, 0, gs].unsqueeze(2).to_broadcast([128, GC, 16])
                pib = PW[:, eb, 1, gs].unsqueeze(2).to_broadcast([128, GC, 16])
                s5_cmul(P, None, O1[:], O2[:], Bb[:, 0, gs, :], Bb[:, 1, gs, :], prb, pib, TT1, TT2, [bBb, bpw], [bO], bT)
                P.op("pool", lambda e, i=i: e.tensor_copy(out=W8[:, 0, :, i, :], in_=O1[:]), [bO], [bW8])
                P.op("pool", lambda e, i=i: e.tensor_copy(out=W8[:, 1, :, i, :], in_=O2[:]), [bO], [bW8])
                i8r = inv8[:, 0, gs].unsqueeze(2).to_broadcast([128, GC, 16])
                i8i = inv8[:, 1, gs].unsqueeze(2).to_broadcast([128, GC, 16])
                s5_cmul(P, None, O3[:], O4[:], O1[:], O2[:], i8r, i8i, TT1, TT2, [bO, binv], [bO], bT)
                P.op("pool", lambda e: e.tensor_scalar(out=O3[:], in0=O3[:], scalar1=msk[:, 0:1], scalar2=None, op0=ALU.mult), [bO, bm], [bO])
                P.op("dve", lambda e, i=i: e.scalar_tensor_tensor(out=W8s[:, :, i, :], in0=O4[:], scalar=msk[:, 1:2], in1=O3[:], op0=ALU.mult, op1=ALU.add), [bO, bm], [bW8s])
                ec_ = (i + 1) if d == 0 else (8 - i)
                prc = PW[:, ec_, 0, gs].unsqueeze(2).to_broadcast([128, GC, 16])
                pic = PW[:, ec_, 1, gs].unsqueeze(2).to_broadcast([128, GC, 16])
                s5_cmul(P, None, O1[:], O2[:], Cri[:, 0, gs, :], Cri[:, 1, gs, :], prc, pic, TT1, TT2, [bC, bpw, bW8, bW8s], [bO], bT)
                P.op("pool", lambda e: e.tensor_scalar(out=O1[:], in0=O1[:], scalar1=msk[:, 0:1], scalar2=None, op0=ALU.mult), [bO, bm], [bO])
                P.op("dve", lambda e, i=i: e.scalar_tensor_tensor(out=C8[:, :, i, :], in0=O2[:], scalar=msk[:, 3:4], in1=O1[:], op0=ALU.mult, op1=ALU.add), [bO, bm], [bC8])
            P.dma("sp", scr.SC8[:, ch * GC:(ch + 1) * GC, :], C8[:].rearrange("p g i c -> p g (i c)"), [bC8], [scr.bK], bC8)
            for j0 in range(0, GC, 8):
                tp_, tpb_ = tpb16.next()
                for j in range(8):
                    for ri in range(2):
                        P.op("pe", lambda e, tp_=tp_, j=j, j0=j0, ri=ri: e.transpose(out=tp_[:, j * 128 + ri * 64:j * 128 + (ri + 1) * 64], in_=W8[0:64, ri, j0 + j, :, :].rearrange("n i c -> n (i c)"), identity=ident[0:64, 0:64]), [bW8, bid], [tpb_])
                st_, stb_ = b8st.next()
                P.op("act", lambda e, tp_=tp_, st_=st_: e.activation(out=st_[:].rearrange("p j n -> p (j n)"), in_=tp_[:], func=AF.Copy), [tpb_], [stb_])
                dg0 = ch * GC + j0
                P.dma("sp", scr.SB8[dg0:dg0 + 8, :, :].rearrange("j p n -> p j n"), st_[:], [stb_], [scr.bK], stb_)
            for j0 in range(0, GC, 4):
                pp_, ppb_ = pd.next()
                for j in range(4):
                    P.op("pe", lambda e, pp_=pp_, j=j, j0=j0: e.matmul(pp_[:, j * 128:(j + 1) * 128], lhsT=W8s[:, j0 + j, :, :].rearrange("n i c -> n (i c)"), rhs=C8[:, j0 + j, :, :].rearrange("n i c -> n (i c)"), start=True, stop=True, skip_group_check=True), [bW8s, bC8], [ppb_])
                f_, fb_ = d8f.next()
                P.op("act", lambda e, pp_=pp_, f_=f_: e.activation(out=f_[:].rearrange("p j c -> p (j c)"), in_=pp_[:], func=AF.Copy), [ppb_], [fb_])
                o_, ob_ = d8st.next()
                mk = mfb[:, d, :].unsqueeze(1).to_broadcast([128, 4, 128])
                P.op("pool", lambda e, f_=f_, o_=o_, mk=mk: e.tensor_tensor(out=o_[:], in0=f_[:], in1=mk, op=ALU.mult), [fb_, bm], [ob_])
                dg0 = ch * GC + j0
                P.dma("sp", scr.SD8[dg0:dg0 + 4, :, :].rearrange("j p n -> p j n"), o_[:], [ob_], [scr.bK], ob_)

    _phase(P, nc, body)


def phase_S5main(P, nc, C, L, NS):
    T = L // 8
    NTT = max(1, T // 128)
    TT = min(T, 128)
    scr = C.scr

    def body(sb, ps):
        ident = sb("identM", [128, 128], BF16)
        identf = sb("identMf", [128, 128], F32)
        swapf = sb("swapMf", [128, 128], F32)
        bid = P.buf("identM")
        P.dma("sp", identf[:], C.ident[:, :], [], [bid], bid)
        P.dma("sp", swapf[:], C.s5_swap[:, :], [], [bid], bid)
        P.op("dve", lambda e: e.tensor_copy(out=ident[:], in_=identf[:]), [bid], [bid])
        SC = sb("SCM", [128, NS * 3, 2, 128], F32)
        bsc = P.buf("SCM")
        P.dma("sp", SC[:], scr.SSC[:, 0:NS * 3, :, :], [], [bsc], bsc)
        uo = Rot(P, sb, "uoM", [128, NTT, 8, 128], BF16, 2)
        uo2 = Rot(P, sb, "uo2M", [128, NTT, 8, 8, 16], BF16, 2)
        yo = Rot(P, sb, "yoM", [128, NTT, 8, 128], BF16, 2)
        wg = Rot(P, sb, "wgM", [128, 6, 128], BF16, 2)
        us = Rot(P, sb, "usM", [128, T], BF16, 2)
        sbuf_ = Rot(P, sb, "sM", [128, T], BF16, 2 * (NS + 1) + 2)
        ys = Rot(P, sb, "ysM", [128, T], BF16, 2)
        Rt = Rot(P, sb, "RtM", [128, 128], F32, 3)
        Rb = Rot(P, sb, "RbM", [128, 128], BF16, 8)
        tp = Rot(P, ps, "tpM", [128, 1024], BF16, 2)
        pp = Rot(P, ps, "ppM", [128, 512], F32, 4)
        chunks = [(c0, min(512, T - c0)) for c0 in range(0, T, 512)]

        def do_group(o, g8, uo_t, uo_b, yo_t, yo_b):
            g = o * 8 + g8
            w_t, w_b = wg.next()
            for d in range(2):
                P.dma("sp", w_t[:, d, :], scr.SB8[d * 64 + g, :, :], [], [w_b], w_b)
                P.dma("sp", w_t[:, 2 + d, :], scr.SC8[:, d * 64 + g, :], [], [w_b], w_b)
            for d in range(2):
                P.dma("sp", w_t[:, 4 + d, :], scr.SD8[d * 64 + g, :, :], [], [w_b], w_b)
            tpp, tpb = tp.next()
            for tt in range(NTT):
                P.op("pe", lambda e, tt=tt: e.transpose(out=tpp[:, tt * TT:(tt + 1) * TT], in_=uo_t[0:TT, tt, g8, :, :].rearrange("p i c -> p (i c)"), identity=ident[0:TT, 0:TT]), [uo_b, bid], [tpb])
            us_t, us_b = us.next()
            P.op("act", lambda e: e.activation(out=us_t[:], in_=tpp[:, 0:T], func=AF.Copy), [tpb], [us_b])
            fin = []
            for d in range(2):
                dg = d * 64 + g
                s_t, s_b = sbuf_.next()
                for (c0, w) in chunks:
                    z, zb = pp.next()
                    P.op("pe", lambda e, z=z, c0=c0, w=w, d=d: e.matmul(z[:, 0:w], lhsT=w_t[:, d, :], rhs=us_t[:, c0:c0 + w], start=True, stop=True), [us_b, w_b], [zb])
                    P.op("act", lambda e, z=z, c0=c0, w=w, s_t=s_t: e.activation(out=s_t[:, c0:c0 + w], in_=z[:, 0:w], func=AF.Copy), [zb], [s_b])
                for m in range(NS):
                    S = 4 ** m
                    Rs = []
                    for J in range(1, 4):
                        if J * S >= T:
                            break
                        r1, r1b = Rt.next()
                        r2, r2b = Rb.next()
                        col = m * 3 + J - 1
                        P.op("pool", lambda e, r1=r1, col=col, dg=dg: e.tensor_scalar(out=r1[:], in0=identf[:], scalar1=SC[:, col, 0, dg:dg + 1], scalar2=None, op0=ALU.mult), [bid, bsc], [r1b])
                        P.op("dve", lambda e, r1=r1, r2=r2, col=col, dg=dg: e.scalar_tensor_tensor(out=r2[:], in0=swapf[:], scalar=SC[:, col, 1, dg:dg + 1], in1=r1[:], op0=ALU.mult, op1=ALU.add), [bid, bsc, r1b], [r2b])
                        Rs.append((J * S, r2, r2b))
                    n_t, n_b = sbuf_.next()
                    for (c0, w) in chunks:
                        z, zb = pp.next()
                        mms = [(0, w, c0, ident, bid)]
                        for (sh, r2, r2b) in Rs:
                            if d == 0:
                                a = max(c0, sh)
                                if a < c0 + w:
                                    mms.append((a - c0, w, a - sh, r2, r2b))
                            else:
                                bnd = min(c0 + w, T - sh)
                                if bnd > c0:
                                    mms.append((0, bnd - c0, c0 + sh, r2, r2b))
                        for k_, (o0, o1, src0, lt, ltb) in enumerate(mms):
                            P.op("pe", lambda e, z=z, o0=o0, o1=o1, src0=src0, lt=lt, s_t=s_t, k_=k_, nm=len(mms): e.matmul(z[:, o0:o1], lhsT=lt[:], rhs=s_t[:, src0:src0 + (o1 - o0)], start=(k_ == 0), stop=(k_ == nm - 1), skip_group_check=True), [s_b, ltb], [zb])
                        P.op("act", lambda e, z=z, c0=c0, w=w, n_t=n_t: e.activation(out=n_t[:, c0:c0 + w], in_=z[:, 0:w], func=AF.Copy), [zb], [n_b])
                    s_t, s_b = n_t, n_b
                fin.append((s_t, s_b))
            y_t, y_b = ys.next()
            for (c0, w) in chunks:
                z, zb = pp.next()
                mms = [(0, w, us_t, us_b, c0, 4), (0, w, us_t, us_b, c0, 5)]
                a = max(c0, 1)
                if a < c0 + w:
                    mms.append((a - c0, w, fin[0][0], fin[0][1], a - 1, 2))
                bnd = min(c0 + w, T - 1)
                if bnd > c0:
                    mms.append((0, bnd - c0, fin[1][0], fin[1][1], c0 + 1, 3))
                for k_, (o0, o1, src, srcb, src0, wi) in enumerate(mms):
                    P.op("pe", lambda e, z=z, o0=o0, o1=o1, src=src, src0=src0, wi=wi, k_=k_, nm=len(mms): e.matmul(z[:, o0:o1], lhsT=w_t[:, wi, :], rhs=src[:, src0:src0 + (o1 - o0)], start=(k_ == 0), stop=(k_ == nm - 1), skip_group_check=True), [srcb, w_b], [zb])
                P.op("act", lambda e, z=z, c0=c0, w=w: e.activation(out=y_t[:, c0:c0 + w], in_=z[:, 0:w], func=AF.Copy), [zb], [y_b])
            tpp2, tpb2 = tp.next()
            for tt in range(NTT):
                P.op("pe", lambda e, tt=tt: e.transpose(out=tpp2[0:TT, tt * 128:(tt + 1) * 128], in_=y_t[:, tt * TT:(tt + 1) * TT], identity=ident[:]), [y_b, bid], [tpb2])
            P.op("act", lambda e: e.activation(out=yo_t[0:TT, :, :, g8 * 16:(g8 + 1) * 16], in_=tpp2[0:TT, 0:NTT * 128].rearrange("p (t i c) -> p t i c", t=NTT, i=8), func=AF.Copy), [tpb2], [yo_b])

        for o in range(8):
            uo_t, uo_b = uo.next()
            yo_t, yo_b = yo.next()
            for tt in range(NTT):
                P.dma("sp", uo_t[0:TT, tt, :, :], scr.U[tt * TT * 8:(tt + 1) * TT * 8, o * 128:(o + 1) * 128].rearrange("(p i) c -> p i c", i=8), [], [uo_b], uo_b)
            u2_t, u2_b = uo2.next()
            for tt in range(NTT):
                P.op("pool", lambda e, tt=tt, u2_t=u2_t, uo_t=uo_t: e.tensor_copy(out=u2_t[0:TT, tt, :, :, :], in_=uo_t[0:TT, tt, :, :].rearrange("p i (g c) -> p g i c", c=16)), [uo_b], [u2_b])
            for g8 in range(8):
                do_group(o, g8, u2_t, u2_b, yo_t, yo_b)
            for tt in range(NTT):
                P.dma("sp", scr.YS[tt * TT * 8:(tt + 1) * TT * 8, o * 128:(o + 1) * 128].rearrange("(p i) c -> p i c", i=8), yo_t[0:TT, tt, :, :], [yo_b], [scr.bV], yo_b)

    _phase(P, nc, body)


def phase_S5post(P, nc, C, L):
    NT = L // 128
    scr = C.scr

    def body(sb, ps):
        ident = sb("identP", [128, 128], BF16)
        identf = sb("identPf", [128, 128], F32)
        bid = P.buf("identP")
        P.dma("sp", identf[:], C.ident[:, :], [], [bid], bid)
        P.op("dve", lambda e: e.tensor_copy(out=ident[:], in_=identf[:]), [bid], [bid])
        stage = Rot(P, sb, "wstP", [128, 1024], F32, 2)
        w_g = sb("w_gP", [128, 8, 1024], BF16)
        bw = P.buf("w_gP")
        load_weight_bf16(P, sb, w_g, bw, C.s5_glu_w[0], 8, 1024, stage)
        drep = sb("drepP", [128, 1024], F32)
        brep = sb("brepP", [128, 1024], F32)
        br = P.buf("repP")
        P.dma("sp", drep[:], C.s5_d[0].partition_broadcast(128), [], [br], br, slow=True)
        P.dma("sp", brep[:], C.s5_glu_b[0].partition_broadcast(128), [], [br], br, slow=True)
        ut = Rot(P, sb, "utP", [128, 1024], BF16, 2)
        yst = Rot(P, sb, "ystP", [128, 1024], BF16, 2)
        y = Rot(P, sb, "yP", [128, 1024], F32, 2)
        w = Rot(P, sb, "wP", [128, 1024], F32, 2)
        sg = Rot(P, sb, "sgP", [128, 1024], F32, 2)
        yg = Rot(P, sb, "ygP", [128, 1024], BF16, 2)
        ygT = Rot(P, sb, "ygTP", [128, 8, 128], BF16, 2)
        ya = Rot(P, sb, "yaP", [128, 1024], BF16, 2)
        tp = Rot(P, ps, "tpP", [128, 1024], BF16, 2)
        zp = Rot(P, ps, "zpP", [128, 512], F32, 4)
        for t in range(NT):
            rows = slice(t * 128, (t + 1) * 128)
            u_t, u_b = ut.next()
            s_t, s_b = yst.next()
            P.dma("sp", u_t[:], scr.U[rows, :], [], [u_b], u_b)
            P.dma("sp", s_t[:], scr.YS[rows, :], [], [s_b], s_b)
            y_t, y_b = y.next()
            w_t, w_b = w.next()
            P.op("pool", lambda e, y_t=y_t, u_t=u_t: e.tensor_tensor(out=y_t[:], in0=u_t[:], in1=drep[:], op=ALU.mult), [u_b, br], [y_b])
            P.op("pool", lambda e, y_t=y_t, s_t=s_t: e.tensor_tensor(out=y_t[:], in0=y_t[:], in1=s_t[:], op=ALU.add), [s_b, y_b], [y_b])
            P.op("dve", lambda e, y_t=y_t, w_t=w_t: e.tensor_tensor(out=w_t[:], in0=y_t[:], in1=y_t[:], op=ALU.mult), [y_b], [w_b])
            P.op("pool", lambda e, w_t=w_t: e.tensor_scalar(out=w_t[:], in0=w_t[:], scalar1=0.044715, scalar2=1.0, op0=ALU.mult, op1=ALU.add), [w_b], [w_b])
            P.op("pool", lambda e, w_t=w_t, y_t=y_t: e.tensor_tensor(out=w_t[:], in0=w_t[:], in1=y_t[:], op=ALU.mult), [w_b, y_b], [w_b])
            g_t, g_b = sg.next()
            P.op("act", lambda e, w_t=w_t, g_t=g_t: e.activation(out=g_t[:], in_=w_t[:], func=AF.Sigmoid, scale=2.0 * math.sqrt(2.0 / math.pi)), [w_b], [g_b])
            yg_t, yg_b = yg.next()
            P.op("dve", lambda e, yg_t=yg_t, y_t=y_t, g_t=g_t: e.tensor_tensor(out=yg_t[:], in0=y_t[:], in1=g_t[:], op=ALU.mult), [y_b, g_b], [yg_b])
            tpp, tpb = tp.next()
            for k in range(8):
                P.op("pe", lambda e, k=k, tpp=tpp, yg_t=yg_t: e.transpose(out=tpp[:, k * 128:(k + 1) * 128], in_=yg_t[:, k * 128:(k + 1) * 128], identity=ident[:]), [yg_b, bid], [tpb])
            yT, yTb = ygT.next()
            P.op("act", lambda e, tpp=tpp, yT=yT: e.activation(out=yT[:].rearrange("p k t -> p (k t)"), in_=tpp[:], func=AF.Copy), [tpb], [yTb])
            for gi in range(2):
                z, zb = zp.next()
                for k in range(8):
                    P.op("pe", lambda e, z=z, k=k, gi=gi, yT=yT: e.matmul(z[:], lhsT=yT[:, k, :], rhs=w_g[:, k, gi * 512:(gi + 1) * 512], start=(k == 0), stop=(k == 7)), [yTb, bw], [zb])
                P.op("act", lambda e, z=z, gi=gi, g_t=g_t: e.activation(out=g_t[:, gi * 512:(gi + 1) * 512], in_=z[:], func=AF.Copy), [zb], [g_b])
            P.op("pool", lambda e, g_t=g_t: e.tensor_tensor(out=g_t[:], in0=g_t[:], in1=brep[:], op=ALU.add), [g_b, br], [g_b])
            P.op("act", lambda e, g_t=g_t: e.activation(out=g_t[:], in_=g_t[:], func=AF.Sigmoid), [g_b], [g_b])
            a_t, a_b = ya.next()
            P.op("dve", lambda e, a_t=a_t, yg_t=yg_t, g_t=g_t: e.tensor_tensor(out=a_t[:], in0=yg_t[:], in1=g_t[:], op=ALU.mult), [yg_b, g_b], [a_b])
            P.dma("pool", scr.YA[rows, :], a_t[:], [a_b], [scr.bU], a_b)

    _phase(P, nc, body)


WNAMES = {
    "norm_g": [2, 1024], "final_g": [1024], "ple_w": [2, 256, 1024], "ple_gate_w": [2, 1024, 1024],
    "ab_w_in": [1, 1024, 3776], "ab_w_out": [1, 2048, 1024],
    "s5_a_re": [1, 2, 64, 64], "s5_a_im": [1, 2, 64, 64], "s5_log_dt": [1, 2, 64],
    "s5_b_re": [1, 2, 64, 64, 16], "s5_b_im": [1, 2, 64, 64, 16], "s5_c_re": [1, 2, 64, 16, 64], "s5_c_im": [1, 2, 64, 16, 64],
    "s5_d": [1, 1024], "s5_glu_w": [1, 1024, 1024], "s5_glu_b": [1, 1024],
    "mla_q_norm": [1, 384], "mla_w_q_up": [1, 384, 1536], "mla_kv_norm": [1, 256], "mla_w_kv_up": [1, 256, 2048],
    "hy_w_in": [1, 1024, 8192], "hy_w_out": [1, 2048, 1024], "hy_conv_w": [1, 3, 6144], "hy_conv_b": [1, 6144],
    "hy_f_w1": [1, 2, 33, 64], "hy_f_b1": [1, 2, 64], "hy_f_freq1": [1, 2, 64], "hy_f_w2": [1, 2, 64, 64], "hy_f_b2": [1, 2, 64],
    "hy_f_freq2": [1, 2, 64], "hy_f_w3": [1, 2, 64, 2048], "hy_bias": [1, 2048],
}


def build(Ls, opts):
    nc = bass.Bass("TRN2", target_bir_lowering=False)
    C = Ctx()
    LM = max(Ls)
    for n, shp in WNAMES.items():
        setattr(C, n, nc.dram_tensor(n, shp, F32, kind="ExternalInput").ap())
    C.ident = nc.dram_tensor("ident", [128, 128], F32, kind="ExternalInput").ap()
    C.rope_cs = nc.dram_tensor("rope_cs", [LM, 64], F32, kind="ExternalInput").ap()
    xs, ps_, ys = [], [], []
    for i, L in enumerate(Ls):
        xs.append(nc.dram_tensor(f"x{i}", [L, 1024], F32, kind="ExternalInput").ap())
        ps_.append(nc.dram_tensor(f"p{i}", [2, L, 256], F32, kind="ExternalInput").ap())
        ys.append(nc.dram_tensor(f"y{i}", [L, 1024], F32, kind="ExternalOutput").ap())
    dbg = opts.get("dbg", ())
    scr = Ctx()
    C.scr = scr

    def scratch(name, shape, dtype):
        kind = "ExternalOutput" if name in dbg else "Internal"
        return nc.dram_tensor("scr_" + name, shape, dtype, kind=kind).ap()

    scr.U = scratch("U", [LM, 1024], BF16)
    scr.G = scratch("G", [LM, 2048], BF16)
    scr.V = scratch("V", [LM, 1024], BF16)
    scr.QN = scratch("QN", [8, 128, LM], BF16)
    scr.QR = scratch("QR", [8, 64, LM], BF16)
    scr.KN = scratch("KN", [8, 128, LM], BF16)
    scr.KR = scratch("KR", [64, LM], BF16)
    scr.O = scratch("O", [LM, 1024], BF16)
    scr.YA = scratch("YA", [LM, 1024], BF16)
    scr.YH = scratch("YH", [LM, 2048], BF16)
    scr.H1 = scratch("H1", [LM, 1024], F32)
    scr.HT = scratch("HT", [8, 128, LM + 2], BF16)
    scr.MT = scratch("MT", [16, 128, LM], BF16)
    N1M = 2 * LM // 128
    scr.AT = scratch("AT", [2, N1M, 128, 128], BF16)
    scr.ZT = scratch("ZT", [2, 128, N1M, 128], BF16)
    scr.YHAT = scratch("YHAT", [16, 2, 128, N1M, 128], BF16)
    scr.VX = scratch("VX", [LM, 2048], BF16)
    scr.XG = scratch("XG", [LM, 2048], BF16)
    scr.CV = scratch("CV", [LM, 2048], BF16)
    C.hy_delta = nc.dram_tensor("hy_delta", [2048], F32, kind="ExternalInput").ap()
    C.s5_rowmask = nc.dram_tensor("s5_rowmask", [128, 4], F32, kind="ExternalInput").ap()
    C.s5_mf = nc.dram_tensor("s5_mf", [128, 128], F32, kind="ExternalInput").ap()
    C.s5_mb = nc.dram_tensor("s5_mb", [128, 128], F32, kind="ExternalInput").ap()
    C.s5_swap = nc.dram_tensor("s5_swap", [128, 128], F32, kind="ExternalInput").ap()
    NSM = s5_nstages(LM // 8)
    scr.SSC = scratch("SSC", [128, NSM * 3, 2, 128], F32)
    scr.SB8 = scratch("SB8", [128, 128, 128], BF16)
    scr.SC8 = scratch("SC8", [128, 128, 128], BF16)
    scr.SD8 = scratch("SD8", [128, 128, 128], BF16)
    scr.YS = scratch("YS", [LM, 1024], BF16)
    fftc = {}
    for L in sorted(set(Ls)):
        tbs, N1, KL = fft_tables_shapes(L)
        d = {"tb": {k_: nc.dram_tensor(f"{k_}_{L}", list(shp), F32, kind="ExternalInput").ap() for k_, shp in tbs.items()}}
        d["zT"] = nc.dram_tensor(f"zT_{L}", [33, 2 * L], F32, kind="ExternalInput").ap()
        d["tl"] = nc.dram_tensor(f"tl_{L}", [2 * L], F32, kind="ExternalInput").ap()
        d["KT"] = scratch(f"KT_{L}", [2 * L, 2048], BF16)
        d["KH"] = scratch(f"KH_{L}", [16, 2, 128, N1, 128], F32)
        d["RS"] = scratch(f"RS_{L}", [1, 2048], F32)
        fftc[L] = d
    with ExitStack() as es:
        P = Prog(nc, es)
        for n in ["bU", "bG", "bV", "bQ", "bK", "bO", "bH", "bA"]:
            b = Buf(n, accum=True)
            setattr(scr, n, b)
        depth = opts.get("depth", 2)
        conv = opts.get("conv", True) and depth == 2
        if opts.get("s5", True):
            phase_S5setup(P, nc, C, NSM)
        if conv:
            for L in sorted(set(Ls)):
                d = fftc[L]
                phase_G(P, nc, C, L, d["zT"], d["tl"], d["KT"], d["RS"])
                phase_F(P, nc, C, L, d["KT"], 2 * L // 128, d["tb"], khat_dst=d["KH"])
        for i, L in enumerate(Ls):
            phs = opts.get("phases", "ABC")
            if "A" in phs:
                phase_A(P, nc, C, L, xs[i])
            if opts.get("s5", True):
                phase_S5main(P, nc, C, L, s5_nstages(L // 8))
                phase_S5post(P, nc, C, L)
            if "B" in phs:
                phase_B(P, nc, C, L)
            last = depth == 1
            if "C" in phs:
                phase_C(P, nc, C, L, 0, xs[i], ps_[i][0], C.ab_w_out[0], scr.H1, ys[i] if last else None, opts.get("s5", True))
            if depth == 2:
                phase_D1(P, nc, C, L)
                phase_D2(P, nc, C, L)
                d = fftc[L]
                phase_F(P, nc, C, L, scr.VX, L // 128, d["tb"], khat_src=d["KH"], yhat_dst=scr.YHAT)
                phase_I(P, nc, C, L, d["tb"], scr.YHAT)
                phase_C(P, nc, C, L, 1, scr.H1, ps_[i][1], C.hy_w_out[0], None, ys[i], False, rs_src=d["RS"])
        C.ninstr = P.ninstr
    return nc, C


def host_consts(LM):
    inv = 1.0 / (10000.0 ** (np.arange(0, 64, 2, dtype=np.float32) / 64.0))
    ang = np.arange(LM, dtype=np.float32)[:, None] * inv[None, :].astype(np.float32)
    cs = np.concatenate([np.cos(ang), np.sin(ang)], axis=1).astype(np.float32)
    out = {"ident": np.eye(128, dtype=np.float32), "rope_cs": cs}
    min_decay = math.log(1e-2) / 1.5
    max_decay = math.log(1e-2) / 0.3
    p_ = np.arange(128)
    rm = np.zeros((128, 4), np.float32)
    rm[:64, 0] = 1.0
    rm[64:, 1] = 1.0
    rm[:, 2] = np.where(p_ < 64, 1.0, -1.0)
    rm[64:, 3] = -1.0
    out["s5_rowmask"] = rm
    ii = p_ // 16
    out["s5_mf"] = (ii[None, :] >= ii[:, None]).astype(np.float32)
    out["s5_mb"] = (ii[None, :] <= ii[:, None]).astype(np.float32)
    sw = np.zeros((128, 128), np.float32)
    sw[p_, (p_ + 64) % 128] = 1.0
    out["s5_swap"] = sw
    out["hy_delta"] = np.abs(np.linspace(min_decay, max_decay, 2048, dtype=np.float32)).astype(np.float32)
    return out


def host_consts_L(L):
    out = {}
    tbs, N1, KL = fft_tables(L)
    for k_, v in tbs.items():
        out[f"{k_}_{L}"] = v
    t = np.linspace(0.0, 1.0, L, dtype=np.float32)[:, None]
    w = (2.0 * math.pi * np.arange(L, dtype=np.float32)[:, None] / L).astype(np.float32)
    bands = np.linspace(1e-4, 15, 16, dtype=np.float32)[None, :]
    z = np.concatenate([t, np.cos(bands * w), -np.sin(bands * w)], axis=-1).astype(np.float32)
    idx = np.concatenate([np.arange(L), np.array([0]), L - np.arange(1, L)])
    z2 = z[idx]
    tl = t[:, 0][idx].copy()
    tl[L] = 1.0e4
    out[f"zT_{L}"] = np.ascontiguousarray(z2.T.astype(np.float32))
    out[f"tl_{L}"] = np.ascontiguousarray(tl.astype(np.float32))
    return out


_CACHE = {}


def kernel(**inputs):
    Ls = [4096, 8192]
    if "nc" not in _CACHE:
        _CACHE["nc"] = build(Ls, {"depth": 2, "s5": True})
    nc, C = _CACHE["nc"]
    consts = host_consts(max(Ls))
    for L_ in Ls:
        consts.update(host_consts_L(L_))
    W = {n: np.ascontiguousarray(np.asarray(inputs[n], dtype=np.float32)) for n in WNAMES}
    xs, xp = np.asarray(inputs["x_sample"]), np.asarray(inputs["x_prompt"])
    psm, ppr = np.asarray(inputs["p_sample"]), np.asarray(inputs["p_prompt"])
    in_maps = []
    for c in range(8):
        m = dict(W)
        m.update(consts)
        m["x0"] = np.ascontiguousarray(xs[c])
        m["p0"] = np.ascontiguousarray(psm[:, c])
        m["x1"] = np.ascontiguousarray(xp[c % 2])
        m["p1"] = np.ascontiguousarray(ppr[:, c % 2])
        in_maps.append(m)
    res = run_bass_kernel_spmd(nc, in_maps, core_ids=list(range(8)))
    y_sample = np.stack([np.asarray(res.results[c]["y0"], dtype=np.float32) for c in range(8)], axis=0)
    y_prompt = np.stack([np.asarray(res.results[c]["y1"], dtype=np.float32) for c in range(2)], axis=0)
    return (y_prompt, y_sample)
```

```python
import math
from contextlib import ExitStack
import numpy as np
import concourse.bass as bass
import concourse.mybir as mybir
from concourse.bass_utils import run_bass_kernel_spmd

F32 = mybir.dt.float32
BF16 = mybir.dt.bfloat16
ALU = mybir.AluOpType
AF = mybir.ActivationFunctionType
AX = mybir.AxisListType

D = 1024
EPS = 1e-6
import os
CUT = float(os.environ.get('CUT', '99'))
CW = int(os.environ.get('CW', '640'))


class Buf:
    def __init__(self, name, accum=False):
        self.name = name
        self.w = {}
        self.r = {}
        self.accum = accum


class Prog:
    ENG = ["pe", "act", "dve", "pool", "sp"]

    def __init__(self, nc, es):
        self.nc = nc
        self.es = es
        self.q = {e: [] for e in self.ENG}
        self.sems = {}
        self.cnt = {}
        self.seen = {e: {} for e in self.ENG}
        ss = os.environ.get("SELF", "act,dve,pool").split(",")
        self.selfsync = {"pe": False, "act": "act" in ss, "dve": "dve" in ss, "pool": "pool" in ss, "sp": False}
        self.bufs = []
        self.ninstr = 0
        self.dma_map = {}
        for e in self.ENG:
            self.newsem(e)

    def newsem(self, key):
        self.sems[key] = self.es.enter_context(self.nc.semaphore("s_" + str(key)))
        self.cnt[key] = 0

    def buf(self, name, accum=False):
        b = Buf(name, accum)
        self.bufs.append(b)
        return b

    def op(self, eng, fn, reads=(), writes=(), dma=None):
        waits = {}

        def need(tok):
            for k, v in tok.items():
                if v > waits.get(k, 0):
                    waits[k] = v

        for b in reads:
            need(b.w)
        raw_self = waits.get(eng, 0)
        for b in writes:
            if not b.accum:
                need(b.w)
            need(b.r)
        wl = []
        for k, v in waits.items():
            if k == eng:
                if not self.selfsync[eng]:
                    continue
            if self.seen[eng].get(k, 0) >= v:
                continue
            self.seen[eng][k] = v
            wl.append((k, v))
        if dma is None:
            key, inc = eng, 1
        else:
            key, inc = dma, 16
            if key not in self.sems:
                self.newsem(key)
        self.cnt[key] += inc
        val = self.cnt[key]
        sems = self.sems

        def emit(e, wl=wl, fn=fn, key=key, inc=inc):
            for k, v in wl:
                e.wait_ge(sems[k], v)
            fn(e).then_inc(sems[key], inc)

        self.q[eng].append(emit)
        self.ninstr += 1
        if os.environ.get("KTRACE"):
            print("OP", self.ninstr, eng, "tok", key, val, "waits", wl, "R", [b.name for b in reads], "W", [b.name for b in writes])
        for b in reads:
            b.r[key] = max(b.r.get(key, 0), val)
        for b in writes:
            if b.accum:
                b.w[key] = max(b.w.get(key, 0), val)
            else:
                b.w = {key: val}
                b.r = {}

    def dma(self, eng, out, in_, reads, writes, key, slow=False):
        if isinstance(key, Buf):
            key = key.name
        key = (eng == "pool", key)
        if key not in self.dma_map:
            n = sum(1 for kk in self.dma_map if kk[0] == key[0])
            self.dma_map[key] = ("dmaG%d" if key[0] else "dmaS%d") % n
        if slow:
            self.op(eng, lambda e: e.dma_start(out=out, in_=in_, allow_slow_non_contiguous=True), reads, writes, dma=self.dma_map[key])
        else:
            self.op(eng, lambda e: e.dma_start(out=out, in_=in_), reads, writes, dma=self.dma_map[key])

    def sync_dram(self, b):
        return b

    def barrier(self):
        snap = dict(self.cnt)
        sems = self.sems
        for e in self.ENG:
            wl = []
            for k, v in snap.items():
                if v > 0 and self.seen[e].get(k, 0) < v:
                    self.seen[e][k] = v
                    wl.append((k, v))

            def emit(eh, wl=wl):
                for k, v in wl:
                    eh.wait_ge(sems[k], v)

            self.q[e].append(emit)
        self.dma_map = {}
        for b in self.bufs:
            b.w = {}
            b.r = {}
        self.bufs = [b for b in self.bufs if getattr(b, "persist", False)]

    def flush(self, block):
        q = self.q

        @block.tensor
        def _(e):
            for f in q["pe"]:
                f(e)

        @block.scalar
        def _(e):
            for f in q["act"]:
                f(e)

        @block.vector
        def _(e):
            for f in q["dve"]:
                f(e)

        @block.gpsimd
        def _(e):
            for f in q["pool"]:
                f(e)

        @block.sync
        def _(e):
            for f in q["sp"]:
                f(e)

        self.q = {e: [] for e in self.ENG}


class Rot:
    def __init__(self, P, alloc, name, shape, dtype, n):
        self.slots = []
        for i in range(n):
            t = alloc(f"{name}{i}", shape, dtype)
            self.slots.append((t, P.buf(f"{name}{i}")))
        self.i = 0

    def next(self):
        s = self.slots[self.i % len(self.slots)]
        self.i += 1
        return s


class Ctx:
    pass


_PH = [0]


def _phase(P, nc, body):
    _PH[0] += 1
    sfx = "_%d" % _PH[0]
    with ExitStack() as ph, nc.Block() as block:
        tot = [0]

        def sb(name, shape, dtype):
            n = 1
            for d in shape[1:]:
                n *= d
            tot[0] += ((n * (2 if dtype == BF16 else 4) + 31) // 32) * 32
            return ph.enter_context(nc.sbuf_tensor(name + sfx, shape, dtype))

        def ps(name, shape, dtype):
            return ph.enter_context(nc.psum_tensor(name + sfx, shape, dtype))

        body(sb, ps)
        if os.environ.get("KDEBUG"):
            print("phase", sfx, body.__qualname__, "sbuf bytes/partition", tot[0], "instr", P.ninstr)
        assert tot[0] <= 190 * 1024, tot[0]
        P.barrier()
        P.flush(block)


def load_weight_bf16(P, sb, wdst, wbuf, wsrc, K, N, stage_rot, rowscale=None, chunk=1024):
    i = 0
    for k in range(K):
        for c0 in range(0, N, chunk):
            c1 = min(N, c0 + chunk)
            st, stb = stage_rot.next()
            P.dma("sp", st[:, 0:c1 - c0], wsrc[k * 128:(k + 1) * 128, c0:c1], [], [stb], stb)
            eng = "dve" if i % 2 == 0 else "pool"
            if rowscale is None:
                P.op(eng, lambda e, st=st, k=k, c0=c0, c1=c1: e.tensor_copy(out=wdst[:, k, c0:c1], in_=st[:, 0:c1 - c0]), [stb], [wbuf])
            else:
                rs, rsb = rowscale
                P.op(eng, lambda e, st=st, k=k, c0=c0, c1=c1, rs=rs: e.tensor_scalar(out=wdst[:, k, c0:c1], in0=st[:, 0:c1 - c0], scalar1=rs[:, k:k + 1], scalar2=None, op0=ALU.mult), [stb, rsb], [wbuf])
            i += 1


def rstd_from_ssq(P, eng, rstd, ssq, n, R, W):
    P.op(eng, lambda e: e.tensor_scalar(out=rstd, in0=ssq, scalar1=1.0 / n, scalar2=EPS, op0=ALU.mult, op1=ALU.add), R, W)
    P.op("act", lambda e: e.activation(out=rstd, in_=rstd, func=AF.Sqrt), W, W)
    P.op(eng, lambda e: e.reciprocal(out=rstd, in_=rstd), W, W)


def phase_A(P, nc, C, L, x_ap):
    NT = L // 128
    scr = C.scr

    def body(sb, ps):
        ident = sb("identA", [128, 128], BF16)
        identf = sb("identAf", [128, 128], F32)
        bid = P.buf("ident")
        P.dma("sp", identf[:], C.ident[:, :], [], [bid], bid)
        P.op("dve", lambda e: e.tensor_copy(out=ident[:], in_=identf[:]), [bid], [bid])
        ng = sb("ngA", [128, 8], F32)
        qn = sb("qnA", [128, 3], F32)
        kvn = sb("kvnA", [128, 2], F32)
        bsm = P.buf("smallA")
        P.dma("sp", ng[:], C.norm_g[0].rearrange("(k p) -> p k", p=128), [], [bsm], bsm, slow=True)
        P.dma("sp", qn[:], C.mla_q_norm[0].rearrange("(k p) -> p k", p=128), [], [bsm], bsm, slow=True)
        P.dma("sp", kvn[:], C.mla_kv_norm[0].rearrange("(k p) -> p k", p=128), [], [bsm], bsm, slow=True)
        stage = Rot(P, sb, "wstA", [128, 1024], F32, 2)
        w_in = sb("w_inA", [128, 8, 3776], BF16)
        w_q = sb("w_qA", [128, 3, 1536], BF16)
        w_kv = sb("w_kvA", [128, 2, 2048], BF16)
        bw_in, bw_q, bw_kv = P.buf("w_in"), P.buf("w_q"), P.buf("w_kv")
        load_weight_bf16(P, sb, w_in, bw_in, C.ab_w_in[0], 8, 3776, stage, rowscale=(ng, bsm))
        load_weight_bf16(P, sb, w_q, bw_q, C.mla_w_q_up[0], 3, 1536, stage, rowscale=(qn, bsm))
        load_weight_bf16(P, sb, w_kv, bw_kv, C.mla_w_kv_up[0], 2, 2048, stage, rowscale=(kvn, bsm))

        xt = Rot(P, sb, "xtA", [128, 1024], F32, 2)
        cst = Rot(P, sb, "csA", [128, 64], F32, 2)
        junk = sb("junkA", [128, 1024], BF16)
        bjunk = P.buf("junkA")
        stat = Rot(P, sb, "statA", [128, 8], F32, 2)
        hn = Rot(P, sb, "hnA", [128, 1024], BF16, 2)
        hnT = Rot(P, sb, "hnTA", [128, 8, 128], BF16, 2)
        u_sb = Rot(P, sb, "uA", [128, 1024], BF16, 2)
        g_sb = Rot(P, sb, "gA", [128, 2048], BF16, 2)
        lat = Rot(P, sb, "latA", [128, 640], BF16, 2)
        latT = Rot(P, sb, "latTA", [128, int(os.environ.get("LATK", "5")), 128], BF16, 2)
        kr32 = Rot(P, sb, "kr32A", [128, 64], F32, 2)
        q_sb = Rot(P, sb, "qA", [128, 1536], BF16, 2)
        qtmp = Rot(P, sb, "qtmpA", [128, 4, 256], F32, 2)
        kn_sb = Rot(P, sb, "knA", [128, 1024], BF16, 2)
        v_sb = Rot(P, sb, "vA", [128, 1024], BF16, 2)
        kr_sb = Rot(P, sb, "krA", [128, 64], BF16, 2)
        qnT = Rot(P, sb, "qnTA", [128, 8, 256], BF16, 2)
        qrT = Rot(P, sb, "qrTA", [64, 8, 256], BF16, 2)
        knT = Rot(P, sb, "knTA", [128, 8, 256], BF16, 2)
        krT = Rot(P, sb, "krTA", [64, 256], BF16, 2)
        zp = Rot(P, ps, "zpA", [128, 512], F32, 2)
        tp = Rot(P, ps, "tpA", [128, 1024], BF16, 2)
        tq = Rot(P, ps, "tqA", [128, 1024], BF16, 2)
        qp = Rot(P, ps, "qpA", [128, 512], F32, 2)

        def in_proj_group(hT, hTb, c0, c1):
            z, zb = zp.next()
            for k in range(8):
                P.op("pe", lambda e, z=z, k=k: e.matmul(z[:, 0:c1 - c0], lhsT=hT[:, k, :], rhs=w_in[:, k, c0:c1], start=(k == 0), stop=(k == 7)), [hTb, bw_in], [zb])
            return z, zb

        cur = None
        for t in range(NT if CUT > 1 else 0):
            t4 = t % 2
            if t4 == 0:
                cur = (qnT.next(), qrT.next(), knT.next(), krT.next())
            (qnT_t, qnT_b), (qrT_t, qrT_b), (knT_t, knT_b), (krT_t, krT_b) = cur
            x, xb = xt.next()
            cs, csb = cst.next()
            P.dma("sp", x[:], x_ap[t * 128:(t + 1) * 128, :], [], [xb], xb)
            P.dma("sp", cs[:], C.rope_cs[t * 128:(t + 1) * 128, :], [], [csb], csb)
            st, stb = stat.next()
            P.op("act", lambda e, x=x, st=st: e.activation(out=junk[:], in_=x[:], func=AF.Square, accum_out=st[:, 0:1]), [xb], [bjunk, stb])
            rstd_from_ssq(P, "dve", st[:, 1:2], st[:, 0:1], D, [stb], [stb])
            h, hb = hn.next()
            P.op("act", lambda e, x=x, st=st, h=h: e.activation(out=h[:], in_=x[:], func=AF.Copy, scale=st[:, 1:2]), [xb, stb], [hb])
            tpp, tpb = tp.next()
            for k in range(8):
                P.op("pe", lambda e, k=k, tpp=tpp, h=h: e.transpose(out=tpp[:, k * 128:(k + 1) * 128], in_=h[:, k * 128:(k + 1) * 128], identity=ident[:]), [hb, bid], [tpb])
            hT, hTb = hnT.next()
            P.op("act", lambda e, hT=hT, tpp=tpp: e.activation(out=hT[:].rearrange("p k t -> p (k t)"), in_=tpp[:], func=AF.Copy), [tpb], [hTb])
            u, ub = u_sb.next()
            for gi in range(2):
                z, zb = in_proj_group(hT, hTb, gi * 512, gi * 512 + 512)
                P.op("act", lambda e, z=z, u=u, gi=gi: e.activation(out=u[:, gi * 512:(gi + 1) * 512], in_=z[:], func=AF.Copy), [zb], [ub])
            P.dma("pool", scr.U[t * 128:(t + 1) * 128, :], u[:], [ub], [scr.bU], ub)
            if CUT <= 2:
                continue
            la, lab = lat.next()
            z, zb = in_proj_group(hT, hTb, 1024, 1408)
            P.op("act", lambda e, z=z, st=st: e.activation(out=junk[:, 0:384], in_=z[:, 0:384], func=AF.Square, accum_out=st[:, 2:3]), [zb], [bjunk, stb])
            rstd_from_ssq(P, "dve", st[:, 3:4], st[:, 2:3], 384, [stb], [stb])
            P.op("act", lambda e, z=z, st=st, la=la: e.activation(out=la[:, 0:384], in_=z[:, 0:384], func=AF.Copy, scale=st[:, 3:4]), [zb, stb], [lab])
            z, zb = in_proj_group(hT, hTb, 1408, 1728)
            P.op("act", lambda e, z=z, st=st: e.activation(out=junk[:, 0:256], in_=z[:, 0:256], func=AF.Square, accum_out=st[:, 4:5]), [zb], [bjunk, stb])
            rstd_from_ssq(P, "dve", st[:, 5:6], st[:, 4:5], 256, [stb], [stb])
            P.op("act", lambda e, z=z, st=st, la=la: e.activation(out=la[:, 384:640], in_=z[:, 0:256], func=AF.Copy, scale=st[:, 5:6]), [zb, stb], [lab])
            k32, k32b = kr32.next()
            P.op("act", lambda e, z=z, k32=k32: e.activation(out=k32[:], in_=z[:, 256:320], func=AF.Copy), [zb], [k32b])
            kr, krb = kr_sb.next()
            qt, qtb = qtmp.next()
            P.op("pool", lambda e, k32=k32, cs=cs, qt=qt: e.tensor_tensor(out=qt[:, 0, 0:32], in0=k32[:, 0:32], in1=cs[:, 0:32], op=ALU.mult), [k32b, csb], [qtb])
            P.op("pool", lambda e, k32=k32, cs=cs, qt=qt: e.tensor_tensor(out=qt[:, 0, 32:64], in0=k32[:, 32:64], in1=cs[:, 32:64], op=ALU.mult), [k32b, csb], [qtb])
            P.op("pool", lambda e, kr=kr, qt=qt: e.tensor_tensor(out=kr[:, 0:32], in0=qt[:, 0, 0:32], in1=qt[:, 0, 32:64], op=ALU.subtract), [qtb], [krb])
            P.op("pool", lambda e, k32=k32, cs=cs, qt=qt: e.tensor_tensor(out=qt[:, 0, 0:32], in0=k32[:, 0:32], in1=cs[:, 32:64], op=ALU.mult), [k32b, csb, krb], [qtb])
            P.op("pool", lambda e, k32=k32, cs=cs, qt=qt: e.tensor_tensor(out=qt[:, 0, 32:64], in0=k32[:, 32:64], in1=cs[:, 0:32], op=ALU.mult), [k32b, csb], [qtb])
            P.op("pool", lambda e, kr=kr, qt=qt: e.tensor_tensor(out=kr[:, 32:64], in0=qt[:, 0, 0:32], in1=qt[:, 0, 32:64], op=ALU.add), [qtb], [krb])
            if CUT <= 3:
                continue
            g, gb = g_sb.next()
            for gi in range(4):
                z, zb = in_proj_group(hT, hTb, 1728 + gi * 512, 1728 + gi * 512 + 512)
                P.op("act", lambda e, z=z, g=g, gi=gi: e.activation(out=g[:, gi * 512:(gi + 1) * 512], in_=z[:], func=AF.Silu), [zb], [gb])
            P.dma("pool", scr.G[t * 128:(t + 1) * 128, :], g[:], [gb], [scr.bG], gb)
            if CUT <= 4:
                continue
            tqq, tqb = tq.next()
            for k in range(int(os.environ.get("NK", "5"))):
                P.op("pe", lambda e, k=k, tqq=tqq, la=la: e.transpose(out=tqq[:, k * 128:(k + 1) * 128], in_=(h if os.environ.get("SRCH") else la)[:, k * 128:(k + 1) * 128], identity=ident[:]), [lab, bid, hb], [tqb])
            lT, lTb = latT.next()
            if not os.environ.get("NOCOPY"):
                if True:
                    P.op("act", lambda e, lT=lT, tqq=tqq: e.activation(out=lT[:].rearrange("p k t -> p (k t)"), in_=tqq[:, 0:640], func=AF.Copy), [tqb], [lTb])
                else:
                    if os.environ.get("JUNKDST"):
                        P.op("dve", lambda e, lT=lT, tqq=tqq: e.tensor_copy(out=junk[:, 0:CW], in_=tqq[:, 0:CW]), [tqb], [bjunk])
                    elif os.environ.get("TSCOPY"):
                        P.op("dve", lambda e, lT=lT, tqq=tqq: e.tensor_scalar(out=lT[:, 0:CW // 128, :].rearrange("p k t -> p (k t)"), in0=tqq[:, 0:CW], scalar1=1.0, scalar2=None, op0=ALU.mult), [tqb], [lTb])
                    else:
                        P.op("dve", lambda e, lT=lT, tqq=tqq: e.tensor_copy(out=lT[:, 0:CW // 128, :].rearrange("p k t -> p (k t)"), in_=tqq[:, 0:CW]), [tqb], [lTb])
            if CUT <= 4.1:
                continue
            q, qb = q_sb.next()
            for gi in range(3):
                pq, pqb = qp.next()
                for k in range(3):
                    P.op("pe", lambda e, pq=pq, k=k, gi=gi, lT=lT: e.matmul(pq[:], lhsT=lT[:, k, :], rhs=w_q[:, k, gi * 512:(gi + 1) * 512], start=(k == 0), stop=(k == 2)), [lTb, bw_q], [pqb])
                P.op("act", lambda e, pq=pq, q=q, gi=gi: e.activation(out=q[:, gi * 512:(gi + 1) * 512], in_=pq[:], func=AF.Copy), [pqb], [qb])
            if CUT <= 4.3:
                continue
            qv = q[:].rearrange("p (h d) -> p h d", h=8)
            x1 = qv[:, :, 128:160]
            x2 = qv[:, :, 160:192]
            cosb = cs[:, 0:32].unsqueeze(1).to_broadcast([128, 8, 32])
            sinb = cs[:, 32:64].unsqueeze(1).to_broadcast([128, 8, 32])
            qt, qtb = qtmp.next()
            a_ = qt[:, 0, :].rearrange("p (h d) -> p h d", h=8)
            b_ = qt[:, 1, :].rearrange("p (h d) -> p h d", h=8)
            c_ = qt[:, 2, :].rearrange("p (h d) -> p h d", h=8)
            d_ = qt[:, 3, :].rearrange("p (h d) -> p h d", h=8)
            P.op("pool", lambda e, a_=a_, x1=x1, cosb=cosb: e.tensor_tensor(out=a_, in0=x1, in1=cosb, op=ALU.mult), [qb, csb], [qtb])
            P.op("pool", lambda e, b_=b_, x2=x2, sinb=sinb: e.tensor_tensor(out=b_, in0=x2, in1=sinb, op=ALU.mult), [qb, csb], [qtb])
            P.op("pool", lambda e, c_=c_, x1=x1, sinb=sinb: e.tensor_tensor(out=c_, in0=x1, in1=sinb, op=ALU.mult), [qb, csb], [qtb])
            P.op("pool", lambda e, d_=d_, x2=x2, cosb=cosb: e.tensor_tensor(out=d_, in0=x2, in1=cosb, op=ALU.mult), [qb, csb], [qtb])
            P.op("pool", lambda e, a_=a_, b_=b_, x1=x1: e.tensor_tensor(out=x1, in0=a_, in1=b_, op=ALU.subtract), [qtb], [qb])
            P.op("pool", lambda e, c_=c_, d_=d_, x2=x2: e.tensor_tensor(out=x2, in0=c_, in1=d_, op=ALU.add), [qtb], [qb])
            if CUT <= 4.6:
                continue
            kn, knb = kn_sb.next()
            v, vb = v_sb.next()
            for gi in range(4):
                pq, pqb = qp.next()
                for k in range(2):
                    P.op("pe", lambda e, pq=pq, k=k, gi=gi, lT=lT: e.matmul(pq[:], lhsT=lT[:, 3 + k, :], rhs=w_kv[:, k, gi * 512:(gi + 1) * 512], start=(k == 0), stop=(k == 1)), [lTb, bw_kv], [pqb])
                pv = pq[:].rearrange("p (h d) -> p h d", h=2)
                P.op("act", lambda e, pv=pv, kn=kn, gi=gi: e.activation(out=kn[:, gi * 256:(gi + 1) * 256].rearrange("p (h d) -> p h d", h=2), in_=pv[:, :, 0:128], func=AF.Copy), [pqb], [knb])
                if os.environ.get("VMODE", "act") == "dve":
                    P.op("dve", lambda e, pv=pv, v=v, gi=gi: e.tensor_copy(out=v[:, gi * 256:(gi + 1) * 256].rearrange("p (h d) -> p h d", h=2), in_=pv[:, :, 128:256]), [pqb], [vb])
                else:
                    P.op("act", lambda e, pv=pv, v=v, gi=gi: e.activation(out=v[:, gi * 256:(gi + 1) * 256].rearrange("p (h d) -> p h d", h=2), in_=pv[:, :, 128:256], func=AF.Copy), [pqb], [vb])
            if os.environ.get("VMODE", "act") != "none":
                P.dma("pool", scr.V[t * 128:(t + 1) * 128, :], v[:], [vb], [scr.bV], vb)
            if CUT <= 5:
                continue
            tqq, tqb = tq.next()
            for hh in range(8):
                P.op("pe", lambda e, hh=hh, tqq=tqq, q=q: e.transpose(out=tqq[:, hh * 128:(hh + 1) * 128], in_=q[:, hh * 192:hh * 192 + 128], identity=ident[:]), [qb, bid], [tqb])
            P.op("act", lambda e, tqq=tqq, qnT_t=qnT_t, t4=t4: e.activation(out=qnT_t[:, :, t4 * 128:(t4 + 1) * 128], in_=tqq[:].rearrange("p (h t) -> p h t", h=8), func=AF.Copy), [tqb], [qnT_b])
            tqq, tqb = tq.next()
            for hh in range(8):
                P.op("pe", lambda e, hh=hh, tqq=tqq, q=q: e.transpose(out=tqq[0:64, hh * 128:(hh + 1) * 128], in_=q[:, hh * 192 + 128:hh * 192 + 192], identity=ident[:]), [qb, bid], [tqb])
            P.op("act", lambda e, tqq=tqq, qrT_t=qrT_t, t4=t4: e.activation(out=qrT_t[:, :, t4 * 128:(t4 + 1) * 128], in_=tqq[0:64, :].rearrange("p (h t) -> p h t", h=8), func=AF.Copy), [tqb], [qrT_b])
            tqq, tqb = tq.next()
            for hh in range(8):
                P.op("pe", lambda e, hh=hh, tqq=tqq, kn=kn: e.transpose(out=tqq[:, hh * 128:(hh + 1) * 128], in_=kn[:, hh * 128:(hh + 1) * 128], identity=ident[:]), [knb, bid], [tqb])
            P.op("act", lambda e, tqq=tqq, knT_t=knT_t, t4=t4: e.activation(out=knT_t[:, :, t4 * 128:(t4 + 1) * 128], in_=tqq[:].rearrange("p (h t) -> p h t", h=8), func=AF.Copy), [tqb], [knT_b])
            tqq, tqb = tq.next()
            P.op("pe", lambda e, tqq=tqq, kr=kr: e.transpose(out=tqq[0:64, 0:128], in_=kr[:, 0:64], identity=ident[:]), [krb, bid], [tqb])
            P.op("act", lambda e, tqq=tqq, krT_t=krT_t, t4=t4: e.activation(out=krT_t[:, t4 * 128:(t4 + 1) * 128], in_=tqq[0:64, 0:128], func=AF.Copy), [tqb], [krT_b])
            if t4 == 1 or t == NT - 1:
                nt = (t4 + 1) * 128
                t0 = (t - t4) * 128
                P.dma("pool", scr.QN[:, :, t0:t0 + nt].rearrange("h d t -> d h t"), qnT_t[:, :, 0:nt], [qnT_b], [scr.bQ], qnT_b)
                P.dma("pool", scr.QR[:, :, t0:t0 + nt].rearrange("h d t -> d h t"), qrT_t[:, :, 0:nt], [qrT_b], [scr.bQ], qrT_b)
                P.dma("pool", scr.KN[:, :, t0:t0 + nt].rearrange("h d t -> d h t"), knT_t[:, :, 0:nt], [knT_b], [scr.bK], knT_b)
                P.dma("pool", scr.KR[:, t0:t0 + nt], krT_t[:, 0:nt], [krT_b], [scr.bK], krT_b)

    _phase(P, nc, body)


def phase_B(P, nc, C, L):
    NKB = L // 128
    NQB = L // 512
    scr = C.scr
    scale = 192.0 ** -0.5

    def body(sb, ps):
        krt = sb("krB", [64, L], BF16)
        bkr = P.buf("krB")
        P.dma("sp", krt[:], scr.KR[:, 0:L], [], [bkr], bkr)
        qn = Rot(P, sb, "qnB", [128, L], BF16, 2)
        qr = Rot(P, sb, "qrB", [64, L], BF16, 2)
        kn = Rot(P, sb, "knB", [128, L], BF16, 2)
        vt = Rot(P, sb, "vtB", [128, NKB, 129], BF16, 2)
        for (v, vb) in vt.slots:
            P.op("pool", lambda e, v=v: e.memset(v[:, :, 128:129], 1.0), [], [vb])
        pT = Rot(P, sb, "pTB", [128, 512], BF16, 4)
        osb = Rot(P, sb, "osbB", [128, 4, 128], BF16, 2)
        rc = Rot(P, sb, "rcB", [128, 4], F32, 2)
        sp_ = Rot(P, ps, "spB", [128, 512], F32, 4)
        oa = Rot(P, ps, "oaB", [128, 512], F32, 2)
        ob = Rot(P, ps, "obB", [128, 512], F32, 2)
        for h in range(8):
            qn_t, qn_b = qn.next()
            qr_t, qr_b = qr.next()
            kn_t, kn_b = kn.next()
            v_t, v_b = vt.next()
            P.dma("sp", qn_t[:], scr.QN[h, :, 0:L], [], [qn_b], qn_b)
            P.dma("sp", qr_t[:], scr.QR[h, :, 0:L], [], [qr_b], qr_b)
            P.dma("sp", kn_t[:], scr.KN[h, :, 0:L], [], [kn_b], kn_b)
            P.dma("sp", v_t[:, :, 0:128], scr.V[0:L, h * 128:(h + 1) * 128].rearrange("(kb p) d -> p kb d", p=128), [], [v_b], v_b)
            for qb in range(NQB):
                oa_t, oa_b = oa.next()
                ob_t, ob_b = ob.next()
                def emit_qk(kb, qb=qb, kn_t=kn_t, qn_t=qn_t, qr_t=qr_t, kn_b=kn_b, qn_b=qn_b, qr_b=qr_b):
                    s_t, s_b = sp_.next()
                    P.op("pe", lambda e, s_t=s_t, kb=kb: e.matmul(s_t[:], lhsT=kn_t[:, kb * 128:(kb + 1) * 128], rhs=qn_t[:, qb * 512:(qb + 1) * 512], start=True, stop=False), [kn_b, qn_b], [s_b])
                    P.op("pe", lambda e, s_t=s_t, kb=kb: e.matmul(s_t[:], lhsT=krt[:, kb * 128:(kb + 1) * 128], rhs=qr_t[:, qb * 512:(qb + 1) * 512], start=False, stop=True), [bkr, qr_b], [s_b])
                    return s_t, s_b

                LA = 2
                pend = [emit_qk(kb) for kb in range(min(LA, NKB))]
                for kb in range(NKB):
                    if kb + LA < NKB:
                        pend.append(emit_qk(kb + LA))
                    s_t, s_b = pend.pop(0)
                    p_t, p_b = pT.next()
                    P.op("act", lambda e, s_t=s_t, p_t=p_t: e.activation(out=p_t[:], in_=s_t[:], func=AF.Exp, scale=scale), [s_b], [p_b])
                    for sub in range(4):
                        acc_t, acc_b = (oa_t, oa_b) if sub < 2 else (ob_t, ob_b)
                        c0 = (sub % 2) * 129
                        P.op("pe", lambda e, acc_t=acc_t, c0=c0, p_t=p_t, sub=sub, v_t=v_t, kb=kb: e.matmul(acc_t[:, c0:c0 + 129], lhsT=p_t[:, sub * 128:(sub + 1) * 128], rhs=v_t[:, kb, :], start=(kb == 0), stop=(kb == NKB - 1), skip_group_check=True), [p_b, v_b], [acc_b])
                o_t, o_b = osb.next()
                r_t, r_b = rc.next()
                for sub in range(4):
                    acc_t, acc_b = (oa_t, oa_b) if sub < 2 else (ob_t, ob_b)
                    c0 = (sub % 2) * 129
                    P.op("act", lambda e, r_t=r_t, acc_t=acc_t, c0=c0, sub=sub: e.activation(out=r_t[:, sub:sub + 1], in_=acc_t[:, c0 + 128:c0 + 129], func=AF.Copy), [acc_b], [r_b])
                    P.op("dve", lambda e, r_t=r_t, sub=sub: e.reciprocal(out=r_t[:, sub:sub + 1], in_=r_t[:, sub:sub + 1]), [r_b], [r_b])
                    P.op("act", lambda e, r_t=r_t, acc_t=acc_t, c0=c0, sub=sub, o_t=o_t: e.activation(out=o_t[:, sub, :], in_=acc_t[:, c0:c0 + 128], func=AF.Copy, scale=r_t[:, sub:sub + 1]), [acc_b, r_b], [o_b])
                P.dma("pool", scr.O[qb * 512:(qb + 1) * 512, h * 128:(h + 1) * 128].rearrange("(s p) d -> p s d", p=128), o_t[:], [o_b], [scr.bO], o_b)

    _phase(P, nc, body)


def phase_C(P, nc, C, L, layer, h_in, p_ap, w_out_ap, h_out, final_out, use_s5, rs_src=None):
    NT = L // 128
    scr = C.scr

    def body(sb, ps):
        ident = sb("identC", [128, 128], BF16)
        identf = sb("identCf", [128, 128], F32)
        bid = P.buf("identC")
        P.dma("sp", identf[:], C.ident[:, :], [], [bid], bid)
        P.op("dve", lambda e: e.tensor_copy(out=ident[:], in_=identf[:]), [bid], [bid])
        stage = Rot(P, sb, "wstC", [128, 1024], F32, 2)
        w_out = sb("w_outC", [128, 16, 1024], BF16)
        w_pg = sb("w_pgC", [128, 8, 1024], BF16)
        w_pl = sb("w_plC", [128, 2, 1024], BF16)
        bw_out, bw_pg, bw_pl = P.buf("w_outC"), P.buf("w_pgC"), P.buf("w_plC")
        load_weight_bf16(P, sb, w_out, bw_out, w_out_ap, 16, 1024, stage)
        load_weight_bf16(P, sb, w_pg, bw_pg, C.ple_gate_w[layer], 8, 1024, stage)
        load_weight_bf16(P, sb, w_pl, bw_pl, C.ple_w[layer], 2, 1024, stage)
        if final_out is not None:
            fg = sb("fgC", [128, 1024], F32)
            bfg = P.buf("fgC")
            P.dma("sp", fg[:], C.final_g.partition_broadcast(128), [], [bfg], bfg, slow=True)
        ht = Rot(P, sb, "htC", [128, 1024], F32, 2)
        if layer == 0:
            gt = Rot(P, sb, "gtC", [128, 2048], BF16, 2)
            yt = Rot(P, sb, "ytC", [128, 2048], BF16, 2)
        else:
            cvt = Rot(P, sb, "cvtC", [128, 2048], BF16, 2)
            vxt = Rot(P, sb, "vxtC", [128, 2048], BF16, 2)
            xgt = Rot(P, sb, "xgtC", [128, 2048], BF16, 2)
            rsr = sb("rsrC", [128, 2048], F32)
            bir = sb("birC", [128, 2048], F32)
            brr = P.buf("rsrC")
            P.dma("sp", rsr[:], rs_src[0].partition_broadcast(128), [], [brr], brr, slow=True)
            P.dma("sp", bir[:], C.hy_bias[0].partition_broadcast(128), [], [brr], brr, slow=True)
            tm1 = sb("tm1C", [128, 2048], F32)
            tm2 = sb("tm2C", [128, 2048], F32)
            btm1, btm2 = P.buf("tm1C"), P.buf("tm2C")
        pt = Rot(P, sb, "ptC", [128, 256], F32, 2)
        pb16 = Rot(P, sb, "pb16C", [128, 256], BF16, 2)
        mt = Rot(P, sb, "mtC", [128, 2048], BF16, 2)
        mT = Rot(P, sb, "mTC", [128, 16, 128], BF16, 2)
        h2 = Rot(P, sb, "h2C", [128, 1024], F32, 2)
        h2b = Rot(P, sb, "h2bC", [128, 1024], BF16, 2)
        h2T = Rot(P, sb, "h2TC", [128, 8, 128], BF16, 2)
        pT = Rot(P, sb, "pTC", [128, 2, 128], BF16, 2)
        sg = Rot(P, sb, "sgC", [128, 1024], F32, 2)
        h3 = Rot(P, sb, "h3C", [128, 1024], F32, 2)
        stat = Rot(P, sb, "statC", [128, 4], F32, 2)
        junk = sb("junkC", [128, 1024], BF16)
        bjunk = P.buf("junkC")
        tp = Rot(P, ps, "tpC", [128, 1024], BF16, 2)
        zp = Rot(P, ps, "zpC", [128, 512], F32, 4)
        for t in range(NT):
            rows = slice(t * 128, (t + 1) * 128)
            h_t, h_b = ht.next()
            if layer == 0:
                g_t, g_b = gt.next()
                y_t, y_b = yt.next()
            p_t, p_b = pt.next()
            P.dma("sp", h_t[:], h_in[rows, :], [], [h_b], h_b)
            P.dma("sp", p_t[:], p_ap[rows, :], [], [p_b], p_b)
            if layer == 0:
                P.dma("sp", g_t[:], scr.G[rows, :], [], [g_b], g_b)
                if use_s5:
                    P.dma("sp", y_t[:, 0:1024], scr.YA[rows, :], [], [y_b], y_b)
                else:
                    P.op("pool", lambda e, y_t=y_t: e.memset(y_t[:, 0:1024], 0.0), [], [y_b])
                P.dma("sp", y_t[:, 1024:2048], scr.O[rows, :], [], [y_b], y_b)
            m_t, m_b = mt.next()
            mT_t, mT_b = mT.next()
            if layer == 0:
                P.op("dve", lambda e, m_t=m_t, y_t=y_t, g_t=g_t: e.tensor_tensor(out=m_t[:], in0=y_t[:], in1=g_t[:], op=ALU.mult), [y_b, g_b], [m_b])
            else:
                cv_t, cv_b = cvt.next()
                vx_t, vx_b = vxt.next()
                xg_t, xg_b = xgt.next()
                P.dma("sp", cv_t[:], scr.CV[rows, :], [], [cv_b], cv_b)
                P.dma("sp", vx_t[:], scr.VX[rows, :], [], [vx_b], vx_b)
                P.dma("sp", xg_t[:], scr.XG[rows, :], [], [xg_b], xg_b)
                P.op("pool", lambda e, cv_t=cv_t: e.tensor_tensor(out=tm1[:], in0=cv_t[:], in1=rsr[:], op=ALU.mult), [cv_b, brr], [btm1])
                P.op("dve", lambda e, vx_t=vx_t: e.tensor_tensor(out=tm2[:], in0=vx_t[:], in1=bir[:], op=ALU.mult), [vx_b, brr], [btm2])
                P.op("pool", lambda e: e.tensor_tensor(out=tm1[:], in0=tm1[:], in1=tm2[:], op=ALU.add), [btm1, btm2], [btm1])
                P.op("dve", lambda e, m_t=m_t, xg_t=xg_t: e.tensor_tensor(out=m_t[:], in0=tm1[:], in1=xg_t[:], op=ALU.mult), [btm1, xg_b], [m_b])
            for half in range(2):
                tpp, tpb = tp.next()
                for k in range(8):
                    kk = half * 8 + k
                    P.op("pe", lambda e, tpp=tpp, k=k, kk=kk, m_t=m_t: e.transpose(out=tpp[:, k * 128:(k + 1) * 128], in_=m_t[:, kk * 128:(kk + 1) * 128], identity=ident[:]), [m_b, bid], [tpb])
                P.op("act", lambda e, tpp=tpp, mT_t=mT_t, half=half: e.activation(out=mT_t[:, half * 8:(half + 1) * 8, :].rearrange("p k t -> p (k t)"), in_=tpp[:], func=AF.Copy), [tpb], [mT_b])
            h2_t, h2_b = h2.next()
            for gi in range(2):
                z, zb = zp.next()
                for k in range(16):
                    P.op("pe", lambda e, z=z, k=k, gi=gi, mT_t=mT_t: e.matmul(z[:], lhsT=mT_t[:, k, :], rhs=w_out[:, k, gi * 512:(gi + 1) * 512], start=(k == 0), stop=(k == 15)), [mT_b, bw_out], [zb])
                P.op("act", lambda e, z=z, gi=gi, h2_t=h2_t: e.activation(out=h2_t[:, gi * 512:(gi + 1) * 512], in_=z[:], func=AF.Copy), [zb], [h2_b])
                P.op("pool", lambda e, gi=gi, h2_t=h2_t, h_t=h_t: e.tensor_tensor(out=h2_t[:, gi * 512:(gi + 1) * 512], in0=h2_t[:, gi * 512:(gi + 1) * 512], in1=h_t[:, gi * 512:(gi + 1) * 512], op=ALU.add), [h2_b, h_b], [h2_b])
            hb_t, hb_b = h2b.next()
            P.op("act", lambda e, hb_t=hb_t, h2_t=h2_t: e.activation(out=hb_t[:], in_=h2_t[:], func=AF.Copy), [h2_b], [hb_b])
            tpp, tpb = tp.next()
            for k in range(8):
                P.op("pe", lambda e, tpp=tpp, k=k, hb_t=hb_t: e.transpose(out=tpp[:, k * 128:(k + 1) * 128], in_=hb_t[:, k * 128:(k + 1) * 128], identity=ident[:]), [hb_b, bid], [tpb])
            hT_t, hT_b = h2T.next()
            P.op("act", lambda e, tpp=tpp, hT_t=hT_t: e.activation(out=hT_t[:].rearrange("p k t -> p (k t)"), in_=tpp[:], func=AF.Copy), [tpb], [hT_b])
            pb_t, pb_b = pb16.next()
            P.op("act", lambda e, pb_t=pb_t, p_t=p_t: e.activation(out=pb_t[:], in_=p_t[:], func=AF.Copy), [p_b], [pb_b])
            tpp, tpb = tp.next()
            for k in range(2):
                P.op("pe", lambda e, tpp=tpp, k=k, pb_t=pb_t: e.transpose(out=tpp[:, k * 128:(k + 1) * 128], in_=pb_t[:, k * 128:(k + 1) * 128], identity=ident[:]), [pb_b, bid], [tpb])
            pT_t, pT_b = pT.next()
            P.op("act", lambda e, tpp=tpp, pT_t=pT_t: e.activation(out=pT_t[:].rearrange("p k t -> p (k t)"), in_=tpp[:, 0:256], func=AF.Copy), [tpb], [pT_b])
            sg_t, sg_b = sg.next()
            h3_t, h3_b = h3.next()
            for gi in range(2):
                z, zb = zp.next()
                for k in range(8):
                    P.op("pe", lambda e, z=z, k=k, gi=gi, hT_t=hT_t: e.matmul(z[:], lhsT=hT_t[:, k, :], rhs=w_pg[:, k, gi * 512:(gi + 1) * 512], start=(k == 0), stop=(k == 7)), [hT_b, bw_pg], [zb])
                P.op("act", lambda e, z=z, gi=gi, sg_t=sg_t: e.activation(out=sg_t[:, gi * 512:(gi + 1) * 512], in_=z[:], func=AF.Sigmoid), [zb], [sg_b])
                z2, z2b = zp.next()
                for k in range(2):
                    P.op("pe", lambda e, z2=z2, k=k, gi=gi, pT_t=pT_t: e.matmul(z2[:], lhsT=pT_t[:, k, :], rhs=w_pl[:, k, gi * 512:(gi + 1) * 512], start=(k == 0), stop=(k == 1)), [pT_b, bw_pl], [z2b])
                P.op("act", lambda e, z2=z2, gi=gi, h3_t=h3_t: e.activation(out=h3_t[:, gi * 512:(gi + 1) * 512], in_=z2[:], func=AF.Copy), [z2b], [h3_b])
                P.op("dve", lambda e, gi=gi, sg_t=sg_t, h3_t=h3_t: e.tensor_tensor(out=sg_t[:, gi * 512:(gi + 1) * 512], in0=sg_t[:, gi * 512:(gi + 1) * 512], in1=h3_t[:, gi * 512:(gi + 1) * 512], op=ALU.mult), [h3_b, sg_b], [sg_b])
            P.op("pool", lambda e, h3_t=h3_t, sg_t=sg_t, h2_t=h2_t: e.tensor_tensor(out=h3_t[:], in0=sg_t[:], in1=h2_t[:], op=ALU.add), [sg_b, h2_b], [h3_b])
            if final_out is None:
                P.dma("pool", h_out[rows, :], h3_t[:], [h3_b], [scr.bH], h3_b)
            else:
                st, stb = stat.next()
                P.op("act", lambda e, h3_t=h3_t, st=st: e.activation(out=junk[:], in_=h3_t[:], func=AF.Square, accum_out=st[:, 0:1]), [h3_b], [bjunk, stb])
                rstd_from_ssq(P, "dve", st[:, 1:2], st[:, 0:1], D, [stb], [stb])
                P.op("dve", lambda e, h3_t=h3_t, st=st, sg_t=sg_t: e.scalar_tensor_tensor(out=sg_t[:], in0=h3_t[:], scalar=st[:, 1:2], in1=fg[:], op0=ALU.mult, op1=ALU.mult), [h3_b, stb, bfg], [sg_b])
                P.dma("pool", final_out[rows, :], sg_t[:], [sg_b], [scr.bH], sg_b)

    _phase(P, nc, body)


def phase_D1(P, nc, C, L):
    NT = L // 128
    scr = C.scr

    def body(sb, ps):
        ident = sb("identD", [128, 128], BF16)
        identf = sb("identDf", [128, 128], F32)
        bid = P.buf("identD")
        P.dma("sp", identf[:], C.ident[:, :], [], [bid], bid)
        P.op("dve", lambda e: e.tensor_copy(out=ident[:], in_=identf[:]), [bid], [bid])
        zc = sb("zcD", [128, 8, 2], BF16)
        bzc = P.buf("zcD")
        P.op("pool", lambda e: e.memset(zc[:], 0.0), [], [bzc])
        P.dma("pool", scr.HT[:, :, 0:1].rearrange("k f t -> f k t"), zc[:, :, 0:1], [bzc], [scr.bH], bzc, slow=True)
        P.dma("pool", scr.HT[:, :, L + 1:L + 2].rearrange("k f t -> f k t"), zc[:, :, 1:2], [bzc], [scr.bH], bzc, slow=True)
        xt = Rot(P, sb, "xtD", [128, 1024], F32, 2)
        junk = sb("junkD", [128, 1024], BF16)
        bjunk = P.buf("junkD")
        stat = Rot(P, sb, "statD", [128, 4], F32, 2)
        hn = Rot(P, sb, "hnD", [128, 1024], BF16, 2)
        hT4 = Rot(P, sb, "hT4D", [128, 8, 512], BF16, 2)
        tp = Rot(P, ps, "tpD", [128, 1024], BF16, 2)
        cur = None
        for t in range(NT):
            t4 = t % 4
            if t4 == 0:
                cur = hT4.next()
            h4, h4b = cur
            x, xb = xt.next()
            P.dma("sp", x[:], scr.H1[t * 128:(t + 1) * 128, :], [], [xb], xb)
            st, stb = stat.next()
            P.op("act", lambda e, x=x, st=st: e.activation(out=junk[:], in_=x[:], func=AF.Square, accum_out=st[:, 0:1]), [xb], [bjunk, stb])
            rstd_from_ssq(P, "dve", st[:, 1:2], st[:, 0:1], D, [stb], [stb])
            h, hb = hn.next()
            P.op("act", lambda e, x=x, st=st, h=h: e.activation(out=h[:], in_=x[:], func=AF.Copy, scale=st[:, 1:2]), [xb, stb], [hb])
            tpp, tpb = tp.next()
            for k in range(8):
                P.op("pe", lambda e, k=k, tpp=tpp, h=h: e.transpose(out=tpp[:, k * 128:(k + 1) * 128], in_=h[:, k * 128:(k + 1) * 128], identity=ident[:]), [hb, bid], [tpb])
            P.op("act", lambda e, tpp=tpp, h4=h4, t4=t4: e.activation(out=h4[:, :, t4 * 128:(t4 + 1) * 128], in_=tpp[:].rearrange("p (k t) -> p k t", k=8), func=AF.Copy), [tpb], [h4b])
            if t4 == 3 or t == NT - 1:
                nt = (t4 + 1) * 128
                t0 = (t - t4) * 128
                P.dma("pool", scr.HT[:, :, 1 + t0:1 + t0 + nt].rearrange("k f t -> f k t"), h4[:, :, 0:nt], [h4b], [scr.bH], h4b)

    _phase(P, nc, body)


def phase_D2(P, nc, C, L):
    scr = C.scr
    TB = 256
    NB = L // TB

    def body(sb, ps):
        ng = sb("ngD", [128, 8], F32)
        bsm = P.buf("smallD")
        P.dma("sp", ng[:], C.norm_g[1].rearrange("(k p) -> p k", p=128), [], [bsm], bsm, slow=True)
        cw = sb("cwD", [128, 3, 48], F32)
        cb = sb("cbD", [128, 48], F32)
        hb_ = sb("hbD", [128, 16], F32)
        P.dma("sp", cw[:], C.hy_conv_w[0].rearrange("j (i p) -> p j i", p=128), [], [bsm], bsm, slow=True)
        P.dma("sp", cb[:], C.hy_conv_b[0].rearrange("(i p) -> p i", p=128), [], [bsm], bsm, slow=True)
        P.dma("sp", hb_[:], C.hy_bias[0].rearrange("(i p) -> p i", p=128), [], [bsm], bsm, slow=True)
        stage = Rot(P, sb, "wstD", [128, 1024], F32, 2)
        w_in = sb("w_inD", [128, 8, 8192], BF16)
        bw_in = P.buf("w_inD")
        load_weight_bf16(P, sb, w_in, bw_in, C.hy_w_in[0], 8, 8192, stage, rowscale=(ng, bsm))
        hT = Rot(P, sb, "hTD", [128, 8, 258], BF16, 2)
        zs = Rot(P, sb, "zsD", [128, 258], F32, 4)
        uc = Rot(P, sb, "ucD", [128, 3, 256], F32, 2)
        sg = Rot(P, sb, "sgD", [128, 256], F32, 2)
        ident = sb("identD2", [128, 128], BF16)
        identf = sb("identD2f", [128, 128], F32)
        bid = P.buf("identD2")
        P.dma("sp", identf[:], C.ident[:, :], [], [bid], bid)
        P.op("dve", lambda e: e.tensor_copy(out=ident[:], in_=identf[:]), [bid], [bid])
        mo = Rot(P, sb, "moD", [128, 2, 256], BF16, 3)
        stg2 = Rot(P, sb, "stg2D", [128, 2, 2, 2048], BF16, 1)
        zp = Rot(P, ps, "zpD", [128, 512], F32, 4)
        tp = Rot(P, ps, "tpD2", [128, 1024], BF16, 2)
        def do_block(b):
            s0 = b * TB
            n_out = min(TB, L - s0)
            n_in = n_out + 2
            h_t, h_b = hT.next()
            st2, st2b = stg2.next()
            P.dma("sp", h_t[:, :, 0:n_in], scr.HT[:, :, s0:s0 + n_in].rearrange("k f t -> f k t"), [], [h_b], h_b)
            for i in range(16):
                u_t, u_b = uc.next()
                for part in range(4):
                    ch = part * 16 + i
                    z, zb = zp.next()
                    for k in range(8):
                        P.op("pe", lambda e, z=z, k=k, ch=ch, h_t=h_t: e.matmul(z[:, 0:n_in], lhsT=w_in[:, k, ch * 128:(ch + 1) * 128], rhs=h_t[:, k, 0:n_in], start=(k == 0), stop=(k == 7)), [h_b, bw_in], [zb])
                    if part < 3:
                        zs_t, zs_b = zs.next()
                        P.op("act", lambda e, z=z, zs_t=zs_t: e.activation(out=zs_t[:, 0:n_in], in_=z[:, 0:n_in], func=AF.Copy), [zb], [zs_b])
                        eng = "pool" if part != 1 else "dve"
                        P.op(eng, lambda e, zs_t=zs_t, u_t=u_t, part=part, ch=ch: e.tensor_scalar(out=u_t[:, part, 0:n_out], in0=zs_t[:, 0:n_out], scalar1=cw[:, 0, ch:ch + 1], scalar2=cb[:, ch:ch + 1], op0=ALU.mult, op1=ALU.add), [zs_b, bsm], [u_b])
                        P.op("dve", lambda e, zs_t=zs_t, u_t=u_t, part=part, ch=ch: e.scalar_tensor_tensor(out=u_t[:, part, 0:n_out], in0=zs_t[:, 1:1 + n_out], scalar=cw[:, 1, ch:ch + 1], in1=u_t[:, part, 0:n_out], op0=ALU.mult, op1=ALU.add), [zs_b, bsm, u_b], [u_b])
                        P.op("dve", lambda e, zs_t=zs_t, u_t=u_t, part=part, ch=ch: e.scalar_tensor_tensor(out=u_t[:, part, 0:n_out], in0=zs_t[:, 2:2 + n_out], scalar=cw[:, 2, ch:ch + 1], in1=u_t[:, part, 0:n_out], op0=ALU.mult, op1=ALU.add), [zs_b, bsm, u_b], [u_b])
                    else:
                        sg_t, sg_b = sg.next()
                        P.op("act", lambda e, z=z, sg_t=sg_t: e.activation(out=sg_t[:, 0:n_out], in_=z[:, 1:1 + n_out], func=AF.Silu), [zb], [sg_b])
                m_t, m_b = mo.next()
                P.op("pool", lambda e, u_t=u_t, m_t=m_t: e.tensor_tensor(out=m_t[:, 0, :], in0=u_t[:, 2, 0:n_out], in1=u_t[:, 1, 0:n_out], op=ALU.mult), [u_b], [m_b])
                P.op("pool", lambda e, u_t=u_t, sg_t=sg_t, m_t=m_t: e.tensor_tensor(out=m_t[:, 1, :], in0=u_t[:, 0, 0:n_out], in1=sg_t[:, 0:n_out], op=ALU.mult), [u_b, sg_b], [m_b])
                tpp, tpb = tp.next()
                for q in range(2):
                    for tl_ in range(2):
                        P.op("pe", lambda e, tpp=tpp, q=q, tl_=tl_, m_t=m_t: e.transpose(out=tpp[:, (q * 2 + tl_) * 128:(q * 2 + tl_ + 1) * 128], in_=m_t[:, q, tl_ * 128:(tl_ + 1) * 128], identity=ident[:]), [m_b, bid], [tpb])
                P.op("act", lambda e, tpp=tpp, st2=st2, i=i: e.activation(out=st2[:, :, :, i * 128:(i + 1) * 128], in_=tpp[:, 0:512].rearrange("p (q t c) -> p q t c", q=2, t=2), func=AF.Copy), [tpb], [st2b])
            for tl_ in range(2):
                r0 = s0 + tl_ * 128
                P.dma("sp", scr.VX[r0:r0 + 128, :], st2[:, 0, tl_, :], [st2b], [scr.bV], st2b)
                P.dma("sp", scr.XG[r0:r0 + 128, :], st2[:, 1, tl_, :], [st2b], [scr.bG], st2b)

        for b in range(NB):
            do_block(b)

    _phase(P, nc, body)


def fft_tables(L):
    N = 2 * L
    N1 = N // 128
    KL = L // 128
    k = np.arange(N1)[:, None].astype(np.float64)
    f1 = np.arange(N1)[None, :].astype(np.float64)
    ang1 = 2 * np.pi * k * f1 / N1
    p = np.arange(128).astype(np.float64)
    f2 = np.arange(128).astype(np.float64)
    f1v = np.arange(N1).astype(np.float64)
    angE = 2 * np.pi * p[:, None, None] * (f1v[None, :, None] + N1 * f2[None, None, :]) / N
    angE2 = 2 * np.pi * p[None, None, :] * (f1v[None, :, None] + N1 * f2[:, None, None]) / N
    t = {}
    t["s1c"] = np.cos(ang1)
    t["s1s"] = -np.sin(ang1)
    t["ec"] = np.cos(angE).reshape(128, N1 * 128)
    t["es"] = np.sin(angE).reshape(128, N1 * 128)
    t["e2c"] = np.cos(angE2).reshape(128, N1 * 128)
    t["e2s"] = np.sin(angE2).reshape(128, N1 * 128)
    t["i1c"] = np.cos(ang1).T[:, :KL] / N
    t["i1s"] = -np.sin(ang1).T[:, :KL] / N
    return {k_: np.ascontiguousarray(v.astype(np.float32)) for k_, v in t.items()}, N1, KL


def fft_tables_shapes(L):
    N1 = 2 * L // 128
    KL = L // 128
    return {"s1c": (N1, N1), "s1s": (N1, N1), "ec": (128, N1 * 128), "es": (128, N1 * 128),
            "e2c": (128, N1 * 128), "e2s": (128, N1 * 128), "i1c": (N1, KL), "i1s": (N1, KL)}, N1, KL


def load_table_bf16(P, sb, name, src, rows, cols, stage):
    t = sb(name, [rows, cols], BF16)
    b = P.buf(name)
    for c0 in range(0, cols, 1024):
        c1 = min(cols, c0 + 1024)
        st, stb = stage.next()
        P.dma("sp", st[0:rows, 0:c1 - c0], src[0:rows, c0:c1], [], [stb], stb)
        P.op("pool", lambda e, st=st, c0=c0, c1=c1: e.tensor_copy(out=t[:, c0:c1], in_=st[0:rows, 0:c1 - c0]), [stb], [b])
    return t, b


def phase_F(P, nc, C, L, src, KS, tb, khat_dst=None, khat_src=None, yhat_dst=None):
    N1 = 2 * L // 128
    scr = C.scr
    FC = min(4, N1)

    def body(sb, ps):
        stage = Rot(P, sb, "wstF", [128, 1024], F32, 2)
        s1c, bs1c = load_table_bf16(P, sb, "s1cF", tb["s1c"], KS, N1, stage)
        s1s, bs1s = load_table_bf16(P, sb, "s1sF", tb["s1s"], KS, N1, stage)
        ec, bec = load_table_bf16(P, sb, "ecF", tb["ec"], 128, N1 * 128, stage)
        es, bes = load_table_bf16(P, sb, "esF", tb["es"], 128, N1 * 128, stage)
        X = Rot(P, sb, "XF", [KS, 128 * 128], BF16, 2)
        Ast = Rot(P, sb, "AstF", [N1, 2, 512], BF16, 3)
        Bt = Rot(P, sb, "BtF", [128, 3, FC, 128], BF16, 2)
        Xs = Rot(P, sb, "XsF", [128, 2, FC * 128], F32, 2)
        Kh = Rot(P, sb, "KhF", [128, 2, FC * 128], F32, 2)
        Tm = Rot(P, sb, "TmF", [128, 4, FC * 128], F32, 2)
        Yo = Rot(P, sb, "YoF", [128, 2, FC * 128], BF16, 2)
        pa = Rot(P, ps, "paF", [128, 512], F32, 4)
        px = Rot(P, ps, "pxF", [128, 512], F32, 4)
        for s in range(16):
            x_t, x_b = X.next()
            P.dma("sp", x_t[:].rearrange("k (p c) -> k p c", c=128), src[0:KS * 128, s * 128:(s + 1) * 128].rearrange("(k p) c -> k p c", p=128), [], [x_b], x_b)
            for cb in range(32):
                a_t, a_b = Ast.next()
                for ri, (tab, tabb) in enumerate(((s1c, bs1c), (s1s, bs1s))):
                    z, zb = pa.next()
                    P.op("pe", lambda e, z=z, tab=tab, x_t=x_t, cb=cb: e.matmul(z[0:N1, :], lhsT=tab[:, :], rhs=x_t[:, cb * 512:(cb + 1) * 512], start=True, stop=True), [x_b, tabb], [zb])
                    P.op("act", lambda e, z=z, a_t=a_t, ri=ri: e.activation(out=a_t[:, ri, :], in_=z[0:N1, :], func=AF.Copy), [zb], [a_b])
                P.dma("pool", scr.AT[:, 0:N1, cb * 4:(cb + 1) * 4, :].rearrange("r f p c -> f r p c"), a_t[:].rearrange("f r (p c) -> f r p c", c=128), [a_b], [scr.bA], a_b)
            P.sync_dram(scr.bA)
            for fc in range(N1 // FC):
                b_t, b_b = Bt.next()
                for r_ in range(2):
                    P.dma("sp", b_t[:, r_, :, :], scr.AT[r_, fc * FC:(fc + 1) * FC, :, :].rearrange("f p c -> p f c"), [scr.bA], [b_b], b_b)
                P.op("pool", lambda e, b_t=b_t: e.tensor_scalar(out=b_t[:, 2, :, :], in0=b_t[:, 0, :, :], scalar1=-1.0, scalar2=None, op0=ALU.mult), [b_b], [b_b])
                zr, zrb = px.next()
                zi, zib = px.next()
                for j in range(FC):
                    f1 = fc * FC + j
                    P.op("pe", lambda e, zr=zr, j=j, f1=f1, b_t=b_t: e.matmul(zr[:, j * 128:(j + 1) * 128], lhsT=ec[:, f1 * 128:(f1 + 1) * 128], rhs=b_t[:, 0, j, :], start=True, stop=False, skip_group_check=True), [b_b, bec], [zrb])
                    P.op("pe", lambda e, zr=zr, j=j, f1=f1, b_t=b_t: e.matmul(zr[:, j * 128:(j + 1) * 128], lhsT=es[:, f1 * 128:(f1 + 1) * 128], rhs=b_t[:, 1, j, :], start=False, stop=True, skip_group_check=True), [b_b, bes], [zrb])
                    P.op("pe", lambda e, zi=zi, j=j, f1=f1, b_t=b_t: e.matmul(zi[:, j * 128:(j + 1) * 128], lhsT=ec[:, f1 * 128:(f1 + 1) * 128], rhs=b_t[:, 1, j, :], start=True, stop=False, skip_group_check=True), [b_b, bec], [zib])
                    P.op("pe", lambda e, zi=zi, j=j, f1=f1, b_t=b_t: e.matmul(zi[:, j * 128:(j + 1) * 128], lhsT=es[:, f1 * 128:(f1 + 1) * 128], rhs=b_t[:, 2, j, :], start=False, stop=True, skip_group_check=True), [b_b, bes], [zib])
                xs_t, xs_b = Xs.next()
                W = FC * 128
                P.op("act", lambda e, zr=zr, xs_t=xs_t: e.activation(out=xs_t[:, 0, :], in_=zr[:, 0:W], func=AF.Copy), [zrb], [xs_b])
                P.op("act", lambda e, zi=zi, xs_t=xs_t: e.activation(out=xs_t[:, 1, :], in_=zi[:, 0:W], func=AF.Copy), [zib], [xs_b])
                if khat_dst is not None:
                    P.dma("pool", khat_dst[s, :, :, fc * FC:(fc + 1) * FC, :].rearrange("r f g c -> f r g c"), xs_t[:].rearrange("f r (g c) -> f r g c", c=128), [xs_b], [scr.bK], xs_b)
                else:
                    kh_t, kh_b = Kh.next()
                    P.dma("sp", kh_t[:].rearrange("f r (g c) -> f r g c", c=128), khat_src[s, :, :, fc * FC:(fc + 1) * FC, :].rearrange("r f g c -> f r g c"), [], [kh_b], kh_b)
                    tm, tmb = Tm.next()
                    yo, yob = Yo.next()
                    P.op("pool", lambda e, tm=tm, xs_t=xs_t, kh_t=kh_t: e.tensor_tensor(out=tm[:, 0, :], in0=xs_t[:, 0, :], in1=kh_t[:, 0, :], op=ALU.mult), [xs_b, kh_b], [tmb])
                    P.op("dve", lambda e, tm=tm, xs_t=xs_t, kh_t=kh_t: e.tensor_tensor(out=tm[:, 1, :], in0=xs_t[:, 1, :], in1=kh_t[:, 1, :], op=ALU.mult), [xs_b, kh_b], [tmb])
                    P.op("pool", lambda e, tm=tm, xs_t=xs_t, kh_t=kh_t: e.tensor_tensor(out=tm[:, 2, :], in0=xs_t[:, 0, :], in1=kh_t[:, 1, :], op=ALU.mult), [xs_b, kh_b], [tmb])
                    P.op("dve", lambda e, tm=tm, xs_t=xs_t, kh_t=kh_t: e.tensor_tensor(out=tm[:, 3, :], in0=xs_t[:, 1, :], in1=kh_t[:, 0, :], op=ALU.mult), [xs_b, kh_b], [tmb])
                    P.op("pool", lambda e, tm=tm, yo=yo: e.tensor_tensor(out=yo[:, 0, :], in0=tm[:, 0, :], in1=tm[:, 1, :], op=ALU.subtract), [tmb], [yob])
                    P.op("pool", lambda e, tm=tm, yo=yo: e.tensor_tensor(out=yo[:, 1, :], in0=tm[:, 2, :], in1=tm[:, 3, :], op=ALU.add), [tmb], [yob])
                    P.dma("pool", yhat_dst[s, :, :, fc * FC:(fc + 1) * FC, :].rearrange("r f g c -> f r g c"), yo[:].rearrange("f r (g c) -> f r g c", c=128), [yob], [scr.bQ], yob)

    _phase(P, nc, body)


def phase_I(P, nc, C, L, tb, yhat_src):
    N1 = 2 * L // 128
    KL = L // 128
    scr = C.scr
    FC = min(4, N1)
    PC = 4

    def body(sb, ps):
        stage = Rot(P, sb, "wstI", [128, 1024], F32, 2)
        i1c, bi1c = load_table_bf16(P, sb, "i1cI", tb["i1c"], N1, KL, stage)
        i1s, bi1s = load_table_bf16(P, sb, "i1sI", tb["i1s"], N1, KL, stage)
        e2c, be2c = load_table_bf16(P, sb, "e2cI", tb["e2c"], 128, N1 * 128, stage)
        e2s, be2s = load_table_bf16(P, sb, "e2sI", tb["e2s"], 128, N1 * 128, stage)
        Yt = Rot(P, sb, "YtI", [128, 3, FC, 128], BF16, 2)
        Zst = Rot(P, sb, "ZstI", [128, 2, FC * 128], BF16, 3)
        Zt = Rot(P, sb, "ZtI", [N1, 2, PC * 128], BF16, 3)
        Ot = Rot(P, sb, "OtI", [KL, 16 * 512], BF16, 2)
        pz = Rot(P, ps, "pzI", [128, 512], F32, 4)
        po = Rot(P, ps, "poI", [128, 512], F32, 2)
        W = FC * 128
        for s in range(16):
            for fc in range(N1 // FC):
                y_t, y_b = Yt.next()
                P.dma("sp", y_t[:, 0:2, :, :], yhat_src[s, :, :, fc * FC:(fc + 1) * FC, :].rearrange("r f g c -> f r g c"), [], [y_b], y_b)
                P.op("pool", lambda e, y_t=y_t: e.tensor_scalar(out=y_t[:, 2, :, :], in0=y_t[:, 1, :, :], scalar1=-1.0, scalar2=None, op0=ALU.mult), [y_b], [y_b])
                zr, zrb = pz.next()
                zi, zib = pz.next()
                for j in range(FC):
                    f1 = fc * FC + j
                    P.op("pe", lambda e, zr=zr, j=j, f1=f1, y_t=y_t: e.matmul(zr[:, j * 128:(j + 1) * 128], lhsT=e2c[:, f1 * 128:(f1 + 1) * 128], rhs=y_t[:, 0, j, :], start=True, stop=False, skip_group_check=True), [y_b, be2c], [zrb])
                    P.op("pe", lambda e, zr=zr, j=j, f1=f1, y_t=y_t: e.matmul(zr[:, j * 128:(j + 1) * 128], lhsT=e2s[:, f1 * 128:(f1 + 1) * 128], rhs=y_t[:, 2, j, :], start=False, stop=True, skip_group_check=True), [y_b, be2s], [zrb])
                    P.op("pe", lambda e, zi=zi, j=j, f1=f1, y_t=y_t: e.matmul(zi[:, j * 128:(j + 1) * 128], lhsT=e2s[:, f1 * 128:(f1 + 1) * 128], rhs=y_t[:, 0, j, :], start=True, stop=False, skip_group_check=True), [y_b, be2s], [zib])
                    P.op("pe", lambda e, zi=zi, j=j, f1=f1, y_t=y_t: e.matmul(zi[:, j * 128:(j + 1) * 128], lhsT=e2c[:, f1 * 128:(f1 + 1) * 128], rhs=y_t[:, 1, j, :], start=False, stop=True, skip_group_check=True), [y_b, be2c], [zib])
                z_t, z_b = Zst.next()
                P.op("act", lambda e, zr=zr, z_t=z_t: e.activation(out=z_t[:, 0, :], in_=zr[:, 0:W], func=AF.Copy), [zrb], [z_b])
                P.op("act", lambda e, zi=zi, z_t=z_t: e.activation(out=z_t[:, 1, :], in_=zi[:, 0:W], func=AF.Copy), [zib], [z_b])
                P.dma("pool", scr.ZT[:, :, fc * FC:(fc + 1) * FC, :].rearrange("r p f c -> p r f c"), z_t[:].rearrange("p r (f c) -> p r f c", c=128), [z_b], [scr.bA], z_b)
            o_t, o_b = Ot.next()
            for pc in range(128 // PC):
                zz, zzb = Zt.next()
                for r_ in range(2):
                    P.dma("sp", zz[:, r_, :].rearrange("f (p c) -> f p c", c=128), scr.ZT[r_, pc * PC:(pc + 1) * PC, 0:N1, :].rearrange("p f c -> f p c"), [scr.bA], [zzb], zzb)
                o, ob = po.next()
                P.op("pe", lambda e, o=o, zz=zz: e.matmul(o[0:KL, :], lhsT=i1c[:, :], rhs=zz[:, 0, :], start=True, stop=False), [zzb, bi1c], [ob])
                P.op("pe", lambda e, o=o, zz=zz: e.matmul(o[0:KL, :], lhsT=i1s[:, :], rhs=zz[:, 1, :], start=False, stop=True), [zzb, bi1s], [ob])
                half = pc % 16
                P.op("act", lambda e, o=o, o_t=o_t, half=half: e.activation(out=o_t[:, half * 512:(half + 1) * 512], in_=o[0:KL, :], func=AF.Copy), [ob], [o_b])
                if half == 15:
                    p0 = (pc - 15) * PC
                    P.dma("pool", scr.CV[0:L, s * 128:(s + 1) * 128].rearrange("(k p) c -> k p c", p=128)[:, p0:p0 + 64, :], o_t[:].rearrange("k (p c) -> k p c", c=128), [o_b], [scr.bO], o_b)
                    if pc != 128 // PC - 1:
                        o_t, o_b = Ot.next()

    _phase(P, nc, body)


def phase_G(P, nc, C, L, zT, tl, ktwo_dst, rs_dst):
    NT2 = 2 * L // 128
    scr = C.scr
    PI = math.pi

    def body(sb, ps):
        w1 = sb("w1G", [33, 2, 64], BF16)
        w2 = sb("w2G", [64, 2, 64], BF16)
        w3 = sb("w3G", [64, 2, 2048], BF16)
        stg = sb("stgG", [64, 2, 2048], F32)
        sm = sb("smG", [64, 2, 8], F32)
        bw = P.buf("wG")
        for d in range(2):
            P.dma("sp", stg[0:33, d, 0:64], C.hy_f_w1[0, d], [], [bw], bw)
        P.op("pool", lambda e: e.tensor_copy(out=w1[:], in_=stg[0:33, :, 0:64]), [bw], [bw])
        for d in range(2):
            P.dma("sp", stg[0:64, d, 64:128], C.hy_f_w2[0, d], [], [bw], bw)
        P.op("pool", lambda e: e.tensor_copy(out=w2[:], in_=stg[0:64, :, 64:128]), [bw], [bw])
        for i, src in enumerate((C.hy_f_b1, C.hy_f_freq1, C.hy_f_b2, C.hy_f_freq2)):
            for d in range(2):
                P.dma("sp", sm[:, d, i:i + 1], src[0, d].rearrange("(o u) -> o u", u=1), [], [bw], bw, slow=True)
        for (bi, fi, oi) in ((0, 1, 4), (2, 3, 5)):
            P.op("pool", lambda e, bi=bi, fi=fi, oi=oi: e.tensor_tensor(out=sm[:, :, oi:oi + 1], in0=sm[:, :, bi:bi + 1], in1=sm[:, :, fi:fi + 1], op=ALU.mult), [bw], [bw])
        bw3 = P.buf("w3G")
        for d in range(2):
            P.dma("sp", stg[:, d, :], C.hy_f_w3[0, d], [bw], [bw3], bw3)
        P.op("pool", lambda e: e.tensor_copy(out=w3[:], in_=stg[:]), [bw3], [bw3])
        negpi = sb("negpiG", [128, 1], F32)
        P.op("pool", lambda e: e.memset(negpi[:], -PI), [], [bw])
        ones = sb("onesG", [128, 128], BF16)
        P.op("pool", lambda e: e.memset(ones[:], 1.0), [], [bw])
        dl = sb("dlG", [128, 2048], F32)
        bdl = P.buf("dlG")
        P.dma("sp", dl[:], C.hy_delta.partition_broadcast(128), [], [bdl], bdl, slow=True)
        tt = sb("ttG", [128, NT2], F32)
        P.dma("sp", tt[:], tl.rearrange("(k p) -> p k", p=128), [], [bdl], bdl, slow=True)
        P.op("pool", lambda e: e.tensor_scalar(out=tt[:], in0=tt[:], scalar1=-1.0, scalar2=None, op0=ALU.mult), [bdl], [bdl])
        zt = Rot(P, sb, "ztG", [33, 512], F32, 2)
        ztb = Rot(P, sb, "ztbG", [33, 512], BF16, 2)
        v1 = Rot(P, sb, "v1G", [64, 512], F32, 2)
        ni = Rot(P, sb, "niG", [64, 512], mybir.dt.int32, 2)
        nf = Rot(P, sb, "nfG", [64, 512], F32, 2)

        def range_reduce(P, v, vb, ni_s, nf_s):
            n_i, nib = ni_s
            n_f, nfb = nf_s
            P.op("dve", lambda e: e.tensor_scalar(out=n_f[:], in0=v[:], scalar1=1.0 / (2 * PI), scalar2=None, op0=ALU.mult), [vb], [nfb])
            P.op("dve", lambda e: e.tensor_copy(out=n_i[:], in_=n_f[:]), [nfb], [nib])
            P.op("dve", lambda e: e.tensor_copy(out=n_f[:], in_=n_i[:]), [nib], [nfb])
            P.op("dve", lambda e: e.scalar_tensor_tensor(out=v[:], in0=n_f[:], scalar=-2 * PI, in1=v[:], op0=ALU.mult, op1=ALU.add), [nfb, vb], [vb])
            P.op("dve", lambda e: e.tensor_scalar(out=n_f[:], in0=v[:], scalar1=PI, scalar2=None, op0=ALU.is_gt), [vb], [nfb])
            P.op("dve", lambda e: e.scalar_tensor_tensor(out=v[:], in0=n_f[:], scalar=-2 * PI, in1=v[:], op0=ALU.mult, op1=ALU.add), [nfb, vb], [vb])
            P.op("dve", lambda e: e.tensor_scalar(out=n_f[:], in0=v[:], scalar1=-PI, scalar2=None, op0=ALU.is_lt), [vb], [nfb])
            P.op("dve", lambda e: e.scalar_tensor_tensor(out=v[:], in0=n_f[:], scalar=2 * PI, in1=v[:], op0=ALU.mult, op1=ALU.add), [nfb, vb], [vb])
        h1 = Rot(P, sb, "h1G", [64, 512], BF16, 2)
        h2 = Rot(P, sb, "h2G", [64, 512], BF16, 2)
        dec = Rot(P, sb, "decG", [128, 2048], F32, 2)
        kf = Rot(P, sb, "kfG", [128, 2048], F32, 2)
        kb16 = Rot(P, sb, "kbG", [128, 2048], BF16, 2)
        sq = Rot(P, sb, "sqG", [128, 2048], BF16, 2)
        pm = Rot(P, ps, "pmG", [128, 512], F32, 2)
        pk = Rot(P, ps, "pkG", [128, 512], F32, 2)
        pss = [ps("pssG%d" % i, [128, 512], F32) for i in range(4)]
        bss = P.buf("pssG")
        for blk in range(2 * L // 512):
            d = 0 if blk * 512 < L else 1
            z_t, z_b = zt.next()
            P.dma("sp", z_t[:], zT[:, blk * 512:(blk + 1) * 512], [], [z_b], z_b)
            zb_t, zb_b = ztb.next()
            P.op("pool", lambda e, z_t=z_t, zb_t=zb_t: e.tensor_copy(out=zb_t[:], in_=z_t[:]), [z_b], [zb_b])
            m, mb = pm.next()
            P.op("pe", lambda e, m=m, zb_t=zb_t, d=d: e.matmul(m[0:64, :], lhsT=w1[:, d, :], rhs=zb_t[:], start=True, stop=True), [zb_b, bw], [mb])
            v, vb = v1.next()
            P.op("act", lambda e, m=m, v=v, d=d: e.activation(out=v[:], in_=m[0:64, :], func=AF.Identity, scale=sm[:, d, 1:2], bias=sm[:, d, 4:5]), [mb, bw], [vb])
            range_reduce(P, v, vb, ni.next(), nf.next())
            h_1, h1b = h1.next()
            P.op("act", lambda e, v=v, h_1=h_1: e.activation(out=h_1[:], in_=v[:], func=AF.Sin), [vb], [h1b])
            m, mb = pm.next()
            P.op("pe", lambda e, m=m, h_1=h_1, d=d: e.matmul(m[0:64, :], lhsT=w2[:, d, :], rhs=h_1[:], start=True, stop=True), [h1b, bw], [mb])
            v, vb = v1.next()
            P.op("act", lambda e, m=m, v=v, d=d: e.activation(out=v[:], in_=m[0:64, :], func=AF.Identity, scale=sm[:, d, 3:4], bias=sm[:, d, 5:6]), [mb, bw], [vb])
            range_reduce(P, v, vb, ni.next(), nf.next())
            h_2, h2b = h2.next()
            P.op("act", lambda e, v=v, h_2=h_2: e.activation(out=h_2[:], in_=v[:], func=AF.Sin), [vb], [h2b])
            for ti in range(4):
                tile_i = blk * 4 + ti
                dc, dcb = dec.next()
                P.op("act", lambda e, dc=dc, tile_i=tile_i: e.activation(out=dc[:], in_=dl[:], func=AF.Exp, scale=tt[:, tile_i:tile_i + 1]), [bdl], [dcb])
                k_t, k_b = kf.next()
                for gi in range(4):
                    kk, kkb = pk.next()
                    P.op("pe", lambda e, kk=kk, h_2=h_2, ti=ti, gi=gi, d=d: e.matmul(kk[:], lhsT=h_2[:, ti * 128:(ti + 1) * 128], rhs=w3[:, d, gi * 512:(gi + 1) * 512], start=True, stop=True), [h2b, bw3], [kkb])
                    P.op("act", lambda e, kk=kk, k_t=k_t, gi=gi: e.activation(out=k_t[:, gi * 512:(gi + 1) * 512], in_=kk[:], func=AF.Copy), [kkb], [k_b])
                P.op("dve", lambda e, k_t=k_t, dc=dc: e.tensor_tensor(out=k_t[:], in0=k_t[:], in1=dc[:], op=ALU.mult), [k_b, dcb], [k_b])
                kb_t, kb_b = kb16.next()
                P.op("pool", lambda e, k_t=k_t, kb_t=kb_t: e.tensor_copy(out=kb_t[:], in_=k_t[:]), [k_b], [kb_b])
                P.dma("sp", ktwo_dst[tile_i * 128:(tile_i + 1) * 128, :], kb_t[:], [kb_b], [scr.bK], kb_b)
                sq_t, sq_b = sq.next()
                P.op("dve", lambda e, k_t=k_t, sq_t=sq_t: e.tensor_tensor(out=sq_t[:], in0=k_t[:], in1=k_t[:], op=ALU.mult), [k_b], [sq_b])
                for gi in range(4):
                    P.op("pe", lambda e, gi=gi, sq_t=sq_t, tile_i=tile_i: e.matmul(pss[gi][:], lhsT=ones[:], rhs=sq_t[:, gi * 512:(gi + 1) * 512], start=(tile_i == 0), stop=(tile_i == NT2 - 1)), [sq_b, bw], [bss])
        rs = sb("rsG", [128, 2048], F32)
        brs = P.buf("rsG")
        for gi in range(4):
            P.op("act", lambda e, gi=gi: e.activation(out=rs[:, gi * 512:(gi + 1) * 512], in_=pss[gi][:], func=AF.Copy), [bss], [brs])
        P.op("pool", lambda e: e.tensor_scalar(out=rs[:], in0=rs[:], scalar1=EPS, scalar2=None, op0=ALU.add), [brs], [brs])
        P.op("act", lambda e: e.activation(out=rs[:], in_=rs[:], func=AF.Sqrt), [brs], [brs])
        P.op("dve", lambda e: e.reciprocal(out=rs[:], in_=rs[:]), [brs], [brs])
        P.dma("sp", rs_dst[0:1, :], rs[0:1, :], [brs], [scr.bK], brs)

    _phase(P, nc, body)


def s5_cmul(P, eng2, out_r, out_i, xr, xi, yr, yi, t1, t2, R, W, TB):
    P.op("pool", lambda e: e.tensor_tensor(out=t1, in0=xr, in1=yr, op=ALU.mult), R, [TB])
    P.op("dve", lambda e: e.tensor_tensor(out=t2, in0=xi, in1=yi, op=ALU.mult), R, [TB])
    P.op("pool", lambda e: e.tensor_tensor(out=out_r, in0=t1, in1=t2, op=ALU.subtract), [TB] + R, W)
    P.op("pool", lambda e: e.tensor_tensor(out=t1, in0=xr, in1=yi, op=ALU.mult), R + W, [TB])
    P.op("dve", lambda e: e.tensor_tensor(out=t2, in0=xi, in1=yr, op=ALU.mult), R + W, [TB])
    P.op("pool", lambda e: e.tensor_tensor(out=out_i, in0=t1, in1=t2, op=ALU.add), [TB] + R, W)


def s5_nstages(T):
    n, span = 0, 1
    while span < T:
        span *= 4
        n += 1
    return n


def phase_S5setup(P, nc, C, NS):
    scr = C.scr
    PI = math.pi
    G2 = 128

    def body(sb, ps):
        ident = sb("identS", [128, 128], BF16)
        identf = sb("identSf", [128, 128], F32)
        bid = P.buf("identS")
        P.dma("sp", identf[:], C.ident[:, :], [], [bid], bid)
        P.op("dve", lambda e: e.tensor_copy(out=ident[:], in_=identf[:]), [bid], [bid])
        msk = sb("mskS", [128, 4], F32)
        bm = P.buf("mskS")
        P.dma("sp", msk[:], C.s5_rowmask[:, :], [], [bm], bm)
        mfb = sb("mfbS", [128, 2, 128], F32)
        P.dma("sp", mfb[:, 0, :], C.s5_mf[:, :], [], [bm], bm)
        P.dma("sp", mfb[:, 1, :], C.s5_mb[:, :], [], [bm], bm)
        ar = sb("arS", [128, G2], F32)
        ai = sb("aiS", [128, G2], F32)
        dt = sb("dtS", [128, G2], F32)
        ba = P.buf("aS")
        for half in range(2):
            P.dma("sp", ar[half * 64:(half + 1) * 64, :].rearrange("n (d g) -> n d g", d=2), C.s5_a_re[0].rearrange("d g n -> n d g"), [], [ba], ba, slow=True)
            P.dma("sp", ai[half * 64:(half + 1) * 64, :].rearrange("n (d g) -> n d g", d=2), C.s5_a_im[0].rearrange("d g n -> n d g"), [], [ba], ba, slow=True)
        P.dma("sp", dt[:], C.s5_log_dt[0].rearrange("d g -> (d g)").partition_broadcast(128), [], [ba], ba, slow=True)
        P.op("act", lambda e: e.activation(out=dt[:], in_=dt[:], func=AF.Exp), [ba], [ba])
        NT_ = 12
        tmp = [sb("tmpS%d" % i, [128, G2], F32) for i in range(NT_)]
        btmp = [P.buf("tmpS%d" % i) for i in range(NT_)]
        ni = sb("niS", [128, G2], mybir.dt.int32)
        lr, li, mag, pr, pi_, cs_arg = tmp[0], tmp[1], tmp[2], tmp[3], tmp[4], tmp[5]
        bl = P.buf("lS")
        P.op("pool", lambda e: e.tensor_tensor(out=lr[:], in0=ar[:], in1=dt[:], op=ALU.mult), [ba], [bl])
        P.op("pool", lambda e: e.tensor_tensor(out=li[:], in0=ai[:], in1=dt[:], op=ALU.mult), [ba], [bl])
        P.op("act", lambda e: e.activation(out=mag[:], in_=lr[:], func=AF.Exp), [bl], [bl])

        def rr(v):
            nf = tmp[6]
            P.op("dve", lambda e: e.tensor_scalar(out=nf[:], in0=v[:], scalar1=1.0 / (2 * PI), scalar2=None, op0=ALU.mult), [bl], [bl])
            P.op("dve", lambda e: e.tensor_copy(out=ni[:], in_=nf[:]), [bl], [bl])
            P.op("dve", lambda e: e.tensor_copy(out=nf[:], in_=ni[:]), [bl], [bl])
            P.op("dve", lambda e: e.scalar_tensor_tensor(out=v[:], in0=nf[:], scalar=-2 * PI, in1=v[:], op0=ALU.mult, op1=ALU.add), [bl], [bl])
            P.op("dve", lambda e: e.tensor_scalar(out=nf[:], in0=v[:], scalar1=PI, scalar2=None, op0=ALU.is_gt), [bl], [bl])
            P.op("dve", lambda e: e.scalar_tensor_tensor(out=v[:], in0=nf[:], scalar=-2 * PI, in1=v[:], op0=ALU.mult, op1=ALU.add), [bl], [bl])
            P.op("dve", lambda e: e.tensor_scalar(out=nf[:], in0=v[:], scalar1=-PI, scalar2=None, op0=ALU.is_lt), [bl], [bl])
            P.op("dve", lambda e: e.scalar_tensor_tensor(out=v[:], in0=nf[:], scalar=2 * PI, in1=v[:], op0=ALU.mult, op1=ALU.add), [bl], [bl])

        P.op("pool", lambda e: e.tensor_scalar(out=cs_arg[:], in0=li[:], scalar1=PI / 2, scalar2=None, op0=ALU.add), [bl], [bl])
        rr(li)
        rr(cs_arg)
        P.op("act", lambda e: e.activation(out=pi_[:], in_=li[:], func=AF.Sin), [bl], [bl])
        P.op("act", lambda e: e.activation(out=pr[:], in_=cs_arg[:], func=AF.Sin), [bl], [bl])
        P.op("pool", lambda e: e.tensor_tensor(out=pr[:], in0=pr[:], in1=mag[:], op=ALU.mult), [bl], [bl])
        P.op("pool", lambda e: e.tensor_tensor(out=pi_[:], in0=pi_[:], in1=mag[:], op=ALU.mult), [bl], [bl])
        PW = sb("PWS", [128, 9, 2, G2], F32)
        bpw = P.buf("PWS")
        P.op("pool", lambda e: e.memset(PW[:, 0, 0, :], 1.0), [], [bpw])
        P.op("pool", lambda e: e.memset(PW[:, 0, 1, :], 0.0), [], [bpw])
        P.op("pool", lambda e: e.tensor_copy(out=PW[:, 1, 0, :], in_=pr[:]), [bl], [bpw])
        P.op("pool", lambda e: e.tensor_copy(out=PW[:, 1, 1, :], in_=pi_[:]), [bl], [bpw])
        t1, t2 = tmp[7], tmp[8]
        btt = P.buf("ttS")
        for e_ in range(2, 9):
            s5_cmul(P, None, PW[:, e_, 0, :], PW[:, e_, 1, :], PW[:, e_ - 1, 0, :], PW[:, e_ - 1, 1, :], PW[:, 1, 0, :], PW[:, 1, 1, :], t1[:], t2[:], [bpw], [bpw], btt)
        inv8 = sb("inv8S", [128, 2, G2], F32)
        binv = P.buf("inv8S")
        P.op("pool", lambda e: e.tensor_tensor(out=t1[:], in0=PW[:, 8, 0, :], in1=PW[:, 8, 0, :], op=ALU.mult), [bpw], [btt])
        P.op("pool", lambda e: e.tensor_tensor(out=t2[:], in0=PW[:, 8, 1, :], in1=PW[:, 8, 1, :], op=ALU.mult), [bpw], [btt])
        P.op("pool", lambda e: e.tensor_tensor(out=t1[:], in0=t1[:], in1=t2[:], op=ALU.add), [btt], [btt])
        P.op("dve", lambda e: e.reciprocal(out=t1[:], in_=t1[:]), [btt], [btt])
        P.op("pool", lambda e: e.tensor_tensor(out=inv8[:, 0, :], in0=PW[:, 8, 0, :], in1=t1[:], op=ALU.mult), [bpw, btt], [binv])
        P.op("pool", lambda e: e.tensor_tensor(out=inv8[:, 1, :], in0=PW[:, 8, 1, :], in1=t1[:], op=ALU.mult), [bpw, btt], [binv])
        P.op("pool", lambda e: e.tensor_scalar(out=inv8[:, 1, :], in0=inv8[:, 1, :], scalar1=-1.0, scalar2=None, op0=ALU.mult), [binv], [binv])
        SC = sb("SCS", [128, NS * 3, 2, G2], F32)
        bsc = P.buf("SCS")
        q = sb("qS", [128, 2, G2], F32)
        bq = P.buf("qS")
        P.op("pool", lambda e: e.tensor_copy(out=q[:], in_=PW[:, 8, :, :]), [bpw], [bq])
        for m in range(NS):
            P.op("pool", lambda e, m=m: e.tensor_copy(out=SC[:, m * 3, :, :], in_=q[:]), [bq], [bsc])
            s5_cmul(P, None, SC[:, m * 3 + 1, 0, :], SC[:, m * 3 + 1, 1, :], q[:, 0, :], q[:, 1, :], q[:, 0, :], q[:, 1, :], t1[:], t2[:], [bq, bsc], [bsc], btt)
            s5_cmul(P, None, SC[:, m * 3 + 2, 0, :], SC[:, m * 3 + 2, 1, :], SC[:, m * 3 + 1, 0, :], SC[:, m * 3 + 1, 1, :], q[:, 0, :], q[:, 1, :], t1[:], t2[:], [bq, bsc], [bsc], btt)
            if m < NS - 1:
                s5_cmul(P, None, q[:, 0, :], q[:, 1, :], SC[:, m * 3 + 1, 0, :], SC[:, m * 3 + 1, 1, :], SC[:, m * 3 + 1, 0, :], SC[:, m * 3 + 1, 1, :], t1[:], t2[:], [bsc], [bq], btt)
        P.op("pool", lambda e: e.tensor_scalar(out=SC[:, :, 1, :], in0=SC[:, :, 1, :], scalar1=msk[:, 2:3], scalar2=None, op0=ALU.mult), [bsc, bm], [bsc])
        P.dma("sp", scr.SSC[:, 0:NS * 3, :, :], SC[:], [bsc], [scr.bK], bsc)
        cr, ci, den = tmp[9], tmp[10], tmp[11]
        bc = P.buf("cS")
        xr = tmp[2]
        P.op("pool", lambda e: e.tensor_scalar(out=xr[:], in0=PW[:, 1, 0, :], scalar1=-1.0, scalar2=None, op0=ALU.add), [bpw, bl], [bl])
        P.op("pool", lambda e: e.tensor_tensor(out=den[:], in0=ar[:], in1=ar[:], op=ALU.mult), [ba], [bc])
        P.op("pool", lambda e: e.tensor_tensor(out=t1[:], in0=ai[:], in1=ai[:], op=ALU.mult), [ba, binv], [btt])
        P.op("pool", lambda e: e.tensor_tensor(out=den[:], in0=den[:], in1=t1[:], op=ALU.add), [btt], [bc])
        P.op("dve", lambda e: e.reciprocal(out=den[:], in_=den[:]), [bc], [bc])
        P.op("pool", lambda e: e.tensor_tensor(out=t1[:], in0=xr[:], in1=ar[:], op=ALU.mult), [bl, ba], [btt])
        P.op("pool", lambda e: e.tensor_tensor(out=t2[:], in0=PW[:, 1, 1, :], in1=ai[:], op=ALU.mult), [bpw, ba], [btt])
        P.op("pool", lambda e: e.tensor_tensor(out=cr[:], in0=t1[:], in1=t2[:], op=ALU.add), [btt], [bc])
        P.op("pool", lambda e: e.tensor_tensor(out=cr[:], in0=cr[:], in1=den[:], op=ALU.mult), [bc], [bc])
        P.op("pool", lambda e: e.tensor_tensor(out=t1[:], in0=PW[:, 1, 1, :], in1=ar[:], op=ALU.mult), [bpw, ba, bc], [btt])
        P.op("pool", lambda e: e.tensor_tensor(out=t2[:], in0=xr[:], in1=ai[:], op=ALU.mult), [bl, ba], [btt])
        P.op("pool", lambda e: e.tensor_tensor(out=ci[:], in0=t1[:], in1=t2[:], op=ALU.subtract), [btt], [bc])
        P.op("pool", lambda e: e.tensor_tensor(out=ci[:], in0=ci[:], in1=den[:], op=ALU.mult), [bc], [bc])
        Bri = sb("BriS", [128, 2, G2, 16], F32)
        bB = P.buf("BriS")
        for half in range(2):
            for d in range(2):
                P.dma("sp", Bri[half * 64:(half + 1) * 64, 0, d * 64:(d + 1) * 64, :], C.s5_b_re[0, d].rearrange("g n c -> n g c"), [], [bB], bB)
                P.dma("sp", Bri[half * 64:(half + 1) * 64, 1, d * 64:(d + 1) * 64, :], C.s5_b_im[0, d].rearrange("g n c -> n g c"), [], [bB], bB)
        Bb = sb("BbS", [128, 2, G2, 16], F32)
        bBb = P.buf("BbS")
        T1 = sb("T1S", [128, 64, 16], F32)
        T2 = sb("T2S", [128, 64, 16], F32)
        bT = P.buf("TS")
        for d in range(2):
            gs = slice(d * 64, (d + 1) * 64)
            crb = cr[:, gs].unsqueeze(2).to_broadcast([128, 64, 16])
            cib = ci[:, gs].unsqueeze(2).to_broadcast([128, 64, 16])
            s5_cmul(P, None, Bb[:, 0, gs, :], Bb[:, 1, gs, :], Bri[:, 0, gs, :], Bri[:, 1, gs, :], crb, cib, T1[:], T2[:], [bB, bc], [bBb], bT)
        Cri = sb("CriS", [128, 2, G2, 16], F32)
        bC = P.buf("CriS")
        cl = Rot(P, sb, "clS", [128, 128], F32, 2)
        tpc = Rot(P, ps, "tpcS", [128, 512], F32, 2)
        for ri, src in enumerate((C.s5_c_re, C.s5_c_im)):
            for d in range(2):
                for o in range(8):
                    c_t, c_b = cl.next()
                    for dup in range(2):
                        P.dma("sp", c_t[:, dup * 64:(dup + 1) * 64], src[0, d, o * 8:(o + 1) * 8].rearrange("g c n -> (g c) n"), [], [c_b], c_b)
                    tp_, tpb_ = tpc.next()
                    P.op("pe", lambda e, tp_=tp_, c_t=c_t: e.transpose(out=tp_[:, 0:128], in_=c_t[:], identity=identf[:]), [c_b, bid], [tpb_])
                    P.op("act", lambda e, tp_=tp_, ri=ri, d=d, o=o: e.activation(out=Cri[:, ri, d * 64 + o * 8:d * 64 + (o + 1) * 8, :], in_=tp_[:, 0:128].rearrange("p (g c) -> p g c", c=16), func=AF.Copy), [tpb_], [bC])
        GC = 32
        W8 = sb("W8S", [128, 2, GC, 8, 16], BF16)
        W8s = sb("W8sS", [128, GC, 8, 16], BF16)
        C8 = sb("C8S", [128, GC, 8, 16], BF16)
        bW8, bW8s, bC8 = P.buf("W8S"), P.buf("W8sS"), P.buf("C8S")
        O1 = sb("O1S", [128, GC, 16], F32)
        O2 = sb("O2S", [128, GC, 16], F32)
        O3 = sb("O3S", [128, GC, 16], F32)
        O4 = sb("O4S", [128, GC, 16], F32)
        bO = P.buf("OS")
        tpb16 = Rot(P, ps, "tpb16S", [128, 1024], BF16, 2)
        pd = Rot(P, ps, "pdS", [128, 512], F32, 2)
        b8st = Rot(P, sb, "b8stS", [128, 8, 128], BF16, 2)
        d8f = Rot(P, sb, "d8fS", [128, 4, 128], F32, 2)
        d8st = Rot(P, sb, "d8stS", [128, 4, 128], BF16, 2)
        TT1 = T1[:, 0:GC, :]
        TT2 = T2[:, 0:GC, :]
        for ch in range(G2 // GC):
            d = (ch * GC) // 64
            gs = slice(ch * GC, (ch + 1) * GC)
            for i in range(8):
                eb = (7 - i) if d == 0 else i
                prb = PW[:, eb, 0, gs].unsqueeze(2).to_broadcast([128, GC, 16])
                pib = PW[:, eb, 1, gs].unsqueeze(2).to_broadcast([128, GC, 16])
                s5_cmul(P, None, O1[:], O2[:], Bb[:, 0, gs, :], Bb[:, 1, gs, :], prb, pib, TT1, TT2, [bBb, bpw], [bO], bT)
                P.op("pool", lambda e, i=i: e.tensor_copy(out=W8[:, 0, :, i, :], in_=O1[:]), [bO], [bW8])
                P.op("pool", lambda e, i=i: e.tensor_copy(out=W8[:, 1, :, i, :], in_=O2[:]), [bO], [bW8])
                i8r = inv8[:, 0, gs].unsqueeze(2).to_broadcast([128, GC, 16])
                i8i = inv8[:, 1, gs].unsqueeze(2).to_broadcast([128, GC, 16])
                s5_cmul(P, None, O3[:], O4[:], O1[:], O2[:], i8r, i8i, TT1, TT2, [bO, binv], [bO], bT)
                P.op("pool", lambda e: e.tensor_scalar(out=O3[:], in0=O3[:], scalar1=msk[:, 0:1], scalar2=None, op0=ALU.mult), [bO, bm], [bO])
                P.op("dve", lambda e, i=i: e.scalar_tensor_tensor(out=W8s[:, :, i, :], in0=O4[:], scalar=msk[:, 1:2], in1=O3[:], op0=ALU.mult, op1=ALU.add), [bO, bm], [bW8s])
                ec_ = (i + 1) if d == 0 else (8 - i)
                prc = PW[:, ec_, 0, gs].unsqueeze(2).to_broadcast([128, GC, 16])
                pic = PW[:, ec_, 1, gs].unsqueeze(2).to_broadcast([128, GC, 16])
                s5_cmul(P, None, O1[:], O2[:], Cri[:, 0, gs, :], Cri[:, 1, gs, :], prc, pic, TT1, TT2, [bC, bpw, bW8, bW8s], [bO], bT)
                P.op("pool", lambda e: e.tensor_scalar(out=O1[:], in0=O1[:], scalar1=msk[:, 0:1], scalar2=None, op0=ALU.mult), [bO, bm], [bO])
                P.op("dve", lambda e, i=i: e.scalar_tensor_tensor(out=C8[:, :, i, :], in0=O2[:], scalar=msk[:, 3:4], in1=O1[:], op0=ALU.mult, op1=ALU.add), [bO, bm], [bC8])
            P.dma("sp", scr.SC8[:, ch * GC:(ch + 1) * GC, :], C8[:].rearrange("p g i c -> p g (i c)"), [bC8], [scr.bK], bC8)
            for j0 in range(0, GC, 8):
                tp_, tpb_ = tpb16.next()
                for j in range(8):
                    for ri in range(2):
                        P.op("pe", lambda e, tp_=tp_, j=j, j0=j0, ri=ri: e.transpose(out=tp_[:, j * 128 + ri * 64:j * 128 + (ri + 1) * 64], in_=W8[0:64, ri, j0 + j, :, :].rearrange("n i c -> n (i c)"), identity=ident[0:64, 0:64]), [bW8, bid], [tpb_])
                st_, stb_ = b8st.next()
                P.op("act", lambda e, tp_=tp_, st_=st_: e.activation(out=st_[:].rearrange("p j n -> p (j n)"), in_=tp_[:], func=AF.Copy), [tpb_], [stb_])
                dg0 = ch * GC + j0
                P.dma("sp", scr.SB8[dg0:dg0 + 8, :, :].rearrange("j p n -> p j n"), st_[:], [stb_], [scr.bK], stb_)
            for j0 in range(0, GC, 4):
                pp_, ppb_ = pd.next()
                for j in range(4):
                    P.op("pe", lambda e, pp_=pp_, j=j, j0=j0: e.matmul(pp_[:, j * 128:(j + 1) * 128], lhsT=W8s[:, j0 + j, :, :].rearrange("n i c -> n (i c)"), rhs=C8[:, j0 + j, :, :].rearrange("n i c -> n (i c)"), start=True, stop=True, skip_group_check=True), [bW8s, bC8], [ppb_])
                f_, fb_ = d8f.next()
                P.op("act", lambda e, pp_=pp_, f_=f_: e.activation(out=f_[:].rearrange("p j c -> p (j c)"), in_=pp_[:], func=AF.Copy), [ppb_], [fb_])
                o_, ob_ = d8st.next()
                mk = mfb[:, d, :].unsqueeze(1).to_broadcast([128, 4, 128])
                P.op("pool", lambda e, f_=f_, o_=o_, mk=mk: e.tensor_tensor(out=o_[:], in0=f_[:], in1=mk, op=ALU.mult), [fb_, bm], [ob_])
                dg0 = ch * GC + j0
                P.dma("sp", scr.SD8[dg0:dg0 + 4, :, :].rearrange("j p n -> p j n"), o_[:], [ob_], [scr.bK], ob_)

    _phase(P, nc, body)


def phase_S5main(P, nc, C, L, NS):
    T = L // 8
    NTT = max(1, T // 128)
    TT = min(T, 128)
    scr = C.scr

    def body(sb, ps):
        ident = sb("identM", [128, 128], BF16)
        identf = sb("identMf", [128, 128], F32)
        swapf = sb("swapMf", [128, 128], F32)
        bid = P.buf("identM")
        P.dma("sp", identf[:], C.ident[:, :], [], [bid], bid)
        P.dma("sp", swapf[:], C.s5_swap[:, :], [], [bid], bid)
        P.op("dve", lambda e: e.tensor_copy(out=ident[:], in_=identf[:]), [bid], [bid])
        SC = sb("SCM", [128, NS * 3, 2, 128], F32)
        bsc = P.buf("SCM")
        P.dma("sp", SC[:], scr.SSC[:, 0:NS * 3, :, :], [], [bsc], bsc)
        uo = Rot(P, sb, "uoM", [128, NTT, 8, 128], BF16, 2)
        uo2 = Rot(P, sb, "uo2M", [128, NTT, 8, 8, 16], BF16, 2)
        yo = Rot(P, sb, "yoM", [128, NTT, 8, 128], BF16, 2)
        wg = Rot(P, sb, "wgM", [128, 6, 128], BF16, 2)
        us = Rot(P, sb, "usM", [128, T], BF16, 2)
        sbuf_ = Rot(P, sb, "sM", [128, T], BF16, 2 * (NS + 1) + 2)
        ys = Rot(P, sb, "ysM", [128, T], BF16, 2)
        Rt = Rot(P, sb, "RtM", [128, 128], F32, 6)
        Rb = Rot(P, sb, "RbM", [128, 128], BF16, 14)
        tp = Rot(P, ps, "tpM", [128, 1024], BF16, 2)
        pp = Rot(P, ps, "ppM", [128, 512], F32, 4)
        chunks = [(c0, min(512, T - c0)) for c0 in range(0, T, 512)]

        def do_group(o, g8, uo_t, uo_b, yo_t, yo_b):
            g = o * 8 + g8
            w_t, w_b = wg.next()
            for d in range(2):
                P.dma("sp", w_t[:, d, :], scr.SB8[d * 64 + g, :, :], [], [w_b], w_b)
                P.dma("sp", w_t[:, 2 + d, :], scr.SC8[:, d * 64 + g, :], [], [w_b], w_b)
            for d in range(2):
                P.dma("sp", w_t[:, 4 + d, :], scr.SD8[d * 64 + g, :, :], [], [w_b], w_b)
            tpp, tpb = tp.next()
            for tt in range(NTT):
                P.op("pe", lambda e, tt=tt: e.transpose(out=tpp[:, tt * TT:(tt + 1) * TT], in_=uo_t[0:TT, tt, g8, :, :].rearrange("p i c -> p (i c)"), identity=ident[0:TT, 0:TT]), [uo_b, bid], [tpb])
            us_t, us_b = us.next()
            P.op("act", lambda e: e.activation(out=us_t[:], in_=tpp[:, 0:T], func=AF.Copy), [tpb], [us_b])
            cur = []
            for d in range(2):
                s_t, s_b = sbuf_.next()
                for (c0, w) in chunks:
                    z, zb = pp.next()
                    P.op("pe", lambda e, z=z, c0=c0, w=w, d=d: e.matmul(z[:, 0:w], lhsT=w_t[:, d, :], rhs=us_t[:, c0:c0 + w], start=True, stop=True), [us_b, w_b], [zb])
                    P.op("act", lambda e, z=z, c0=c0, w=w, s_t=s_t: e.activation(out=s_t[:, c0:c0 + w], in_=z[:, 0:w], func=AF.Copy), [zb], [s_b])
                cur.append((s_t, s_b))
            for m in range(NS):
                S = 4 ** m
                Rall = []
                for d in range(2):
                    dg = d * 64 + g
                    Rs = []
                    for J in range(1, 4):
                        if J * S >= T:
                            break
                        r1, r1b = Rt.next()
                        r2, r2b = Rb.next()
                        col = m * 3 + J - 1
                        P.op("pool", lambda e, r1=r1, col=col, dg=dg: e.tensor_scalar(out=r1[:], in0=identf[:], scalar1=SC[:, col, 0, dg:dg + 1], scalar2=None, op0=ALU.mult), [bid, bsc], [r1b])
                        P.op("dve", lambda e, r1=r1, r2=r2, col=col, dg=dg: e.scalar_tensor_tensor(out=r2[:], in0=swapf[:], scalar=SC[:, col, 1, dg:dg + 1], in1=r1[:], op0=ALU.mult, op1=ALU.add), [bid, bsc, r1b], [r2b])
                        Rs.append((J * S, r2, r2b))
                    Rall.append(Rs)
                nxt = []
                for d in range(2):
                    s_t, s_b = cur[d]
                    n_t, n_b = sbuf_.next()
                    for (c0, w) in chunks:
                        z, zb = pp.next()
                        mms = [(0, w, c0, ident, bid)]
                        for (sh, r2, r2b) in Rall[d]:
                            if d == 0:
                                a = max(c0, sh)
                                if a < c0 + w:
                                    mms.append((a - c0, w, a - sh, r2, r2b))
                            else:
                                bnd = min(c0 + w, T - sh)
                                if bnd > c0:
                                    mms.append((0, bnd - c0, c0 + sh, r2, r2b))
                        for k_, (o0, o1, src0, lt, ltb) in enumerate(mms):
                            P.op("pe", lambda e, z=z, o0=o0, o1=o1, src0=src0, lt=lt, s_t=s_t, k_=k_, nm=len(mms): e.matmul(z[:, o0:o1], lhsT=lt[:], rhs=s_t[:, src0:src0 + (o1 - o0)], start=(k_ == 0), stop=(k_ == nm - 1), skip_group_check=True), [s_b, ltb], [zb])
                        P.op("act", lambda e, z=z, c0=c0, w=w, n_t=n_t: e.activation(out=n_t[:, c0:c0 + w], in_=z[:, 0:w], func=AF.Copy), [zb], [n_b])
                    nxt.append((n_t, n_b))
                cur = nxt
            fin = cur
            y_t, y_b = ys.next()
            for (c0, w) in chunks:
                z, zb = pp.next()
                mms = [(0, w, us_t, us_b, c0, 4), (0, w, us_t, us_b, c0, 5)]
                a = max(c0, 1)
                if a < c0 + w:
                    mms.append((a - c0, w, fin[0][0], fin[0][1], a - 1, 2))
                bnd = min(c0 + w, T - 1)
                if bnd > c0:
                    mms.append((0, bnd - c0, fin[1][0], fin[1][1], c0 + 1, 3))
                for k_, (o0, o1, src, srcb, src0, wi) in enumerate(mms):
                    P.op("pe", lambda e, z=z, o0=o0, o1=o1, src=src, src0=src0, wi=wi, k_=k_, nm=len(mms): e.matmul(z[:, o0:o1], lhsT=w_t[:, wi, :], rhs=src[:, src0:src0 + (o1 - o0)], start=(k_ == 0), stop=(k_ == nm - 1), skip_group_check=True), [srcb, w_b], [zb])
                P.op("act", lambda e, z=z, c0=c0, w=w: e.activation(out=y_t[:, c0:c0 + w], in_=z[:, 0:w], func=AF.Copy), [zb], [y_b])
            tpp2, tpb2 = tp.next()
            for tt in range(NTT):
                P.op("pe", lambda e, tt=tt: e.transpose(out=tpp2[0:TT, tt * 128:(tt + 1) * 128], in_=y_t[:, tt * TT:(tt + 1) * TT], identity=ident[:]), [y_b, bid], [tpb2])
            P.op("act", lambda e: e.activation(out=yo_t[0:TT, :, :, g8 * 16:(g8 + 1) * 16], in_=tpp2[0:TT, 0:NTT * 128].rearrange("p (t i c) -> p t i c", t=NTT, i=8), func=AF.Copy), [tpb2], [yo_b])

        for o in range(8):
            uo_t, uo_b = uo.next()
            yo_t, yo_b = yo.next()
            for tt in range(NTT):
                P.dma("sp", uo_t[0:TT, tt, :, :], scr.U[tt * TT * 8:(tt + 1) * TT * 8, o * 128:(o + 1) * 128].rearrange("(p i) c -> p i c", i=8), [], [uo_b], uo_b)
            u2_t, u2_b = uo2.next()
            for tt in range(NTT):
                P.op("pool", lambda e, tt=tt, u2_t=u2_t, uo_t=uo_t: e.tensor_copy(out=u2_t[0:TT, tt, :, :, :], in_=uo_t[0:TT, tt, :, :].rearrange("p i (g c) -> p g i c", c=16)), [uo_b], [u2_b])
            for g8 in range(8):
                do_group(o, g8, u2_t, u2_b, yo_t, yo_b)
            for tt in range(NTT):
                P.dma("sp", scr.YS[tt * TT * 8:(tt + 1) * TT * 8, o * 128:(o + 1) * 128].rearrange("(p i) c -> p i c", i=8), yo_t[0:TT, tt, :, :], [yo_b], [scr.bV], yo_b)

    _phase(P, nc, body)


def phase_S5post(P, nc, C, L):
    NT = L // 128
    scr = C.scr

    def body(sb, ps):
        ident = sb("identP", [128, 128], BF16)
        identf = sb("identPf", [128, 128], F32)
        bid = P.buf("identP")
        P.dma("sp", identf[:], C.ident[:, :], [], [bid], bid)
        P.op("dve", lambda e: e.tensor_copy(out=ident[:], in_=identf[:]), [bid], [bid])
        stage = Rot(P, sb, "wstP", [128, 1024], F32, 2)
        w_g = sb("w_gP", [128, 8, 1024], BF16)
        bw = P.buf("w_gP")
        load_weight_bf16(P, sb, w_g, bw, C.s5_glu_w[0], 8, 1024, stage)
        drep = sb("drepP", [128, 1024], F32)
        brep = sb("brepP", [128, 1024], F32)
        br = P.buf("repP")
        P.dma("sp", drep[:], C.s5_d[0].partition_broadcast(128), [], [br], br, slow=True)
        P.dma("sp", brep[:], C.s5_glu_b[0].partition_broadcast(128), [], [br], br, slow=True)
        ut = Rot(P, sb, "utP", [128, 1024], BF16, 2)
        yst = Rot(P, sb, "ystP", [128, 1024], BF16, 2)
        y = Rot(P, sb, "yP", [128, 1024], F32, 2)
        w = Rot(P, sb, "wP", [128, 1024], F32, 2)
        sg = Rot(P, sb, "sgP", [128, 1024], F32, 2)
        yg = Rot(P, sb, "ygP", [128, 1024], BF16, 2)
        ygT = Rot(P, sb, "ygTP", [128, 8, 128], BF16, 2)
        ya = Rot(P, sb, "yaP", [128, 1024], BF16, 2)
        tp = Rot(P, ps, "tpP", [128, 1024], BF16, 2)
        zp = Rot(P, ps, "zpP", [128, 512], F32, 4)
        for t in range(NT):
            rows = slice(t * 128, (t + 1) * 128)
            u_t, u_b = ut.next()
            s_t, s_b = yst.next()
            P.dma("sp", u_t[:], scr.U[rows, :], [], [u_b], u_b)
            P.dma("sp", s_t[:], scr.YS[rows, :], [], [s_b], s_b)
            y_t, y_b = y.next()
            w_t, w_b = w.next()
            P.op("pool", lambda e, y_t=y_t, u_t=u_t: e.tensor_tensor(out=y_t[:], in0=u_t[:], in1=drep[:], op=ALU.mult), [u_b, br], [y_b])
            P.op("pool", lambda e, y_t=y_t, s_t=s_t: e.tensor_tensor(out=y_t[:], in0=y_t[:], in1=s_t[:], op=ALU.add), [s_b, y_b], [y_b])
            P.op("dve", lambda e, y_t=y_t, w_t=w_t: e.tensor_tensor(out=w_t[:], in0=y_t[:], in1=y_t[:], op=ALU.mult), [y_b], [w_b])
            P.op("pool", lambda e, w_t=w_t: e.tensor_scalar(out=w_t[:], in0=w_t[:], scalar1=0.044715, scalar2=1.0, op0=ALU.mult, op1=ALU.add), [w_b], [w_b])
            P.op("pool", lambda e, w_t=w_t, y_t=y_t: e.tensor_tensor(out=w_t[:], in0=w_t[:], in1=y_t[:], op=ALU.mult), [w_b, y_b], [w_b])
            g_t, g_b = sg.next()
            P.op("act", lambda e, w_t=w_t, g_t=g_t: e.activation(out=g_t[:], in_=w_t[:], func=AF.Sigmoid, scale=2.0 * math.sqrt(2.0 / math.pi)), [w_b], [g_b])
            yg_t, yg_b = yg.next()
            P.op("dve", lambda e, yg_t=yg_t, y_t=y_t, g_t=g_t: e.tensor_tensor(out=yg_t[:], in0=y_t[:], in1=g_t[:], op=ALU.mult), [y_b, g_b], [yg_b])
            tpp, tpb = tp.next()
            for k in range(8):
                P.op("pe", lambda e, k=k, tpp=tpp, yg_t=yg_t: e.transpose(out=tpp[:, k * 128:(k + 1) * 128], in_=yg_t[:, k * 128:(k + 1) * 128], identity=ident[:]), [yg_b, bid], [tpb])
            yT, yTb = ygT.next()
            P.op("act", lambda e, tpp=tpp, yT=yT: e.activation(out=yT[:].rearrange("p k t -> p (k t)"), in_=tpp[:], func=AF.Copy), [tpb], [yTb])
            for gi in range(2):
                z, zb = zp.next()
                for k in range(8):
                    P.op("pe", lambda e, z=z, k=k, gi=gi, yT=yT: e.matmul(z[:], lhsT=yT[:, k, :], rhs=w_g[:, k, gi * 512:(gi + 1) * 512], start=(k == 0), stop=(k == 7)), [yTb, bw], [zb])
                P.op("act", lambda e, z=z, gi=gi, g_t=g_t: e.activation(out=g_t[:, gi * 512:(gi + 1) * 512], in_=z[:], func=AF.Copy), [zb], [g_b])
            P.op("pool", lambda e, g_t=g_t: e.tensor_tensor(out=g_t[:], in0=g_t[:], in1=brep[:], op=ALU.add), [g_b, br], [g_b])
            P.op("act", lambda e, g_t=g_t: e.activation(out=g_t[:], in_=g_t[:], func=AF.Sigmoid), [g_b], [g_b])
            a_t, a_b = ya.next()
            P.op("dve", lambda e, a_t=a_t, yg_t=yg_t, g_t=g_t: e.tensor_tensor(out=a_t[:], in0=yg_t[:], in1=g_t[:], op=ALU.mult), [yg_b, g_b], [a_b])
            P.dma("pool", scr.YA[rows, :], a_t[:], [a_b], [scr.bU], a_b)

    _phase(P, nc, body)


WNAMES = {
    "norm_g": [2, 1024], "final_g": [1024], "ple_w": [2, 256, 1024], "ple_gate_w": [2, 1024, 1024],
    "ab_w_in": [1, 1024, 3776], "ab_w_out": [1, 2048, 1024],
    "s5_a_re": [1, 2, 64, 64], "s5_a_im": [1, 2, 64, 64], "s5_log_dt": [1, 2, 64],
    "s5_b_re": [1, 2, 64, 64, 16], "s5_b_im": [1, 2, 64, 64, 16], "s5_c_re": [1, 2, 64, 16, 64], "s5_c_im": [1, 2, 64, 16, 64],
    "s5_d": [1, 1024], "s5_glu_w": [1, 1024, 1024], "s5_glu_b": [1, 1024],
    "mla_q_norm": [1, 384], "mla_w_q_up": [1, 384, 1536], "mla_kv_norm": [1, 256], "mla_w_kv_up": [1, 256, 2048],
    "hy_w_in": [1, 1024, 8192], "hy_w_out": [1, 2048, 1024], "hy_conv_w": [1, 3, 6144], "hy_conv_b": [1, 6144],
    "hy_f_w1": [1, 2, 33, 64], "hy_f_b1": [1, 2, 64], "hy_f_freq1": [1, 2, 64], "hy_f_w2": [1, 2, 64, 64], "hy_f_b2": [1, 2, 64],
    "hy_f_freq2": [1, 2, 64], "hy_f_w3": [1, 2, 64, 2048], "hy_bias": [1, 2048],
}


def build(Ls, opts):
    nc = bass.Bass("TRN2", target_bir_lowering=False)
    C = Ctx()
    LM = max(Ls)
    for n, shp in WNAMES.items():
        setattr(C, n, nc.dram_tensor(n, shp, F32, kind="ExternalInput").ap())
    C.ident = nc.dram_tensor("ident", [128, 128], F32, kind="ExternalInput").ap()
    C.rope_cs = nc.dram_tensor("rope_cs", [LM, 64], F32, kind="ExternalInput").ap()
    xs, ps_, ys = [], [], []
    for i, L in enumerate(Ls):
        xs.append(nc.dram_tensor(f"x{i}", [L, 1024], F32, kind="ExternalInput").ap())
        ps_.append(nc.dram_tensor(f"p{i}", [2, L, 256], F32, kind="ExternalInput").ap())
        ys.append(nc.dram_tensor(f"y{i}", [L, 1024], F32, kind="ExternalOutput").ap())
    dbg = opts.get("dbg", ())
    scr = Ctx()
    C.scr = scr

    def scratch(name, shape, dtype):
        kind = "ExternalOutput" if name in dbg else "Internal"
        return nc.dram_tensor("scr_" + name, shape, dtype, kind=kind).ap()

    scr.U = scratch("U", [LM, 1024], BF16)
    scr.G = scratch("G", [LM, 2048], BF16)
    scr.V = scratch("V", [LM, 1024], BF16)
    scr.QN = scratch("QN", [8, 128, LM], BF16)
    scr.QR = scratch("QR", [8, 64, LM], BF16)
    scr.KN = scratch("KN", [8, 128, LM], BF16)
    scr.KR = scratch("KR", [64, LM], BF16)
    scr.O = scratch("O", [LM, 1024], BF16)
    scr.YA = scratch("YA", [LM, 1024], BF16)
    scr.YH = scratch("YH", [LM, 2048], BF16)
    scr.H1 = scratch("H1", [LM, 1024], F32)
    scr.HT = scratch("HT", [8, 128, LM + 2], BF16)
    scr.MT = scratch("MT", [16, 128, LM], BF16)
    N1M = 2 * LM // 128
    scr.AT = scratch("AT", [2, N1M, 128, 128], BF16)
    scr.ZT = scratch("ZT", [2, 128, N1M, 128], BF16)
    scr.YHAT = scratch("YHAT", [16, 2, 128, N1M, 128], BF16)
    scr.VX = scratch("VX", [LM, 2048], BF16)
    scr.XG = scratch("XG", [LM, 2048], BF16)
    scr.CV = scratch("CV", [LM, 2048], BF16)
    C.hy_delta = nc.dram_tensor("hy_delta", [2048], F32, kind="ExternalInput").ap()
    C.s5_rowmask = nc.dram_tensor("s5_rowmask", [128, 4], F32, kind="ExternalInput").ap()
    C.s5_mf = nc.dram_tensor("s5_mf", [128, 128], F32, kind="ExternalInput").ap()
    C.s5_mb = nc.dram_tensor("s5_mb", [128, 128], F32, kind="ExternalInput").ap()
    C.s5_swap = nc.dram_tensor("s5_swap", [128, 128], F32, kind="ExternalInput").ap()
    NSM = s5_nstages(LM // 8)
    scr.SSC = scratch("SSC", [128, NSM * 3, 2, 128], F32)
    scr.SB8 = scratch("SB8", [128, 128, 128], BF16)
    scr.SC8 = scratch("SC8", [128, 128, 128], BF16)
    scr.SD8 = scratch("SD8", [128, 128, 128], BF16)
    scr.YS = scratch("YS", [LM, 1024], BF16)
    fftc = {}
    for L in sorted(set(Ls)):
        tbs, N1, KL = fft_tables_shapes(L)
        d = {"tb": {k_: nc.dram_tensor(f"{k_}_{L}", list(shp), F32, kind="ExternalInput").ap() for k_, shp in tbs.items()}}
        d["zT"] = nc.dram_tensor(f"zT_{L}", [33, 2 * L], F32, kind="ExternalInput").ap()
        d["tl"] = nc.dram_tensor(f"tl_{L}", [2 * L], F32, kind="ExternalInput").ap()
        d["KT"] = scratch(f"KT_{L}", [2 * L, 2048], BF16)
        d["KH"] = scratch(f"KH_{L}", [16, 2, 128, N1, 128], F32)
        d["RS"] = scratch(f"RS_{L}", [1, 2048], F32)
        fftc[L] = d
    with ExitStack() as es:
        P = Prog(nc, es)
        for n in ["bU", "bG", "bV", "bQ", "bK", "bO", "bH", "bA"]:
            b = Buf(n, accum=True)
            setattr(scr, n, b)
        depth = opts.get("depth", 2)
        conv = opts.get("conv", True) and depth == 2
        if opts.get("s5", True):
            phase_S5setup(P, nc, C, NSM)
        if conv:
            for L in sorted(set(Ls)):
                d = fftc[L]
                phase_G(P, nc, C, L, d["zT"], d["tl"], d["KT"], d["RS"])
                phase_F(P, nc, C, L, d["KT"], 2 * L // 128, d["tb"], khat_dst=d["KH"])
        for i, L in enumerate(Ls):
            phs = opts.get("phases", "ABC")
            if "A" in phs:
                phase_A(P, nc, C, L, xs[i])
            if opts.get("s5", True):
                phase_S5main(P, nc, C, L, s5_nstages(L // 8))
                phase_S5post(P, nc, C, L)
            if "B" in phs:
                phase_B(P, nc, C, L)
            last = depth == 1
            if "C" in phs:
                phase_C(P, nc, C, L, 0, xs[i], ps_[i][0], C.ab_w_out[0], scr.H1, ys[i] if last else None, opts.get("s5", True))
            if depth == 2:
                phase_D1(P, nc, C, L)
                phase_D2(P, nc, C, L)
                d = fftc[L]
                phase_F(P, nc, C, L, scr.VX, L // 128, d["tb"], khat_src=d["KH"], yhat_dst=scr.YHAT)
                phase_I(P, nc, C, L, d["tb"], scr.YHAT)
                phase_C(P, nc, C, L, 1, scr.H1, ps_[i][1], C.hy_w_out[0], None, ys[i], False, rs_src=d["RS"])
        C.ninstr = P.ninstr
    return nc, C


def host_consts(LM):
    inv = 1.0 / (10000.0 ** (np.arange(0, 64, 2, dtype=np.float32) / 64.0))
    ang = np.arange(LM, dtype=np.float32)[:, None] * inv[None, :].astype(np.float32)
    cs = np.concatenate([np.cos(ang), np.sin(ang)], axis=1).astype(np.float32)
    out = {"ident": np.eye(128, dtype=np.float32), "rope_cs": cs}
    min_decay = math.log(1e-2) / 1.5
    max_decay = math.log(1e-2) / 0.3
    p_ = np.arange(128)
    rm = np.zeros((128, 4), np.float32)
    rm[:64, 0] = 1.0
    rm[64:, 1] = 1.0
    rm[:, 2] = np.where(p_ < 64, 1.0, -1.0)
    rm[64:, 3] = -1.0
    out["s5_rowmask"] = rm
    ii = p_ // 16
    out["s5_mf"] = (ii[None, :] >= ii[:, None]).astype(np.float32)
    out["s5_mb"] = (ii[None, :] <= ii[:, None]).astype(np.float32)
    sw = np.zeros((128, 128), np.float32)
    sw[p_, (p_ + 64) % 128] = 1.0
    out["s5_swap"] = sw
    out["hy_delta"] = np.abs(np.linspace(min_decay, max_decay, 2048, dtype=np.float32)).astype(np.float32)
    return out


def host_consts_L(L):
    out = {}
    tbs, N1, KL = fft_tables(L)
    for k_, v in tbs.items():
        out[f"{k_}_{L}"] = v
    t = np.linspace(0.0, 1.0, L, dtype=np.float32)[:, None]
    w = (2.0 * math.pi * np.arange(L, dtype=np.float32)[:, None] / L).astype(np.float32)
    bands = np.linspace(1e-4, 15, 16, dtype=np.float32)[None, :]
    z = np.concatenate([t, np.cos(bands * w), -np.sin(bands * w)], axis=-1).astype(np.float32)
    idx = np.concatenate([np.arange(L), np.array([0]), L - np.arange(1, L)])
    z2 = z[idx]
    tl = t[:, 0][idx].copy()
    tl[L] = 1.0e4
    out[f"zT_{L}"] = np.ascontiguousarray(z2.T.astype(np.float32))
    out[f"tl_{L}"] = np.ascontiguousarray(tl.astype(np.float32))
    return out


_CACHE = {}


def kernel(**inputs):
    Ls = [4096, 8192]
    if "nc" not in _CACHE:
        _CACHE["nc"] = build(Ls, {"depth": 2, "s5": True})
    nc, C = _CACHE["nc"]
    consts = host_consts(max(Ls))
    for L_ in Ls:
        consts.update(host_consts_L(L_))
    W = {n: np.ascontiguousarray(np.asarray(inputs[n], dtype=np.float32)) for n in WNAMES}
    xs, xp = np.asarray(inputs["x_sample"]), np.asarray(inputs["x_prompt"])
    psm, ppr = np.asarray(inputs["p_sample"]), np.asarray(inputs["p_prompt"])
    in_maps = []
    for c in range(8):
        m = dict(W)
        m.update(consts)
        m["x0"] = np.ascontiguousarray(xs[c])
        m["p0"] = np.ascontiguousarray(psm[:, c])
        m["x1"] = np.ascontiguousarray(xp[c % 2])
        m["p1"] = np.ascontiguousarray(ppr[:, c % 2])
        in_maps.append(m)
    res = run_bass_kernel_spmd(nc, in_maps, core_ids=list(range(8)))
    y_sample = np.stack([np.asarray(res.results[c]["y0"], dtype=np.float32) for c in range(8)], axis=0)
    y_prompt = np.stack([np.asarray(res.results[c]["y1"], dtype=np.float32) for c in range(2)], axis=0)
    return (y_prompt, y_sample)
```

```python
import math
from contextlib import ExitStack
import numpy as np
import concourse.bass as bass
import concourse.mybir as mybir
from concourse.bass_utils import run_bass_kernel_spmd

F32 = mybir.dt.float32
BF16 = mybir.dt.bfloat16
ALU = mybir.AluOpType
AF = mybir.ActivationFunctionType
AX = mybir.AxisListType

D = 1024
EPS = 1e-6
import os
CUT = float(os.environ.get('CUT', '99'))
CW = int(os.environ.get('CW', '640'))


class Buf:
    def __init__(self, name, accum=False):
        self.name = name
        self.w = {}
        self.r = {}
        self.accum = accum


class Prog:
    ENG = ["pe", "act", "dve", "pool", "sp"]

    def __init__(self, nc, es):
        self.nc = nc
        self.es = es
        self.q = {e: [] for e in self.ENG}
        self.sems = {}
        self.cnt = {}
        self.seen = {e: {} for e in self.ENG}
        ss = os.environ.get("SELF", "act,dve,pool").split(",")
        self.selfsync = {"pe": False, "act": "act" in ss, "dve": "dve" in ss, "pool": "pool" in ss, "sp": False}
        self.bufs = []
        self.ninstr = 0
        self.dma_map = {}
        for e in self.ENG:
            self.newsem(e)

    def newsem(self, key):
        self.sems[key] = self.es.enter_context(self.nc.semaphore("s_" + str(key)))
        self.cnt[key] = 0

    def buf(self, name, accum=False):
        b = Buf(name, accum)
        self.bufs.append(b)
        return b

    def op(self, eng, fn, reads=(), writes=(), dma=None):
        waits = {}

        def need(tok):
            for k, v in tok.items():
                if v > waits.get(k, 0):
                    waits[k] = v

        for b in reads:
            need(b.w)
        raw_self = waits.get(eng, 0)
        for b in writes:
            if not b.accum:
                need(b.w)
            need(b.r)
        wl = []
        for k, v in waits.items():
            if k == eng:
                if not self.selfsync[eng]:
                    continue
            if self.seen[eng].get(k, 0) >= v:
                continue
            self.seen[eng][k] = v
            wl.append((k, v))
        if dma is None:
            key, inc = eng, 1
        else:
            key, inc = dma, 16
            if key not in self.sems:
                self.newsem(key)
        self.cnt[key] += inc
        val = self.cnt[key]
        sems = self.sems

        def emit(e, wl=wl, fn=fn, key=key, inc=inc):
            for k, v in wl:
                e.wait_ge(sems[k], v)
            fn(e).then_inc(sems[key], inc)

        self.q[eng].append(emit)
        self.ninstr += 1
        if os.environ.get("KTRACE"):
            print("OP", self.ninstr, eng, "tok", key, val, "waits", wl, "R", [b.name for b in reads], "W", [b.name for b in writes])
        for b in reads:
            b.r[key] = max(b.r.get(key, 0), val)
        for b in writes:
            if b.accum:
                b.w[key] = max(b.w.get(key, 0), val)
            else:
                b.w = {key: val}
                b.r = {}

    def dma(self, eng, out, in_, reads, writes, key, slow=False):
        if isinstance(key, Buf):
            key = key.name
        key = (eng == "pool", key)
        if key not in self.dma_map:
            n = sum(1 for kk in self.dma_map if kk[0] == key[0])
            self.dma_map[key] = ("dmaG%d" if key[0] else "dmaS%d") % n
        if slow:
            self.op(eng, lambda e: e.dma_start(out=out, in_=in_, allow_slow_non_contiguous=True), reads, writes, dma=self.dma_map[key])
        else:
            self.op(eng, lambda e: e.dma_start(out=out, in_=in_), reads, writes, dma=self.dma_map[key])

    def sync_dram(self, b):
        return b

    def barrier(self):
        snap = dict(self.cnt)
        sems = self.sems
        for e in self.ENG:
            wl = []
            for k, v in snap.items():
                if v > 0 and self.seen[e].get(k, 0) < v:
                    self.seen[e][k] = v
                    wl.append((k, v))

            def emit(eh, wl=wl):
                for k, v in wl:
                    eh.wait_ge(sems[k], v)

            self.q[e].append(emit)
        self.dma_map = {}
        for b in self.bufs:
            b.w = {}
            b.r = {}
        self.bufs = [b for b in self.bufs if getattr(b, "persist", False)]

    def flush(self, block):
        q = self.q

        @block.tensor
        def _(e):
            for f in q["pe"]:
                f(e)

        @block.scalar
        def _(e):
            for f in q["act"]:
                f(e)

        @block.vector
        def _(e):
            for f in q["dve"]:
                f(e)

        @block.gpsimd
        def _(e):
            for f in q["pool"]:
                f(e)

        @block.sync
        def _(e):
            for f in q["sp"]:
                f(e)

        self.q = {e: [] for e in self.ENG}


class Rot:
    def __init__(self, P, alloc, name, shape, dtype, n):
        self.slots = []
        for i in range(n):
            t = alloc(f"{name}{i}", shape, dtype)
            self.slots.append((t, P.buf(f"{name}{i}")))
        self.i = 0

    def next(self):
        s = self.slots[self.i % len(self.slots)]
        self.i += 1
        return s


class Ctx:
    pass


_PH = [0]


_ONLY = [None]


def _phase(P, nc, body):
    if _ONLY[0] is not None and body.__qualname__.split(".")[0] not in _ONLY[0]:
        return
    _PH[0] += 1
    sfx = "_%d" % _PH[0]
    with ExitStack() as ph, nc.Block() as block:
        tot = [0]

        def sb(name, shape, dtype):
            n = 1
            for d in shape[1:]:
                n *= d
            tot[0] += ((n * (2 if dtype == BF16 else 4) + 31) // 32) * 32
            return ph.enter_context(nc.sbuf_tensor(name + sfx, shape, dtype))

        def ps(name, shape, dtype):
            return ph.enter_context(nc.psum_tensor(name + sfx, shape, dtype))

        body(sb, ps)
        if os.environ.get("KDEBUG"):
            print("phase", sfx, body.__qualname__, "sbuf bytes/partition", tot[0], "instr", P.ninstr)
        assert tot[0] <= 206 * 1024, tot[0]
        P.barrier()
        P.flush(block)


def load_weight_bf16(P, sb, wdst, wbuf, wsrc, K, N, stage_rot, rowscale=None, chunk=1024):
    i = 0
    for k in range(K):
        for c0 in range(0, N, chunk):
            c1 = min(N, c0 + chunk)
            st, stb = stage_rot.next()
            P.dma("sp", st[:, 0:c1 - c0], wsrc[k * 128:(k + 1) * 128, c0:c1], [], [stb], stb)
            eng = "dve" if i % 2 == 0 else "pool"
            if rowscale is None:
                P.op(eng, lambda e, st=st, k=k, c0=c0, c1=c1: e.tensor_copy(out=wdst[:, k, c0:c1], in_=st[:, 0:c1 - c0]), [stb], [wbuf])
            else:
                rs, rsb = rowscale
                P.op(eng, lambda e, st=st, k=k, c0=c0, c1=c1, rs=rs: e.tensor_scalar(out=wdst[:, k, c0:c1], in0=st[:, 0:c1 - c0], scalar1=rs[:, k:k + 1], scalar2=None, op0=ALU.mult), [stb, rsb], [wbuf])
            i += 1


def rstd_from_ssq(P, eng, rstd, ssq, n, R, W):
    P.op(eng, lambda e: e.tensor_scalar(out=rstd, in0=ssq, scalar1=1.0 / n, scalar2=EPS, op0=ALU.mult, op1=ALU.add), R, W)
    P.op("act", lambda e: e.activation(out=rstd, in_=rstd, func=AF.Sqrt), W, W)
    P.op(eng, lambda e: e.reciprocal(out=rstd, in_=rstd), W, W)


def phase_A(P, nc, C, L, x_ap):
    NT = L // 128
    scr = C.scr

    def body(sb, ps):
        ident = sb("identA", [128, 128], BF16)
        identf = sb("identAf", [128, 128], F32)
        bid = P.buf("ident")
        P.dma("sp", identf[:], C.ident[:, :], [], [bid], bid)
        P.op("dve", lambda e: e.tensor_copy(out=ident[:], in_=identf[:]), [bid], [bid])
        ng = sb("ngA", [128, 8], F32)
        qn = sb("qnA", [128, 3], F32)
        kvn = sb("kvnA", [128, 2], F32)
        bsm = P.buf("smallA")
        P.dma("sp", ng[:], C.norm_g[0].rearrange("(k p) -> p k", p=128), [], [bsm], bsm, slow=True)
        P.dma("sp", qn[:], C.mla_q_norm[0].rearrange("(k p) -> p k", p=128), [], [bsm], bsm, slow=True)
        P.dma("sp", kvn[:], C.mla_kv_norm[0].rearrange("(k p) -> p k", p=128), [], [bsm], bsm, slow=True)
        stage = Rot(P, sb, "wstA", [128, 1024], F32, 2)
        w_in = sb("w_inA", [128, 8, 3776], BF16)
        w_q = sb("w_qA", [128, 3, 1536], BF16)
        w_kv = sb("w_kvA", [128, 2, 2048], BF16)
        bw_in, bw_q, bw_kv = P.buf("w_in"), P.buf("w_q"), P.buf("w_kv")
        load_weight_bf16(P, sb, w_in, bw_in, C.ab_w_in[0], 8, 3776, stage, rowscale=(ng, bsm))
        load_weight_bf16(P, sb, w_q, bw_q, C.mla_w_q_up[0], 3, 1536, stage, rowscale=(qn, bsm))
        load_weight_bf16(P, sb, w_kv, bw_kv, C.mla_w_kv_up[0], 2, 2048, stage, rowscale=(kvn, bsm))

        xt = Rot(P, sb, "xtA", [128, 1024], F32, 2)
        cst = Rot(P, sb, "csA", [128, 64], F32, 2)
        junk = sb("junkA", [128, 1024], BF16)
        bjunk = P.buf("junkA")
        stat = Rot(P, sb, "statA", [128, 8], F32, 2)
        hn = Rot(P, sb, "hnA", [128, 1024], BF16, 2)
        hnT = Rot(P, sb, "hnTA", [128, 8, 128], BF16, 2)
        u_sb = Rot(P, sb, "uA", [128, 1024], BF16, 2)
        g_sb = Rot(P, sb, "gA", [128, 2048], BF16, 2)
        lat = Rot(P, sb, "latA", [128, 640], BF16, 2)
        latT = Rot(P, sb, "latTA", [128, int(os.environ.get("LATK", "5")), 128], BF16, 2)
        kr32 = Rot(P, sb, "kr32A", [128, 64], F32, 2)
        q_sb = Rot(P, sb, "qA", [128, 1536], BF16, 2)
        qtmp = Rot(P, sb, "qtmpA", [128, 4, 256], F32, 2)
        kn_sb = Rot(P, sb, "knA", [128, 1024], BF16, 2)
        v_sb = Rot(P, sb, "vA", [128, 1024], BF16, 2)
        kr_sb = Rot(P, sb, "krA", [128, 64], BF16, 2)
        qnT = Rot(P, sb, "qnTA", [128, 8, 256], BF16, 2)
        qrT = Rot(P, sb, "qrTA", [64, 8, 256], BF16, 2)
        knT = Rot(P, sb, "knTA", [128, 8, 256], BF16, 2)
        krT = Rot(P, sb, "krTA", [64, 256], BF16, 2)
        zp = Rot(P, ps, "zpA", [128, 512], F32, 2)
        tp = Rot(P, ps, "tpA", [128, 1024], BF16, 2)
        tq = Rot(P, ps, "tqA", [128, 1024], BF16, 2)
        qp = Rot(P, ps, "qpA", [128, 512], F32, 2)

        def in_proj_group(hT, hTb, c0, c1):
            z, zb = zp.next()
            for k in range(8):
                P.op("pe", lambda e, z=z, k=k: e.matmul(z[:, 0:c1 - c0], lhsT=hT[:, k, :], rhs=w_in[:, k, c0:c1], start=(k == 0), stop=(k == 7)), [hTb, bw_in], [zb])
            return z, zb

        cur = None
        for t in range(NT if CUT > 1 else 0):
            t4 = t % 2
            if t4 == 0:
                cur = (qnT.next(), qrT.next(), knT.next(), krT.next())
            (qnT_t, qnT_b), (qrT_t, qrT_b), (knT_t, knT_b), (krT_t, krT_b) = cur
            x, xb = xt.next()
            cs, csb = cst.next()
            P.dma("sp", x[:], x_ap[t * 128:(t + 1) * 128, :], [], [xb], xb)
            P.dma("sp", cs[:], C.rope_cs[t * 128:(t + 1) * 128, :], [], [csb], csb)
            st, stb = stat.next()
            P.op("act", lambda e, x=x, st=st: e.activation(out=junk[:], in_=x[:], func=AF.Square, accum_out=st[:, 0:1]), [xb], [bjunk, stb])
            rstd_from_ssq(P, "dve", st[:, 1:2], st[:, 0:1], D, [stb], [stb])
            h, hb = hn.next()
            P.op("act", lambda e, x=x, st=st, h=h: e.activation(out=h[:], in_=x[:], func=AF.Copy, scale=st[:, 1:2]), [xb, stb], [hb])
            tpp, tpb = tp.next()
            for k in range(8):
                P.op("pe", lambda e, k=k, tpp=tpp, h=h: e.transpose(out=tpp[:, k * 128:(k + 1) * 128], in_=h[:, k * 128:(k + 1) * 128], identity=ident[:]), [hb, bid], [tpb])
            hT, hTb = hnT.next()
            P.op("act", lambda e, hT=hT, tpp=tpp: e.activation(out=hT[:].rearrange("p k t -> p (k t)"), in_=tpp[:], func=AF.Copy), [tpb], [hTb])
            u, ub = u_sb.next()
            for gi in range(2):
                z, zb = in_proj_group(hT, hTb, gi * 512, gi * 512 + 512)
                P.op("act", lambda e, z=z, u=u, gi=gi: e.activation(out=u[:, gi * 512:(gi + 1) * 512], in_=z[:], func=AF.Copy), [zb], [ub])
            P.dma("pool", scr.U[t * 128:(t + 1) * 128, :], u[:], [ub], [scr.bU], ub)
            if CUT <= 2:
                continue
            la, lab = lat.next()
            z, zb = in_proj_group(hT, hTb, 1024, 1408)
            P.op("act", lambda e, z=z, st=st: e.activation(out=junk[:, 0:384], in_=z[:, 0:384], func=AF.Square, accum_out=st[:, 2:3]), [zb], [bjunk, stb])
            rstd_from_ssq(P, "dve", st[:, 3:4], st[:, 2:3], 384, [stb], [stb])
            P.op("act", lambda e, z=z, st=st, la=la: e.activation(out=la[:, 0:384], in_=z[:, 0:384], func=AF.Copy, scale=st[:, 3:4]), [zb, stb], [lab])
            z, zb = in_proj_group(hT, hTb, 1408, 1728)
            P.op("act", lambda e, z=z, st=st: e.activation(out=junk[:, 0:256], in_=z[:, 0:256], func=AF.Square, accum_out=st[:, 4:5]), [zb], [bjunk, stb])
            rstd_from_ssq(P, "dve", st[:, 5:6], st[:, 4:5], 256, [stb], [stb])
            P.op("act", lambda e, z=z, st=st, la=la: e.activation(out=la[:, 384:640], in_=z[:, 0:256], func=AF.Copy, scale=st[:, 5:6]), [zb, stb], [lab])
            k32, k32b = kr32.next()
            P.op("act", lambda e, z=z, k32=k32: e.activation(out=k32[:], in_=z[:, 256:320], func=AF.Copy), [zb], [k32b])
            kr, krb = kr_sb.next()
            qt, qtb = qtmp.next()
            P.op("pool", lambda e, k32=k32, cs=cs, qt=qt: e.tensor_tensor(out=qt[:, 0, 0:32], in0=k32[:, 0:32], in1=cs[:, 0:32], op=ALU.mult), [k32b, csb], [qtb])
            P.op("pool", lambda e, k32=k32, cs=cs, qt=qt: e.tensor_tensor(out=qt[:, 0, 32:64], in0=k32[:, 32:64], in1=cs[:, 32:64], op=ALU.mult), [k32b, csb], [qtb])
            P.op("pool", lambda e, kr=kr, qt=qt: e.tensor_tensor(out=kr[:, 0:32], in0=qt[:, 0, 0:32], in1=qt[:, 0, 32:64], op=ALU.subtract), [qtb], [krb])
            P.op("pool", lambda e, k32=k32, cs=cs, qt=qt: e.tensor_tensor(out=qt[:, 0, 0:32], in0=k32[:, 0:32], in1=cs[:, 32:64], op=ALU.mult), [k32b, csb, krb], [qtb])
            P.op("pool", lambda e, k32=k32, cs=cs, qt=qt: e.tensor_tensor(out=qt[:, 0, 32:64], in0=k32[:, 32:64], in1=cs[:, 0:32], op=ALU.mult), [k32b, csb], [qtb])
            P.op("pool", lambda e, kr=kr, qt=qt: e.tensor_tensor(out=kr[:, 32:64], in0=qt[:, 0, 0:32], in1=qt[:, 0, 32:64], op=ALU.add), [qtb], [krb])
            if CUT <= 3:
                continue
            g, gb = g_sb.next()
            for gi in range(4):
                z, zb = in_proj_group(hT, hTb, 1728 + gi * 512, 1728 + gi * 512 + 512)
                P.op("act", lambda e, z=z, g=g, gi=gi: e.activation(out=g[:, gi * 512:(gi + 1) * 512], in_=z[:], func=AF.Silu), [zb], [gb])
            P.dma("pool", scr.G[t * 128:(t + 1) * 128, :], g[:], [gb], [scr.bG], gb)
            if CUT <= 4:
                continue
            tqq, tqb = tq.next()
            for k in range(int(os.environ.get("NK", "5"))):
                P.op("pe", lambda e, k=k, tqq=tqq, la=la: e.transpose(out=tqq[:, k * 128:(k + 1) * 128], in_=(h if os.environ.get("SRCH") else la)[:, k * 128:(k + 1) * 128], identity=ident[:]), [lab, bid, hb], [tqb])
            lT, lTb = latT.next()
            if not os.environ.get("NOCOPY"):
                if True:
                    P.op("act", lambda e, lT=lT, tqq=tqq: e.activation(out=lT[:].rearrange("p k t -> p (k t)"), in_=tqq[:, 0:640], func=AF.Copy), [tqb], [lTb])
                else:
                    if os.environ.get("JUNKDST"):
                        P.op("dve", lambda e, lT=lT, tqq=tqq: e.tensor_copy(out=junk[:, 0:CW], in_=tqq[:, 0:CW]), [tqb], [bjunk])
                    elif os.environ.get("TSCOPY"):
                        P.op("dve", lambda e, lT=lT, tqq=tqq: e.tensor_scalar(out=lT[:, 0:CW // 128, :].rearrange("p k t -> p (k t)"), in0=tqq[:, 0:CW], scalar1=1.0, scalar2=None, op0=ALU.mult), [tqb], [lTb])
                    else:
                        P.op("dve", lambda e, lT=lT, tqq=tqq: e.tensor_copy(out=lT[:, 0:CW // 128, :].rearrange("p k t -> p (k t)"), in_=tqq[:, 0:CW]), [tqb], [lTb])
            if CUT <= 4.1:
                continue
            q, qb = q_sb.next()
            for gi in range(3):
                pq, pqb = qp.next()
                for k in range(3):
                    P.op("pe", lambda e, pq=pq, k=k, gi=gi, lT=lT: e.matmul(pq[:], lhsT=lT[:, k, :], rhs=w_q[:, k, gi * 512:(gi + 1) * 512], start=(k == 0), stop=(k == 2)), [lTb, bw_q], [pqb])
                P.op("act", lambda e, pq=pq, q=q, gi=gi: e.activation(out=q[:, gi * 512:(gi + 1) * 512], in_=pq[:], func=AF.Copy), [pqb], [qb])
            if CUT <= 4.3:
                continue
            qv = q[:].rearrange("p (h d) -> p h d", h=8)
            x1 = qv[:, :, 128:160]
            x2 = qv[:, :, 160:192]
            cosb = cs[:, 0:32].unsqueeze(1).to_broadcast([128, 8, 32])
            sinb = cs[:, 32:64].unsqueeze(1).to_broadcast([128, 8, 32])
            qt, qtb = qtmp.next()
            a_ = qt[:, 0, :].rearrange("p (h d) -> p h d", h=8)
            b_ = qt[:, 1, :].rearrange("p (h d) -> p h d", h=8)
            c_ = qt[:, 2, :].rearrange("p (h d) -> p h d", h=8)
            d_ = qt[:, 3, :].rearrange("p (h d) -> p h d", h=8)
            P.op("pool", lambda e, a_=a_, x1=x1, cosb=cosb: e.tensor_tensor(out=a_, in0=x1, in1=cosb, op=ALU.mult), [qb, csb], [qtb])
            P.op("pool", lambda e, b_=b_, x2=x2, sinb=sinb: e.tensor_tensor(out=b_, in0=x2, in1=sinb, op=ALU.mult), [qb, csb], [qtb])
            P.op("pool", lambda e, c_=c_, x1=x1, sinb=sinb: e.tensor_tensor(out=c_, in0=x1, in1=sinb, op=ALU.mult), [qb, csb], [qtb])
            P.op("pool", lambda e, d_=d_, x2=x2, cosb=cosb: e.tensor_tensor(out=d_, in0=x2, in1=cosb, op=ALU.mult), [qb, csb], [qtb])
            P.op("pool", lambda e, a_=a_, b_=b_, x1=x1: e.tensor_tensor(out=x1, in0=a_, in1=b_, op=ALU.subtract), [qtb], [qb])
            P.op("pool", lambda e, c_=c_, d_=d_, x2=x2: e.tensor_tensor(out=x2, in0=c_, in1=d_, op=ALU.add), [qtb], [qb])
            if CUT <= 4.6:
                continue
            kn, knb = kn_sb.next()
            v, vb = v_sb.next()
            for gi in range(4):
                pq, pqb = qp.next()
                for k in range(2):
                    P.op("pe", lambda e, pq=pq, k=k, gi=gi, lT=lT: e.matmul(pq[:], lhsT=lT[:, 3 + k, :], rhs=w_kv[:, k, gi * 512:(gi + 1) * 512], start=(k == 0), stop=(k == 1)), [lTb, bw_kv], [pqb])
                pv = pq[:].rearrange("p (h d) -> p h d", h=2)
                P.op("act", lambda e, pv=pv, kn=kn, gi=gi: e.activation(out=kn[:, gi * 256:(gi + 1) * 256].rearrange("p (h d) -> p h d", h=2), in_=pv[:, :, 0:128], func=AF.Copy), [pqb], [knb])
                if os.environ.get("VMODE", "act") == "dve":
                    P.op("dve", lambda e, pv=pv, v=v, gi=gi: e.tensor_copy(out=v[:, gi * 256:(gi + 1) * 256].rearrange("p (h d) -> p h d", h=2), in_=pv[:, :, 128:256]), [pqb], [vb])
                else:
                    P.op("act", lambda e, pv=pv, v=v, gi=gi: e.activation(out=v[:, gi * 256:(gi + 1) * 256].rearrange("p (h d) -> p h d", h=2), in_=pv[:, :, 128:256], func=AF.Copy), [pqb], [vb])
            if os.environ.get("VMODE", "act") != "none":
                P.dma("pool", scr.V[t * 128:(t + 1) * 128, :], v[:], [vb], [scr.bV], vb)
            if CUT <= 5:
                continue
            tqq, tqb = tq.next()
            for hh in range(8):
                P.op("pe", lambda e, hh=hh, tqq=tqq, q=q: e.transpose(out=tqq[:, hh * 128:(hh + 1) * 128], in_=q[:, hh * 192:hh * 192 + 128], identity=ident[:]), [qb, bid], [tqb])
            P.op("act", lambda e, tqq=tqq, qnT_t=qnT_t, t4=t4: e.activation(out=qnT_t[:, :, t4 * 128:(t4 + 1) * 128], in_=tqq[:].rearrange("p (h t) -> p h t", h=8), func=AF.Copy), [tqb], [qnT_b])
            tqq, tqb = tq.next()
            for hh in range(8):
                P.op("pe", lambda e, hh=hh, tqq=tqq, q=q: e.transpose(out=tqq[0:64, hh * 128:(hh + 1) * 128], in_=q[:, hh * 192 + 128:hh * 192 + 192], identity=ident[:]), [qb, bid], [tqb])
            P.op("act", lambda e, tqq=tqq, qrT_t=qrT_t, t4=t4: e.activation(out=qrT_t[:, :, t4 * 128:(t4 + 1) * 128], in_=tqq[0:64, :].rearrange("p (h t) -> p h t", h=8), func=AF.Copy), [tqb], [qrT_b])
            tqq, tqb = tq.next()
            for hh in range(8):
                P.op("pe", lambda e, hh=hh, tqq=tqq, kn=kn: e.transpose(out=tqq[:, hh * 128:(hh + 1) * 128], in_=kn[:, hh * 128:(hh + 1) * 128], identity=ident[:]), [knb, bid], [tqb])
            P.op("act", lambda e, tqq=tqq, knT_t=knT_t, t4=t4: e.activation(out=knT_t[:, :, t4 * 128:(t4 + 1) * 128], in_=tqq[:].rearrange("p (h t) -> p h t", h=8), func=AF.Copy), [tqb], [knT_b])
            tqq, tqb = tq.next()
            P.op("pe", lambda e, tqq=tqq, kr=kr: e.transpose(out=tqq[0:64, 0:128], in_=kr[:, 0:64], identity=ident[:]), [krb, bid], [tqb])
            P.op("act", lambda e, tqq=tqq, krT_t=krT_t, t4=t4: e.activation(out=krT_t[:, t4 * 128:(t4 + 1) * 128], in_=tqq[0:64, 0:128], func=AF.Copy), [tqb], [krT_b])
            if t4 == 1 or t == NT - 1:
                nt = (t4 + 1) * 128
                t0 = (t - t4) * 128
                P.dma("pool", scr.QN[:, :, t0:t0 + nt].rearrange("h d t -> d h t"), qnT_t[:, :, 0:nt], [qnT_b], [scr.bQ], qnT_b)
                P.dma("pool", scr.QR[:, :, t0:t0 + nt].rearrange("h d t -> d h t"), qrT_t[:, :, 0:nt], [qrT_b], [scr.bQ], qrT_b)
                P.dma("pool", scr.KN[:, :, t0:t0 + nt].rearrange("h d t -> d h t"), knT_t[:, :, 0:nt], [knT_b], [scr.bK], knT_b)
                P.dma("pool", scr.KR[:, t0:t0 + nt], krT_t[:, 0:nt], [krT_b], [scr.bK], krT_b)

    _phase(P, nc, body)


def phase_B(P, nc, C, L):
    NKB = L // 128
    NQB = L // 512
    scr = C.scr
    scale = 192.0 ** -0.5

    def body(sb, ps):
        krt = sb("krB", [64, L], BF16)
        bkr = P.buf("krB")
        P.dma("sp", krt[:], scr.KR[:, 0:L], [], [bkr], bkr)
        qn = Rot(P, sb, "qnB", [128, L], BF16, 2)
        qr = Rot(P, sb, "qrB", [64, L], BF16, 2)
        kn = Rot(P, sb, "knB", [128, L], BF16, 2)
        vt = Rot(P, sb, "vtB", [128, NKB, 129], BF16, 2)
        for (v, vb) in vt.slots:
            P.op("pool", lambda e, v=v: e.memset(v[:, :, 128:129], 1.0), [], [vb])
        pT = Rot(P, sb, "pTB", [128, 1024], BF16, 3)
        osb = Rot(P, sb, "osbB", [128, 4, 128], BF16, 2)
        rc = Rot(P, sb, "rcB", [128, 4], F32, 2)
        sp_ = Rot(P, ps, "spB", [128, 1024], F32, 2)
        oa = Rot(P, ps, "oaB", [128, 512], F32, 2)
        ob = Rot(P, ps, "obB", [128, 512], F32, 2)
        for h in range(8):
            qn_t, qn_b = qn.next()
            qr_t, qr_b = qr.next()
            kn_t, kn_b = kn.next()
            v_t, v_b = vt.next()
            P.dma("sp", qn_t[:], scr.QN[h, :, 0:L], [], [qn_b], qn_b)
            P.dma("sp", qr_t[:], scr.QR[h, :, 0:L], [], [qr_b], qr_b)
            P.dma("sp", kn_t[:], scr.KN[h, :, 0:L], [], [kn_b], kn_b)
            P.dma("sp", v_t[:, :, 0:128], scr.V[0:L, h * 128:(h + 1) * 128].rearrange("(kb p) d -> p kb d", p=128), [], [v_b], v_b)
            for qb in range(NQB):
                oa_t, oa_b = oa.next()
                ob_t, ob_b = ob.next()
                def emit_qk(kp, qb=qb, kn_t=kn_t, qn_t=qn_t, qr_t=qr_t, kn_b=kn_b, qn_b=qn_b, qr_b=qr_b):
                    s_t, s_b = sp_.next()
                    for hh in range(2):
                        kb = kp * 2 + hh
                        P.op("pe", lambda e, s_t=s_t, kb=kb, hh=hh: e.matmul(s_t[:, hh * 512:(hh + 1) * 512], lhsT=kn_t[:, kb * 128:(kb + 1) * 128], rhs=qn_t[:, qb * 512:(qb + 1) * 512], start=True, stop=False), [kn_b, qn_b], [s_b])
                        P.op("pe", lambda e, s_t=s_t, kb=kb, hh=hh: e.matmul(s_t[:, hh * 512:(hh + 1) * 512], lhsT=krt[:, kb * 128:(kb + 1) * 128], rhs=qr_t[:, qb * 512:(qb + 1) * 512], start=False, stop=True), [bkr, qr_b], [s_b])
                    return s_t, s_b

                NKP = NKB // 2
                pend = [emit_qk(0)]
                for kp in range(NKP):
                    if kp + 1 < NKP:
                        pend.append(emit_qk(kp + 1))
                    s_t, s_b = pend.pop(0)
                    p_t, p_b = pT.next()
                    P.op("act", lambda e, s_t=s_t, p_t=p_t: e.activation(out=p_t[:], in_=s_t[:], func=AF.Exp, scale=scale), [s_b], [p_b])
                    for hh in range(2):
                        kb = kp * 2 + hh
                        for sub in range(4):
                            acc_t, acc_b = (oa_t, oa_b) if sub < 2 else (ob_t, ob_b)
                            c0 = (sub % 2) * 129
                            P.op("pe", lambda e, acc_t=acc_t, c0=c0, p_t=p_t, sub=sub, v_t=v_t, kb=kb, hh=hh: e.matmul(acc_t[:, c0:c0 + 129], lhsT=p_t[:, hh * 512 + sub * 128:hh * 512 + (sub + 1) * 128], rhs=v_t[:, kb, :], start=(kb == 0), stop=(kb == NKB - 1), skip_group_check=True), [p_b, v_b], [acc_b])
                o_t, o_b = osb.next()
                r_t, r_b = rc.next()
                for sub in range(4):
                    acc_t, acc_b = (oa_t, oa_b) if sub < 2 else (ob_t, ob_b)
                    c0 = (sub % 2) * 129
                    P.op("act", lambda e, r_t=r_t, acc_t=acc_t, c0=c0, sub=sub: e.activation(out=r_t[:, sub:sub + 1], in_=acc_t[:, c0 + 128:c0 + 129], func=AF.Copy), [acc_b], [r_b])
                    P.op("dve", lambda e, r_t=r_t, sub=sub: e.reciprocal(out=r_t[:, sub:sub + 1], in_=r_t[:, sub:sub + 1]), [r_b], [r_b])
                    P.op("act", lambda e, r_t=r_t, acc_t=acc_t, c0=c0, sub=sub, o_t=o_t: e.activation(out=o_t[:, sub, :], in_=acc_t[:, c0:c0 + 128], func=AF.Copy, scale=r_t[:, sub:sub + 1]), [acc_b, r_b], [o_b])
                P.dma("pool", scr.O[qb * 512:(qb + 1) * 512, h * 128:(h + 1) * 128].rearrange("(s p) d -> p s d", p=128), o_t[:], [o_b], [scr.bO], o_b)

    _phase(P, nc, body)


def phase_C(P, nc, C, L, layer, h_in, p_ap, w_out_ap, h_out, final_out, use_s5, rs_src=None):
    NT = L // 128
    scr = C.scr

    def body(sb, ps):
        ident = sb("identC", [128, 128], BF16)
        identf = sb("identCf", [128, 128], F32)
        bid = P.buf("identC")
        P.dma("sp", identf[:], C.ident[:, :], [], [bid], bid)
        P.op("dve", lambda e: e.tensor_copy(out=ident[:], in_=identf[:]), [bid], [bid])
        stage = Rot(P, sb, "wstC", [128, 1024], F32, 2)
        w_out = sb("w_outC", [128, 16, 1024], BF16)
        w_pg = sb("w_pgC", [128, 8, 1024], BF16)
        w_pl = sb("w_plC", [128, 2, 1024], BF16)
        bw_out, bw_pg, bw_pl = P.buf("w_outC"), P.buf("w_pgC"), P.buf("w_plC")
        load_weight_bf16(P, sb, w_out, bw_out, w_out_ap, 16, 1024, stage)
        load_weight_bf16(P, sb, w_pg, bw_pg, C.ple_gate_w[layer], 8, 1024, stage)
        load_weight_bf16(P, sb, w_pl, bw_pl, C.ple_w[layer], 2, 1024, stage)
        if final_out is not None:
            fg = sb("fgC", [128, 1024], F32)
            bfg = P.buf("fgC")
            P.dma("sp", fg[:], C.final_g.partition_broadcast(128), [], [bfg], bfg, slow=True)
        ht = Rot(P, sb, "htC", [128, 1024], F32, 2)
        if layer == 0:
            gt = Rot(P, sb, "gtC", [128, 2048], BF16, 2)
            yt = Rot(P, sb, "ytC", [128, 2048], BF16, 2)
        else:
            cvt = Rot(P, sb, "cvtC", [128, 2048], BF16, 2)
            vxt = Rot(P, sb, "vxtC", [128, 2048], BF16, 2)
            xgt = Rot(P, sb, "xgtC", [128, 2048], BF16, 2)
            rsr = sb("rsrC", [128, 2048], F32)
            bir = sb("birC", [128, 2048], F32)
            brr = P.buf("rsrC")
            P.dma("sp", rsr[:], rs_src[0].partition_broadcast(128), [], [brr], brr, slow=True)
            P.dma("sp", bir[:], C.hy_bias[0].partition_broadcast(128), [], [brr], brr, slow=True)
            tm1 = sb("tm1C", [128, 2048], F32)
            tm2 = sb("tm2C", [128, 2048], F32)
            btm1, btm2 = P.buf("tm1C"), P.buf("tm2C")
        pt = Rot(P, sb, "ptC", [128, 256], F32, 2)
        pb16 = Rot(P, sb, "pb16C", [128, 256], BF16, 2)
        mt = Rot(P, sb, "mtC", [128, 2048], BF16, 2)
        mT = Rot(P, sb, "mTC", [128, 16, 128], BF16, 2)
        h2 = Rot(P, sb, "h2C", [128, 1024], F32, 2)
        h2b = Rot(P, sb, "h2bC", [128, 1024], BF16, 2)
        h2T = Rot(P, sb, "h2TC", [128, 8, 128], BF16, 2)
        pT = Rot(P, sb, "pTC", [128, 2, 128], BF16, 2)
        sg = Rot(P, sb, "sgC", [128, 1024], F32, 2)
        h3 = Rot(P, sb, "h3C", [128, 1024], F32, 2)
        stat = Rot(P, sb, "statC", [128, 4], F32, 2)
        junk = sb("junkC", [128, 1024], BF16)
        bjunk = P.buf("junkC")
        tp = Rot(P, ps, "tpC", [128, 1024], BF16, 2)
        zp = Rot(P, ps, "zpC", [128, 512], F32, 4)
        for t in range(NT):
            rows = slice(t * 128, (t + 1) * 128)
            h_t, h_b = ht.next()
            if layer == 0:
                g_t, g_b = gt.next()
                y_t, y_b = yt.next()
            p_t, p_b = pt.next()
            P.dma("sp", h_t[:], h_in[rows, :], [], [h_b], h_b)
            P.dma("sp", p_t[:], p_ap[rows, :], [], [p_b], p_b)
            if layer == 0:
                P.dma("sp", g_t[:], scr.G[rows, :], [], [g_b], g_b)
                if use_s5:
                    P.dma("sp", y_t[:, 0:1024], scr.YA[rows, :], [], [y_b], y_b)
                else:
                    P.op("pool", lambda e, y_t=y_t: e.memset(y_t[:, 0:1024], 0.0), [], [y_b])
                P.dma("sp", y_t[:, 1024:2048], scr.O[rows, :], [], [y_b], y_b)
            m_t, m_b = mt.next()
            mT_t, mT_b = mT.next()
            if layer == 0:
                P.op("dve", lambda e, m_t=m_t, y_t=y_t, g_t=g_t: e.tensor_tensor(out=m_t[:], in0=y_t[:], in1=g_t[:], op=ALU.mult), [y_b, g_b], [m_b])
            else:
                cv_t, cv_b = cvt.next()
                vx_t, vx_b = vxt.next()
                xg_t, xg_b = xgt.next()
                P.dma("sp", cv_t[:], scr.CV[rows, :], [], [cv_b], cv_b)
                P.dma("sp", vx_t[:], scr.VX[rows, :], [], [vx_b], vx_b)
                P.dma("sp", xg_t[:], scr.XG[rows, :], [], [xg_b], xg_b)
                P.op("pool", lambda e, cv_t=cv_t: e.tensor_tensor(out=tm1[:], in0=cv_t[:], in1=rsr[:], op=ALU.mult), [cv_b, brr], [btm1])
                P.op("dve", lambda e, vx_t=vx_t: e.tensor_tensor(out=tm2[:], in0=vx_t[:], in1=bir[:], op=ALU.mult), [vx_b, brr], [btm2])
                P.op("pool", lambda e: e.tensor_tensor(out=tm1[:], in0=tm1[:], in1=tm2[:], op=ALU.add), [btm1, btm2], [btm1])
                P.op("dve", lambda e, m_t=m_t, xg_t=xg_t: e.tensor_tensor(out=m_t[:], in0=tm1[:], in1=xg_t[:], op=ALU.mult), [btm1, xg_b], [m_b])
            for half in range(2):
                tpp, tpb = tp.next()
                for k in range(8):
                    kk = half * 8 + k
                    P.op("pe", lambda e, tpp=tpp, k=k, kk=kk, m_t=m_t: e.transpose(out=tpp[:, k * 128:(k + 1) * 128], in_=m_t[:, kk * 128:(kk + 1) * 128], identity=ident[:]), [m_b, bid], [tpb])
                P.op("act", lambda e, tpp=tpp, mT_t=mT_t, half=half: e.activation(out=mT_t[:, half * 8:(half + 1) * 8, :].rearrange("p k t -> p (k t)"), in_=tpp[:], func=AF.Copy), [tpb], [mT_b])
            h2_t, h2_b = h2.next()
            for gi in range(2):
                z, zb = zp.next()
                for k in range(16):
                    P.op("pe", lambda e, z=z, k=k, gi=gi, mT_t=mT_t: e.matmul(z[:], lhsT=mT_t[:, k, :], rhs=w_out[:, k, gi * 512:(gi + 1) * 512], start=(k == 0), stop=(k == 15)), [mT_b, bw_out], [zb])
                P.op("act", lambda e, z=z, gi=gi, h2_t=h2_t: e.activation(out=h2_t[:, gi * 512:(gi + 1) * 512], in_=z[:], func=AF.Copy), [zb], [h2_b])
                P.op("pool", lambda e, gi=gi, h2_t=h2_t, h_t=h_t: e.tensor_tensor(out=h2_t[:, gi * 512:(gi + 1) * 512], in0=h2_t[:, gi * 512:(gi + 1) * 512], in1=h_t[:, gi * 512:(gi + 1) * 512], op=ALU.add), [h2_b, h_b], [h2_b])
            hb_t, hb_b = h2b.next()
            P.op("act", lambda e, hb_t=hb_t, h2_t=h2_t: e.activation(out=hb_t[:], in_=h2_t[:], func=AF.Copy), [h2_b], [hb_b])
            tpp, tpb = tp.next()
            for k in range(8):
                P.op("pe", lambda e, tpp=tpp, k=k, hb_t=hb_t: e.transpose(out=tpp[:, k * 128:(k + 1) * 128], in_=hb_t[:, k * 128:(k + 1) * 128], identity=ident[:]), [hb_b, bid], [tpb])
            hT_t, hT_b = h2T.next()
            P.op("act", lambda e, tpp=tpp, hT_t=hT_t: e.activation(out=hT_t[:].rearrange("p k t -> p (k t)"), in_=tpp[:], func=AF.Copy), [tpb], [hT_b])
            pb_t, pb_b = pb16.next()
            P.op("act", lambda e, pb_t=pb_t, p_t=p_t: e.activation(out=pb_t[:], in_=p_t[:], func=AF.Copy), [p_b], [pb_b])
            tpp, tpb = tp.next()
            for k in range(2):
                P.op("pe", lambda e, tpp=tpp, k=k, pb_t=pb_t: e.transpose(out=tpp[:, k * 128:(k + 1) * 128], in_=pb_t[:, k * 128:(k + 1) * 128], identity=ident[:]), [pb_b, bid], [tpb])
            pT_t, pT_b = pT.next()
            P.op("act", lambda e, tpp=tpp, pT_t=pT_t: e.activation(out=pT_t[:].rearrange("p k t -> p (k t)"), in_=tpp[:, 0:256], func=AF.Copy), [tpb], [pT_b])
            sg_t, sg_b = sg.next()
            h3_t, h3_b = h3.next()
            for gi in range(2):
                z, zb = zp.next()
                for k in range(8):
                    P.op("pe", lambda e, z=z, k=k, gi=gi, hT_t=hT_t: e.matmul(z[:], lhsT=hT_t[:, k, :], rhs=w_pg[:, k, gi * 512:(gi + 1) * 512], start=(k == 0), stop=(k == 7)), [hT_b, bw_pg], [zb])
                P.op("act", lambda e, z=z, gi=gi, sg_t=sg_t: e.activation(out=sg_t[:, gi * 512:(gi + 1) * 512], in_=z[:], func=AF.Sigmoid), [zb], [sg_b])
                z2, z2b = zp.next()
                for k in range(2):
                    P.op("pe", lambda e, z2=z2, k=k, gi=gi, pT_t=pT_t: e.matmul(z2[:], lhsT=pT_t[:, k, :], rhs=w_pl[:, k, gi * 512:(gi + 1) * 512], start=(k == 0), stop=(k == 1)), [pT_b, bw_pl], [z2b])
                P.op("act", lambda e, z2=z2, gi=gi, h3_t=h3_t: e.activation(out=h3_t[:, gi * 512:(gi + 1) * 512], in_=z2[:], func=AF.Copy), [z2b], [h3_b])
                P.op("dve", lambda e, gi=gi, sg_t=sg_t, h3_t=h3_t: e.tensor_tensor(out=sg_t[:, gi * 512:(gi + 1) * 512], in0=sg_t[:, gi * 512:(gi + 1) * 512], in1=h3_t[:, gi * 512:(gi + 1) * 512], op=ALU.mult), [h3_b, sg_b], [sg_b])
            P.op("pool", lambda e, h3_t=h3_t, sg_t=sg_t, h2_t=h2_t: e.tensor_tensor(out=h3_t[:], in0=sg_t[:], in1=h2_t[:], op=ALU.add), [sg_b, h2_b], [h3_b])
            if final_out is None:
                P.dma("pool", h_out[rows, :], h3_t[:], [h3_b], [scr.bH], h3_b)
            else:
                st, stb = stat.next()
                P.op("act", lambda e, h3_t=h3_t, st=st: e.activation(out=junk[:], in_=h3_t[:], func=AF.Square, accum_out=st[:, 0:1]), [h3_b], [bjunk, stb])
                rstd_from_ssq(P, "dve", st[:, 1:2], st[:, 0:1], D, [stb], [stb])
                P.op("dve", lambda e, h3_t=h3_t, st=st, sg_t=sg_t: e.scalar_tensor_tensor(out=sg_t[:], in0=h3_t[:], scalar=st[:, 1:2], in1=fg[:], op0=ALU.mult, op1=ALU.mult), [h3_b, stb, bfg], [sg_b])
                P.dma("pool", final_out[rows, :], sg_t[:], [sg_b], [scr.bH], sg_b)

    _phase(P, nc, body)


def phase_D1(P, nc, C, L):
    NT = L // 128
    scr = C.scr

    def body(sb, ps):
        ident = sb("identD", [128, 128], BF16)
        identf = sb("identDf", [128, 128], F32)
        bid = P.buf("identD")
        P.dma("sp", identf[:], C.ident[:, :], [], [bid], bid)
        P.op("dve", lambda e: e.tensor_copy(out=ident[:], in_=identf[:]), [bid], [bid])
        zc = sb("zcD", [128, 8, 2], BF16)
        bzc = P.buf("zcD")
        P.op("pool", lambda e: e.memset(zc[:], 0.0), [], [bzc])
        P.dma("pool", scr.HT[:, :, 0:1].rearrange("k f t -> f k t"), zc[:, :, 0:1], [bzc], [scr.bH], bzc, slow=True)
        P.dma("pool", scr.HT[:, :, L + 1:L + 2].rearrange("k f t -> f k t"), zc[:, :, 1:2], [bzc], [scr.bH], bzc, slow=True)
        xt = Rot(P, sb, "xtD", [128, 1024], F32, 2)
        junk = sb("junkD", [128, 1024], BF16)
        bjunk = P.buf("junkD")
        stat = Rot(P, sb, "statD", [128, 4], F32, 2)
        hn = Rot(P, sb, "hnD", [128, 1024], BF16, 2)
        hT4 = Rot(P, sb, "hT4D", [128, 8, 512], BF16, 2)
        tp = Rot(P, ps, "tpD", [128, 1024], BF16, 2)
        cur = None
        for t in range(NT):
            t4 = t % 4
            if t4 == 0:
                cur = hT4.next()
            h4, h4b = cur
            x, xb = xt.next()
            P.dma("sp", x[:], scr.H1[t * 128:(t + 1) * 128, :], [], [xb], xb)
            st, stb = stat.next()
            P.op("act", lambda e, x=x, st=st: e.activation(out=junk[:], in_=x[:], func=AF.Square, accum_out=st[:, 0:1]), [xb], [bjunk, stb])
            rstd_from_ssq(P, "dve", st[:, 1:2], st[:, 0:1], D, [stb], [stb])
            h, hb = hn.next()
            P.op("act", lambda e, x=x, st=st, h=h: e.activation(out=h[:], in_=x[:], func=AF.Copy, scale=st[:, 1:2]), [xb, stb], [hb])
            tpp, tpb = tp.next()
            for k in range(8):
                P.op("pe", lambda e, k=k, tpp=tpp, h=h: e.transpose(out=tpp[:, k * 128:(k + 1) * 128], in_=h[:, k * 128:(k + 1) * 128], identity=ident[:]), [hb, bid], [tpb])
            P.op("act", lambda e, tpp=tpp, h4=h4, t4=t4: e.activation(out=h4[:, :, t4 * 128:(t4 + 1) * 128], in_=tpp[:].rearrange("p (k t) -> p k t", k=8), func=AF.Copy), [tpb], [h4b])
            if t4 == 3 or t == NT - 1:
                nt = (t4 + 1) * 128
                t0 = (t - t4) * 128
                P.dma("pool", scr.HT[:, :, 1 + t0:1 + t0 + nt].rearrange("k f t -> f k t"), h4[:, :, 0:nt], [h4b], [scr.bH], h4b)

    _phase(P, nc, body)


def phase_D2(P, nc, C, L):
    scr = C.scr
    TB = 256
    NB = L // TB

    def body(sb, ps):
        ng = sb("ngD", [128, 8], F32)
        bsm = P.buf("smallD")
        P.dma("sp", ng[:], C.norm_g[1].rearrange("(k p) -> p k", p=128), [], [bsm], bsm, slow=True)
        cw = sb("cwD", [128, 3, 48], F32)
        cb = sb("cbD", [128, 48], F32)
        hb_ = sb("hbD", [128, 16], F32)
        P.dma("sp", cw[:], C.hy_conv_w[0].rearrange("j (i p) -> p j i", p=128), [], [bsm], bsm, slow=True)
        P.dma("sp", cb[:], C.hy_conv_b[0].rearrange("(i p) -> p i", p=128), [], [bsm], bsm, slow=True)
        P.dma("sp", hb_[:], C.hy_bias[0].rearrange("(i p) -> p i", p=128), [], [bsm], bsm, slow=True)
        stage = Rot(P, sb, "wstD", [128, 1024], F32, 2)
        w_in = sb("w_inD", [128, 8, 8192], BF16)
        bw_in = P.buf("w_inD")
        load_weight_bf16(P, sb, w_in, bw_in, C.hy_w_in[0], 8, 8192, stage, rowscale=(ng, bsm))
        hT = Rot(P, sb, "hTD", [128, 8, 258], BF16, 2)
        zs = Rot(P, sb, "zsD", [128, 258], F32, 4)
        uc = Rot(P, sb, "ucD", [128, 3, 256], F32, 2)
        sg = Rot(P, sb, "sgD", [128, 256], F32, 2)
        ident = sb("identD2", [128, 128], BF16)
        identf = sb("identD2f", [128, 128], F32)
        bid = P.buf("identD2")
        P.dma("sp", identf[:], C.ident[:, :], [], [bid], bid)
        P.op("dve", lambda e: e.tensor_copy(out=ident[:], in_=identf[:]), [bid], [bid])
        mo = Rot(P, sb, "moD", [128, 2, 256], BF16, 4)
        stg2 = Rot(P, sb, "stg2D", [128, 2, 2, 2048], BF16, 1)
        zp = Rot(P, ps, "zpD", [128, 512], F32, 4)
        tp = Rot(P, ps, "tpD2", [128, 1024], BF16, 2)
        def do_block(b):
            s0 = b * TB
            n_out = min(TB, L - s0)
            n_in = n_out + 2
            h_t, h_b = hT.next()
            st2, st2b = stg2.next()
            pending = []
            P.dma("sp", h_t[:, :, 0:n_in], scr.HT[:, :, s0:s0 + n_in].rearrange("k f t -> f k t"), [], [h_b], h_b)
            for i in range(16):
                u_t, u_b = uc.next()
                for part in range(4):
                    ch = part * 16 + i
                    z, zb = zp.next()
                    for k in range(8):
                        P.op("pe", lambda e, z=z, k=k, ch=ch, h_t=h_t: e.matmul(z[:, 0:n_in], lhsT=w_in[:, k, ch * 128:(ch + 1) * 128], rhs=h_t[:, k, 0:n_in], start=(k == 0), stop=(k == 7)), [h_b, bw_in], [zb])
                    if part < 3:
                        zs_t, zs_b = zs.next()
                        P.op("act", lambda e, z=z, zs_t=zs_t: e.activation(out=zs_t[:, 0:n_in], in_=z[:, 0:n_in], func=AF.Copy), [zb], [zs_b])
                        eng = "pool" if part != 1 else "dve"
                        P.op(eng, lambda e, zs_t=zs_t, u_t=u_t, part=part, ch=ch: e.tensor_scalar(out=u_t[:, part, 0:n_out], in0=zs_t[:, 0:n_out], scalar1=cw[:, 0, ch:ch + 1], scalar2=cb[:, ch:ch + 1], op0=ALU.mult, op1=ALU.add), [zs_b, bsm], [u_b])
                        P.op("dve", lambda e, zs_t=zs_t, u_t=u_t, part=part, ch=ch: e.scalar_tensor_tensor(out=u_t[:, part, 0:n_out], in0=zs_t[:, 1:1 + n_out], scalar=cw[:, 1, ch:ch + 1], in1=u_t[:, part, 0:n_out], op0=ALU.mult, op1=ALU.add), [zs_b, bsm, u_b], [u_b])
                        P.op("dve", lambda e, zs_t=zs_t, u_t=u_t, part=part, ch=ch: e.scalar_tensor_tensor(out=u_t[:, part, 0:n_out], in0=zs_t[:, 2:2 + n_out], scalar=cw[:, 2, ch:ch + 1], in1=u_t[:, part, 0:n_out], op0=ALU.mult, op1=ALU.add), [zs_b, bsm, u_b], [u_b])
                    else:
                        sg_t, sg_b = sg.next()
                        P.op("act", lambda e, z=z, sg_t=sg_t: e.activation(out=sg_t[:, 0:n_out], in_=z[:, 1:1 + n_out], func=AF.Silu), [zb], [sg_b])
                m_t, m_b = mo.next()
                P.op("pool", lambda e, u_t=u_t, m_t=m_t: e.tensor_tensor(out=m_t[:, 0, :], in0=u_t[:, 2, 0:n_out], in1=u_t[:, 1, 0:n_out], op=ALU.mult), [u_b], [m_b])
                P.op("pool", lambda e, u_t=u_t, sg_t=sg_t, m_t=m_t: e.tensor_tensor(out=m_t[:, 1, :], in0=u_t[:, 0, 0:n_out], in1=sg_t[:, 0:n_out], op=ALU.mult), [u_b, sg_b], [m_b])
                def finish(i=i, m_t=m_t, m_b=m_b):
                    tpp, tpb = tp.next()
                    for q in range(2):
                        for tl_ in range(2):
                            P.op("pe", lambda e, tpp=tpp, q=q, tl_=tl_: e.transpose(out=tpp[:, (q * 2 + tl_) * 128:(q * 2 + tl_ + 1) * 128], in_=m_t[:, q, tl_ * 128:(tl_ + 1) * 128], identity=ident[:]), [m_b, bid], [tpb])
                    P.op("act", lambda e, tpp=tpp: e.activation(out=st2[:, :, :, i * 128:(i + 1) * 128], in_=tpp[:, 0:512].rearrange("p (q t c) -> p q t c", q=2, t=2), func=AF.Copy), [tpb], [st2b])

                pending.append(finish)
                if len(pending) > 1:
                    pending.pop(0)()
            while pending:
                pending.pop(0)()
            for tl_ in range(2):
                r0 = s0 + tl_ * 128
                P.dma("sp", scr.VX[r0:r0 + 128, :], st2[:, 0, tl_, :], [st2b], [scr.bV], st2b)
                P.dma("sp", scr.XG[r0:r0 + 128, :], st2[:, 1, tl_, :], [st2b], [scr.bG], st2b)

        for b in range(NB):
            do_block(b)

    _phase(P, nc, body)


def fft_tables(L):
    N = 2 * L
    N1 = N // 128
    KL = L // 128
    k = np.arange(N1)[:, None].astype(np.float64)
    f1 = np.arange(N1)[None, :].astype(np.float64)
    ang1 = 2 * np.pi * k * f1 / N1
    p = np.arange(128).astype(np.float64)
    f2 = np.arange(128).astype(np.float64)
    f1v = np.arange(N1).astype(np.float64)
    angE = 2 * np.pi * p[:, None, None] * (f1v[None, :, None] + N1 * f2[None, None, :]) / N
    angE2 = 2 * np.pi * p[None, None, :] * (f1v[None, :, None] + N1 * f2[:, None, None]) / N
    t = {}
    t["s1c"] = np.cos(ang1)
    t["s1s"] = -np.sin(ang1)
    t["ec"] = np.cos(angE).reshape(128, N1 * 128)
    t["es"] = np.sin(angE).reshape(128, N1 * 128)
    t["e2c"] = np.cos(angE2).reshape(128, N1 * 128)
    t["e2s"] = np.sin(angE2).reshape(128, N1 * 128)
    t["i1c"] = np.cos(ang1).T[:, :KL] / N
    t["i1s"] = -np.sin(ang1).T[:, :KL] / N
    return {k_: np.ascontiguousarray(v.astype(np.float32)) for k_, v in t.items()}, N1, KL


def fft_tables_shapes(L):
    N1 = 2 * L // 128
    KL = L // 128
    return {"s1c": (N1, N1), "s1s": (N1, N1), "ec": (128, N1 * 128), "es": (128, N1 * 128),
            "e2c": (128, N1 * 128), "e2s": (128, N1 * 128), "i1c": (N1, KL), "i1s": (N1, KL)}, N1, KL


def load_table_bf16(P, sb, name, src, rows, cols, stage):
    t = sb(name, [rows, cols], BF16)
    b = P.buf(name)
    for c0 in range(0, cols, 1024):
        c1 = min(cols, c0 + 1024)
        st, stb = stage.next()
        P.dma("sp", st[0:rows, 0:c1 - c0], src[0:rows, c0:c1], [], [stb], stb)
        P.op("pool", lambda e, st=st, c0=c0, c1=c1: e.tensor_copy(out=t[:, c0:c1], in_=st[0:rows, 0:c1 - c0]), [stb], [b])
    return t, b


def phase_F(P, nc, C, L, src, KS, tb, khat_dst=None, khat_src=None, yhat_dst=None):
    N1 = 2 * L // 128
    scr = C.scr
    FC = min(4, N1)

    def body(sb, ps):
        stage = Rot(P, sb, "wstF", [128, 1024], F32, 2)
        s1c, bs1c = load_table_bf16(P, sb, "s1cF", tb["s1c"], KS, N1, stage)
        s1s, bs1s = load_table_bf16(P, sb, "s1sF", tb["s1s"], KS, N1, stage)
        ec, bec = load_table_bf16(P, sb, "ecF", tb["ec"], 128, N1 * 128, stage)
        es, bes = load_table_bf16(P, sb, "esF", tb["es"], 128, N1 * 128, stage)
        X = Rot(P, sb, "XF", [KS, 128 * 128], BF16, 2)
        Ast = Rot(P, sb, "AstF", [N1, 2, 512], BF16, 3)
        Bt = Rot(P, sb, "BtF", [128, 3, FC, 128], BF16, 3)
        Xs = Rot(P, sb, "XsF", [128, 2, FC * 128], F32, 2)
        Kh = Rot(P, sb, "KhF", [128, 2, FC * 128], F32, 3)
        Tm = Rot(P, sb, "TmF", [128, 4, FC * 128], F32, 2)
        Yo = Rot(P, sb, "YoF", [128, 2, FC * 128], BF16, 2)
        pa = Rot(P, ps, "paF", [128, 512], F32, 4)
        px = Rot(P, ps, "pxF", [128, 512], F32, 4)
        def load_x(s_):
            x_t, x_b = X.next()
            P.dma("sp", x_t[:].rearrange("k (p c) -> k p c", c=128), src[0:KS * 128, s_ * 128:(s_ + 1) * 128].rearrange("(k p) c -> k p c", p=128), [], [x_b], x_b)
            return x_t, x_b

        NFC = N1 // FC
        W = FC * 128

        def load_chunk(s_, fc):
            b_t, b_b = Bt.next()
            for r_ in range(2):
                P.dma("sp", b_t[:, r_, :, :], scr.AT[r_, fc * FC:(fc + 1) * FC, :, :].rearrange("f p c -> p f c"), [scr.bA], [b_b], b_b)
            kh = None
            if khat_dst is None:
                kh_t, kh_b = Kh.next()
                P.dma("sp", kh_t[:].rearrange("f r (g c) -> f r g c", c=128), khat_src[s_, :, :, fc * FC:(fc + 1) * FC, :].rearrange("r f g c -> f r g c"), [], [kh_b], kh_b)
                kh = (kh_t, kh_b)
            return (b_t, b_b, kh)

        nxt_x = load_x(0)
        for s in range(16):
            x_t, x_b = nxt_x
            if s + 1 < 16:
                nxt_x = load_x(s + 1)
            for cb in range(32):
                a_t, a_b = Ast.next()
                for ri, (tab, tabb) in enumerate(((s1c, bs1c), (s1s, bs1s))):
                    z, zb = pa.next()
                    P.op("pe", lambda e, z=z, tab=tab, x_t=x_t, cb=cb: e.matmul(z[0:N1, :], lhsT=tab[:, :], rhs=x_t[:, cb * 512:(cb + 1) * 512], start=True, stop=True), [x_b, tabb], [zb])
                    P.op("act", lambda e, z=z, a_t=a_t, ri=ri: e.activation(out=a_t[:, ri, :], in_=z[0:N1, :], func=AF.Copy), [zb], [a_b])
                P.dma("sp", scr.AT[:, 0:N1, cb * 4:(cb + 1) * 4, :].rearrange("r f p c -> f r p c"), a_t[:].rearrange("f r (p c) -> f r p c", c=128), [a_b], [scr.bA], a_b)
            nxt_c = load_chunk(s, 0)
            for fc in range(NFC):
                b_t, b_b, kh = nxt_c
                if fc + 1 < NFC:
                    nxt_c = load_chunk(s, fc + 1)
                P.op("dve", lambda e, b_t=b_t: e.tensor_scalar(out=b_t[:, 2, :, :], in0=b_t[:, 0, :, :], scalar1=-1.0, scalar2=None, op0=ALU.mult), [b_b], [b_b])
                zr, zrb = px.next()
                zi, zib = px.next()
                for j in range(FC):
                    f1 = fc * FC + j
                    P.op("pe", lambda e, zr=zr, j=j, f1=f1, b_t=b_t: e.matmul(zr[:, j * 128:(j + 1) * 128], lhsT=ec[:, f1 * 128:(f1 + 1) * 128], rhs=b_t[:, 0, j, :], start=True, stop=False, skip_group_check=True), [b_b, bec], [zrb])
                    P.op("pe", lambda e, zr=zr, j=j, f1=f1, b_t=b_t: e.matmul(zr[:, j * 128:(j + 1) * 128], lhsT=es[:, f1 * 128:(f1 + 1) * 128], rhs=b_t[:, 1, j, :], start=False, stop=True, skip_group_check=True), [b_b, bes], [zrb])
                    P.op("pe", lambda e, zi=zi, j=j, f1=f1, b_t=b_t: e.matmul(zi[:, j * 128:(j + 1) * 128], lhsT=ec[:, f1 * 128:(f1 + 1) * 128], rhs=b_t[:, 1, j, :], start=True, stop=False, skip_group_check=True), [b_b, bec], [zib])
                    P.op("pe", lambda e, zi=zi, j=j, f1=f1, b_t=b_t: e.matmul(zi[:, j * 128:(j + 1) * 128], lhsT=es[:, f1 * 128:(f1 + 1) * 128], rhs=b_t[:, 2, j, :], start=False, stop=True, skip_group_check=True), [b_b, bes], [zib])
                xs_t, xs_b = Xs.next()
                P.op("act", lambda e, zr=zr, xs_t=xs_t: e.activation(out=xs_t[:, 0, :], in_=zr[:, 0:W], func=AF.Copy), [zrb], [xs_b])
                P.op("act", lambda e, zi=zi, xs_t=xs_t: e.activation(out=xs_t[:, 1, :], in_=zi[:, 0:W], func=AF.Copy), [zib], [xs_b])
                if khat_dst is not None:
                    P.dma("sp", khat_dst[s, :, :, fc * FC:(fc + 1) * FC, :].rearrange("r f g c -> f r g c"), xs_t[:].rearrange("f r (g c) -> f r g c", c=128), [xs_b], [scr.bK], xs_b)
                else:
                    kh_t, kh_b = kh
                    tm, tmb = Tm.next()
                    yo, yob = Yo.next()
                    P.op("dve", lambda e, tm=tm, xs_t=xs_t, kh_t=kh_t: e.tensor_tensor(out=tm[:, 0, :], in0=xs_t[:, 0, :], in1=kh_t[:, 0, :], op=ALU.mult), [xs_b, kh_b], [tmb])
                    P.op("dve", lambda e, tm=tm, xs_t=xs_t, kh_t=kh_t: e.tensor_tensor(out=tm[:, 1, :], in0=xs_t[:, 1, :], in1=kh_t[:, 1, :], op=ALU.mult), [xs_b, kh_b], [tmb])
                    P.op("pool", lambda e, tm=tm, xs_t=xs_t, kh_t=kh_t: e.tensor_tensor(out=tm[:, 2, :], in0=xs_t[:, 0, :], in1=kh_t[:, 1, :], op=ALU.mult), [xs_b, kh_b], [tmb])
                    P.op("dve", lambda e, tm=tm, xs_t=xs_t, kh_t=kh_t: e.tensor_tensor(out=tm[:, 3, :], in0=xs_t[:, 1, :], in1=kh_t[:, 0, :], op=ALU.mult), [xs_b, kh_b], [tmb])
                    P.op("dve", lambda e, tm=tm, yo=yo: e.tensor_tensor(out=yo[:, 0, :], in0=tm[:, 0, :], in1=tm[:, 1, :], op=ALU.subtract), [tmb], [yob])
                    P.op("pool", lambda e, tm=tm, yo=yo: e.tensor_tensor(out=yo[:, 1, :], in0=tm[:, 2, :], in1=tm[:, 3, :], op=ALU.add), [tmb], [yob])
                    P.dma("sp", yhat_dst[s, :, :, fc * FC:(fc + 1) * FC, :].rearrange("r f g c -> f r g c"), yo[:].rearrange("f r (g c) -> f r g c", c=128), [yob], [scr.bQ], yob)

    _phase(P, nc, body)


def phase_I(P, nc, C, L, tb, yhat_src):
    N1 = 2 * L // 128
    KL = L // 128
    scr = C.scr
    FC = min(4, N1)
    PC = 4

    def body(sb, ps):
        stage = Rot(P, sb, "wstI", [128, 1024], F32, 2)
        i1c, bi1c = load_table_bf16(P, sb, "i1cI", tb["i1c"], N1, KL, stage)
        i1s, bi1s = load_table_bf16(P, sb, "i1sI", tb["i1s"], N1, KL, stage)
        e2c, be2c = load_table_bf16(P, sb, "e2cI", tb["e2c"], 128, N1 * 128, stage)
        e2s, be2s = load_table_bf16(P, sb, "e2sI", tb["e2s"], 128, N1 * 128, stage)
        Yt = Rot(P, sb, "YtI", [128, 3, FC, 128], BF16, 3)
        Zst = Rot(P, sb, "ZstI", [128, 2, FC * 128], BF16, 3)
        Zt = Rot(P, sb, "ZtI", [N1, 2, PC * 128], BF16, 4)
        Ot = Rot(P, sb, "OtI", [KL, 16 * 512], BF16, 2)
        pz = Rot(P, ps, "pzI", [128, 512], F32, 4)
        po = Rot(P, ps, "poI", [128, 512], F32, 2)
        W = FC * 128
        NFC = N1 // FC
        NPC = 128 // PC

        def load_y(s_, fc):
            y_t, y_b = Yt.next()
            P.dma("sp", y_t[:, 0:2, :, :], yhat_src[s_, :, :, fc * FC:(fc + 1) * FC, :].rearrange("r f g c -> f r g c"), [], [y_b], y_b)
            return y_t, y_b

        def load_z(pc):
            zz, zzb = Zt.next()
            for r_ in range(2):
                P.dma("sp", zz[:, r_, :].rearrange("f (p c) -> f p c", c=128), scr.ZT[r_, pc * PC:(pc + 1) * PC, 0:N1, :].rearrange("p f c -> f p c"), [scr.bA], [zzb], zzb)
            return zz, zzb

        for s in range(16):
            nxt_y = load_y(s, 0)
            for fc in range(NFC):
                y_t, y_b = nxt_y
                if fc + 1 < NFC:
                    nxt_y = load_y(s, fc + 1)
                P.op("dve", lambda e, y_t=y_t: e.tensor_scalar(out=y_t[:, 2, :, :], in0=y_t[:, 1, :, :], scalar1=-1.0, scalar2=None, op0=ALU.mult), [y_b], [y_b])
                zr, zrb = pz.next()
                zi, zib = pz.next()
                for j in range(FC):
                    f1 = fc * FC + j
                    P.op("pe", lambda e, zr=zr, j=j, f1=f1, y_t=y_t: e.matmul(zr[:, j * 128:(j + 1) * 128], lhsT=e2c[:, f1 * 128:(f1 + 1) * 128], rhs=y_t[:, 0, j, :], start=True, stop=False, skip_group_check=True), [y_b, be2c], [zrb])
                    P.op("pe", lambda e, zr=zr, j=j, f1=f1, y_t=y_t: e.matmul(zr[:, j * 128:(j + 1) * 128], lhsT=e2s[:, f1 * 128:(f1 + 1) * 128], rhs=y_t[:, 2, j, :], start=False, stop=True, skip_group_check=True), [y_b, be2s], [zrb])
                    P.op("pe", lambda e, zi=zi, j=j, f1=f1, y_t=y_t: e.matmul(zi[:, j * 128:(j + 1) * 128], lhsT=e2s[:, f1 * 128:(f1 + 1) * 128], rhs=y_t[:, 0, j, :], start=True, stop=False, skip_group_check=True), [y_b, be2s], [zib])
                    P.op("pe", lambda e, zi=zi, j=j, f1=f1, y_t=y_t: e.matmul(zi[:, j * 128:(j + 1) * 128], lhsT=e2c[:, f1 * 128:(f1 + 1) * 128], rhs=y_t[:, 1, j, :], start=False, stop=True, skip_group_check=True), [y_b, be2c], [zib])
                z_t, z_b = Zst.next()
                P.op("act", lambda e, zr=zr, z_t=z_t: e.activation(out=z_t[:, 0, :], in_=zr[:, 0:W], func=AF.Copy), [zrb], [z_b])
                P.op("act", lambda e, zi=zi, z_t=z_t: e.activation(out=z_t[:, 1, :], in_=zi[:, 0:W], func=AF.Copy), [zib], [z_b])
                P.dma("sp", scr.ZT[:, :, fc * FC:(fc + 1) * FC, :].rearrange("r p f c -> p r f c"), z_t[:].rearrange("p r (f c) -> p r f c", c=128), [z_b], [scr.bA], z_b)
            o_t, o_b = Ot.next()
            nxt_z = load_z(0)
            for pc in range(NPC):
                zz, zzb = nxt_z
                if pc + 1 < NPC:
                    nxt_z = load_z(pc + 1)
                o, ob = po.next()
                P.op("pe", lambda e, o=o, zz=zz: e.matmul(o[0:KL, :], lhsT=i1c[:, :], rhs=zz[:, 0, :], start=True, stop=False), [zzb, bi1c], [ob])
                P.op("pe", lambda e, o=o, zz=zz: e.matmul(o[0:KL, :], lhsT=i1s[:, :], rhs=zz[:, 1, :], start=False, stop=True), [zzb, bi1s], [ob])
                half = pc % 16
                P.op("act", lambda e, o=o, o_t=o_t, half=half: e.activation(out=o_t[:, half * 512:(half + 1) * 512], in_=o[0:KL, :], func=AF.Copy), [ob], [o_b])
                if half == 15:
                    p0 = (pc - 15) * PC
                    P.dma("sp", scr.CV[0:L, s * 128:(s + 1) * 128].rearrange("(k p) c -> k p c", p=128)[:, p0:p0 + 64, :], o_t[:].rearrange("k (p c) -> k p c", c=128), [o_b], [scr.bO], o_b)
                    if pc != NPC - 1:
                        o_t, o_b = Ot.next()

    _phase(P, nc, body)


def phase_G(P, nc, C, L, zT, tl, ktwo_dst, rs_dst):
    NT2 = 2 * L // 128
    scr = C.scr
    PI = math.pi

    def body(sb, ps):
        w1 = sb("w1G", [33, 2, 64], BF16)
        w2 = sb("w2G", [64, 2, 64], BF16)
        w3 = sb("w3G", [64, 2, 2048], BF16)
        stg = sb("stgG", [64, 2, 2048], F32)
        sm = sb("smG", [64, 2, 8], F32)
        bw = P.buf("wG")
        for d in range(2):
            P.dma("sp", stg[0:33, d, 0:64], C.hy_f_w1[0, d], [], [bw], bw)
        P.op("pool", lambda e: e.tensor_copy(out=w1[:], in_=stg[0:33, :, 0:64]), [bw], [bw])
        for d in range(2):
            P.dma("sp", stg[0:64, d, 64:128], C.hy_f_w2[0, d], [], [bw], bw)
        P.op("pool", lambda e: e.tensor_copy(out=w2[:], in_=stg[0:64, :, 64:128]), [bw], [bw])
        for i, src in enumerate((C.hy_f_b1, C.hy_f_freq1, C.hy_f_b2, C.hy_f_freq2)):
            for d in range(2):
                P.dma("sp", sm[:, d, i:i + 1], src[0, d].rearrange("(o u) -> o u", u=1), [], [bw], bw, slow=True)
        for (bi, fi, oi) in ((0, 1, 4), (2, 3, 5)):
            P.op("pool", lambda e, bi=bi, fi=fi, oi=oi: e.tensor_tensor(out=sm[:, :, oi:oi + 1], in0=sm[:, :, bi:bi + 1], in1=sm[:, :, fi:fi + 1], op=ALU.mult), [bw], [bw])
        bw3 = P.buf("w3G")
        for d in range(2):
            P.dma("sp", stg[:, d, :], C.hy_f_w3[0, d], [bw], [bw3], bw3)
        P.op("pool", lambda e: e.tensor_copy(out=w3[:], in_=stg[:]), [bw3], [bw3])
        negpi = sb("negpiG", [128, 1], F32)
        P.op("pool", lambda e: e.memset(negpi[:], -PI), [], [bw])
        ones = sb("onesG", [128, 128], BF16)
        P.op("pool", lambda e: e.memset(ones[:], 1.0), [], [bw])
        dl = sb("dlG", [128, 2048], F32)
        bdl = P.buf("dlG")
        P.dma("sp", dl[:], C.hy_delta.partition_broadcast(128), [], [bdl], bdl, slow=True)
        tt = sb("ttG", [128, NT2], F32)
        P.dma("sp", tt[:], tl.rearrange("(k p) -> p k", p=128), [], [bdl], bdl, slow=True)
        P.op("pool", lambda e: e.tensor_scalar(out=tt[:], in0=tt[:], scalar1=-1.0, scalar2=None, op0=ALU.mult), [bdl], [bdl])
        zt = Rot(P, sb, "ztG", [33, 512], F32, 2)
        ztb = Rot(P, sb, "ztbG", [33, 512], BF16, 2)
        v1 = Rot(P, sb, "v1G", [64, 512], F32, 2)
        ni = Rot(P, sb, "niG", [64, 512], mybir.dt.int32, 2)
        nf = Rot(P, sb, "nfG", [64, 512], F32, 2)

        def range_reduce(P, v, vb, ni_s, nf_s):
            n_i, nib = ni_s
            n_f, nfb = nf_s
            P.op("dve", lambda e: e.tensor_scalar(out=n_f[:], in0=v[:], scalar1=1.0 / (2 * PI), scalar2=None, op0=ALU.mult), [vb], [nfb])
            P.op("dve", lambda e: e.tensor_copy(out=n_i[:], in_=n_f[:]), [nfb], [nib])
            P.op("dve", lambda e: e.tensor_copy(out=n_f[:], in_=n_i[:]), [nib], [nfb])
            P.op("dve", lambda e: e.scalar_tensor_tensor(out=v[:], in0=n_f[:], scalar=-2 * PI, in1=v[:], op0=ALU.mult, op1=ALU.add), [nfb, vb], [vb])
            P.op("dve", lambda e: e.tensor_scalar(out=n_f[:], in0=v[:], scalar1=PI, scalar2=None, op0=ALU.is_gt), [vb], [nfb])
            P.op("dve", lambda e: e.scalar_tensor_tensor(out=v[:], in0=n_f[:], scalar=-2 * PI, in1=v[:], op0=ALU.mult, op1=ALU.add), [nfb, vb], [vb])
            P.op("dve", lambda e: e.tensor_scalar(out=n_f[:], in0=v[:], scalar1=-PI, scalar2=None, op0=ALU.is_lt), [vb], [nfb])
            P.op("dve", lambda e: e.scalar_tensor_tensor(out=v[:], in0=n_f[:], scalar=2 * PI, in1=v[:], op0=ALU.mult, op1=ALU.add), [nfb, vb], [vb])
        h1 = Rot(P, sb, "h1G", [64, 512], BF16, 2)
        h2 = Rot(P, sb, "h2G", [64, 512], BF16, 2)
        dec = Rot(P, sb, "decG", [128, 2048], F32, 2)
        kf = Rot(P, sb, "kfG", [128, 2048], F32, 2)
        kb16 = Rot(P, sb, "kbG", [128, 2048], BF16, 2)
        sq = Rot(P, sb, "sqG", [128, 2048], BF16, 2)
        pm = Rot(P, ps, "pmG", [128, 512], F32, 2)
        pk = Rot(P, ps, "pkG", [128, 512], F32, 2)
        pss = [ps("pssG%d" % i, [128, 512], F32) for i in range(4)]
        bss = P.buf("pssG")
        for blk in range(2 * L // 512):
            d = 0 if blk * 512 < L else 1
            z_t, z_b = zt.next()
            P.dma("sp", z_t[:], zT[:, blk * 512:(blk + 1) * 512], [], [z_b], z_b)
            zb_t, zb_b = ztb.next()
            P.op("pool", lambda e, z_t=z_t, zb_t=zb_t: e.tensor_copy(out=zb_t[:], in_=z_t[:]), [z_b], [zb_b])
            m, mb = pm.next()
            P.op("pe", lambda e, m=m, zb_t=zb_t, d=d: e.matmul(m[0:64, :], lhsT=w1[:, d, :], rhs=zb_t[:], start=True, stop=True), [zb_b, bw], [mb])
            v, vb = v1.next()
            P.op("act", lambda e, m=m, v=v, d=d: e.activation(out=v[:], in_=m[0:64, :], func=AF.Identity, scale=sm[:, d, 1:2], bias=sm[:, d, 4:5]), [mb, bw], [vb])
            range_reduce(P, v, vb, ni.next(), nf.next())
            h_1, h1b = h1.next()
            P.op("act", lambda e, v=v, h_1=h_1: e.activation(out=h_1[:], in_=v[:], func=AF.Sin), [vb], [h1b])
            m, mb = pm.next()
            P.op("pe", lambda e, m=m, h_1=h_1, d=d: e.matmul(m[0:64, :], lhsT=w2[:, d, :], rhs=h_1[:], start=True, stop=True), [h1b, bw], [mb])
            v, vb = v1.next()
            P.op("act", lambda e, m=m, v=v, d=d: e.activation(out=v[:], in_=m[0:64, :], func=AF.Identity, scale=sm[:, d, 3:4], bias=sm[:, d, 5:6]), [mb, bw], [vb])
            range_reduce(P, v, vb, ni.next(), nf.next())
            h_2, h2b = h2.next()
            P.op("act", lambda e, v=v, h_2=h_2: e.activation(out=h_2[:], in_=v[:], func=AF.Sin), [vb], [h2b])
            for ti in range(4):
                tile_i = blk * 4 + ti
                dc, dcb = dec.next()
                P.op("act", lambda e, dc=dc, tile_i=tile_i: e.activation(out=dc[:], in_=dl[:], func=AF.Exp, scale=tt[:, tile_i:tile_i + 1]), [bdl], [dcb])
                k_t, k_b = kf.next()
                for gi in range(4):
                    kk, kkb = pk.next()
                    P.op("pe", lambda e, kk=kk, h_2=h_2, ti=ti, gi=gi, d=d: e.matmul(kk[:], lhsT=h_2[:, ti * 128:(ti + 1) * 128], rhs=w3[:, d, gi * 512:(gi + 1) * 512], start=True, stop=True), [h2b, bw3], [kkb])
                    P.op("act", lambda e, kk=kk, k_t=k_t, gi=gi: e.activation(out=k_t[:, gi * 512:(gi + 1) * 512], in_=kk[:], func=AF.Copy), [kkb], [k_b])
                P.op("dve", lambda e, k_t=k_t, dc=dc: e.tensor_tensor(out=k_t[:], in0=k_t[:], in1=dc[:], op=ALU.mult), [k_b, dcb], [k_b])
                kb_t, kb_b = kb16.next()
                P.op("pool", lambda e, k_t=k_t, kb_t=kb_t: e.tensor_copy(out=kb_t[:], in_=k_t[:]), [k_b], [kb_b])
                P.dma("sp", ktwo_dst[tile_i * 128:(tile_i + 1) * 128, :], kb_t[:], [kb_b], [scr.bK], kb_b)
                sq_t, sq_b = sq.next()
                P.op("dve", lambda e, k_t=k_t, sq_t=sq_t: e.tensor_tensor(out=sq_t[:], in0=k_t[:], in1=k_t[:], op=ALU.mult), [k_b], [sq_b])
                for gi in range(4):
                    P.op("pe", lambda e, gi=gi, sq_t=sq_t, tile_i=tile_i: e.matmul(pss[gi][:], lhsT=ones[:], rhs=sq_t[:, gi * 512:(gi + 1) * 512], start=(tile_i == 0), stop=(tile_i == NT2 - 1)), [sq_b, bw], [bss])
        rs = sb("rsG", [128, 2048], F32)
        brs = P.buf("rsG")
        for gi in range(4):
            P.op("act", lambda e, gi=gi: e.activation(out=rs[:, gi * 512:(gi + 1) * 512], in_=pss[gi][:], func=AF.Copy), [bss], [brs])
        P.op("pool", lambda e: e.tensor_scalar(out=rs[:], in0=rs[:], scalar1=EPS, scalar2=None, op0=ALU.add), [brs], [brs])
        P.op("act", lambda e: e.activation(out=rs[:], in_=rs[:], func=AF.Sqrt), [brs], [brs])
        P.op("dve", lambda e: e.reciprocal(out=rs[:], in_=rs[:]), [brs], [brs])
        P.dma("sp", rs_dst[0:1, :], rs[0:1, :], [brs], [scr.bK], brs)

    _phase(P, nc, body)


def s5_cmul(P, eng2, out_r, out_i, xr, xi, yr, yi, t1, t2, R, W, TB):
    P.op("pool", lambda e: e.tensor_tensor(out=t1, in0=xr, in1=yr, op=ALU.mult), R, [TB])
    P.op("dve", lambda e: e.tensor_tensor(out=t2, in0=xi, in1=yi, op=ALU.mult), R, [TB])
    P.op("pool", lambda e: e.tensor_tensor(out=out_r, in0=t1, in1=t2, op=ALU.subtract), [TB] + R, W)
    P.op("pool", lambda e: e.tensor_tensor(out=t1, in0=xr, in1=yi, op=ALU.mult), R + W, [TB])
    P.op("dve", lambda e: e.tensor_tensor(out=t2, in0=xi, in1=yr, op=ALU.mult), R + W, [TB])
    P.op("pool", lambda e: e.tensor_tensor(out=out_i, in0=t1, in1=t2, op=ALU.add), [TB] + R, W)


def s5_nstages(T):
    n, span = 0, 1
    while span < T:
        span *= 4
        n += 1
    return n


def phase_S5setup(P, nc, C, NS):
    scr = C.scr
    PI = math.pi
    G2 = 128

    def body(sb, ps):
        ident = sb("identS", [128, 128], BF16)
        identf = sb("identSf", [128, 128], F32)
        bid = P.buf("identS")
        P.dma("sp", identf[:], C.ident[:, :], [], [bid], bid)
        P.op("dve", lambda e: e.tensor_copy(out=ident[:], in_=identf[:]), [bid], [bid])
        msk = sb("mskS", [128, 4], F32)
        bm = P.buf("mskS")
        P.dma("sp", msk[:], C.s5_rowmask[:, :], [], [bm], bm)
        mfb = sb("mfbS", [128, 2, 128], F32)
        P.dma("sp", mfb[:, 0, :], C.s5_mf[:, :], [], [bm], bm)
        P.dma("sp", mfb[:, 1, :], C.s5_mb[:, :], [], [bm], bm)
        ar = sb("arS", [128, G2], F32)
        ai = sb("aiS", [128, G2], F32)
        dt = sb("dtS", [128, G2], F32)
        ba = P.buf("aS")
        for half in range(2):
            P.dma("sp", ar[half * 64:(half + 1) * 64, :].rearrange("n (d g) -> n d g", d=2), C.s5_a_re[0].rearrange("d g n -> n d g"), [], [ba], ba, slow=True)
            P.dma("sp", ai[half * 64:(half + 1) * 64, :].rearrange("n (d g) -> n d g", d=2), C.s5_a_im[0].rearrange("d g n -> n d g"), [], [ba], ba, slow=True)
        P.dma("sp", dt[:], C.s5_log_dt[0].rearrange("d g -> (d g)").partition_broadcast(128), [], [ba], ba, slow=True)
        P.op("act", lambda e: e.activation(out=dt[:], in_=dt[:], func=AF.Exp), [ba], [ba])
        NT_ = 12
        tmp = [sb("tmpS%d" % i, [128, G2], F32) for i in range(NT_)]
        btmp = [P.buf("tmpS%d" % i) for i in range(NT_)]
        ni = sb("niS", [128, G2], mybir.dt.int32)
        lr, li, mag, pr, pi_, cs_arg = tmp[0], tmp[1], tmp[2], tmp[3], tmp[4], tmp[5]
        bl = P.buf("lS")
        P.op("pool", lambda e: e.tensor_tensor(out=lr[:], in0=ar[:], in1=dt[:], op=ALU.mult), [ba], [bl])
        P.op("pool", lambda e: e.tensor_tensor(out=li[:], in0=ai[:], in1=dt[:], op=ALU.mult), [ba], [bl])
        P.op("act", lambda e: e.activation(out=mag[:], in_=lr[:], func=AF.Exp), [bl], [bl])

        def rr(v):
            nf = tmp[6]
            P.op("dve", lambda e: e.tensor_scalar(out=nf[:], in0=v[:], scalar1=1.0 / (2 * PI), scalar2=None, op0=ALU.mult), [bl], [bl])
            P.op("dve", lambda e: e.tensor_copy(out=ni[:], in_=nf[:]), [bl], [bl])
            P.op("dve", lambda e: e.tensor_copy(out=nf[:], in_=ni[:]), [bl], [bl])
            P.op("dve", lambda e: e.scalar_tensor_tensor(out=v[:], in0=nf[:], scalar=-2 * PI, in1=v[:], op0=ALU.mult, op1=ALU.add), [bl], [bl])
            P.op("dve", lambda e: e.tensor_scalar(out=nf[:], in0=v[:], scalar1=PI, scalar2=None, op0=ALU.is_gt), [bl], [bl])
            P.op("dve", lambda e: e.scalar_tensor_tensor(out=v[:], in0=nf[:], scalar=-2 * PI, in1=v[:], op0=ALU.mult, op1=ALU.add), [bl], [bl])
            P.op("dve", lambda e: e.tensor_scalar(out=nf[:], in0=v[:], scalar1=-PI, scalar2=None, op0=ALU.is_lt), [bl], [bl])
            P.op("dve", lambda e: e.scalar_tensor_tensor(out=v[:], in0=nf[:], scalar=2 * PI, in1=v[:], op0=ALU.mult, op1=ALU.add), [bl], [bl])

        P.op("pool", lambda e: e.tensor_scalar(out=cs_arg[:], in0=li[:], scalar1=PI / 2, scalar2=None, op0=ALU.add), [bl], [bl])
        rr(li)
        rr(cs_arg)
        P.op("act", lambda e: e.activation(out=pi_[:], in_=li[:], func=AF.Sin), [bl], [bl])
        P.op("act", lambda e: e.activation(out=pr[:], in_=cs_arg[:], func=AF.Sin), [bl], [bl])
        P.op("pool", lambda e: e.tensor_tensor(out=pr[:], in0=pr[:], in1=mag[:], op=ALU.mult), [bl], [bl])
        P.op("pool", lambda e: e.tensor_tensor(out=pi_[:], in0=pi_[:], in1=mag[:], op=ALU.mult), [bl], [bl])
        PW = sb("PWS", [128, 9, 2, G2], F32)
        bpw = P.buf("PWS")
        P.op("pool", lambda e: e.memset(PW[:, 0, 0, :], 1.0), [], [bpw])
        P.op("pool", lambda e: e.memset(PW[:, 0, 1, :], 0.0), [], [bpw])
        P.op("pool", lambda e: e.tensor_copy(out=PW[:, 1, 0, :], in_=pr[:]), [bl], [bpw])
        P.op("pool", lambda e: e.tensor_copy(out=PW[:, 1, 1, :], in_=pi_[:]), [bl], [bpw])
        t1, t2 = tmp[7], tmp[8]
        btt = P.buf("ttS")
        for e_ in range(2, 9):
            s5_cmul(P, None, PW[:, e_, 0, :], PW[:, e_, 1, :], PW[:, e_ - 1, 0, :], PW[:, e_ - 1, 1, :], PW[:, 1, 0, :], PW[:, 1, 1, :], t1[:], t2[:], [bpw], [bpw], btt)
        inv8 = sb("inv8S", [128, 2, G2], F32)
        binv = P.buf("inv8S")
        P.op("pool", lambda e: e.tensor_tensor(out=t1[:], in0=PW[:, 8, 0, :], in1=PW[:, 8, 0, :], op=ALU.mult), [bpw], [btt])
        P.op("pool", lambda e: e.tensor_tensor(out=t2[:], in0=PW[:, 8, 1, :], in1=PW[:, 8, 1, :], op=ALU.mult), [bpw], [btt])
        P.op("pool", lambda e: e.tensor_tensor(out=t1[:], in0=t1[:], in1=t2[:], op=ALU.add), [btt], [btt])
        P.op("dve", lambda e: e.reciprocal(out=t1[:], in_=t1[:]), [btt], [btt])
        P.op("pool", lambda e: e.tensor_tensor(out=inv8[:, 0, :], in0=PW[:, 8, 0, :], in1=t1[:], op=ALU.mult), [bpw, btt], [binv])
        P.op("pool", lambda e: e.tensor_tensor(out=inv8[:, 1, :], in0=PW[:, 8, 1, :], in1=t1[:], op=ALU.mult), [bpw, btt], [binv])
        P.op("pool", lambda e: e.tensor_scalar(out=inv8[:, 1, :], in0=inv8[:, 1, :], scalar1=-1.0, scalar2=None, op0=ALU.mult), [binv], [binv])
        SC = sb("SCS", [128, NS * 3, 2, G2], F32)
        bsc = P.buf("SCS")
        q = sb("qS", [128, 2, G2], F32)
        bq = P.buf("qS")
        P.op("pool", lambda e: e.tensor_copy(out=q[:], in_=PW[:, 8, :, :]), [bpw], [bq])
        for m in range(NS):
            P.op("pool", lambda e, m=m: e.tensor_copy(out=SC[:, m * 3, :, :], in_=q[:]), [bq], [bsc])
            s5_cmul(P, None, SC[:, m * 3 + 1, 0, :], SC[:, m * 3 + 1, 1, :], q[:, 0, :], q[:, 1, :], q[:, 0, :], q[:, 1, :], t1[:], t2[:], [bq, bsc], [bsc], btt)
            s5_cmul(P, None, SC[:, m * 3 + 2, 0, :], SC[:, m * 3 + 2, 1, :], SC[:, m * 3 + 1, 0, :], SC[:, m * 3 + 1, 1, :], q[:, 0, :], q[:, 1, :], t1[:], t2[:], [bq, bsc], [bsc], btt)
            if m < NS - 1:
                s5_cmul(P, None, q[:, 0, :], q[:, 1, :], SC[:, m * 3 + 1, 0, :], SC[:, m * 3 + 1, 1, :], SC[:, m * 3 + 1, 0, :], SC[:, m * 3 + 1, 1, :], t1[:], t2[:], [bsc], [bq], btt)
        P.op("pool", lambda e: e.tensor_scalar(out=SC[:, :, 1, :], in0=SC[:, :, 1, :], scalar1=msk[:, 2:3], scalar2=None, op0=ALU.mult), [bsc, bm], [bsc])
        P.dma("sp", scr.SSC[:, 0:NS * 3, :, :], SC[:], [bsc], [scr.bK], bsc)
        cr, ci, den = tmp[9], tmp[10], tmp[11]
        bc = P.buf("cS")
        xr = tmp[2]
        P.op("pool", lambda e: e.tensor_scalar(out=xr[:], in0=PW[:, 1, 0, :], scalar1=-1.0, scalar2=None, op0=ALU.add), [bpw, bl], [bl])
        P.op("pool", lambda e: e.tensor_tensor(out=den[:], in0=ar[:], in1=ar[:], op=ALU.mult), [ba], [bc])
        P.op("pool", lambda e: e.tensor_tensor(out=t1[:], in0=ai[:], in1=ai[:], op=ALU.mult), [ba, binv], [btt])
        P.op("pool", lambda e: e.tensor_tensor(out=den[:], in0=den[:], in1=t1[:], op=ALU.add), [btt], [bc])
        P.op("dve", lambda e: e.reciprocal(out=den[:], in_=den[:]), [bc], [bc])
        P.op("pool", lambda e: e.tensor_tensor(out=t1[:], in0=xr[:], in1=ar[:], op=ALU.mult), [bl, ba], [btt])
        P.op("pool", lambda e: e.tensor_tensor(out=t2[:], in0=PW[:, 1, 1, :], in1=ai[:], op=ALU.mult), [bpw, ba], [btt])
        P.op("pool", lambda e: e.tensor_tensor(out=cr[:], in0=t1[:], in1=t2[:], op=ALU.add), [btt], [bc])
        P.op("pool", lambda e: e.tensor_tensor(out=cr[:], in0=cr[:], in1=den[:], op=ALU.mult), [bc], [bc])
        P.op("pool", lambda e: e.tensor_tensor(out=t1[:], in0=PW[:, 1, 1, :], in1=ar[:], op=ALU.mult), [bpw, ba, bc], [btt])
        P.op("pool", lambda e: e.tensor_tensor(out=t2[:], in0=xr[:], in1=ai[:], op=ALU.mult), [bl, ba], [btt])
        P.op("pool", lambda e: e.tensor_tensor(out=ci[:], in0=t1[:], in1=t2[:], op=ALU.subtract), [btt], [bc])
        P.op("pool", lambda e: e.tensor_tensor(out=ci[:], in0=ci[:], in1=den[:], op=ALU.mult), [bc], [bc])
        Bri = sb("BriS", [128, 2, G2, 16], F32)
        bB = P.buf("BriS")
        for half in range(2):
            for d in range(2):
                P.dma("sp", Bri[half * 64:(half + 1) * 64, 0, d * 64:(d + 1) * 64, :], C.s5_b_re[0, d].rearrange("g n c -> n g c"), [], [bB], bB)
                P.dma("sp", Bri[half * 64:(half + 1) * 64, 1, d * 64:(d + 1) * 64, :], C.s5_b_im[0, d].rearrange("g n c -> n g c"), [], [bB], bB)
        Bb = sb("BbS", [128, 2, G2, 16], F32)
        bBb = P.buf("BbS")
        T1 = sb("T1S", [128, 64, 16], F32)
        T2 = sb("T2S", [128, 64, 16], F32)
        bT = P.buf("TS")
        for d in range(2):
            gs = slice(d * 64, (d + 1) * 64)
            crb = cr[:, gs].unsqueeze(2).to_broadcast([128, 64, 16])
            cib = ci[:, gs].unsqueeze(2).to_broadcast([128, 64, 16])
            s5_cmul(P, None, Bb[:, 0, gs, :], Bb[:, 1, gs, :], Bri[:, 0, gs, :], Bri[:, 1, gs, :], crb, cib, T1[:], T2[:], [bB, bc], [bBb], bT)
        Cri = sb("CriS", [128, 2, G2, 16], F32)
        bC = P.buf("CriS")
        cl = Rot(P, sb, "clS", [128, 128], F32, 2)
        tpc = Rot(P, ps, "tpcS", [128, 512], F32, 2)
        for ri, src in enumerate((C.s5_c_re, C.s5_c_im)):
            for d in range(2):
                for o in range(8):
                    c_t, c_b = cl.next()
                    for dup in range(2):
                        P.dma("sp", c_t[:, dup * 64:(dup + 1) * 64], src[0, d, o * 8:(o + 1) * 8].rearrange("g c n -> (g c) n"), [], [c_b], c_b)
                    tp_, tpb_ = tpc.next()
                    P.op("pe", lambda e, tp_=tp_, c_t=c_t: e.transpose(out=tp_[:, 0:128], in_=c_t[:], identity=identf[:]), [c_b, bid], [tpb_])
                    P.op("act", lambda e, tp_=tp_, ri=ri, d=d, o=o: e.activation(out=Cri[:, ri, d * 64 + o * 8:d * 64 + (o + 1) * 8, :], in_=tp_[:, 0:128].rearrange("p (g c) -> p g c", c=16), func=AF.Copy), [tpb_], [bC])
        GC = 32
        W8 = sb("W8S", [128, 2, GC, 8, 16], BF16)
        W8s = sb("W8sS", [128, GC, 8, 16], BF16)
        C8 = sb("C8S", [128, GC, 8, 16], BF16)
        bW8, bW8s, bC8 = P.buf("W8S"), P.buf("W8sS"), P.buf("C8S")
        O1 = sb("O1S", [128, GC, 16], F32)
        O2 = sb("O2S", [128, GC, 16], F32)
        O3 = sb("O3S", [128, GC, 16], F32)
        O4 = sb("O4S", [128, GC, 16], F32)
        bO = P.buf("OS")
        tpb16 = Rot(P, ps, "tpb16S", [128, 1024], BF16, 2)
        pd = Rot(P, ps, "pdS", [128, 512], F32, 2)
        b8st = Rot(P, sb, "b8stS", [128, 8, 128], BF16, 2)
        d8f = Rot(P, sb, "d8fS", [128, 4, 128], F32, 2)
        d8st = Rot(P, sb, "d8stS", [128, 4, 128], BF16, 2)
        TT1 = T1[:, 0:GC, :]
        TT2 = T2[:, 0:GC, :]
        for ch in range(G2 // GC):
            d = (ch * GC) // 64
            gs = slice(ch * GC, (ch + 1) * GC)
            for i in range(8):
                eb = (7 - i) if d == 0 else i
                prb = PW[:, eb, 0, gs].unsqueeze(2).to_broadcast([128, GC, 16])
                pib = PW[:, eb, 1, gs].unsqueeze(2).to_broadcast([128, GC, 16])
                s5_cmul(P, None, O1[:], O2[:], Bb[:, 0, gs, :], Bb[:, 1, gs, :], prb, pib, TT1, TT2, [bBb, bpw], [bO], bT)
                P.op("pool", lambda e, i=i: e.tensor_copy(out=W8[:, 0, :, i, :], in_=O1[:]), [bO], [bW8])
                P.op("pool", lambda e, i=i: e.tensor_copy(out=W8[:, 1, :, i, :], in_=O2[:]), [bO], [bW8])
                i8r = inv8[:, 0, gs].unsqueeze(2).to_broadcast([128, GC, 16])
                i8i = inv8[:, 1, gs].unsqueeze(2).to_broadcast([128, GC, 16])
                s5_cmul(P, None, O3[:], O4[:], O1[:], O2[:], i8r, i8i, TT1, TT2, [bO, binv], [bO], bT)
                P.op("pool", lambda e: e.tensor_scalar(out=O3[:], in0=O3[:], scalar1=msk[:, 0:1], scalar2=None, op0=ALU.mult), [bO, bm], [bO])
                P.op("dve", lambda e, i=i: e.scalar_tensor_tensor(out=W8s[:, :, i, :], in0=O4[:], scalar=msk[:, 1:2], in1=O3[:], op0=ALU.mult, op1=ALU.add), [bO, bm], [bW8s])
                ec_ = (i + 1) if d == 0 else (8 - i)
                prc = PW[:, ec_, 0, gs].unsqueeze(2).to_broadcast([128, GC, 16])
                pic = PW[:, ec_, 1, gs].unsqueeze(2).to_broadcast([128, GC, 16])
                s5_cmul(P, None, O1[:], O2[:], Cri[:, 0, gs, :], Cri[:, 1, gs, :], prc, pic, TT1, TT2, [bC, bpw, bW8, bW8s], [bO], bT)
                P.op("pool", lambda e: e.tensor_scalar(out=O1[:], in0=O1[:], scalar1=msk[:, 0:1], scalar2=None, op0=ALU.mult), [bO, bm], [bO])
                P.op("dve", lambda e, i=i: e.scalar_tensor_tensor(out=C8[:, :, i, :], in0=O2[:], scalar=msk[:, 3:4], in1=O1[:], op0=ALU.mult, op1=ALU.add), [bO, bm], [bC8])
            P.dma("sp", scr.SC8[:, ch * GC:(ch + 1) * GC, :], C8[:].rearrange("p g i c -> p g (i c)"), [bC8], [scr.bK], bC8)
            for j0 in range(0, GC, 8):
                tp_, tpb_ = tpb16.next()
                for j in range(8):
                    for ri in range(2):
                        P.op("pe", lambda e, tp_=tp_, j=j, j0=j0, ri=ri: e.transpose(out=tp_[:, j * 128 + ri * 64:j * 128 + (ri + 1) * 64], in_=W8[0:64, ri, j0 + j, :, :].rearrange("n i c -> n (i c)"), identity=ident[0:64, 0:64]), [bW8, bid], [tpb_])
                st_, stb_ = b8st.next()
                P.op("act", lambda e, tp_=tp_, st_=st_: e.activation(out=st_[:].rearrange("p j n -> p (j n)"), in_=tp_[:], func=AF.Copy), [tpb_], [stb_])
                dg0 = ch * GC + j0
                P.dma("sp", scr.SB8[dg0:dg0 + 8, :, :].rearrange("j p n -> p j n"), st_[:], [stb_], [scr.bK], stb_)
            for j0 in range(0, GC, 4):
                pp_, ppb_ = pd.next()
                for j in range(4):
                    P.op("pe", lambda e, pp_=pp_, j=j, j0=j0: e.matmul(pp_[:, j * 128:(j + 1) * 128], lhsT=W8s[:, j0 + j, :, :].rearrange("n i c -> n (i c)"), rhs=C8[:, j0 + j, :, :].rearrange("n i c -> n (i c)"), start=True, stop=True, skip_group_check=True), [bW8s, bC8], [ppb_])
                f_, fb_ = d8f.next()
                P.op("act", lambda e, pp_=pp_, f_=f_: e.activation(out=f_[:].rearrange("p j c -> p (j c)"), in_=pp_[:], func=AF.Copy), [ppb_], [fb_])
                o_, ob_ = d8st.next()
                mk = mfb[:, d, :].unsqueeze(1).to_broadcast([128, 4, 128])
                P.op("pool", lambda e, f_=f_, o_=o_, mk=mk: e.tensor_tensor(out=o_[:], in0=f_[:], in1=mk, op=ALU.mult), [fb_, bm], [ob_])
                dg0 = ch * GC + j0
                P.dma("sp", scr.SD8[dg0:dg0 + 4, :, :].rearrange("j p n -> p j n"), o_[:], [ob_], [scr.bK], ob_)

    _phase(P, nc, body)


def phase_S5main(P, nc, C, L, NS):
    T = L // 8
    NTT = max(1, T // 128)
    TT = min(T, 128)
    scr = C.scr

    def body(sb, ps):
        ident = sb("identM", [128, 128], BF16)
        identf = sb("identMf", [128, 128], F32)
        swapf = sb("swapMf", [128, 128], F32)
        bid = P.buf("identM")
        P.dma("sp", identf[:], C.ident[:, :], [], [bid], bid)
        P.dma("sp", swapf[:], C.s5_swap[:, :], [], [bid], bid)
        P.op("dve", lambda e: e.tensor_copy(out=ident[:], in_=identf[:]), [bid], [bid])
        SC = sb("SCM", [128, NS * 3, 2, 128], F32)
        bsc = P.buf("SCM")
        P.dma("sp", SC[:], scr.SSC[:, 0:NS * 3, :, :], [], [bsc], bsc)
        uo = Rot(P, sb, "uoM", [128, NTT, 8, 128], BF16, 2)
        uo2 = Rot(P, sb, "uo2M", [128, NTT, 8, 8, 16], BF16, 2)
        yo = Rot(P, sb, "yoM", [128, NTT, 8, 128], BF16, 2)
        wg = Rot(P, sb, "wgM", [128, 6, 128], BF16, 2)
        us = Rot(P, sb, "usM", [128, T], BF16, 2)
        sbuf_ = Rot(P, sb, "sM", [128, T], BF16, 2 * (NS + 1) + 2)
        ys = Rot(P, sb, "ysM", [128, T], BF16, 2)
        Rt = Rot(P, sb, "RtM", [128, 128], F32, 6)
        Rb = Rot(P, sb, "RbM", [128, 128], BF16, 14)
        tp = Rot(P, ps, "tpM", [128, 1024], BF16, 2)
        pp = Rot(P, ps, "ppM", [128, 512], F32, 4)
        chunks = [(c0, min(512, T - c0)) for c0 in range(0, T, 512)]

        def do_group(o, g8, uo_t, uo_b, yo_t, yo_b):
            g = o * 8 + g8
            w_t, w_b = wg.next()
            for d in range(2):
                P.dma("sp", w_t[:, d, :], scr.SB8[d * 64 + g, :, :], [], [w_b], w_b)
                P.dma("sp", w_t[:, 2 + d, :], scr.SC8[:, d * 64 + g, :], [], [w_b], w_b)
            for d in range(2):
                P.dma("sp", w_t[:, 4 + d, :], scr.SD8[d * 64 + g, :, :], [], [w_b], w_b)
            tpp, tpb = tp.next()
            for tt in range(NTT):
                P.op("pe", lambda e, tt=tt: e.transpose(out=tpp[:, tt * TT:(tt + 1) * TT], in_=uo_t[0:TT, tt, g8, :, :].rearrange("p i c -> p (i c)"), identity=ident[0:TT, 0:TT]), [uo_b, bid], [tpb])
            us_t, us_b = us.next()
            P.op("act", lambda e: e.activation(out=us_t[:], in_=tpp[:, 0:T], func=AF.Copy), [tpb], [us_b])
            cur = []
            for d in range(2):
                s_t, s_b = sbuf_.next()
                for (c0, w) in chunks:
                    z, zb = pp.next()
                    P.op("pe", lambda e, z=z, c0=c0, w=w, d=d: e.matmul(z[:, 0:w], lhsT=w_t[:, d, :], rhs=us_t[:, c0:c0 + w], start=True, stop=True), [us_b, w_b], [zb])
                    P.op("act", lambda e, z=z, c0=c0, w=w, s_t=s_t: e.activation(out=s_t[:, c0:c0 + w], in_=z[:, 0:w], func=AF.Copy), [zb], [s_b])
                cur.append((s_t, s_b))
            for m in range(NS):
                S = 4 ** m
                Rall = []
                for d in range(2):
                    dg = d * 64 + g
                    Rs = []
                    for J in range(1, 4):
                        if J * S >= T:
                            break
                        r1, r1b = Rt.next()
                        r2, r2b = Rb.next()
                        col = m * 3 + J - 1
                        P.op("pool", lambda e, r1=r1, col=col, dg=dg: e.tensor_scalar(out=r1[:], in0=identf[:], scalar1=SC[:, col, 0, dg:dg + 1], scalar2=None, op0=ALU.mult), [bid, bsc], [r1b])
                        P.op("dve", lambda e, r1=r1, r2=r2, col=col, dg=dg: e.scalar_tensor_tensor(out=r2[:], in0=swapf[:], scalar=SC[:, col, 1, dg:dg + 1], in1=r1[:], op0=ALU.mult, op1=ALU.add), [bid, bsc, r1b], [r2b])
                        Rs.append((J * S, r2, r2b))
                    Rall.append(Rs)
                nxt = []
                for d in range(2):
                    s_t, s_b = cur[d]
                    n_t, n_b = sbuf_.next()
                    for (c0, w) in chunks:
                        z, zb = pp.next()
                        mms = [(0, w, c0, ident, bid)]
                        for (sh, r2, r2b) in Rall[d]:
                            if d == 0:
                                a = max(c0, sh)
                                if a < c0 + w:
                                    mms.append((a - c0, w, a - sh, r2, r2b))
                            else:
                                bnd = min(c0 + w, T - sh)
                                if bnd > c0:
                                    mms.append((0, bnd - c0, c0 + sh, r2, r2b))
                        for k_, (o0, o1, src0, lt, ltb) in enumerate(mms):
                            P.op("pe", lambda e, z=z, o0=o0, o1=o1, src0=src0, lt=lt, s_t=s_t, k_=k_, nm=len(mms): e.matmul(z[:, o0:o1], lhsT=lt[:], rhs=s_t[:, src0:src0 + (o1 - o0)], start=(k_ == 0), stop=(k_ == nm - 1), skip_group_check=True), [s_b, ltb], [zb])
                        P.op("act", lambda e, z=z, c0=c0, w=w, n_t=n_t: e.activation(out=n_t[:, c0:c0 + w], in_=z[:, 0:w], func=AF.Copy), [zb], [n_b])
                    nxt.append((n_t, n_b))
                cur = nxt
            fin = cur
            y_t, y_b = ys.next()
            for (c0, w) in chunks:
                z, zb = pp.next()
                mms = [(0, w, us_t, us_b, c0, 4), (0, w, us_t, us_b, c0, 5)]
                a = max(c0, 1)
                if a < c0 + w:
                    mms.append((a - c0, w, fin[0][0], fin[0][1], a - 1, 2))
                bnd = min(c0 + w, T - 1)
                if bnd > c0:
                    mms.append((0, bnd - c0, fin[1][0], fin[1][1], c0 + 1, 3))
                for k_, (o0, o1, src, srcb, src0, wi) in enumerate(mms):
                    P.op("pe", lambda e, z=z, o0=o0, o1=o1, src=src, src0=src0, wi=wi, k_=k_, nm=len(mms): e.matmul(z[:, o0:o1], lhsT=w_t[:, wi, :], rhs=src[:, src0:src0 + (o1 - o0)], start=(k_ == 0), stop=(k_ == nm - 1), skip_group_check=True), [srcb, w_b], [zb])
                P.op("act", lambda e, z=z, c0=c0, w=w: e.activation(out=y_t[:, c0:c0 + w], in_=z[:, 0:w], func=AF.Copy), [zb], [y_b])
            tpp2, tpb2 = tp.next()
            for tt in range(NTT):
                P.op("pe", lambda e, tt=tt: e.transpose(out=tpp2[0:TT, tt * 128:(tt + 1) * 128], in_=y_t[:, tt * TT:(tt + 1) * TT], identity=ident[:]), [y_b, bid], [tpb2])
            P.op("act", lambda e: e.activation(out=yo_t[0:TT, :, :, g8 * 16:(g8 + 1) * 16], in_=tpp2[0:TT, 0:NTT * 128].rearrange("p (t i c) -> p t i c", t=NTT, i=8), func=AF.Copy), [tpb2], [yo_b])

        for o in range(8):
            uo_t, uo_b = uo.next()
            yo_t, yo_b = yo.next()
            for tt in range(NTT):
                P.dma("sp", uo_t[0:TT, tt, :, :], scr.U[tt * TT * 8:(tt + 1) * TT * 8, o * 128:(o + 1) * 128].rearrange("(p i) c -> p i c", i=8), [], [uo_b], uo_b)
            u2_t, u2_b = uo2.next()
            for tt in range(NTT):
                P.op("pool", lambda e, tt=tt, u2_t=u2_t, uo_t=uo_t: e.tensor_copy(out=u2_t[0:TT, tt, :, :, :], in_=uo_t[0:TT, tt, :, :].rearrange("p i (g c) -> p g i c", c=16)), [uo_b], [u2_b])
            for g8 in range(8):
                do_group(o, g8, u2_t, u2_b, yo_t, yo_b)
            for tt in range(NTT):
                P.dma("sp", scr.YS[tt * TT * 8:(tt + 1) * TT * 8, o * 128:(o + 1) * 128].rearrange("(p i) c -> p i c", i=8), yo_t[0:TT, tt, :, :], [yo_b], [scr.bV], yo_b)

    _phase(P, nc, body)


def phase_S5post(P, nc, C, L):
    NT = L // 128
    scr = C.scr

    def body(sb, ps):
        ident = sb("identP", [128, 128], BF16)
        identf = sb("identPf", [128, 128], F32)
        bid = P.buf("identP")
        P.dma("sp", identf[:], C.ident[:, :], [], [bid], bid)
        P.op("dve", lambda e: e.tensor_copy(out=ident[:], in_=identf[:]), [bid], [bid])
        stage = Rot(P, sb, "wstP", [128, 1024], F32, 2)
        w_g = sb("w_gP", [128, 8, 1024], BF16)
        bw = P.buf("w_gP")
        load_weight_bf16(P, sb, w_g, bw, C.s5_glu_w[0], 8, 1024, stage)
        drep = sb("drepP", [128, 1024], F32)
        brep = sb("brepP", [128, 1024], F32)
        br = P.buf("repP")
        P.dma("sp", drep[:], C.s5_d[0].partition_broadcast(128), [], [br], br, slow=True)
        P.dma("sp", brep[:], C.s5_glu_b[0].partition_broadcast(128), [], [br], br, slow=True)
        ut = Rot(P, sb, "utP", [128, 1024], BF16, 2)
        yst = Rot(P, sb, "ystP", [128, 1024], BF16, 2)
        y = Rot(P, sb, "yP", [128, 1024], F32, 2)
        w = Rot(P, sb, "wP", [128, 1024], F32, 2)
        sg = Rot(P, sb, "sgP", [128, 1024], F32, 2)
        yg = Rot(P, sb, "ygP", [128, 1024], BF16, 2)
        ygT = Rot(P, sb, "ygTP", [128, 8, 128], BF16, 2)
        ya = Rot(P, sb, "yaP", [128, 1024], BF16, 2)
        tp = Rot(P, ps, "tpP", [128, 1024], BF16, 2)
        zp = Rot(P, ps, "zpP", [128, 512], F32, 4)
        for t in range(NT):
            rows = slice(t * 128, (t + 1) * 128)
            u_t, u_b = ut.next()
            s_t, s_b = yst.next()
            P.dma("sp", u_t[:], scr.U[rows, :], [], [u_b], u_b)
            P.dma("sp", s_t[:], scr.YS[rows, :], [], [s_b], s_b)
            y_t, y_b = y.next()
            w_t, w_b = w.next()
            P.op("pool", lambda e, y_t=y_t, u_t=u_t: e.tensor_tensor(out=y_t[:], in0=u_t[:], in1=drep[:], op=ALU.mult), [u_b, br], [y_b])
            P.op("pool", lambda e, y_t=y_t, s_t=s_t: e.tensor_tensor(out=y_t[:], in0=y_t[:], in1=s_t[:], op=ALU.add), [s_b, y_b], [y_b])
            P.op("dve", lambda e, y_t=y_t, w_t=w_t: e.tensor_tensor(out=w_t[:], in0=y_t[:], in1=y_t[:], op=ALU.mult), [y_b], [w_b])
            P.op("pool", lambda e, w_t=w_t: e.tensor_scalar(out=w_t[:], in0=w_t[:], scalar1=0.044715, scalar2=1.0, op0=ALU.mult, op1=ALU.add), [w_b], [w_b])
            P.op("pool", lambda e, w_t=w_t, y_t=y_t: e.tensor_tensor(out=w_t[:], in0=w_t[:], in1=y_t[:], op=ALU.mult), [w_b, y_b], [w_b])
            g_t, g_b = sg.next()
            P.op("act", lambda e, w_t=w_t, g_t=g_t: e.activation(out=g_t[:], in_=w_t[:], func=AF.Sigmoid, scale=2.0 * math.sqrt(2.0 / math.pi)), [w_b], [g_b])
            yg_t, yg_b = yg.next()
            P.op("dve", lambda e, yg_t=yg_t, y_t=y_t, g_t=g_t: e.tensor_tensor(out=yg_t[:], in0=y_t[:], in1=g_t[:], op=ALU.mult), [y_b, g_b], [yg_b])
            tpp, tpb = tp.next()
            for k in range(8):
                P.op("pe", lambda e, k=k, tpp=tpp, yg_t=yg_t: e.transpose(out=tpp[:, k * 128:(k + 1) * 128], in_=yg_t[:, k * 128:(k + 1) * 128], identity=ident[:]), [yg_b, bid], [tpb])
            yT, yTb = ygT.next()
            P.op("act", lambda e, tpp=tpp, yT=yT: e.activation(out=yT[:].rearrange("p k t -> p (k t)"), in_=tpp[:], func=AF.Copy), [tpb], [yTb])
            for gi in range(2):
                z, zb = zp.next()
                for k in range(8):
                    P.op("pe", lambda e, z=z, k=k, gi=gi, yT=yT: e.matmul(z[:], lhsT=yT[:, k, :], rhs=w_g[:, k, gi * 512:(gi + 1) * 512], start=(k == 0), stop=(k == 7)), [yTb, bw], [zb])
                P.op("act", lambda e, z=z, gi=gi, g_t=g_t: e.activation(out=g_t[:, gi * 512:(gi + 1) * 512], in_=z[:], func=AF.Copy), [zb], [g_b])
            P.op("pool", lambda e, g_t=g_t: e.tensor_tensor(out=g_t[:], in0=g_t[:], in1=brep[:], op=ALU.add), [g_b, br], [g_b])
            P.op("act", lambda e, g_t=g_t: e.activation(out=g_t[:], in_=g_t[:], func=AF.Sigmoid), [g_b], [g_b])
            a_t, a_b = ya.next()
            P.op("dve", lambda e, a_t=a_t, yg_t=yg_t, g_t=g_t: e.tensor_tensor(out=a_t[:], in0=yg_t[:], in1=g_t[:], op=ALU.mult), [yg_b, g_b], [a_b])
            P.dma("pool", scr.YA[rows, :], a_t[:], [a_b], [scr.bU], a_b)

    _phase(P, nc, body)


WNAMES = {
    "norm_g": [2, 1024], "final_g": [1024], "ple_w": [2, 256, 1024], "ple_gate_w": [2, 1024, 1024],
    "ab_w_in": [1, 1024, 3776], "ab_w_out": [1, 2048, 1024],
    "s5_a_re": [1, 2, 64, 64], "s5_a_im": [1, 2, 64, 64], "s5_log_dt": [1, 2, 64],
    "s5_b_re": [1, 2, 64, 64, 16], "s5_b_im": [1, 2, 64, 64, 16], "s5_c_re": [1, 2, 64, 16, 64], "s5_c_im": [1, 2, 64, 16, 64],
    "s5_d": [1, 1024], "s5_glu_w": [1, 1024, 1024], "s5_glu_b": [1, 1024],
    "mla_q_norm": [1, 384], "mla_w_q_up": [1, 384, 1536], "mla_kv_norm": [1, 256], "mla_w_kv_up": [1, 256, 2048],
    "hy_w_in": [1, 1024, 8192], "hy_w_out": [1, 2048, 1024], "hy_conv_w": [1, 3, 6144], "hy_conv_b": [1, 6144],
    "hy_f_w1": [1, 2, 33, 64], "hy_f_b1": [1, 2, 64], "hy_f_freq1": [1, 2, 64], "hy_f_w2": [1, 2, 64, 64], "hy_f_b2": [1, 2, 64],
    "hy_f_freq2": [1, 2, 64], "hy_f_w3": [1, 2, 64, 2048], "hy_bias": [1, 2048],
}


def build(Ls, opts):
    nc = bass.Bass("TRN2", target_bir_lowering=False)
    C = Ctx()
    LM = max(Ls)
    for n, shp in WNAMES.items():
        setattr(C, n, nc.dram_tensor(n, shp, F32, kind="ExternalInput").ap())
    C.ident = nc.dram_tensor("ident", [128, 128], F32, kind="ExternalInput").ap()
    C.rope_cs = nc.dram_tensor("rope_cs", [LM, 64], F32, kind="ExternalInput").ap()
    xs, ps_, ys = [], [], []
    for i, L in enumerate(Ls):
        xs.append(nc.dram_tensor(f"x{i}", [L, 1024], F32, kind="ExternalInput").ap())
        ps_.append(nc.dram_tensor(f"p{i}", [2, L, 256], F32, kind="ExternalInput").ap())
        ys.append(nc.dram_tensor(f"y{i}", [L, 1024], F32, kind="ExternalOutput").ap())
    dbg = opts.get("dbg", ())
    scr = Ctx()
    C.scr = scr

    def scratch(name, shape, dtype):
        kind = "ExternalOutput" if name in dbg else "Internal"
        return nc.dram_tensor("scr_" + name, shape, dtype, kind=kind).ap()

    scr.U = scratch("U", [LM, 1024], BF16)
    scr.G = scratch("G", [LM, 2048], BF16)
    scr.V = scratch("V", [LM, 1024], BF16)
    scr.QN = scratch("QN", [8, 128, LM], BF16)
    scr.QR = scratch("QR", [8, 64, LM], BF16)
    scr.KN = scratch("KN", [8, 128, LM], BF16)
    scr.KR = scratch("KR", [64, LM], BF16)
    scr.O = scratch("O", [LM, 1024], BF16)
    scr.YA = scratch("YA", [LM, 1024], BF16)
    scr.YH = scratch("YH", [LM, 2048], BF16)
    scr.H1 = scratch("H1", [LM, 1024], F32)
    scr.HT = scratch("HT", [8, 128, LM + 2], BF16)
    scr.MT = scratch("MT", [16, 128, LM], BF16)
    N1M = 2 * LM // 128
    scr.AT = scratch("AT", [2, N1M, 128, 128], BF16)
    scr.ZT = scratch("ZT", [2, 128, N1M, 128], BF16)
    scr.YHAT = scratch("YHAT", [16, 2, 128, N1M, 128], BF16)
    scr.VX = scratch("VX", [LM, 2048], BF16)
    scr.XG = scratch("XG", [LM, 2048], BF16)
    scr.CV = scratch("CV", [LM, 2048], BF16)
    C.hy_delta = nc.dram_tensor("hy_delta", [2048], F32, kind="ExternalInput").ap()
    C.s5_rowmask = nc.dram_tensor("s5_rowmask", [128, 4], F32, kind="ExternalInput").ap()
    C.s5_mf = nc.dram_tensor("s5_mf", [128, 128], F32, kind="ExternalInput").ap()
    C.s5_mb = nc.dram_tensor("s5_mb", [128, 128], F32, kind="ExternalInput").ap()
    C.s5_swap = nc.dram_tensor("s5_swap", [128, 128], F32, kind="ExternalInput").ap()
    NSM = s5_nstages(LM // 8)
    scr.SSC = scratch("SSC", [128, NSM * 3, 2, 128], F32)
    scr.SB8 = scratch("SB8", [128, 128, 128], BF16)
    scr.SC8 = scratch("SC8", [128, 128, 128], BF16)
    scr.SD8 = scratch("SD8", [128, 128, 128], BF16)
    scr.YS = scratch("YS", [LM, 1024], BF16)
    fftc = {}
    for L in sorted(set(Ls)):
        tbs, N1, KL = fft_tables_shapes(L)
        d = {"tb": {k_: nc.dram_tensor(f"{k_}_{L}", list(shp), F32, kind="ExternalInput").ap() for k_, shp in tbs.items()}}
        d["zT"] = nc.dram_tensor(f"zT_{L}", [33, 2 * L], F32, kind="ExternalInput").ap()
        d["tl"] = nc.dram_tensor(f"tl_{L}", [2 * L], F32, kind="ExternalInput").ap()
        d["KT"] = scratch(f"KT_{L}", [2 * L, 2048], BF16)
        d["KH"] = scratch(f"KH_{L}", [16, 2, 128, N1, 128], F32)
        d["RS"] = scratch(f"RS_{L}", [1, 2048], F32)
        fftc[L] = d
    with ExitStack() as es:
        P = Prog(nc, es)
        for n in ["bU", "bG", "bV", "bQ", "bK", "bO", "bH", "bA"]:
            b = Buf(n, accum=True)
            setattr(scr, n, b)
        depth = opts.get("depth", 2)
        conv = opts.get("conv", True) and depth == 2
        if opts.get("s5", True):
            phase_S5setup(P, nc, C, NSM)
        if conv:
            for L in sorted(set(Ls)):
                d = fftc[L]
                phase_G(P, nc, C, L, d["zT"], d["tl"], d["KT"], d["RS"])
                phase_F(P, nc, C, L, d["KT"], 2 * L // 128, d["tb"], khat_dst=d["KH"])
        for i, L in enumerate(Ls):
            phs = opts.get("phases", "ABC")
            if "A" in phs:
                phase_A(P, nc, C, L, xs[i])
            if opts.get("s5", True):
                phase_S5main(P, nc, C, L, s5_nstages(L // 8))
                phase_S5post(P, nc, C, L)
            if "B" in phs:
                phase_B(P, nc, C, L)
            last = depth == 1
            if "C" in phs:
                phase_C(P, nc, C, L, 0, xs[i], ps_[i][0], C.ab_w_out[0], scr.H1, ys[i] if last else None, opts.get("s5", True))
            if depth == 2:
                phase_D1(P, nc, C, L)
                phase_D2(P, nc, C, L)
                d = fftc[L]
                phase_F(P, nc, C, L, scr.VX, L // 128, d["tb"], khat_src=d["KH"], yhat_dst=scr.YHAT)
                phase_I(P, nc, C, L, d["tb"], scr.YHAT)
                phase_C(P, nc, C, L, 1, scr.H1, ps_[i][1], C.hy_w_out[0], None, ys[i], False, rs_src=d["RS"])
        C.ninstr = P.ninstr
    return nc, C


def host_consts(LM):
    inv = 1.0 / (10000.0 ** (np.arange(0, 64, 2, dtype=np.float32) / 64.0))
    ang = np.arange(LM, dtype=np.float32)[:, None] * inv[None, :].astype(np.float32)
    cs = np.concatenate([np.cos(ang), np.sin(ang)], axis=1).astype(np.float32)
    out = {"ident": np.eye(128, dtype=np.float32), "rope_cs": cs}
    min_decay = math.log(1e-2) / 1.5
    max_decay = math.log(1e-2) / 0.3
    p_ = np.arange(128)
    rm = np.zeros((128, 4), np.float32)
    rm[:64, 0] = 1.0
    rm[64:, 1] = 1.0
    rm[:, 2] = np.where(p_ < 64, 1.0, -1.0)
    rm[64:, 3] = -1.0
    out["s5_rowmask"] = rm
    ii = p_ // 16
    out["s5_mf"] = (ii[None, :] >= ii[:, None]).astype(np.float32)
    out["s5_mb"] = (ii[None, :] <= ii[:, None]).astype(np.float32)
    sw = np.zeros((128, 128), np.float32)
    sw[p_, (p_ + 64) % 128] = 1.0
    out["s5_swap"] = sw
    out["hy_delta"] = np.abs(np.linspace(min_decay, max_decay, 2048, dtype=np.float32)).astype(np.float32)
    return out


def host_consts_L(L):
    out = {}
    tbs, N1, KL = fft_tables(L)
    for k_, v in tbs.items():
        out[f"{k_}_{L}"] = v
    t = np.linspace(0.0, 1.0, L, dtype=np.float32)[:, None]
    w = (2.0 * math.pi * np.arange(L, dtype=np.float32)[:, None] / L).astype(np.float32)
    bands = np.linspace(1e-4, 15, 16, dtype=np.float32)[None, :]
    z = np.concatenate([t, np.cos(bands * w), -np.sin(bands * w)], axis=-1).astype(np.float32)
    idx = np.concatenate([np.arange(L), np.array([0]), L - np.arange(1, L)])
    z2 = z[idx]
    tl = t[:, 0][idx].copy()
    tl[L] = 1.0e4
    out[f"zT_{L}"] = np.ascontiguousarray(z2.T.astype(np.float32))
    out[f"tl_{L}"] = np.ascontiguousarray(tl.astype(np.float32))
    return out


_CACHE = {}


def kernel(**inputs):
    Ls = [4096, 8192]
    if "nc" not in _CACHE:
        _CACHE["nc"] = build(Ls, {"depth": 2, "s5": True})
    nc, C = _CACHE["nc"]
    consts = host_consts(max(Ls))
    for L_ in Ls:
        consts.update(host_consts_L(L_))
    W = {n: np.ascontiguousarray(np.asarray(inputs[n], dtype=np.float32)) for n in WNAMES}
    xs, xp = np.asarray(inputs["x_sample"]), np.asarray(inputs["x_prompt"])
    psm, ppr = np.asarray(inputs["p_sample"]), np.asarray(inputs["p_prompt"])
    in_maps = []
    for c in range(8):
        m = dict(W)
        m.update(consts)
        m["x0"] = np.ascontiguousarray(xs[c])
        m["p0"] = np.ascontiguousarray(psm[:, c])
        m["x1"] = np.ascontiguousarray(xp[c % 2])
        m["p1"] = np.ascontiguousarray(ppr[:, c % 2])
        in_maps.append(m)
    res = run_bass_kernel_spmd(nc, in_maps, core_ids=list(range(8)))
    y_sample = np.stack([np.asarray(res.results[c]["y0"], dtype=np.float32) for c in range(8)], axis=0)
    y_prompt = np.stack([np.asarray(res.results[c]["y1"], dtype=np.float32) for c in range(2)], axis=0)
    return (y_prompt, y_sample)
```

```python
import math
from contextlib import ExitStack
import numpy as np
import concourse.bass as bass
import concourse.mybir as mybir
from concourse.bass_utils import run_bass_kernel_spmd

F32 = mybir.dt.float32
BF16 = mybir.dt.bfloat16
ALU = mybir.AluOpType
AF = mybir.ActivationFunctionType
AX = mybir.AxisListType

D = 1024
EPS = 1e-6
import os
CUT = float(os.environ.get('CUT', '99'))
CW = int(os.environ.get('CW', '640'))


class Buf:
    def __init__(self, name, accum=False):
        self.name = name
        self.w = {}
        self.r = {}
        self.accum = accum


class Prog:
    ENG = ["pe", "act", "dve", "pool", "sp"]

    def __init__(self, nc, es):
        self.nc = nc
        self.es = es
        self.q = {e: [] for e in self.ENG}
        self.sems = {}
        self.cnt = {}
        self.seen = {e: {} for e in self.ENG}
        ss = os.environ.get("SELF", "act,dve,pool").split(",")
        self.selfsync = {"pe": False, "act": "act" in ss, "dve": "dve" in ss, "pool": "pool" in ss, "sp": False}
        self.bufs = []
        self.ninstr = 0
        self.dma_map = {}
        for e in self.ENG:
            self.newsem(e)

    def newsem(self, key):
        self.sems[key] = self.es.enter_context(self.nc.semaphore("s_" + str(key)))
        self.cnt[key] = 0

    def buf(self, name, accum=False):
        b = Buf(name, accum)
        self.bufs.append(b)
        return b

    def op(self, eng, fn, reads=(), writes=(), dma=None):
        waits = {}

        def need(tok):
            for k, v in tok.items():
                if v > waits.get(k, 0):
                    waits[k] = v

        for b in reads:
            need(b.w)
        raw_self = waits.get(eng, 0)
        for b in writes:
            if not b.accum:
                need(b.w)
            need(b.r)
        wl = []
        for k, v in waits.items():
            if k == eng:
                if not self.selfsync[eng]:
                    continue
            if self.seen[eng].get(k, 0) >= v:
                continue
            self.seen[eng][k] = v
            wl.append((k, v))
        if dma is None:
            key, inc = eng, 1
        else:
            key, inc = dma, 16
            if key not in self.sems:
                self.newsem(key)
        self.cnt[key] += inc
        val = self.cnt[key]
        sems = self.sems

        def emit(e, wl=wl, fn=fn, key=key, inc=inc):
            for k, v in wl:
                e.wait_ge(sems[k], v)
            fn(e).then_inc(sems[key], inc)

        self.q[eng].append(emit)
        self.ninstr += 1
        if os.environ.get("KTRACE"):
            print("OP", self.ninstr, eng, "tok", key, val, "waits", wl, "R", [b.name for b in reads], "W", [b.name for b in writes])
        for b in reads:
            b.r[key] = max(b.r.get(key, 0), val)
        for b in writes:
            if b.accum:
                b.w[key] = max(b.w.get(key, 0), val)
            else:
                b.w = {key: val}
                b.r = {}

    def dma(self, eng, out, in_, reads, writes, key, slow=False):
        if isinstance(key, Buf):
            key = key.name
        key = (eng == "pool", key)
        if key not in self.dma_map:
            n = sum(1 for kk in self.dma_map if kk[0] == key[0])
            self.dma_map[key] = ("dmaG%d" if key[0] else "dmaS%d") % n
        if slow:
            self.op(eng, lambda e: e.dma_start(out=out, in_=in_, allow_slow_non_contiguous=True), reads, writes, dma=self.dma_map[key])
        else:
            self.op(eng, lambda e: e.dma_start(out=out, in_=in_), reads, writes, dma=self.dma_map[key])

    def sync_dram(self, b):
        return b

    def barrier(self):
        snap = dict(self.cnt)
        sems = self.sems
        for e in self.ENG:
            wl = []
            for k, v in snap.items():
                if v > 0 and self.seen[e].get(k, 0) < v:
                    self.seen[e][k] = v
                    wl.append((k, v))

            def emit(eh, wl=wl):
                for k, v in wl:
                    eh.wait_ge(sems[k], v)

            self.q[e].append(emit)
        self.dma_map = {}
        for b in self.bufs:
            b.w = {}
            b.r = {}
        self.bufs = [b for b in self.bufs if getattr(b, "persist", False)]

    def flush(self, block):
        q = self.q

        @block.tensor
        def _(e):
            for f in q["pe"]:
                f(e)

        @block.scalar
        def _(e):
            for f in q["act"]:
                f(e)

        @block.vector
        def _(e):
            for f in q["dve"]:
                f(e)

        @block.gpsimd
        def _(e):
            for f in q["pool"]:
                f(e)

        @block.sync
        def _(e):
            for f in q["sp"]:
                f(e)

        self.q = {e: [] for e in self.ENG}


class Rot:
    def __init__(self, P, alloc, name, shape, dtype, n):
        self.slots = []
        for i in range(n):
            t = alloc(f"{name}{i}", shape, dtype)
            self.slots.append((t, P.buf(f"{name}{i}")))
        self.i = 0

    def next(self):
        s = self.slots[self.i % len(self.slots)]
        self.i += 1
        return s


class Ctx:
    pass


_PH = [0]


_ONLY = [None]


def _phase(P, nc, body):
    if _ONLY[0] is not None and body.__qualname__.split(".")[0] not in _ONLY[0]:
        return
    _PH[0] += 1
    sfx = "_%d" % _PH[0]
    with ExitStack() as ph, nc.Block() as block:
        tot = [0]

        def sb(name, shape, dtype):
            n = 1
            for d in shape[1:]:
                n *= d
            tot[0] += ((n * (2 if dtype == BF16 else 4) + 31) // 32) * 32
            return ph.enter_context(nc.sbuf_tensor(name + sfx, shape, dtype))

        def ps(name, shape, dtype):
            return ph.enter_context(nc.psum_tensor(name + sfx, shape, dtype))

        body(sb, ps)
        if os.environ.get("KDEBUG"):
            print("phase", sfx, body.__qualname__, "sbuf bytes/partition", tot[0], "instr", P.ninstr)
        assert tot[0] <= 206 * 1024, tot[0]
        P.barrier()
        P.flush(block)


def load_weight_bf16(P, sb, wdst, wbuf, wsrc, K, N, stage_rot, rowscale=None, chunk=1024):
    i = 0
    for k in range(K):
        for c0 in range(0, N, chunk):
            c1 = min(N, c0 + chunk)
            st, stb = stage_rot.next()
            P.dma("sp", st[:, 0:c1 - c0], wsrc[k * 128:(k + 1) * 128, c0:c1], [], [stb], stb)
            eng = "dve" if i % 2 == 0 else "pool"
            if rowscale is None:
                P.op(eng, lambda e, st=st, k=k, c0=c0, c1=c1: e.tensor_copy(out=wdst[:, k, c0:c1], in_=st[:, 0:c1 - c0]), [stb], [wbuf])
            else:
                rs, rsb = rowscale
                P.op(eng, lambda e, st=st, k=k, c0=c0, c1=c1, rs=rs: e.tensor_scalar(out=wdst[:, k, c0:c1], in0=st[:, 0:c1 - c0], scalar1=rs[:, k:k + 1], scalar2=None, op0=ALU.mult), [stb, rsb], [wbuf])
            i += 1


def rstd_from_ssq(P, eng, rstd, ssq, n, R, W):
    P.op(eng, lambda e: e.tensor_scalar(out=rstd, in0=ssq, scalar1=1.0 / n, scalar2=EPS, op0=ALU.mult, op1=ALU.add), R, W)
    P.op("act", lambda e: e.activation(out=rstd, in_=rstd, func=AF.Sqrt), W, W)
    P.op(eng, lambda e: e.reciprocal(out=rstd, in_=rstd), W, W)


def phase_A(P, nc, C, L, x_ap):
    NT = L // 128
    scr = C.scr

    def body(sb, ps):
        ident = sb("identA", [128, 128], BF16)
        identf = sb("identAf", [128, 128], F32)
        bid = P.buf("ident")
        P.dma("sp", identf[:], C.ident[:, :], [], [bid], bid)
        P.op("dve", lambda e: e.tensor_copy(out=ident[:], in_=identf[:]), [bid], [bid])
        ng = sb("ngA", [128, 8], F32)
        qn = sb("qnA", [128, 3], F32)
        kvn = sb("kvnA", [128, 2], F32)
        bsm = P.buf("smallA")
        P.dma("sp", ng[:], C.norm_g[0].rearrange("(k p) -> p k", p=128), [], [bsm], bsm, slow=True)
        P.dma("sp", qn[:], C.mla_q_norm[0].rearrange("(k p) -> p k", p=128), [], [bsm], bsm, slow=True)
        P.dma("sp", kvn[:], C.mla_kv_norm[0].rearrange("(k p) -> p k", p=128), [], [bsm], bsm, slow=True)
        stage = Rot(P, sb, "wstA", [128, 1024], F32, 2)
        w_in = sb("w_inA", [128, 8, 3776], BF16)
        w_q = sb("w_qA", [128, 3, 1536], BF16)
        w_kv = sb("w_kvA", [128, 2, 2048], BF16)
        bw_in, bw_q, bw_kv = P.buf("w_in"), P.buf("w_q"), P.buf("w_kv")
        load_weight_bf16(P, sb, w_in, bw_in, C.ab_w_in[0], 8, 3776, stage, rowscale=(ng, bsm))
        load_weight_bf16(P, sb, w_q, bw_q, C.mla_w_q_up[0], 3, 1536, stage, rowscale=(qn, bsm))
        load_weight_bf16(P, sb, w_kv, bw_kv, C.mla_w_kv_up[0], 2, 2048, stage, rowscale=(kvn, bsm))

        xt = Rot(P, sb, "xtA", [128, 1024], F32, 2)
        cst = Rot(P, sb, "csA", [128, 64], F32, 2)
        junk = sb("junkA", [128, 1024], BF16)
        bjunk = P.buf("junkA")
        stat = Rot(P, sb, "statA", [128, 8], F32, 2)
        hn = Rot(P, sb, "hnA", [128, 1024], BF16, 2)
        hnT = Rot(P, sb, "hnTA", [128, 8, 128], BF16, 2)
        u_sb = Rot(P, sb, "uA", [128, 1024], BF16, 2)
        g_sb = Rot(P, sb, "gA", [128, 2048], BF16, 2)
        lat = Rot(P, sb, "latA", [128, 640], BF16, 2)
        latT = Rot(P, sb, "latTA", [128, int(os.environ.get("LATK", "5")), 128], BF16, 2)
        kr32 = Rot(P, sb, "kr32A", [128, 64], F32, 2)
        q_sb = Rot(P, sb, "qA", [128, 1536], BF16, 2)
        qtmp = Rot(P, sb, "qtmpA", [128, 4, 256], F32, 2)
        kn_sb = Rot(P, sb, "knA", [128, 1024], BF16, 2)
        v_sb = Rot(P, sb, "vA", [128, 1024], BF16, 2)
        kr_sb = Rot(P, sb, "krA", [128, 64], BF16, 2)
        qnT = Rot(P, sb, "qnTA", [128, 8, 256], BF16, 2)
        qrT = Rot(P, sb, "qrTA", [64, 8, 256], BF16, 2)
        knT = Rot(P, sb, "knTA", [128, 8, 256], BF16, 2)
        krT = Rot(P, sb, "krTA", [64, 256], BF16, 2)
        zp = Rot(P, ps, "zpA", [128, 512], F32, 2)
        tp = Rot(P, ps, "tpA", [128, 1024], BF16, 2)
        tq = Rot(P, ps, "tqA", [128, 1024], BF16, 2)
        qp = Rot(P, ps, "qpA", [128, 512], F32, 2)

        def in_proj_group(hT, hTb, c0, c1):
            z, zb = zp.next()
            for k in range(8):
                P.op("pe", lambda e, z=z, k=k: e.matmul(z[:, 0:c1 - c0], lhsT=hT[:, k, :], rhs=w_in[:, k, c0:c1], start=(k == 0), stop=(k == 7)), [hTb, bw_in], [zb])
            return z, zb

        cur = None
        for t in range(NT if CUT > 1 else 0):
            t4 = t % 2
            if t4 == 0:
                cur = (qnT.next(), qrT.next(), knT.next(), krT.next())
            (qnT_t, qnT_b), (qrT_t, qrT_b), (knT_t, knT_b), (krT_t, krT_b) = cur
            x, xb = xt.next()
            cs, csb = cst.next()
            P.dma("sp", x[:], x_ap[t * 128:(t + 1) * 128, :], [], [xb], xb)
            P.dma("sp", cs[:], C.rope_cs[t * 128:(t + 1) * 128, :], [], [csb], csb)
            st, stb = stat.next()
            P.op("act", lambda e, x=x, st=st: e.activation(out=junk[:], in_=x[:], func=AF.Square, accum_out=st[:, 0:1]), [xb], [bjunk, stb])
            rstd_from_ssq(P, "dve", st[:, 1:2], st[:, 0:1], D, [stb], [stb])
            h, hb = hn.next()
            P.op("act", lambda e, x=x, st=st, h=h: e.activation(out=h[:], in_=x[:], func=AF.Copy, scale=st[:, 1:2]), [xb, stb], [hb])
            tpp, tpb = tp.next()
            for k in range(8):
                P.op("pe", lambda e, k=k, tpp=tpp, h=h: e.transpose(out=tpp[:, k * 128:(k + 1) * 128], in_=h[:, k * 128:(k + 1) * 128], identity=ident[:]), [hb, bid], [tpb])
            hT, hTb = hnT.next()
            P.op("act", lambda e, hT=hT, tpp=tpp: e.activation(out=hT[:].rearrange("p k t -> p (k t)"), in_=tpp[:], func=AF.Copy), [tpb], [hTb])
            u, ub = u_sb.next()
            for gi in range(2):
                z, zb = in_proj_group(hT, hTb, gi * 512, gi * 512 + 512)
                P.op("act", lambda e, z=z, u=u, gi=gi: e.activation(out=u[:, gi * 512:(gi + 1) * 512], in_=z[:], func=AF.Copy), [zb], [ub])
            P.dma("pool", scr.U[t * 128:(t + 1) * 128, :], u[:], [ub], [scr.bU], ub)
            if CUT <= 2:
                continue
            la, lab = lat.next()
            z, zb = in_proj_group(hT, hTb, 1024, 1408)
            P.op("act", lambda e, z=z, st=st: e.activation(out=junk[:, 0:384], in_=z[:, 0:384], func=AF.Square, accum_out=st[:, 2:3]), [zb], [bjunk, stb])
            rstd_from_ssq(P, "dve", st[:, 3:4], st[:, 2:3], 384, [stb], [stb])
            P.op("act", lambda e, z=z, st=st, la=la: e.activation(out=la[:, 0:384], in_=z[:, 0:384], func=AF.Copy, scale=st[:, 3:4]), [zb, stb], [lab])
            z, zb = in_proj_group(hT, hTb, 1408, 1728)
            P.op("act", lambda e, z=z, st=st: e.activation(out=junk[:, 0:256], in_=z[:, 0:256], func=AF.Square, accum_out=st[:, 4:5]), [zb], [bjunk, stb])
            rstd_from_ssq(P, "dve", st[:, 5:6], st[:, 4:5], 256, [stb], [stb])
            P.op("act", lambda e, z=z, st=st, la=la: e.activation(out=la[:, 384:640], in_=z[:, 0:256], func=AF.Copy, scale=st[:, 5:6]), [zb, stb], [lab])
            k32, k32b = kr32.next()
            P.op("act", lambda e, z=z, k32=k32: e.activation(out=k32[:], in_=z[:, 256:320], func=AF.Copy), [zb], [k32b])
            kr, krb = kr_sb.next()
            qt, qtb = qtmp.next()
            P.op("pool", lambda e, k32=k32, cs=cs, qt=qt: e.tensor_tensor(out=qt[:, 0, 0:32], in0=k32[:, 0:32], in1=cs[:, 0:32], op=ALU.mult), [k32b, csb], [qtb])
            P.op("pool", lambda e, k32=k32, cs=cs, qt=qt: e.tensor_tensor(out=qt[:, 0, 32:64], in0=k32[:, 32:64], in1=cs[:, 32:64], op=ALU.mult), [k32b, csb], [qtb])
            P.op("pool", lambda e, kr=kr, qt=qt: e.tensor_tensor(out=kr[:, 0:32], in0=qt[:, 0, 0:32], in1=qt[:, 0, 32:64], op=ALU.subtract), [qtb], [krb])
            P.op("pool", lambda e, k32=k32, cs=cs, qt=qt: e.tensor_tensor(out=qt[:, 0, 0:32], in0=k32[:, 0:32], in1=cs[:, 32:64], op=ALU.mult), [k32b, csb, krb], [qtb])
            P.op("pool", lambda e, k32=k32, cs=cs, qt=qt: e.tensor_tensor(out=qt[:, 0, 32:64], in0=k32[:, 32:64], in1=cs[:, 0:32], op=ALU.mult), [k32b, csb], [qtb])
            P.op("pool", lambda e, kr=kr, qt=qt: e.tensor_tensor(out=kr[:, 32:64], in0=qt[:, 0, 0:32], in1=qt[:, 0, 32:64], op=ALU.add), [qtb], [krb])
            if CUT <= 3:
                continue
            g, gb = g_sb.next()
            for gi in range(4):
                z, zb = in_proj_group(hT, hTb, 1728 + gi * 512, 1728 + gi * 512 + 512)
                P.op("act", lambda e, z=z, g=g, gi=gi: e.activation(out=g[:, gi * 512:(gi + 1) * 512], in_=z[:], func=AF.Silu), [zb], [gb])
            P.dma("pool", scr.G[t * 128:(t + 1) * 128, :], g[:], [gb], [scr.bG], gb)
            if CUT <= 4:
                continue
            tqq, tqb = tq.next()
            for k in range(int(os.environ.get("NK", "5"))):
                P.op("pe", lambda e, k=k, tqq=tqq, la=la: e.transpose(out=tqq[:, k * 128:(k + 1) * 128], in_=(h if os.environ.get("SRCH") else la)[:, k * 128:(k + 1) * 128], identity=ident[:]), [lab, bid, hb], [tqb])
            lT, lTb = latT.next()
            if not os.environ.get("NOCOPY"):
                if True:
                    P.op("act", lambda e, lT=lT, tqq=tqq: e.activation(out=lT[:].rearrange("p k t -> p (k t)"), in_=tqq[:, 0:640], func=AF.Copy), [tqb], [lTb])
                else:
                    if os.environ.get("JUNKDST"):
                        P.op("dve", lambda e, lT=lT, tqq=tqq: e.tensor_copy(out=junk[:, 0:CW], in_=tqq[:, 0:CW]), [tqb], [bjunk])
                    elif os.environ.get("TSCOPY"):
                        P.op("dve", lambda e, lT=lT, tqq=tqq: e.tensor_scalar(out=lT[:, 0:CW // 128, :].rearrange("p k t -> p (k t)"), in0=tqq[:, 0:CW], scalar1=1.0, scalar2=None, op0=ALU.mult), [tqb], [lTb])
                    else:
                        P.op("dve", lambda e, lT=lT, tqq=tqq: e.tensor_copy(out=lT[:, 0:CW // 128, :].rearrange("p k t -> p (k t)"), in_=tqq[:, 0:CW]), [tqb], [lTb])
            if CUT <= 4.1:
                continue
            q, qb = q_sb.next()
            for gi in range(3):
                pq, pqb = qp.next()
                for k in range(3):
                    P.op("pe", lambda e, pq=pq, k=k, gi=gi, lT=lT: e.matmul(pq[:], lhsT=lT[:, k, :], rhs=w_q[:, k, gi * 512:(gi + 1) * 512], start=(k == 0), stop=(k == 2)), [lTb, bw_q], [pqb])
                P.op("act", lambda e, pq=pq, q=q, gi=gi: e.activation(out=q[:, gi * 512:(gi + 1) * 512], in_=pq[:], func=AF.Copy), [pqb], [qb])
            if CUT <= 4.3:
                continue
            qv = q[:].rearrange("p (h d) -> p h d", h=8)
            x1 = qv[:, :, 128:160]
            x2 = qv[:, :, 160:192]
            cosb = cs[:, 0:32].unsqueeze(1).to_broadcast([128, 8, 32])
            sinb = cs[:, 32:64].unsqueeze(1).to_broadcast([128, 8, 32])
            qt, qtb = qtmp.next()
            a_ = qt[:, 0, :].rearrange("p (h d) -> p h d", h=8)
            b_ = qt[:, 1, :].rearrange("p (h d) -> p h d", h=8)
            c_ = qt[:, 2, :].rearrange("p (h d) -> p h d", h=8)
            d_ = qt[:, 3, :].rearrange("p (h d) -> p h d", h=8)
            P.op("pool", lambda e, a_=a_, x1=x1, cosb=cosb: e.tensor_tensor(out=a_, in0=x1, in1=cosb, op=ALU.mult), [qb, csb], [qtb])
            P.op("pool", lambda e, b_=b_, x2=x2, sinb=sinb: e.tensor_tensor(out=b_, in0=x2, in1=sinb, op=ALU.mult), [qb, csb], [qtb])
            P.op("pool", lambda e, c_=c_, x1=x1, sinb=sinb: e.tensor_tensor(out=c_, in0=x1, in1=sinb, op=ALU.mult), [qb, csb], [qtb])
            P.op("pool", lambda e, d_=d_, x2=x2, cosb=cosb: e.tensor_tensor(out=d_, in0=x2, in1=cosb, op=ALU.mult), [qb, csb], [qtb])
            P.op("pool", lambda e, a_=a_, b_=b_, x1=x1: e.tensor_tensor(out=x1, in0=a_, in1=b_, op=ALU.subtract), [qtb], [qb])
            P.op("pool", lambda e, c_=c_, d_=d_, x2=x2: e.tensor_tensor(out=x2, in0=c_, in1=d_, op=ALU.add), [qtb], [qb])
            if CUT <= 4.6:
                continue
            kn, knb = kn_sb.next()
            v, vb = v_sb.next()
            for gi in range(4):
                pq, pqb = qp.next()
                for k in range(2):
                    P.op("pe", lambda e, pq=pq, k=k, gi=gi, lT=lT: e.matmul(pq[:], lhsT=lT[:, 3 + k, :], rhs=w_kv[:, k, gi * 512:(gi + 1) * 512], start=(k == 0), stop=(k == 1)), [lTb, bw_kv], [pqb])
                pv = pq[:].rearrange("p (h d) -> p h d", h=2)
                P.op("act", lambda e, pv=pv, kn=kn, gi=gi: e.activation(out=kn[:, gi * 256:(gi + 1) * 256].rearrange("p (h d) -> p h d", h=2), in_=pv[:, :, 0:128], func=AF.Copy), [pqb], [knb])
                if os.environ.get("VMODE", "act") == "dve":
                    P.op("dve", lambda e, pv=pv, v=v, gi=gi: e.tensor_copy(out=v[:, gi * 256:(gi + 1) * 256].rearrange("p (h d) -> p h d", h=2), in_=pv[:, :, 128:256]), [pqb], [vb])
                else:
                    P.op("act", lambda e, pv=pv, v=v, gi=gi: e.activation(out=v[:, gi * 256:(gi + 1) * 256].rearrange("p (h d) -> p h d", h=2), in_=pv[:, :, 128:256], func=AF.Copy), [pqb], [vb])
            if os.environ.get("VMODE", "act") != "none":
                P.dma("pool", scr.V[t * 128:(t + 1) * 128, :], v[:], [vb], [scr.bV], vb)
            if CUT <= 5:
                continue
            tqq, tqb = tq.next()
            for hh in range(8):
                P.op("pe", lambda e, hh=hh, tqq=tqq, q=q: e.transpose(out=tqq[:, hh * 128:(hh + 1) * 128], in_=q[:, hh * 192:hh * 192 + 128], identity=ident[:]), [qb, bid], [tqb])
            P.op("act", lambda e, tqq=tqq, qnT_t=qnT_t, t4=t4: e.activation(out=qnT_t[:, :, t4 * 128:(t4 + 1) * 128], in_=tqq[:].rearrange("p (h t) -> p h t", h=8), func=AF.Copy), [tqb], [qnT_b])
            tqq, tqb = tq.next()
            for hh in range(8):
                P.op("pe", lambda e, hh=hh, tqq=tqq, q=q: e.transpose(out=tqq[0:64, hh * 128:(hh + 1) * 128], in_=q[:, hh * 192 + 128:hh * 192 + 192], identity=ident[:]), [qb, bid], [tqb])
            P.op("act", lambda e, tqq=tqq, qrT_t=qrT_t, t4=t4: e.activation(out=qrT_t[:, :, t4 * 128:(t4 + 1) * 128], in_=tqq[0:64, :].rearrange("p (h t) -> p h t", h=8), func=AF.Copy), [tqb], [qrT_b])
            tqq, tqb = tq.next()
            for hh in range(8):
                P.op("pe", lambda e, hh=hh, tqq=tqq, kn=kn: e.transpose(out=tqq[:, hh * 128:(hh + 1) * 128], in_=kn[:, hh * 128:(hh + 1) * 128], identity=ident[:]), [knb, bid], [tqb])
            P.op("act", lambda e, tqq=tqq, knT_t=knT_t, t4=t4: e.activation(out=knT_t[:, :, t4 * 128:(t4 + 1) * 128], in_=tqq[:].rearrange("p (h t) -> p h t", h=8), func=AF.Copy), [tqb], [knT_b])
            tqq, tqb = tq.next()
            P.op("pe", lambda e, tqq=tqq, kr=kr: e.transpose(out=tqq[0:64, 0:128], in_=kr[:, 0:64], identity=ident[:]), [krb, bid], [tqb])
            P.op("act", lambda e, tqq=tqq, krT_t=krT_t, t4=t4: e.activation(out=krT_t[:, t4 * 128:(t4 + 1) * 128], in_=tqq[0:64, 0:128], func=AF.Copy), [tqb], [krT_b])
            if t4 == 1 or t == NT - 1:
                nt = (t4 + 1) * 128
                t0 = (t - t4) * 128
                P.dma("pool", scr.QN[:, :, t0:t0 + nt].rearrange("h d t -> d h t"), qnT_t[:, :, 0:nt], [qnT_b], [scr.bQ], qnT_b)
                P.dma("pool", scr.QR[:, :, t0:t0 + nt].rearrange("h d t -> d h t"), qrT_t[:, :, 0:nt], [qrT_b], [scr.bQ], qrT_b)
                P.dma("pool", scr.KN[:, :, t0:t0 + nt].rearrange("h d t -> d h t"), knT_t[:, :, 0:nt], [knT_b], [scr.bK], knT_b)
                P.dma("pool", scr.KR[:, t0:t0 + nt], krT_t[:, 0:nt], [krT_b], [scr.bK], krT_b)

    _phase(P, nc, body)


def phase_B(P, nc, C, L):
    NKB = L // 128
    NQB = L // 512
    scr = C.scr
    scale = 192.0 ** -0.5

    def body(sb, ps):
        krt = sb("krB", [64, L], BF16)
        bkr = P.buf("krB")
        P.dma("sp", krt[:], scr.KR[:, 0:L], [], [bkr], bkr)
        qn = Rot(P, sb, "qnB", [128, L], BF16, 2)
        qr = Rot(P, sb, "qrB", [64, L], BF16, 2)
        kn = Rot(P, sb, "knB", [128, L], BF16, 2)
        vt = Rot(P, sb, "vtB", [128, NKB, 129], BF16, 2)
        for (v, vb) in vt.slots:
            P.op("pool", lambda e, v=v: e.memset(v[:, :, 128:129], 1.0), [], [vb])
        pT = Rot(P, sb, "pTB", [128, 1024], BF16, 3)
        osb = Rot(P, sb, "osbB", [128, 4, 128], BF16, 2)
        rc = Rot(P, sb, "rcB", [128, 4], F32, 2)
        sp_ = Rot(P, ps, "spB", [128, 1024], F32, 2)
        oa = Rot(P, ps, "oaB", [128, 512], F32, 2)
        ob = Rot(P, ps, "obB", [128, 512], F32, 2)
        for h in range(8):
            qn_t, qn_b = qn.next()
            qr_t, qr_b = qr.next()
            kn_t, kn_b = kn.next()
            v_t, v_b = vt.next()
            P.dma("sp", qn_t[:], scr.QN[h, :, 0:L], [], [qn_b], qn_b)
            P.dma("sp", qr_t[:], scr.QR[h, :, 0:L], [], [qr_b], qr_b)
            P.dma("sp", kn_t[:], scr.KN[h, :, 0:L], [], [kn_b], kn_b)
            P.dma("sp", v_t[:, :, 0:128], scr.V[0:L, h * 128:(h + 1) * 128].rearrange("(kb p) d -> p kb d", p=128), [], [v_b], v_b)
            for qb in range(NQB):
                oa_t, oa_b = oa.next()
                ob_t, ob_b = ob.next()
                def emit_qk(kp, qb=qb, kn_t=kn_t, qn_t=qn_t, qr_t=qr_t, kn_b=kn_b, qn_b=qn_b, qr_b=qr_b):
                    s_t, s_b = sp_.next()
                    for hh in range(2):
                        kb = kp * 2 + hh
                        P.op("pe", lambda e, s_t=s_t, kb=kb, hh=hh: e.matmul(s_t[:, hh * 512:(hh + 1) * 512], lhsT=kn_t[:, kb * 128:(kb + 1) * 128], rhs=qn_t[:, qb * 512:(qb + 1) * 512], start=True, stop=False), [kn_b, qn_b], [s_b])
                        P.op("pe", lambda e, s_t=s_t, kb=kb, hh=hh: e.matmul(s_t[:, hh * 512:(hh + 1) * 512], lhsT=krt[:, kb * 128:(kb + 1) * 128], rhs=qr_t[:, qb * 512:(qb + 1) * 512], start=False, stop=True), [bkr, qr_b], [s_b])
                    return s_t, s_b

                NKP = NKB // 2
                pend = [emit_qk(0)]
                for kp in range(NKP):
                    if kp + 1 < NKP:
                        pend.append(emit_qk(kp + 1))
                    s_t, s_b = pend.pop(0)
                    p_t, p_b = pT.next()
                    P.op("act", lambda e, s_t=s_t, p_t=p_t: e.activation(out=p_t[:], in_=s_t[:], func=AF.Exp, scale=scale), [s_b], [p_b])
                    for hh in range(2):
                        kb = kp * 2 + hh
                        for sub in range(4):
                            acc_t, acc_b = (oa_t, oa_b) if sub < 2 else (ob_t, ob_b)
                            c0 = (sub % 2) * 129
                            P.op("pe", lambda e, acc_t=acc_t, c0=c0, p_t=p_t, sub=sub, v_t=v_t, kb=kb, hh=hh: e.matmul(acc_t[:, c0:c0 + 129], lhsT=p_t[:, hh * 512 + sub * 128:hh * 512 + (sub + 1) * 128], rhs=v_t[:, kb, :], start=(kb == 0), stop=(kb == NKB - 1), skip_group_check=True), [p_b, v_b], [acc_b])
                o_t, o_b = osb.next()
                r_t, r_b = rc.next()
                for sub in range(4):
                    acc_t, acc_b = (oa_t, oa_b) if sub < 2 else (ob_t, ob_b)
                    c0 = (sub % 2) * 129
                    P.op("act", lambda e, r_t=r_t, acc_t=acc_t, c0=c0, sub=sub: e.activation(out=r_t[:, sub:sub + 1], in_=acc_t[:, c0 + 128:c0 + 129], func=AF.Copy), [acc_b], [r_b])
                    P.op("dve", lambda e, r_t=r_t, sub=sub: e.reciprocal(out=r_t[:, sub:sub + 1], in_=r_t[:, sub:sub + 1]), [r_b], [r_b])
                    P.op("act", lambda e, r_t=r_t, acc_t=acc_t, c0=c0, sub=sub, o_t=o_t: e.activation(out=o_t[:, sub, :], in_=acc_t[:, c0:c0 + 128], func=AF.Copy, scale=r_t[:, sub:sub + 1]), [acc_b, r_b], [o_b])
                P.dma("pool", scr.O[qb * 512:(qb + 1) * 512, h * 128:(h + 1) * 128].rearrange("(s p) d -> p s d", p=128), o_t[:], [o_b], [scr.bO], o_b)

    _phase(P, nc, body)


def phase_C(P, nc, C, L, layer, h_in, p_ap, w_out_ap, h_out, final_out, use_s5, rs_src=None):
    NT = L // 128
    scr = C.scr

    def body(sb, ps):
        ident = sb("identC", [128, 128], BF16)
        identf = sb("identCf", [128, 128], F32)
        bid = P.buf("identC")
        P.dma("sp", identf[:], C.ident[:, :], [], [bid], bid)
        P.op("dve", lambda e: e.tensor_copy(out=ident[:], in_=identf[:]), [bid], [bid])
        stage = Rot(P, sb, "wstC", [128, 1024], F32, 2)
        w_out = sb("w_outC", [128, 16, 1024], BF16)
        w_pg = sb("w_pgC", [128, 8, 1024], BF16)
        w_pl = sb("w_plC", [128, 2, 1024], BF16)
        bw_out, bw_pg, bw_pl = P.buf("w_outC"), P.buf("w_pgC"), P.buf("w_plC")
        load_weight_bf16(P, sb, w_out, bw_out, w_out_ap, 16, 1024, stage)
        load_weight_bf16(P, sb, w_pg, bw_pg, C.ple_gate_w[layer], 8, 1024, stage)
        load_weight_bf16(P, sb, w_pl, bw_pl, C.ple_w[layer], 2, 1024, stage)
        if final_out is not None:
            fg = sb("fgC", [128, 1024], F32)
            bfg = P.buf("fgC")
            P.dma("sp", fg[:], C.final_g.partition_broadcast(128), [], [bfg], bfg, slow=True)
        ht = Rot(P, sb, "htC", [128, 1024], F32, 2)
        if layer == 0:
            gt = Rot(P, sb, "gtC", [128, 2048], BF16, 2)
            yt = Rot(P, sb, "ytC", [128, 2048], BF16, 2)
        else:
            cvt = Rot(P, sb, "cvtC", [128, 2048], BF16, 2)
            vxt = Rot(P, sb, "vxtC", [128, 2048], BF16, 2)
            xgt = Rot(P, sb, "xgtC", [128, 2048], BF16, 2)
            rsr = sb("rsrC", [128, 2048], F32)
            bir = sb("birC", [128, 2048], F32)
            brr = P.buf("rsrC")
            P.dma("sp", rsr[:], rs_src[0].partition_broadcast(128), [], [brr], brr, slow=True)
            P.dma("sp", bir[:], C.hy_bias[0].partition_broadcast(128), [], [brr], brr, slow=True)
            tm1 = sb("tm1C", [128, 2048], F32)
            tm2 = sb("tm2C", [128, 2048], F32)
            btm1, btm2 = P.buf("tm1C"), P.buf("tm2C")
        pt = Rot(P, sb, "ptC", [128, 256], F32, 2)
        pb16 = Rot(P, sb, "pb16C", [128, 256], BF16, 2)
        mt = Rot(P, sb, "mtC", [128, 2048], BF16, 2)
        mT = Rot(P, sb, "mTC", [128, 16, 128], BF16, 2)
        h2 = Rot(P, sb, "h2C", [128, 1024], F32, 2)
        h2b = Rot(P, sb, "h2bC", [128, 1024], BF16, 2)
        h2T = Rot(P, sb, "h2TC", [128, 8, 128], BF16, 2)
        pT = Rot(P, sb, "pTC", [128, 2, 128], BF16, 2)
        sg = Rot(P, sb, "sgC", [128, 1024], F32, 2)
        h3 = Rot(P, sb, "h3C", [128, 1024], F32, 2)
        stat = Rot(P, sb, "statC", [128, 4], F32, 2)
        junk = sb("junkC", [128, 1024], BF16)
        bjunk = P.buf("junkC")
        tp = Rot(P, ps, "tpC", [128, 1024], BF16, 2)
        zp = Rot(P, ps, "zpC", [128, 512], F32, 4)
        for t in range(NT):
            rows = slice(t * 128, (t + 1) * 128)
            h_t, h_b = ht.next()
            if layer == 0:
                g_t, g_b = gt.next()
                y_t, y_b = yt.next()
            p_t, p_b = pt.next()
            P.dma("sp", h_t[:], h_in[rows, :], [], [h_b], h_b)
            P.dma("sp", p_t[:], p_ap[rows, :], [], [p_b], p_b)
            if layer == 0:
                P.dma("sp", g_t[:], scr.G[rows, :], [], [g_b], g_b)
                if use_s5:
                    P.dma("sp", y_t[:, 0:1024], scr.YA[rows, :], [], [y_b], y_b)
                else:
                    P.op("pool", lambda e, y_t=y_t: e.memset(y_t[:, 0:1024], 0.0), [], [y_b])
                P.dma("sp", y_t[:, 1024:2048], scr.O[rows, :], [], [y_b], y_b)
            m_t, m_b = mt.next()
            mT_t, mT_b = mT.next()
            if layer == 0:
                P.op("dve", lambda e, m_t=m_t, y_t=y_t, g_t=g_t: e.tensor_tensor(out=m_t[:], in0=y_t[:], in1=g_t[:], op=ALU.mult), [y_b, g_b], [m_b])
            else:
                cv_t, cv_b = cvt.next()
                vx_t, vx_b = vxt.next()
                xg_t, xg_b = xgt.next()
                P.dma("sp", cv_t[:], scr.CV[rows, :], [], [cv_b], cv_b)
                P.dma("sp", vx_t[:], scr.VX[rows, :], [], [vx_b], vx_b)
                P.dma("sp", xg_t[:], scr.XG[rows, :], [], [xg_b], xg_b)
                P.op("pool", lambda e, cv_t=cv_t: e.tensor_tensor(out=tm1[:], in0=cv_t[:], in1=rsr[:], op=ALU.mult), [cv_b, brr], [btm1])
                P.op("dve", lambda e, vx_t=vx_t: e.tensor_tensor(out=tm2[:], in0=vx_t[:], in1=bir[:], op=ALU.mult), [vx_b, brr], [btm2])
                P.op("pool", lambda e: e.tensor_tensor(out=tm1[:], in0=tm1[:], in1=tm2[:], op=ALU.add), [btm1, btm2], [btm1])
                P.op("dve", lambda e, m_t=m_t, xg_t=xg_t: e.tensor_tensor(out=m_t[:], in0=tm1[:], in1=xg_t[:], op=ALU.mult), [btm1, xg_b], [m_b])
            for half in range(2):
                tpp, tpb = tp.next()
                for k in range(8):
                    kk = half * 8 + k
                    P.op("pe", lambda e, tpp=tpp, k=k, kk=kk, m_t=m_t: e.transpose(out=tpp[:, k * 128:(k + 1) * 128], in_=m_t[:, kk * 128:(kk + 1) * 128], identity=ident[:]), [m_b, bid], [tpb])
                P.op("act", lambda e, tpp=tpp, mT_t=mT_t, half=half: e.activation(out=mT_t[:, half * 8:(half + 1) * 8, :].rearrange("p k t -> p (k t)"), in_=tpp[:], func=AF.Copy), [tpb], [mT_b])
            h2_t, h2_b = h2.next()
            for gi in range(2):
                z, zb = zp.next()
                for k in range(16):
                    P.op("pe", lambda e, z=z, k=k, gi=gi, mT_t=mT_t: e.matmul(z[:], lhsT=mT_t[:, k, :], rhs=w_out[:, k, gi * 512:(gi + 1) * 512], start=(k == 0), stop=(k == 15)), [mT_b, bw_out], [zb])
                P.op("act", lambda e, z=z, gi=gi, h2_t=h2_t: e.activation(out=h2_t[:, gi * 512:(gi + 1) * 512], in_=z[:], func=AF.Copy), [zb], [h2_b])
                P.op("pool", lambda e, gi=gi, h2_t=h2_t, h_t=h_t: e.tensor_tensor(out=h2_t[:, gi * 512:(gi + 1) * 512], in0=h2_t[:, gi * 512:(gi + 1) * 512], in1=h_t[:, gi * 512:(gi + 1) * 512], op=ALU.add), [h2_b, h_b], [h2_b])
            hb_t, hb_b = h2b.next()
            P.op("act", lambda e, hb_t=hb_t, h2_t=h2_t: e.activation(out=hb_t[:], in_=h2_t[:], func=AF.Copy), [h2_b], [hb_b])
            tpp, tpb = tp.next()
            for k in range(8):
                P.op("pe", lambda e, tpp=tpp, k=k, hb_t=hb_t: e.transpose(out=tpp[:, k * 128:(k + 1) * 128], in_=hb_t[:, k * 128:(k + 1) * 128], identity=ident[:]), [hb_b, bid], [tpb])
            hT_t, hT_b = h2T.next()
            P.op("act", lambda e, tpp=tpp, hT_t=hT_t: e.activation(out=hT_t[:].rearrange("p k t -> p (k t)"), in_=tpp[:], func=AF.Copy), [tpb], [hT_b])
            pb_t, pb_b = pb16.next()
            P.op("act", lambda e, pb_t=pb_t, p_t=p_t: e.activation(out=pb_t[:], in_=p_t[:], func=AF.Copy), [p_b], [pb_b])
            tpp, tpb = tp.next()
            for k in range(2):
                P.op("pe", lambda e, tpp=tpp, k=k, pb_t=pb_t: e.transpose(out=tpp[:, k * 128:(k + 1) * 128], in_=pb_t[:, k * 128:(k + 1) * 128], identity=ident[:]), [pb_b, bid], [tpb])
            pT_t, pT_b = pT.next()
            P.op("act", lambda e, tpp=tpp, pT_t=pT_t: e.activation(out=pT_t[:].rearrange("p k t -> p (k t)"), in_=tpp[:, 0:256], func=AF.Copy), [tpb], [pT_b])
            sg_t, sg_b = sg.next()
            h3_t, h3_b = h3.next()
            for gi in range(2):
                z, zb = zp.next()
                for k in range(8):
                    P.op("pe", lambda e, z=z, k=k, gi=gi, hT_t=hT_t: e.matmul(z[:], lhsT=hT_t[:, k, :], rhs=w_pg[:, k, gi * 512:(gi + 1) * 512], start=(k == 0), stop=(k == 7)), [hT_b, bw_pg], [zb])
                P.op("act", lambda e, z=z, gi=gi, sg_t=sg_t: e.activation(out=sg_t[:, gi * 512:(gi + 1) * 512], in_=z[:], func=AF.Sigmoid), [zb], [sg_b])
                z2, z2b = zp.next()
                for k in range(2):
                    P.op("pe", lambda e, z2=z2, k=k, gi=gi, pT_t=pT_t: e.matmul(z2[:], lhsT=pT_t[:, k, :], rhs=w_pl[:, k, gi * 512:(gi + 1) * 512], start=(k == 0), stop=(k == 1)), [pT_b, bw_pl], [z2b])
                P.op("act", lambda e, z2=z2, gi=gi, h3_t=h3_t: e.activation(out=h3_t[:, gi * 512:(gi + 1) * 512], in_=z2[:], func=AF.Copy), [z2b], [h3_b])
                P.op("dve", lambda e, gi=gi, sg_t=sg_t, h3_t=h3_t: e.tensor_tensor(out=sg_t[:, gi * 512:(gi + 1) * 512], in0=sg_t[:, gi * 512:(gi + 1) * 512], in1=h3_t[:, gi * 512:(gi + 1) * 512], op=ALU.mult), [h3_b, sg_b], [sg_b])
            P.op("pool", lambda e, h3_t=h3_t, sg_t=sg_t, h2_t=h2_t: e.tensor_tensor(out=h3_t[:], in0=sg_t[:], in1=h2_t[:], op=ALU.add), [sg_b, h2_b], [h3_b])
            if final_out is None:
                P.dma("pool", h_out[rows, :], h3_t[:], [h3_b], [scr.bH], h3_b)
            else:
                st, stb = stat.next()
                P.op("act", lambda e, h3_t=h3_t, st=st: e.activation(out=junk[:], in_=h3_t[:], func=AF.Square, accum_out=st[:, 0:1]), [h3_b], [bjunk, stb])
                rstd_from_ssq(P, "dve", st[:, 1:2], st[:, 0:1], D, [stb], [stb])
                P.op("dve", lambda e, h3_t=h3_t, st=st, sg_t=sg_t: e.scalar_tensor_tensor(out=sg_t[:], in0=h3_t[:], scalar=st[:, 1:2], in1=fg[:], op0=ALU.mult, op1=ALU.mult), [h3_b, stb, bfg], [sg_b])
                P.dma("pool", final_out[rows, :], sg_t[:], [sg_b], [scr.bH], sg_b)

    _phase(P, nc, body)


def phase_D1(P, nc, C, L):
    NT = L // 128
    scr = C.scr

    def body(sb, ps):
        ident = sb("identD", [128, 128], BF16)
        identf = sb("identDf", [128, 128], F32)
        bid = P.buf("identD")
        P.dma("sp", identf[:], C.ident[:, :], [], [bid], bid)
        P.op("dve", lambda e: e.tensor_copy(out=ident[:], in_=identf[:]), [bid], [bid])
        zc = sb("zcD", [128, 8, 2], BF16)
        bzc = P.buf("zcD")
        P.op("pool", lambda e: e.memset(zc[:], 0.0), [], [bzc])
        P.dma("pool", scr.HT[:, :, 0:1].rearrange("k f t -> f k t"), zc[:, :, 0:1], [bzc], [scr.bH], bzc, slow=True)
        P.dma("pool", scr.HT[:, :, L + 1:L + 2].rearrange("k f t -> f k t"), zc[:, :, 1:2], [bzc], [scr.bH], bzc, slow=True)
        xt = Rot(P, sb, "xtD", [128, 1024], F32, 2)
        junk = sb("junkD", [128, 1024], BF16)
        bjunk = P.buf("junkD")
        stat = Rot(P, sb, "statD", [128, 4], F32, 2)
        hn = Rot(P, sb, "hnD", [128, 1024], BF16, 2)
        hT4 = Rot(P, sb, "hT4D", [128, 8, 512], BF16, 2)
        tp = Rot(P, ps, "tpD", [128, 1024], BF16, 2)
        cur = None
        for t in range(NT):
            t4 = t % 4
            if t4 == 0:
                cur = hT4.next()
            h4, h4b = cur
            x, xb = xt.next()
            P.dma("sp", x[:], scr.H1[t * 128:(t + 1) * 128, :], [], [xb], xb)
            st, stb = stat.next()
            P.op("act", lambda e, x=x, st=st: e.activation(out=junk[:], in_=x[:], func=AF.Square, accum_out=st[:, 0:1]), [xb], [bjunk, stb])
            rstd_from_ssq(P, "dve", st[:, 1:2], st[:, 0:1], D, [stb], [stb])
            h, hb = hn.next()
            P.op("act", lambda e, x=x, st=st, h=h: e.activation(out=h[:], in_=x[:], func=AF.Copy, scale=st[:, 1:2]), [xb, stb], [hb])
            tpp, tpb = tp.next()
            for k in range(8):
                P.op("pe", lambda e, k=k, tpp=tpp, h=h: e.transpose(out=tpp[:, k * 128:(k + 1) * 128], in_=h[:, k * 128:(k + 1) * 128], identity=ident[:]), [hb, bid], [tpb])
            P.op("act", lambda e, tpp=tpp, h4=h4, t4=t4: e.activation(out=h4[:, :, t4 * 128:(t4 + 1) * 128], in_=tpp[:].rearrange("p (k t) -> p k t", k=8), func=AF.Copy), [tpb], [h4b])
            if t4 == 3 or t == NT - 1:
                nt = (t4 + 1) * 128
                t0 = (t - t4) * 128
                P.dma("pool", scr.HT[:, :, 1 + t0:1 + t0 + nt].rearrange("k f t -> f k t"), h4[:, :, 0:nt], [h4b], [scr.bH], h4b)

    _phase(P, nc, body)


def phase_D2(P, nc, C, L):
    scr = C.scr
    TB = 256
    NB = L // TB

    def body(sb, ps):
        ng = sb("ngD", [128, 8], F32)
        bsm = P.buf("smallD")
        P.dma("sp", ng[:], C.norm_g[1].rearrange("(k p) -> p k", p=128), [], [bsm], bsm, slow=True)
        cw = sb("cwD", [128, 3, 48], F32)
        cb = sb("cbD", [128, 48], F32)
        hb_ = sb("hbD", [128, 16], F32)
        P.dma("sp", cw[:], C.hy_conv_w[0].rearrange("j (i p) -> p j i", p=128), [], [bsm], bsm, slow=True)
        P.dma("sp", cb[:], C.hy_conv_b[0].rearrange("(i p) -> p i", p=128), [], [bsm], bsm, slow=True)
        P.dma("sp", hb_[:], C.hy_bias[0].rearrange("(i p) -> p i", p=128), [], [bsm], bsm, slow=True)
        stage = Rot(P, sb, "wstD", [128, 1024], F32, 2)
        w_in = sb("w_inD", [128, 8, 8192], BF16)
        bw_in = P.buf("w_inD")
        load_weight_bf16(P, sb, w_in, bw_in, C.hy_w_in[0], 8, 8192, stage, rowscale=(ng, bsm))
        hT = Rot(P, sb, "hTD", [128, 8, 258], BF16, 2)
        zs = Rot(P, sb, "zsD", [128, 258], F32, 4)
        uc = Rot(P, sb, "ucD", [128, 3, 256], F32, 2)
        sg = Rot(P, sb, "sgD", [128, 256], F32, 2)
        ident = sb("identD2", [128, 128], BF16)
        identf = sb("identD2f", [128, 128], F32)
        bid = P.buf("identD2")
        P.dma("sp", identf[:], C.ident[:, :], [], [bid], bid)
        P.op("dve", lambda e: e.tensor_copy(out=ident[:], in_=identf[:]), [bid], [bid])
        mo = Rot(P, sb, "moD", [128, 2, 256], BF16, 4)
        stg2 = Rot(P, sb, "stg2D", [128, 2, 2, 2048], BF16, 1)
        zp = Rot(P, ps, "zpD", [128, 512], F32, 4)
        tp = Rot(P, ps, "tpD2", [128, 1024], BF16, 2)
        def do_block(b):
            s0 = b * TB
            n_out = min(TB, L - s0)
            n_in = n_out + 2
            h_t, h_b = hT.next()
            st2, st2b = stg2.next()
            pending = []
            P.dma("sp", h_t[:, :, 0:n_in], scr.HT[:, :, s0:s0 + n_in].rearrange("k f t -> f k t"), [], [h_b], h_b)
            for i in range(16):
                u_t, u_b = uc.next()
                for part in range(4):
                    ch = part * 16 + i
                    z, zb = zp.next()
                    for k in range(8):
                        P.op("pe", lambda e, z=z, k=k, ch=ch, h_t=h_t: e.matmul(z[:, 0:n_in], lhsT=w_in[:, k, ch * 128:(ch + 1) * 128], rhs=h_t[:, k, 0:n_in], start=(k == 0), stop=(k == 7)), [h_b, bw_in], [zb])
                    if part < 3:
                        zs_t, zs_b = zs.next()
                        P.op("act", lambda e, z=z, zs_t=zs_t: e.activation(out=zs_t[:, 0:n_in], in_=z[:, 0:n_in], func=AF.Copy), [zb], [zs_b])
                        eng = "pool" if part != 1 else "dve"
                        P.op(eng, lambda e, zs_t=zs_t, u_t=u_t, part=part, ch=ch: e.tensor_scalar(out=u_t[:, part, 0:n_out], in0=zs_t[:, 0:n_out], scalar1=cw[:, 0, ch:ch + 1], scalar2=cb[:, ch:ch + 1], op0=ALU.mult, op1=ALU.add), [zs_b, bsm], [u_b])
                        P.op("dve", lambda e, zs_t=zs_t, u_t=u_t, part=part, ch=ch: e.scalar_tensor_tensor(out=u_t[:, part, 0:n_out], in0=zs_t[:, 1:1 + n_out], scalar=cw[:, 1, ch:ch + 1], in1=u_t[:, part, 0:n_out], op0=ALU.mult, op1=ALU.add), [zs_b, bsm, u_b], [u_b])
                        P.op("dve", lambda e, zs_t=zs_t, u_t=u_t, part=part, ch=ch: e.scalar_tensor_tensor(out=u_t[:, part, 0:n_out], in0=zs_t[:, 2:2 + n_out], scalar=cw[:, 2, ch:ch + 1], in1=u_t[:, part, 0:n_out], op0=ALU.mult, op1=ALU.add), [zs_b, bsm, u_b], [u_b])
                    else:
                        sg_t, sg_b = sg.next()
                        P.op("act", lambda e, z=z, sg_t=sg_t: e.activation(out=sg_t[:, 0:n_out], in_=z[:, 1:1 + n_out], func=AF.Silu), [zb], [sg_b])
                m_t, m_b = mo.next()
                P.op("pool", lambda e, u_t=u_t, m_t=m_t: e.tensor_tensor(out=m_t[:, 0, :], in0=u_t[:, 2, 0:n_out], in1=u_t[:, 1, 0:n_out], op=ALU.mult), [u_b], [m_b])
                P.op("pool", lambda e, u_t=u_t, sg_t=sg_t, m_t=m_t: e.tensor_tensor(out=m_t[:, 1, :], in0=u_t[:, 0, 0:n_out], in1=sg_t[:, 0:n_out], op=ALU.mult), [u_b, sg_b], [m_b])
                def finish(i=i, m_t=m_t, m_b=m_b):
                    tpp, tpb = tp.next()
                    for q in range(2):
                        for tl_ in range(2):
                            P.op("pe", lambda e, tpp=tpp, q=q, tl_=tl_: e.transpose(out=tpp[:, (q * 2 + tl_) * 128:(q * 2 + tl_ + 1) * 128], in_=m_t[:, q, tl_ * 128:(tl_ + 1) * 128], identity=ident[:]), [m_b, bid], [tpb])
                    P.op("act", lambda e, tpp=tpp: e.activation(out=st2[:, :, :, i * 128:(i + 1) * 128], in_=tpp[:, 0:512].rearrange("p (q t c) -> p q t c", q=2, t=2), func=AF.Copy), [tpb], [st2b])

                pending.append(finish)
                if len(pending) > 1:
                    pending.pop(0)()
            while pending:
                pending.pop(0)()
            for tl_ in range(2):
                r0 = s0 + tl_ * 128
                P.dma("sp", scr.VX[r0:r0 + 128, :], st2[:, 0, tl_, :], [st2b], [scr.bV], st2b)
                P.dma("sp", scr.XG[r0:r0 + 128, :], st2[:, 1, tl_, :], [st2b], [scr.bG], st2b)

        for b in range(NB):
            do_block(b)

    _phase(P, nc, body)


def fft_tables(L):
    N = 2 * L
    N1 = N // 128
    KL = L // 128
    k = np.arange(N1)[:, None].astype(np.float64)
    f1 = np.arange(N1)[None, :].astype(np.float64)
    ang1 = 2 * np.pi * k * f1 / N1
    p = np.arange(128).astype(np.float64)
    f2 = np.arange(128).astype(np.float64)
    f1v = np.arange(N1).astype(np.float64)
    angE = 2 * np.pi * p[:, None, None] * (f1v[None, :, None] + N1 * f2[None, None, :]) / N
    angE2 = 2 * np.pi * p[None, None, :] * (f1v[None, :, None] + N1 * f2[:, None, None]) / N
    t = {}
    t["s1c"] = np.cos(ang1)
    t["s1s"] = -np.sin(ang1)
    t["ec"] = np.cos(angE).reshape(128, N1 * 128)
    t["es"] = np.sin(angE).reshape(128, N1 * 128)
    t["e2c"] = np.cos(angE2).reshape(128, N1 * 128)
    t["e2s"] = np.sin(angE2).reshape(128, N1 * 128)
    t["i1c"] = np.cos(ang1).T[:, :KL] / N
    t["i1s"] = -np.sin(ang1).T[:, :KL] / N
    return {k_: np.ascontiguousarray(v.astype(np.float32)) for k_, v in t.items()}, N1, KL


def fft_tables_shapes(L):
    N1 = 2 * L // 128
    KL = L // 128
    return {"s1c": (N1, N1), "s1s": (N1, N1), "ec": (128, N1 * 128), "es": (128, N1 * 128),
            "e2c": (128, N1 * 128), "e2s": (128, N1 * 128), "i1c": (N1, KL), "i1s": (N1, KL)}, N1, KL


def load_table_bf16(P, sb, name, src, rows, cols, stage):
    t = sb(name, [rows, cols], BF16)
    b = P.buf(name)
    for c0 in range(0, cols, 1024):
        c1 = min(cols, c0 + 1024)
        st, stb = stage.next()
        P.dma("sp", st[0:rows, 0:c1 - c0], src[0:rows, c0:c1], [], [stb], stb)
        P.op("pool", lambda e, st=st, c0=c0, c1=c1: e.tensor_copy(out=t[:, c0:c1], in_=st[0:rows, 0:c1 - c0]), [stb], [b])
    return t, b


def phase_F(P, nc, C, L, src, KS, tb, khat_dst=None, khat_src=None, yhat_dst=None):
    N1 = 2 * L // 128
    scr = C.scr
    FC = min(4, N1)

    def body(sb, ps):
        stage = Rot(P, sb, "wstF", [128, 1024], F32, 2)
        s1c, bs1c = load_table_bf16(P, sb, "s1cF", tb["s1c"], KS, N1, stage)
        s1s, bs1s = load_table_bf16(P, sb, "s1sF", tb["s1s"], KS, N1, stage)
        ec, bec = load_table_bf16(P, sb, "ecF", tb["ec"], 128, N1 * 128, stage)
        es, bes = load_table_bf16(P, sb, "esF", tb["es"], 128, N1 * 128, stage)
        X = Rot(P, sb, "XF", [KS, 128 * 128], BF16, 2)
        Ast = Rot(P, sb, "AstF", [N1, 2, 512], BF16, 3)
        Bt = Rot(P, sb, "BtF", [128, 3, FC, 128], BF16, 3)
        Xs = Rot(P, sb, "XsF", [128, 2, FC * 128], F32, 2)
        Kh = Rot(P, sb, "KhF", [128, 2, FC * 128], F32, 3)
        Tm = Rot(P, sb, "TmF", [128, 4, FC * 128], F32, 2)
        Yo = Rot(P, sb, "YoF", [128, 2, FC * 128], BF16, 2)
        pa = Rot(P, ps, "paF", [128, 512], F32, 4)
        px = Rot(P, ps, "pxF", [128, 512], F32, 4)
        def load_x(s_):
            x_t, x_b = X.next()
            P.dma("sp", x_t[:].rearrange("k (p c) -> k p c", c=128), src[0:KS * 128, s_ * 128:(s_ + 1) * 128].rearrange("(k p) c -> k p c", p=128), [], [x_b], x_b)
            return x_t, x_b

        NFC = N1 // FC
        W = FC * 128

        def load_chunk(s_, fc):
            b_t, b_b = Bt.next()
            for r_ in range(2):
                P.dma("sp", b_t[:, r_, :, :], scr.AT[r_, fc * FC:(fc + 1) * FC, :, :].rearrange("f p c -> p f c"), [scr.bA], [b_b], b_b)
            kh = None
            if khat_dst is None:
                kh_t, kh_b = Kh.next()
                P.dma("sp", kh_t[:].rearrange("f r (g c) -> f r g c", c=128), khat_src[s_, :, :, fc * FC:(fc + 1) * FC, :].rearrange("r f g c -> f r g c"), [], [kh_b], kh_b)
                kh = (kh_t, kh_b)
            return (b_t, b_b, kh)

        nxt_x = load_x(0)
        for s in range(16):
            x_t, x_b = nxt_x
            if s + 1 < 16:
                nxt_x = load_x(s + 1)
            for cb in range(32):
                a_t, a_b = Ast.next()
                for ri, (tab, tabb) in enumerate(((s1c, bs1c), (s1s, bs1s))):
                    z, zb = pa.next()
                    P.op("pe", lambda e, z=z, tab=tab, x_t=x_t, cb=cb: e.matmul(z[0:N1, :], lhsT=tab[:, :], rhs=x_t[:, cb * 512:(cb + 1) * 512], start=True, stop=True), [x_b, tabb], [zb])
                    P.op("act", lambda e, z=z, a_t=a_t, ri=ri: e.activation(out=a_t[:, ri, :], in_=z[0:N1, :], func=AF.Copy), [zb], [a_b])
                P.dma("sp", scr.AT[:, 0:N1, cb * 4:(cb + 1) * 4, :].rearrange("r f p c -> f r p c"), a_t[:].rearrange("f r (p c) -> f r p c", c=128), [a_b], [scr.bA], a_b)
            nxt_c = load_chunk(s, 0)
            for fc in range(NFC):
                b_t, b_b, kh = nxt_c
                if fc + 1 < NFC:
                    nxt_c = load_chunk(s, fc + 1)
                P.op("dve", lambda e, b_t=b_t: e.tensor_scalar(out=b_t[:, 2, :, :], in0=b_t[:, 0, :, :], scalar1=-1.0, scalar2=None, op0=ALU.mult), [b_b], [b_b])
                zr, zrb = px.next()
                zi, zib = px.next()
                for j in range(FC):
                    f1 = fc * FC + j
                    P.op("pe", lambda e, zr=zr, j=j, f1=f1, b_t=b_t: e.matmul(zr[:, j * 128:(j + 1) * 128], lhsT=ec[:, f1 * 128:(f1 + 1) * 128], rhs=b_t[:, 0, j, :], start=True, stop=False, skip_group_check=True), [b_b, bec], [zrb])
                    P.op("pe", lambda e, zr=zr, j=j, f1=f1, b_t=b_t: e.matmul(zr[:, j * 128:(j + 1) * 128], lhsT=es[:, f1 * 128:(f1 + 1) * 128], rhs=b_t[:, 1, j, :], start=False, stop=True, skip_group_check=True), [b_b, bes], [zrb])
                    P.op("pe", lambda e, zi=zi, j=j, f1=f1, b_t=b_t: e.matmul(zi[:, j * 128:(j + 1) * 128], lhsT=ec[:, f1 * 128:(f1 + 1) * 128], rhs=b_t[:, 1, j, :], start=True, stop=False, skip_group_check=True), [b_b, bec], [zib])
                    P.op("pe", lambda e, zi=zi, j=j, f1=f1, b_t=b_t: e.matmul(zi[:, j * 128:(j + 1) * 128], lhsT=es[:, f1 * 128:(f1 + 1) * 128], rhs=b_t[:, 2, j, :], start=False, stop=True, skip_group_check=True), [b_b, bes], [zib])
                xs_t, xs_b = Xs.next()
                P.op("act", lambda e, zr=zr, xs_t=xs_t: e.activation(out=xs_t[:, 0, :], in_=zr[:, 0:W], func=AF.Copy), [zrb], [xs_b])
                P.op("act", lambda e, zi=zi, xs_t=xs_t: e.activation(out=xs_t[:, 1, :], in_=zi[:, 0:W], func=AF.Copy), [zib], [xs_b])
                if khat_dst is not None:
                    P.dma("sp", khat_dst[s, :, :, fc * FC:(fc + 1) * FC, :].rearrange("r f g c -> f r g c"), xs_t[:].rearrange("f r (g c) -> f r g c", c=128), [xs_b], [scr.bK], xs_b)
                else:
                    kh_t, kh_b = kh
                    tm, tmb = Tm.next()
                    yo, yob = Yo.next()
                    P.op("dve", lambda e, tm=tm, xs_t=xs_t, kh_t=kh_t: e.tensor_tensor(out=tm[:, 0, :], in0=xs_t[:, 0, :], in1=kh_t[:, 0, :], op=ALU.mult), [xs_b, kh_b], [tmb])
                    P.op("dve", lambda e, tm=tm, xs_t=xs_t, kh_t=kh_t: e.tensor_tensor(out=tm[:, 1, :], in0=xs_t[:, 1, :], in1=kh_t[:, 1, :], op=ALU.mult), [xs_b, kh_b], [tmb])
                    P.op("pool", lambda e, tm=tm, xs_t=xs_t, kh_t=kh_t: e.tensor_tensor(out=tm[:, 2, :], in0=xs_t[:, 0, :], in1=kh_t[:, 1, :], op=ALU.mult), [xs_b, kh_b], [tmb])
                    P.op("dve", lambda e, tm=tm, xs_t=xs_t, kh_t=kh_t: e.tensor_tensor(out=tm[:, 3, :], in0=xs_t[:, 1, :], in1=kh_t[:, 0, :], op=ALU.mult), [xs_b, kh_b], [tmb])
                    P.op("dve", lambda e, tm=tm, yo=yo: e.tensor_tensor(out=yo[:, 0, :], in0=tm[:, 0, :], in1=tm[:, 1, :], op=ALU.subtract), [tmb], [yob])
                    P.op("pool", lambda e, tm=tm, yo=yo: e.tensor_tensor(out=yo[:, 1, :], in0=tm[:, 2, :], in1=tm[:, 3, :], op=ALU.add), [tmb], [yob])
                    P.dma("sp", yhat_dst[s, :, :, fc * FC:(fc + 1) * FC, :].rearrange("r f g c -> f r g c"), yo[:].rearrange("f r (g c) -> f r g c", c=128), [yob], [scr.bQ], yob)

    _phase(P, nc, body)


def phase_I(P, nc, C, L, tb, yhat_src):
    N1 = 2 * L // 128
    KL = L // 128
    scr = C.scr
    FC = min(4, N1)
    PC = 4

    def body(sb, ps):
        stage = Rot(P, sb, "wstI", [128, 1024], F32, 2)
        i1c, bi1c = load_table_bf16(P, sb, "i1cI", tb["i1c"], N1, KL, stage)
        i1s, bi1s = load_table_bf16(P, sb, "i1sI", tb["i1s"], N1, KL, stage)
        e2c, be2c = load_table_bf16(P, sb, "e2cI", tb["e2c"], 128, N1 * 128, stage)
        e2s, be2s = load_table_bf16(P, sb, "e2sI", tb["e2s"], 128, N1 * 128, stage)
        Yt = Rot(P, sb, "YtI", [128, 3, FC, 128], BF16, 3)
        Zst = Rot(P, sb, "ZstI", [128, 2, FC * 128], BF16, 3)
        Zt = Rot(P, sb, "ZtI", [N1, 2, PC * 128], BF16, 4)
        Ot = Rot(P, sb, "OtI", [KL, 16 * 512], BF16, 2)
        pz = Rot(P, ps, "pzI", [128, 512], F32, 4)
        po = Rot(P, ps, "poI", [128, 512], F32, 2)
        W = FC * 128
        NFC = N1 // FC
        NPC = 128 // PC

        def load_y(s_, fc):
            y_t, y_b = Yt.next()
            P.dma("sp", y_t[:, 0:2, :, :], yhat_src[s_, :, :, fc * FC:(fc + 1) * FC, :].rearrange("r f g c -> f r g c"), [], [y_b], y_b)
            return y_t, y_b

        def load_z(pc):
            zz, zzb = Zt.next()
            for r_ in range(2):
                P.dma("sp", zz[:, r_, :].rearrange("f (p c) -> f p c", c=128), scr.ZT[r_, pc * PC:(pc + 1) * PC, 0:N1, :].rearrange("p f c -> f p c"), [scr.bA], [zzb], zzb)
            return zz, zzb

        for s in range(16):
            nxt_y = load_y(s, 0)
            for fc in range(NFC):
                y_t, y_b = nxt_y
                if fc + 1 < NFC:
                    nxt_y = load_y(s, fc + 1)
                P.op("dve", lambda e, y_t=y_t: e.tensor_scalar(out=y_t[:, 2, :, :], in0=y_t[:, 1, :, :], scalar1=-1.0, scalar2=None, op0=ALU.mult), [y_b], [y_b])
                zr, zrb = pz.next()
                zi, zib = pz.next()
                for j in range(FC):
                    f1 = fc * FC + j
                    P.op("pe", lambda e, zr=zr, j=j, f1=f1, y_t=y_t: e.matmul(zr[:, j * 128:(j + 1) * 128], lhsT=e2c[:, f1 * 128:(f1 + 1) * 128], rhs=y_t[:, 0, j, :], start=True, stop=False, skip_group_check=True), [y_b, be2c], [zrb])
                    P.op("pe", lambda e, zr=zr, j=j, f1=f1, y_t=y_t: e.matmul(zr[:, j * 128:(j + 1) * 128], lhsT=e2s[:, f1 * 128:(f1 + 1) * 128], rhs=y_t[:, 2, j, :], start=False, stop=True, skip_group_check=True), [y_b, be2s], [zrb])
                    P.op("pe", lambda e, zi=zi, j=j, f1=f1, y_t=y_t: e.matmul(zi[:, j * 128:(j + 1) * 128], lhsT=e2s[:, f1 * 128:(f1 + 1) * 128], rhs=y_t[:, 0, j, :], start=True, stop=False, skip_group_check=True), [y_b, be2s], [zib])
                    P.op("pe", lambda e, zi=zi, j=j, f1=f1, y_t=y_t: e.matmul(zi[:, j * 128:(j + 1) * 128], lhsT=e2c[:, f1 * 128:(f1 + 1) * 128], rhs=y_t[:, 1, j, :], start=False, stop=True, skip_group_check=True), [y_b, be2c], [zib])
                z_t, z_b = Zst.next()
                P.op("act", lambda e, zr=zr, z_t=z_t: e.activation(out=z_t[:, 0, :], in_=zr[:, 0:W], func=AF.Copy), [zrb], [z_b])
                P.op("act", lambda e, zi=zi, z_t=z_t: e.activation(out=z_t[:, 1, :], in_=zi[:, 0:W], func=AF.Copy), [zib], [z_b])
                P.dma("sp", scr.ZT[:, :, fc * FC:(fc + 1) * FC, :].rearrange("r p f c -> p r f c"), z_t[:].rearrange("p r (f c) -> p r f c", c=128), [z_b], [scr.bA], z_b)
            o_t, o_b = Ot.next()
            nxt_z = load_z(0)
            for pc in range(NPC):
                zz, zzb = nxt_z
                if pc + 1 < NPC:
                    nxt_z = load_z(pc + 1)
                o, ob = po.next()
                P.op("pe", lambda e, o=o, zz=zz: e.matmul(o[0:KL, :], lhsT=i1c[:, :], rhs=zz[:, 0, :], start=True, stop=False), [zzb, bi1c], [ob])
                P.op("pe", lambda e, o=o, zz=zz: e.matmul(o[0:KL, :], lhsT=i1s[:, :], rhs=zz[:, 1, :], start=False, stop=True), [zzb, bi1s], [ob])
                half = pc % 16
                P.op("act", lambda e, o=o, o_t=o_t, half=half: e.activation(out=o_t[:, half * 512:(half + 1) * 512], in_=o[0:KL, :], func=AF.Copy), [ob], [o_b])
                if half == 15:
                    p0 = (pc - 15) * PC
                    P.dma("sp", scr.CV[0:L, s * 128:(s + 1) * 128].rearrange("(k p) c -> k p c", p=128)[:, p0:p0 + 64, :], o_t[:].rearrange("k (p c) -> k p c", c=128), [o_b], [scr.bO], o_b)
                    if pc != NPC - 1:
                        o_t, o_b = Ot.next()

    _phase(P, nc, body)


def phase_G(P, nc, C, L, zT, tl, ktwo_dst, rs_dst):
    NT2 = 2 * L // 128
    scr = C.scr
    PI = math.pi

    def body(sb, ps):
        w1 = sb("w1G", [33, 2, 64], BF16)
        w2 = sb("w2G", [64, 2, 64], BF16)
        w3 = sb("w3G", [64, 2, 2048], BF16)
        stg = sb("stgG", [64, 2, 2048], F32)
        sm = sb("smG", [64, 2, 8], F32)
        bw = P.buf("wG")
        for d in range(2):
            P.dma("sp", stg[0:33, d, 0:64], C.hy_f_w1[0, d], [], [bw], bw)
        P.op("pool", lambda e: e.tensor_copy(out=w1[:], in_=stg[0:33, :, 0:64]), [bw], [bw])
        for d in range(2):
            P.dma("sp", stg[0:64, d, 64:128], C.hy_f_w2[0, d], [], [bw], bw)
        P.op("pool", lambda e: e.tensor_copy(out=w2[:], in_=stg[0:64, :, 64:128]), [bw], [bw])
        for i, src in enumerate((C.hy_f_b1, C.hy_f_freq1, C.hy_f_b2, C.hy_f_freq2)):
            for d in range(2):
                P.dma("sp", sm[:, d, i:i + 1], src[0, d].rearrange("(o u) -> o u", u=1), [], [bw], bw, slow=True)
        for (bi, fi, oi) in ((0, 1, 4), (2, 3, 5)):
            P.op("pool", lambda e, bi=bi, fi=fi, oi=oi: e.tensor_tensor(out=sm[:, :, oi:oi + 1], in0=sm[:, :, bi:bi + 1], in1=sm[:, :, fi:fi + 1], op=ALU.mult), [bw], [bw])
        bw3 = P.buf("w3G")
        for d in range(2):
            P.dma("sp", stg[:, d, :], C.hy_f_w3[0, d], [bw], [bw3], bw3)
        P.op("pool", lambda e: e.tensor_copy(out=w3[:], in_=stg[:]), [bw3], [bw3])
        negpi = sb("negpiG", [128, 1], F32)
        P.op("pool", lambda e: e.memset(negpi[:], -PI), [], [bw])
        ones = sb("onesG", [128, 128], BF16)
        P.op("pool", lambda e: e.memset(ones[:], 1.0), [], [bw])
        dl = sb("dlG", [128, 2048], F32)
        bdl = P.buf("dlG")
        P.dma("sp", dl[:], C.hy_delta.partition_broadcast(128), [], [bdl], bdl, slow=True)
        tt = sb("ttG", [128, NT2], F32)
        P.dma("sp", tt[:], tl.rearrange("(k p) -> p k", p=128), [], [bdl], bdl, slow=True)
        P.op("pool", lambda e: e.tensor_scalar(out=tt[:], in0=tt[:], scalar1=-1.0, scalar2=None, op0=ALU.mult), [bdl], [bdl])
        zt = Rot(P, sb, "ztG", [33, 512], F32, 2)
        ztb = Rot(P, sb, "ztbG", [33, 512], BF16, 2)
        v1 = Rot(P, sb, "v1G", [64, 512], F32, 2)
        ni = Rot(P, sb, "niG", [64, 512], mybir.dt.int32, 2)
        nf = Rot(P, sb, "nfG", [64, 512], F32, 2)

        def range_reduce(P, v, vb, ni_s, nf_s):
            n_i, nib = ni_s
            n_f, nfb = nf_s
            P.op("dve", lambda e: e.tensor_scalar(out=n_f[:], in0=v[:], scalar1=1.0 / (2 * PI), scalar2=None, op0=ALU.mult), [vb], [nfb])
            P.op("dve", lambda e: e.tensor_copy(out=n_i[:], in_=n_f[:]), [nfb], [nib])
            P.op("dve", lambda e: e.tensor_copy(out=n_f[:], in_=n_i[:]), [nib], [nfb])
            P.op("dve", lambda e: e.scalar_tensor_tensor(out=v[:], in0=n_f[:], scalar=-2 * PI, in1=v[:], op0=ALU.mult, op1=ALU.add), [nfb, vb], [vb])
            P.op("dve", lambda e: e.tensor_scalar(out=n_f[:], in0=v[:], scalar1=PI, scalar2=None, op0=ALU.is_gt), [vb], [nfb])
            P.op("dve", lambda e: e.scalar_tensor_tensor(out=v[:], in0=n_f[:], scalar=-2 * PI, in1=v[:], op0=ALU.mult, op1=ALU.add), [nfb, vb], [vb])
            P.op("dve", lambda e: e.tensor_scalar(out=n_f[:], in0=v[:], scalar1=-PI, scalar2=None, op0=ALU.is_lt), [vb], [nfb])
            P.op("dve", lambda e: e.scalar_tensor_tensor(out=v[:], in0=n_f[:], scalar=2 * PI, in1=v[:], op0=ALU.mult, op1=ALU.add), [nfb, vb], [vb])
        h1 = Rot(P, sb, "h1G", [64, 512], BF16, 2)
        h2 = Rot(P, sb, "h2G", [64, 512], BF16, 2)
        dec = Rot(P, sb, "decG", [128, 2048], F32, 2)
        kf = Rot(P, sb, "kfG", [128, 2048], F32, 2)
        kb16 = Rot(P, sb, "kbG", [128, 2048], BF16, 2)
        sq = Rot(P, sb, "sqG", [128, 2048], BF16, 2)
        pm = Rot(P, ps, "pmG", [128, 512], F32, 2)
        pk = Rot(P, ps, "pkG", [128, 512], F32, 2)
        pss = [ps("pssG%d" % i, [128, 512], F32) for i in range(4)]
        bss = P.buf("pssG")
        for blk in range(2 * L // 512):
            d = 0 if blk * 512 < L else 1
            z_t, z_b = zt.next()
            P.dma("sp", z_t[:], zT[:, blk * 512:(blk + 1) * 512], [], [z_b], z_b)
            zb_t, zb_b = ztb.next()
            P.op("pool", lambda e, z_t=z_t, zb_t=zb_t: e.tensor_copy(out=zb_t[:], in_=z_t[:]), [z_b], [zb_b])
            m, mb = pm.next()
            P.op("pe", lambda e, m=m, zb_t=zb_t, d=d: e.matmul(m[0:64, :], lhsT=w1[:, d, :], rhs=zb_t[:], start=True, stop=True), [zb_b, bw], [mb])
            v, vb = v1.next()
            P.op("act", lambda e, m=m, v=v, d=d: e.activation(out=v[:], in_=m[0:64, :], func=AF.Identity, scale=sm[:, d, 1:2], bias=sm[:, d, 4:5]), [mb, bw], [vb])
            range_reduce(P, v, vb, ni.next(), nf.next())
            h_1, h1b = h1.next()
            P.op("act", lambda e, v=v, h_1=h_1: e.activation(out=h_1[:], in_=v[:], func=AF.Sin), [vb], [h1b])
            m, mb = pm.next()
            P.op("pe", lambda e, m=m, h_1=h_1, d=d: e.matmul(m[0:64, :], lhsT=w2[:, d, :], rhs=h_1[:], start=True, stop=True), [h1b, bw], [mb])
            v, vb = v1.next()
            P.op("act", lambda e, m=m, v=v, d=d: e.activation(out=v[:], in_=m[0:64, :], func=AF.Identity, scale=sm[:, d, 3:4], bias=sm[:, d, 5:6]), [mb, bw], [vb])
            range_reduce(P, v, vb, ni.next(), nf.next())
            h_2, h2b = h2.next()
            P.op("act", lambda e, v=v, h_2=h_2: e.activation(out=h_2[:], in_=v[:], func=AF.Sin), [vb], [h2b])
            for ti in range(4):
                tile_i = blk * 4 + ti
                dc, dcb = dec.next()
                P.op("act", lambda e, dc=dc, tile_i=tile_i: e.activation(out=dc[:], in_=dl[:], func=AF.Exp, scale=tt[:, tile_i:tile_i + 1]), [bdl], [dcb])
                k_t, k_b = kf.next()
                for gi in range(4):
                    kk, kkb = pk.next()
                    P.op("pe", lambda e, kk=kk, h_2=h_2, ti=ti, gi=gi, d=d: e.matmul(kk[:], lhsT=h_2[:, ti * 128:(ti + 1) * 128], rhs=w3[:, d, gi * 512:(gi + 1) * 512], start=True, stop=True), [h2b, bw3], [kkb])
                    P.op("act", lambda e, kk=kk, k_t=k_t, gi=gi: e.activation(out=k_t[:, gi * 512:(gi + 1) * 512], in_=kk[:], func=AF.Copy), [kkb], [k_b])
                P.op("dve", lambda e, k_t=k_t, dc=dc: e.tensor_tensor(out=k_t[:], in0=k_t[:], in1=dc[:], op=ALU.mult), [k_b, dcb], [k_b])
                kb_t, kb_b = kb16.next()
                P.op("pool", lambda e, k_t=k_t, kb_t=kb_t: e.tensor_copy(out=kb_t[:], in_=k_t[:]), [k_b], [kb_b])
                P.dma("sp", ktwo_dst[tile_i * 128:(tile_i + 1) * 128, :], kb_t[:], [kb_b], [scr.bK], kb_b)
                sq_t, sq_b = sq.next()
                P.op("dve", lambda e, k_t=k_t, sq_t=sq_t: e.tensor_tensor(out=sq_t[:], in0=k_t[:], in1=k_t[:], op=ALU.mult), [k_b], [sq_b])
                for gi in range(4):
                    P.op("pe", lambda e, gi=gi, sq_t=sq_t, tile_i=tile_i: e.matmul(pss[gi][:], lhsT=ones[:], rhs=sq_t[:, gi * 512:(gi + 1) * 512], start=(tile_i == 0), stop=(tile_i == NT2 - 1)), [sq_b, bw], [bss])
        rs = sb("rsG", [128, 2048], F32)
        brs = P.buf("rsG")
        for gi in range(4):
            P.op("act", lambda e, gi=gi: e.activation(out=rs[:, gi * 512:(gi + 1) * 512], in_=pss[gi][:], func=AF.Copy), [bss], [brs])
        P.op("pool", lambda e: e.tensor_scalar(out=rs[:], in0=rs[:], scalar1=EPS, scalar2=None, op0=ALU.add), [brs], [brs])
        P.op("act", lambda e: e.activation(out=rs[:], in_=rs[:], func=AF.Sqrt), [brs], [brs])
        P.op("dve", lambda e: e.reciprocal(out=rs[:], in_=rs[:]), [brs], [brs])
        P.dma("sp", rs_dst[0:1, :], rs[0:1, :], [brs], [scr.bK], brs)

    _phase(P, nc, body)


def s5_cmul(P, eng2, out_r, out_i, xr, xi, yr, yi, t1, t2, R, W, TB):
    P.op("pool", lambda e: e.tensor_tensor(out=t1, in0=xr, in1=yr, op=ALU.mult), R, [TB])
    P.op("dve", lambda e: e.tensor_tensor(out=t2, in0=xi, in1=yi, op=ALU.mult), R, [TB])
    P.op("pool", lambda e: e.tensor_tensor(out=out_r, in0=t1, in1=t2, op=ALU.subtract), [TB] + R, W)
    P.op("pool", lambda e: e.tensor_tensor(out=t1, in0=xr, in1=yi, op=ALU.mult), R + W, [TB])
    P.op("dve", lambda e: e.tensor_tensor(out=t2, in0=xi, in1=yr, op=ALU.mult), R + W, [TB])
    P.op("pool", lambda e: e.tensor_tensor(out=out_i, in0=t1, in1=t2, op=ALU.add), [TB] + R, W)


def s5_nstages(T):
    n, span = 0, 1
    while span < T:
        span *= 4
        n += 1
    return n


def phase_S5setup(P, nc, C, NS):
    scr = C.scr
    PI = math.pi
    G2 = 128

    def body(sb, ps):
        ident = sb("identS", [128, 128], BF16)
        identf = sb("identSf", [128, 128], F32)
        bid = P.buf("identS")
        P.dma("sp", identf[:], C.ident[:, :], [], [bid], bid)
        P.op("dve", lambda e: e.tensor_copy(out=ident[:], in_=identf[:]), [bid], [bid])
        msk = sb("mskS", [128, 4], F32)
        bm = P.buf("mskS")
        P.dma("sp", msk[:], C.s5_rowmask[:, :], [], [bm], bm)
        mfb = sb("mfbS", [128, 2, 128], F32)
        P.dma("sp", mfb[:, 0, :], C.s5_mf[:, :], [], [bm], bm)
        P.dma("sp", mfb[:, 1, :], C.s5_mb[:, :], [], [bm], bm)
        ar = sb("arS", [128, G2], F32)
        ai = sb("aiS", [128, G2], F32)
        dt = sb("dtS", [128, G2], F32)
        ba = P.buf("aS")
        for half in range(2):
            P.dma("sp", ar[half * 64:(half + 1) * 64, :].rearrange("n (d g) -> n d g", d=2), C.s5_a_re[0].rearrange("d g n -> n d g"), [], [ba], ba, slow=True)
            P.dma("sp", ai[half * 64:(half + 1) * 64, :].rearrange("n (d g) -> n d g", d=2), C.s5_a_im[0].rearrange("d g n -> n d g"), [], [ba], ba, slow=True)
        P.dma("sp", dt[:], C.s5_log_dt[0].rearrange("d g -> (d g)").partition_broadcast(128), [], [ba], ba, slow=True)
        P.op("act", lambda e: e.activation(out=dt[:], in_=dt[:], func=AF.Exp), [ba], [ba])
        NT_ = 12
        tmp = [sb("tmpS%d" % i, [128, G2], F32) for i in range(NT_)]
        btmp = [P.buf("tmpS%d" % i) for i in range(NT_)]
        ni = sb("niS", [128, G2], mybir.dt.int32)
        lr, li, mag, pr, pi_, cs_arg = tmp[0], tmp[1], tmp[2], tmp[3], tmp[4], tmp[5]
        bl = P.buf("lS")
        P.op("pool", lambda e: e.tensor_tensor(out=lr[:], in0=ar[:], in1=dt[:], op=ALU.mult), [ba], [bl])
        P.op("pool", lambda e: e.tensor_tensor(out=li[:], in0=ai[:], in1=dt[:], op=ALU.mult), [ba], [bl])
        P.op("act", lambda e: e.activation(out=mag[:], in_=lr[:], func=AF.Exp), [bl], [bl])

        def rr(v):
            nf = tmp[6]
            P.op("dve", lambda e: e.tensor_scalar(out=nf[:], in0=v[:], scalar1=1.0 / (2 * PI), scalar2=None, op0=ALU.mult), [bl], [bl])
            P.op("dve", lambda e: e.tensor_copy(out=ni[:], in_=nf[:]), [bl], [bl])
            P.op("dve", lambda e: e.tensor_copy(out=nf[:], in_=ni[:]), [bl], [bl])
            P.op("dve", lambda e: e.scalar_tensor_tensor(out=v[:], in0=nf[:], scalar=-2 * PI, in1=v[:], op0=ALU.mult, op1=ALU.add), [bl], [bl])
            P.op("dve", lambda e: e.tensor_scalar(out=nf[:], in0=v[:], scalar1=PI, scalar2=None, op0=ALU.is_gt), [bl], [bl])
            P.op("dve", lambda e: e.scalar_tensor_tensor(out=v[:], in0=nf[:], scalar=-2 * PI, in1=v[:], op0=ALU.mult, op1=ALU.add), [bl], [bl])
            P.op("dve", lambda e: e.tensor_scalar(out=nf[:], in0=v[:], scalar1=-PI, scalar2=None, op0=ALU.is_lt), [bl], [bl])
            P.op("dve", lambda e: e.scalar_tensor_tensor(out=v[:], in0=nf[:], scalar=2 * PI, in1=v[:], op0=ALU.mult, op1=ALU.add), [bl], [bl])

        P.op("pool", lambda e: e.tensor_scalar(out=cs_arg[:], in0=li[:], scalar1=PI / 2, scalar2=None, op0=ALU.add), [bl], [bl])
        rr(li)
        rr(cs_arg)
        P.op("act", lambda e: e.activation(out=pi_[:], in_=li[:], func=AF.Sin), [bl], [bl])
        P.op("act", lambda e: e.activation(out=pr[:], in_=cs_arg[:], func=AF.Sin), [bl], [bl])
        P.op("pool", lambda e: e.tensor_tensor(out=pr[:], in0=pr[:], in1=mag[:], op=ALU.mult), [bl], [bl])
        P.op("pool", lambda e: e.tensor_tensor(out=pi_[:], in0=pi_[:], in1=mag[:], op=ALU.mult), [bl], [bl])
        PW = sb("PWS", [128, 9, 2, G2], F32)
        bpw = P.buf("PWS")
        P.op("pool", lambda e: e.memset(PW[:, 0, 0, :], 1.0), [], [bpw])
        P.op("pool", lambda e: e.memset(PW[:, 0, 1, :], 0.0), [], [bpw])
        P.op("pool", lambda e: e.tensor_copy(out=PW[:, 1, 0, :], in_=pr[:]), [bl], [bpw])
        P.op("pool", lambda e: e.tensor_copy(out=PW[:, 1, 1, :], in_=pi_[:]), [bl], [bpw])
        t1, t2 = tmp[7], tmp[8]
        btt = P.buf("ttS")
        for e_ in range(2, 9):
            s5_cmul(P, None, PW[:, e_, 0, :], PW[:, e_, 1, :], PW[:, e_ - 1, 0, :], PW[:, e_ - 1, 1, :], PW[:, 1, 0, :], PW[:, 1, 1, :], t1[:], t2[:], [bpw], [bpw], btt)
        inv8 = sb("inv8S", [128, 2, G2], F32)
        binv = P.buf("inv8S")
        P.op("pool", lambda e: e.tensor_tensor(out=t1[:], in0=PW[:, 8, 0, :], in1=PW[:, 8, 0, :], op=ALU.mult), [bpw], [btt])
        P.op("pool", lambda e: e.tensor_tensor(out=t2[:], in0=PW[:, 8, 1, :], in1=PW[:, 8, 1, :], op=ALU.mult), [bpw], [btt])
        P.op("pool", lambda e: e.tensor_tensor(out=t1[:], in0=t1[:], in1=t2[:], op=ALU.add), [btt], [btt])
        P.op("dve", lambda e: e.reciprocal(out=t1[:], in_=t1[:]), [btt], [btt])
        P.op("pool", lambda e: e.tensor_tensor(out=inv8[:, 0, :], in0=PW[:, 8, 0, :], in1=t1[:], op=ALU.mult), [bpw, btt], [binv])
        P.op("pool", lambda e: e.tensor_tensor(out=inv8[:, 1, :], in0=PW[:, 8, 1, :], in1=t1[:], op=ALU.mult), [bpw, btt], [binv])
        P.op("pool", lambda e: e.tensor_scalar(out=inv8[:, 1, :], in0=inv8[:, 1, :], scalar1=-1.0, scalar2=None, op0=ALU.mult), [binv], [binv])
        SC = sb("SCS", [128, NS * 3, 2, G2], F32)
        bsc = P.buf("SCS")
        q = sb("qS", [128, 2, G2], F32)
        bq = P.buf("qS")
        P.op("pool", lambda e: e.tensor_copy(out=q[:], in_=PW[:, 8, :, :]), [bpw], [bq])
        for m in range(NS):
            P.op("pool", lambda e, m=m: e.tensor_copy(out=SC[:, m * 3, :, :], in_=q[:]), [bq], [bsc])
            s5_cmul(P, None, SC[:, m * 3 + 1, 0, :], SC[:, m * 3 + 1, 1, :], q[:, 0, :], q[:, 1, :], q[:, 0, :], q[:, 1, :], t1[:], t2[:], [bq, bsc], [bsc], btt)
            s5_cmul(P, None, SC[:, m * 3 + 2, 0, :], SC[:, m * 3 + 2, 1, :], SC[:, m * 3 + 1, 0, :], SC[:, m * 3 + 1, 1, :], q[:, 0, :], q[:, 1, :], t1[:], t2[:], [bq, bsc], [bsc], btt)
            if m < NS - 1:
                s5_cmul(P, None, q[:, 0, :], q[:, 1, :], SC[:, m * 3 + 1, 0, :], SC[:, m * 3 + 1, 1, :], SC[:, m * 3 + 1, 0, :], SC[:, m * 3 + 1, 1, :], t1[:], t2[:], [bsc], [bq], btt)
        P.op("pool", lambda e: e.tensor_scalar(out=SC[:, :, 1, :], in0=SC[:, :, 1, :], scalar1=msk[:, 2:3], scalar2=None, op0=ALU.mult), [bsc, bm], [bsc])
        P.dma("sp", scr.SSC[:, 0:NS * 3, :, :], SC[:], [bsc], [scr.bK], bsc)
        swapf = sb("swapSf", [128, 128], F32)
        P.dma("sp", swapf[:], C.s5_swap[:, :], [], [bid], bid)
        NC3 = NS * 3
        r1s = Rot(P, sb, "r1S", [128, NC3, 128], F32, 2)
        r2s = Rot(P, sb, "r2S", [128, NC3, 128], F32, 2)
        rst = Rot(P, sb, "rstS", [128, NC3, 128], BF16, 3)
        idb = identf[:].unsqueeze(1).to_broadcast([128, NC3, 128])
        swb = swapf[:].unsqueeze(1).to_broadcast([128, NC3, 128])
        for dg in range(G2):
            rs_t, rs_b = rst.next()
            r1, r1b = r1s.next()
            r2, r2b = r2s.next()
            sc1 = SC[:, :, 0, dg:dg + 1].to_broadcast([128, NC3, 128])
            sc2 = SC[:, :, 1, dg:dg + 1].to_broadcast([128, NC3, 128])
            P.op("pool", lambda e, r1=r1, sc1=sc1: e.tensor_tensor(out=r1[:], in0=idb, in1=sc1, op=ALU.mult), [bid, bsc], [r1b])
            P.op("dve", lambda e, r2=r2, sc2=sc2: e.tensor_tensor(out=r2[:], in0=swb, in1=sc2, op=ALU.mult), [bid, bsc], [r2b])
            P.op("pool", lambda e, r1=r1, r2=r2, rs_t=rs_t: e.tensor_tensor(out=rs_t[:], in0=r1[:], in1=r2[:], op=ALU.add), [r1b, r2b], [rs_b])
            P.dma("sp", scr.SR[dg, :, 0:NS * 3, :], rs_t[:], [rs_b], [scr.bK], rs_b)
        cr, ci, den = tmp[9], tmp[10], tmp[11]
        bc = P.buf("cS")
        xr = tmp[2]
        P.op("pool", lambda e: e.tensor_scalar(out=xr[:], in0=PW[:, 1, 0, :], scalar1=-1.0, scalar2=None, op0=ALU.add), [bpw, bl], [bl])
        P.op("pool", lambda e: e.tensor_tensor(out=den[:], in0=ar[:], in1=ar[:], op=ALU.mult), [ba], [bc])
        P.op("pool", lambda e: e.tensor_tensor(out=t1[:], in0=ai[:], in1=ai[:], op=ALU.mult), [ba, binv], [btt])
        P.op("pool", lambda e: e.tensor_tensor(out=den[:], in0=den[:], in1=t1[:], op=ALU.add), [btt], [bc])
        P.op("dve", lambda e: e.reciprocal(out=den[:], in_=den[:]), [bc], [bc])
        P.op("pool", lambda e: e.tensor_tensor(out=t1[:], in0=xr[:], in1=ar[:], op=ALU.mult), [bl, ba], [btt])
        P.op("pool", lambda e: e.tensor_tensor(out=t2[:], in0=PW[:, 1, 1, :], in1=ai[:], op=ALU.mult), [bpw, ba], [btt])
        P.op("pool", lambda e: e.tensor_tensor(out=cr[:], in0=t1[:], in1=t2[:], op=ALU.add), [btt], [bc])
        P.op("pool", lambda e: e.tensor_tensor(out=cr[:], in0=cr[:], in1=den[:], op=ALU.mult), [bc], [bc])
        P.op("pool", lambda e: e.tensor_tensor(out=t1[:], in0=PW[:, 1, 1, :], in1=ar[:], op=ALU.mult), [bpw, ba, bc], [btt])
        P.op("pool", lambda e: e.tensor_tensor(out=t2[:], in0=xr[:], in1=ai[:], op=ALU.mult), [bl, ba], [btt])
        P.op("pool", lambda e: e.tensor_tensor(out=ci[:], in0=t1[:], in1=t2[:], op=ALU.subtract), [btt], [bc])
        P.op("pool", lambda e: e.tensor_tensor(out=ci[:], in0=ci[:], in1=den[:], op=ALU.mult), [bc], [bc])
        Bri = sb("BriS", [128, 2, G2, 16], F32)
        bB = P.buf("BriS")
        for half in range(2):
            for d in range(2):
                P.dma("sp", Bri[half * 64:(half + 1) * 64, 0, d * 64:(d + 1) * 64, :], C.s5_b_re[0, d].rearrange("g n c -> n g c"), [], [bB], bB)
                P.dma("sp", Bri[half * 64:(half + 1) * 64, 1, d * 64:(d + 1) * 64, :], C.s5_b_im[0, d].rearrange("g n c -> n g c"), [], [bB], bB)
        Bb = sb("BbS", [128, 2, G2, 16], F32)
        bBb = P.buf("BbS")
        T1 = sb("T1S", [128, 64, 16], F32)
        T2 = sb("T2S", [128, 64, 16], F32)
        bT = P.buf("TS")
        for d in range(2):
            gs = slice(d * 64, (d + 1) * 64)
            crb = cr[:, gs].unsqueeze(2).to_broadcast([128, 64, 16])
            cib = ci[:, gs].unsqueeze(2).to_broadcast([128, 64, 16])
            s5_cmul(P, None, Bb[:, 0, gs, :], Bb[:, 1, gs, :], Bri[:, 0, gs, :], Bri[:, 1, gs, :], crb, cib, T1[:], T2[:], [bB, bc], [bBb], bT)
        Cri = sb("CriS", [128, 2, G2, 16], F32)
        bC = P.buf("CriS")
        cl = Rot(P, sb, "clS", [128, 128], F32, 2)
        tpc = Rot(P, ps, "tpcS", [128, 512], F32, 2)
        for ri, src in enumerate((C.s5_c_re, C.s5_c_im)):
            for d in range(2):
                for o in range(8):
                    c_t, c_b = cl.next()
                    for dup in range(2):
                        P.dma("sp", c_t[:, dup * 64:(dup + 1) * 64], src[0, d, o * 8:(o + 1) * 8].rearrange("g c n -> (g c) n"), [], [c_b], c_b)
                    tp_, tpb_ = tpc.next()
                    P.op("pe", lambda e, tp_=tp_, c_t=c_t: e.transpose(out=tp_[:, 0:128], in_=c_t[:], identity=identf[:]), [c_b, bid], [tpb_])
                    P.op("act", lambda e, tp_=tp_, ri=ri, d=d, o=o: e.activation(out=Cri[:, ri, d * 64 + o * 8:d * 64 + (o + 1) * 8, :], in_=tp_[:, 0:128].rearrange("p (g c) -> p g c", c=16), func=AF.Copy), [tpb_], [bC])
        GC = 32
        W8 = sb("W8S", [128, 2, GC, 8, 16], BF16)
        W8s = sb("W8sS", [128, GC, 8, 16], BF16)
        C8 = sb("C8S", [128, GC, 8, 16], BF16)
        bW8, bW8s, bC8 = P.buf("W8S"), P.buf("W8sS"), P.buf("C8S")
        O1 = sb("O1S", [128, GC, 16], F32)
        O2 = sb("O2S", [128, GC, 16], F32)
        O3 = sb("O3S", [128, GC, 16], F32)
        O4 = sb("O4S", [128, GC, 16], F32)
        bO = P.buf("OS")
        tpb16 = Rot(P, ps, "tpb16S", [128, 1024], BF16, 2)
        pd = Rot(P, ps, "pdS", [128, 512], F32, 2)
        b8st = Rot(P, sb, "b8stS", [128, 8, 128], BF16, 2)
        d8f = Rot(P, sb, "d8fS", [128, 4, 128], F32, 2)
        d8st = Rot(P, sb, "d8stS", [128, 4, 128], BF16, 2)
        TT1 = T1[:, 0:GC, :]
        TT2 = T2[:, 0:GC, :]
        for ch in range(G2 // GC):
            d = (ch * GC) // 64
            gs = slice(ch * GC, (ch + 1) * GC)
            for i in range(8):
                eb = (7 - i) if d == 0 else i
                prb = PW[:, eb, 0, gs].unsqueeze(2).to_broadcast([128, GC, 16])
                pib = PW[:, eb, 1, gs].unsqueeze(2).to_broadcast([128, GC, 16])
                s5_cmul(P, None, O1[:], O2[:], Bb[:, 0, gs, :], Bb[:, 1, gs, :], prb, pib, TT1, TT2, [bBb, bpw], [bO], bT)
                P.op("pool", lambda e, i=i: e.tensor_copy(out=W8[:, 0, :, i, :], in_=O1[:]), [bO], [bW8])
                P.op("pool", lambda e, i=i: e.tensor_copy(out=W8[:, 1, :, i, :], in_=O2[:]), [bO], [bW8])
                i8r = inv8[:, 0, gs].unsqueeze(2).to_broadcast([128, GC, 16])
                i8i = inv8[:, 1, gs].unsqueeze(2).to_broadcast([128, GC, 16])
                s5_cmul(P, None, O3[:], O4[:], O1[:], O2[:], i8r, i8i, TT1, TT2, [bO, binv], [bO], bT)
                P.op("pool", lambda e: e.tensor_scalar(out=O3[:], in0=O3[:], scalar1=msk[:, 0:1], scalar2=None, op0=ALU.mult), [bO, bm], [bO])
                P.op("dve", lambda e, i=i: e.scalar_tensor_tensor(out=W8s[:, :, i, :], in0=O4[:], scalar=msk[:, 1:2], in1=O3[:], op0=ALU.mult, op1=ALU.add), [bO, bm], [bW8s])
                ec_ = (i + 1) if d == 0 else (8 - i)
                prc = PW[:, ec_, 0, gs].unsqueeze(2).to_broadcast([128, GC, 16])
                pic = PW[:, ec_, 1, gs].unsqueeze(2).to_broadcast([128, GC, 16])
                s5_cmul(P, None, O1[:], O2[:], Cri[:, 0, gs, :], Cri[:, 1, gs, :], prc, pic, TT1, TT2, [bC, bpw, bW8, bW8s], [bO], bT)
                P.op("pool", lambda e: e.tensor_scalar(out=O1[:], in0=O1[:], scalar1=msk[:, 0:1], scalar2=None, op0=ALU.mult), [bO, bm], [bO])
                P.op("dve", lambda e, i=i: e.scalar_tensor_tensor(out=C8[:, :, i, :], in0=O2[:], scalar=msk[:, 3:4], in1=O1[:], op0=ALU.mult, op1=ALU.add), [bO, bm], [bC8])
            P.dma("sp", scr.SC8[:, ch * GC:(ch + 1) * GC, :], C8[:].rearrange("p g i c -> p g (i c)"), [bC8], [scr.bK], bC8)
            for j0 in range(0, GC, 8):
                tp_, tpb_ = tpb16.next()
                for j in range(8):
                    for ri in range(2):
                        P.op("pe", lambda e, tp_=tp_, j=j, j0=j0, ri=ri: e.transpose(out=tp_[:, j * 128 + ri * 64:j * 128 + (ri + 1) * 64], in_=W8[0:64, ri, j0 + j, :, :].rearrange("n i c -> n (i c)"), identity=ident[0:64, 0:64]), [bW8, bid], [tpb_])
                st_, stb_ = b8st.next()
                P.op("act", lambda e, tp_=tp_, st_=st_: e.activation(out=st_[:].rearrange("p j n -> p (j n)"), in_=tp_[:], func=AF.Copy), [tpb_], [stb_])
                dg0 = ch * GC + j0
                P.dma("sp", scr.SB8[dg0:dg0 + 8, :, :].rearrange("j p n -> p j n"), st_[:], [stb_], [scr.bK], stb_)
            for j0 in range(0, GC, 4):
                pp_, ppb_ = pd.next()
                for j in range(4):
                    P.op("pe", lambda e, pp_=pp_, j=j, j0=j0: e.matmul(pp_[:, j * 128:(j + 1) * 128], lhsT=W8s[:, j0 + j, :, :].rearrange("n i c -> n (i c)"), rhs=C8[:, j0 + j, :, :].rearrange("n i c -> n (i c)"), start=True, stop=True, skip_group_check=True), [bW8s, bC8], [ppb_])
                f_, fb_ = d8f.next()
                P.op("act", lambda e, pp_=pp_, f_=f_: e.activation(out=f_[:].rearrange("p j c -> p (j c)"), in_=pp_[:], func=AF.Copy), [ppb_], [fb_])
                o_, ob_ = d8st.next()
                mk = mfb[:, d, :].unsqueeze(1).to_broadcast([128, 4, 128])
                P.op("pool", lambda e, f_=f_, o_=o_, mk=mk: e.tensor_tensor(out=o_[:], in0=f_[:], in1=mk, op=ALU.mult), [fb_, bm], [ob_])
                dg0 = ch * GC + j0
                P.dma("sp", scr.SD8[dg0:dg0 + 4, :, :].rearrange("j p n -> p j n"), o_[:], [ob_], [scr.bK], ob_)

    _phase(P, nc, body)


def phase_S5main(P, nc, C, L, NS):
    T = L // 8
    NTT = max(1, T // 128)
    TT = min(T, 128)
    scr = C.scr

    def body(sb, ps):
        ident = sb("identM", [128, 128], BF16)
        identf = sb("identMf", [128, 128], F32)
        swapf = sb("swapMf", [128, 128], F32)
        bid = P.buf("identM")
        P.dma("sp", identf[:], C.ident[:, :], [], [bid], bid)
        P.dma("sp", swapf[:], C.s5_swap[:, :], [], [bid], bid)
        P.op("dve", lambda e: e.tensor_copy(out=ident[:], in_=identf[:]), [bid], [bid])
        SC = sb("SCM", [128, NS * 3, 2, 128], F32)
        bsc = P.buf("SCM")
        P.dma("sp", SC[:], scr.SSC[:, 0:NS * 3, :, :], [], [bsc], bsc)
        uo = Rot(P, sb, "uoM", [128, NTT, 8, 128], BF16, 2)
        uo2 = Rot(P, sb, "uo2M", [128, NTT, 8, 8, 16], BF16, 2)
        yo = Rot(P, sb, "yoM", [128, NTT, 8, 128], BF16, 2)
        wg = Rot(P, sb, "wgM", [128, 6, 128], BF16, 4)
        us = Rot(P, sb, "usM", [128, T], BF16, 4)
        sbuf_ = Rot(P, sb, "sM", [128, T], BF16, 12)
        ys = Rot(P, sb, "ysM", [128, T], BF16, 3)
        Ra = Rot(P, sb, "RaM", [128, NS * 3, 128], BF16, 8)
        Rt = Rot(P, sb, "RtM", [128, 128], F32, 8)
        Rb = Rot(P, sb, "RbM", [128, 128], BF16, 26)
        tp = Rot(P, ps, "tpM", [128, 1024], BF16, 2)
        pp = Rot(P, ps, "ppM", [128, 512], F32, 6)
        chunks = [(c0, min(512, T - c0)) for c0 in range(0, T, 512)]

        def do_groups(o, g8s, uo_t, uo_b, yo_t, yo_b):
            G = []
            for g8 in g8s:
                g = o * 8 + g8
                w_t, w_b = wg.next()
                for d in range(2):
                    P.dma("sp", w_t[:, d, :], scr.SB8[d * 64 + g, :, :], [], [w_b], w_b)
                    P.dma("sp", w_t[:, 2 + d, :], scr.SC8[:, d * 64 + g, :], [], [w_b], w_b)
                    P.dma("sp", w_t[:, 4 + d, :], scr.SD8[d * 64 + g, :, :], [], [w_b], w_b)
                tpp, tpb = tp.next()
                for tt in range(NTT):
                    P.op("pe", lambda e, tt=tt, tpp=tpp, g8=g8: e.transpose(out=tpp[:, tt * TT:(tt + 1) * TT], in_=uo_t[0:TT, tt, g8, :, :].rearrange("p i c -> p (i c)"), identity=ident[0:TT, 0:TT]), [uo_b, bid], [tpb])
                us_t, us_b = us.next()
                P.op("act", lambda e, us_t=us_t, tpp=tpp: e.activation(out=us_t[:], in_=tpp[:, 0:T], func=AF.Copy), [tpb], [us_b])
                G.append(dict(g=g, g8=g8, w_t=w_t, w_b=w_b, us_t=us_t, us_b=us_b))
            lanes = []
            for gi_, gd in enumerate(G):
                for d in range(2):
                    s_t, s_b = sbuf_.next()
                    for (c0, w) in chunks:
                        z, zb = pp.next()
                        P.op("pe", lambda e, z=z, c0=c0, w=w, d=d, gd=gd: e.matmul(z[:, 0:w], lhsT=gd["w_t"][:, d, :], rhs=gd["us_t"][:, c0:c0 + w], start=True, stop=True), [gd["us_b"], gd["w_b"]], [zb])
                        P.op("act", lambda e, z=z, c0=c0, w=w, s_t=s_t: e.activation(out=s_t[:, c0:c0 + w], in_=z[:, 0:w], func=AF.Copy), [zb], [s_b])
                    ra_t, ra_b = Ra.next()
                    P.dma("sp", ra_t[:], scr.SR[d * 64 + gd["g"], :, 0:NS * 3, :], [], [ra_b], ra_b)
                    lanes.append(dict(d=d, dg=d * 64 + gd["g"], cur=(s_t, s_b), ra=(ra_t, ra_b)))
            for m in range(NS):
                S = 4 ** m
                for ln in lanes:
                    ra_t, ra_b = ln["ra"]
                    ln["Rs"] = [(J * S, ra_t[:, m * 3 + J - 1, :], ra_b) for J in range(1, 4) if J * S < T]
                for ln in lanes:
                    d = ln["d"]
                    s_t, s_b = ln["cur"]
                    n_t, n_b = sbuf_.next()
                    for (c0, w) in chunks:
                        z, zb = pp.next()
                        mms = [(0, w, c0, ident[:], bid)]
                        for (sh, r2, r2b) in ln["Rs"]:
                            if d == 0:
                                a = max(c0, sh)
                                if a < c0 + w:
                                    mms.append((a - c0, w, a - sh, r2, r2b))
                            else:
                                bnd = min(c0 + w, T - sh)
                                if bnd > c0:
                                    mms.append((0, bnd - c0, c0 + sh, r2, r2b))
                        for k_, (o0, o1, src0, lt, ltb) in enumerate(mms):
                            P.op("pe", lambda e, z=z, o0=o0, o1=o1, src0=src0, lt=lt, s_t=s_t, k_=k_, nm=len(mms): e.matmul(z[:, o0:o1], lhsT=lt, rhs=s_t[:, src0:src0 + (o1 - o0)], start=(k_ == 0), stop=(k_ == nm - 1), skip_group_check=True), [s_b, ltb], [zb])
                        P.op("act", lambda e, z=z, c0=c0, w=w, n_t=n_t: e.activation(out=n_t[:, c0:c0 + w], in_=z[:, 0:w], func=AF.Copy), [zb], [n_b])
                    ln["cur"] = (n_t, n_b)
            for gi_, gd in enumerate(G):
                fin = [lanes[gi_ * 2]["cur"], lanes[gi_ * 2 + 1]["cur"]]
                w_t, w_b, us_t, us_b, g8 = gd["w_t"], gd["w_b"], gd["us_t"], gd["us_b"], gd["g8"]
                y_t, y_b = ys.next()
                for (c0, w) in chunks:
                    z, zb = pp.next()
                    mms = [(0, w, us_t, us_b, c0, 4), (0, w, us_t, us_b, c0, 5)]
                    a = max(c0, 1)
                    if a < c0 + w:
                        mms.append((a - c0, w, fin[0][0], fin[0][1], a - 1, 2))
                    bnd = min(c0 + w, T - 1)
                    if bnd > c0:
                        mms.append((0, bnd - c0, fin[1][0], fin[1][1], c0 + 1, 3))
                    for k_, (o0, o1, src, srcb, src0, wi) in enumerate(mms):
                        P.op("pe", lambda e, z=z, o0=o0, o1=o1, src=src, src0=src0, wi=wi, k_=k_, nm=len(mms), w_t=w_t: e.matmul(z[:, o0:o1], lhsT=w_t[:, wi, :], rhs=src[:, src0:src0 + (o1 - o0)], start=(k_ == 0), stop=(k_ == nm - 1), skip_group_check=True), [srcb, w_b], [zb])
                    P.op("act", lambda e, z=z, c0=c0, w=w, y_t=y_t: e.activation(out=y_t[:, c0:c0 + w], in_=z[:, 0:w], func=AF.Copy), [zb], [y_b])
                tpp2, tpb2 = tp.next()
                for tt in range(NTT):
                    P.op("pe", lambda e, tt=tt, tpp2=tpp2, y_t=y_t: e.transpose(out=tpp2[0:TT, tt * 128:(tt + 1) * 128], in_=y_t[:, tt * TT:(tt + 1) * TT], identity=ident[:]), [y_b, bid], [tpb2])
                P.op("act", lambda e, tpp2=tpp2, g8=g8: e.activation(out=yo_t[0:TT, :, :, g8 * 16:(g8 + 1) * 16], in_=tpp2[0:TT, 0:NTT * 128].rearrange("p (t i c) -> p t i c", t=NTT, i=8), func=AF.Copy), [tpb2], [yo_b])

        for o in range(8):
            uo_t, uo_b = uo.next()
            yo_t, yo_b = yo.next()
            for tt in range(NTT):
                P.dma("sp", uo_t[0:TT, tt, :, :], scr.U[tt * TT * 8:(tt + 1) * TT * 8, o * 128:(o + 1) * 128].rearrange("(p i) c -> p i c", i=8), [], [uo_b], uo_b)
            u2_t, u2_b = uo2.next()
            for tt in range(NTT):
                P.op("pool", lambda e, tt=tt, u2_t=u2_t, uo_t=uo_t: e.tensor_copy(out=u2_t[0:TT, tt, :, :, :], in_=uo_t[0:TT, tt, :, :].rearrange("p i (g c) -> p g i c", c=16)), [uo_b], [u2_b])
            for g8 in range(0, 8, 2):
                do_groups(o, [g8, g8 + 1], u2_t, u2_b, yo_t, yo_b)
            for tt in range(NTT):
                P.dma("sp", scr.YS[tt * TT * 8:(tt + 1) * TT * 8, o * 128:(o + 1) * 128].rearrange("(p i) c -> p i c", i=8), yo_t[0:TT, tt, :, :], [yo_b], [scr.bV], yo_b)

    _phase(P, nc, body)


def phase_S5post(P, nc, C, L):
    NT = L // 128
    scr = C.scr

    def body(sb, ps):
        ident = sb("identP", [128, 128], BF16)
        identf = sb("identPf", [128, 128], F32)
        bid = P.buf("identP")
        P.dma("sp", identf[:], C.ident[:, :], [], [bid], bid)
        P.op("dve", lambda e: e.tensor_copy(out=ident[:], in_=identf[:]), [bid], [bid])
        stage = Rot(P, sb, "wstP", [128, 1024], F32, 2)
        w_g = sb("w_gP", [128, 8, 1024], BF16)
        bw = P.buf("w_gP")
        load_weight_bf16(P, sb, w_g, bw, C.s5_glu_w[0], 8, 1024, stage)
        drep = sb("drepP", [128, 1024], F32)
        brep = sb("brepP", [128, 1024], F32)
        br = P.buf("repP")
        P.dma("sp", drep[:], C.s5_d[0].partition_broadcast(128), [], [br], br, slow=True)
        P.dma("sp", brep[:], C.s5_glu_b[0].partition_broadcast(128), [], [br], br, slow=True)
        ut = Rot(P, sb, "utP", [128, 1024], BF16, 2)
        yst = Rot(P, sb, "ystP", [128, 1024], BF16, 2)
        y = Rot(P, sb, "yP", [128, 1024], F32, 2)
        w = Rot(P, sb, "wP", [128, 1024], F32, 2)
        sg = Rot(P, sb, "sgP", [128, 1024], F32, 2)
        yg = Rot(P, sb, "ygP", [128, 1024], BF16, 2)
        ygT = Rot(P, sb, "ygTP", [128, 8, 128], BF16, 2)
        ya = Rot(P, sb, "yaP", [128, 1024], BF16, 2)
        tp = Rot(P, ps, "tpP", [128, 1024], BF16, 2)
        zp = Rot(P, ps, "zpP", [128, 512], F32, 4)
        for t in range(NT):
            rows = slice(t * 128, (t + 1) * 128)
            u_t, u_b = ut.next()
            s_t, s_b = yst.next()
            P.dma("sp", u_t[:], scr.U[rows, :], [], [u_b], u_b)
            P.dma("sp", s_t[:], scr.YS[rows, :], [], [s_b], s_b)
            y_t, y_b = y.next()
            w_t, w_b = w.next()
            P.op("pool", lambda e, y_t=y_t, u_t=u_t: e.tensor_tensor(out=y_t[:], in0=u_t[:], in1=drep[:], op=ALU.mult), [u_b, br], [y_b])
            P.op("pool", lambda e, y_t=y_t, s_t=s_t: e.tensor_tensor(out=y_t[:], in0=y_t[:], in1=s_t[:], op=ALU.add), [s_b, y_b], [y_b])
            P.op("dve", lambda e, y_t=y_t, w_t=w_t: e.tensor_tensor(out=w_t[:], in0=y_t[:], in1=y_t[:], op=ALU.mult), [y_b], [w_b])
            P.op("pool", lambda e, w_t=w_t: e.tensor_scalar(out=w_t[:], in0=w_t[:], scalar1=0.044715, scalar2=1.0, op0=ALU.mult, op1=ALU.add), [w_b], [w_b])
            P.op("pool", lambda e, w_t=w_t, y_t=y_t: e.tensor_tensor(out=w_t[:], in0=w_t[:], in1=y_t[:], op=ALU.mult), [w_b, y_b], [w_b])
            g_t, g_b = sg.next()
            P.op("act", lambda e, w_t=w_t, g_t=g_t: e.activation(out=g_t[:], in_=w_t[:], func=AF.Sigmoid, scale=2.0 * math.sqrt(2.0 / math.pi)), [w_b], [g_b])
            yg_t, yg_b = yg.next()
            P.op("dve", lambda e, yg_t=yg_t, y_t=y_t, g_t=g_t: e.tensor_tensor(out=yg_t[:], in0=y_t[:], in1=g_t[:], op=ALU.mult), [y_b, g_b], [yg_b])
            tpp, tpb = tp.next()
            for k in range(8):
                P.op("pe", lambda e, k=k, tpp=tpp, yg_t=yg_t: e.transpose(out=tpp[:, k * 128:(k + 1) * 128], in_=yg_t[:, k * 128:(k + 1) * 128], identity=ident[:]), [yg_b, bid], [tpb])
            yT, yTb = ygT.next()
            P.op("act", lambda e, tpp=tpp, yT=yT: e.activation(out=yT[:].rearrange("p k t -> p (k t)"), in_=tpp[:], func=AF.Copy), [tpb], [yTb])
            for gi in range(2):
                z, zb = zp.next()
                for k in range(8):
                    P.op("pe", lambda e, z=z, k=k, gi=gi, yT=yT: e.matmul(z[:], lhsT=yT[:, k, :], rhs=w_g[:, k, gi * 512:(gi + 1) * 512], start=(k == 0), stop=(k == 7)), [yTb, bw], [zb])
                P.op("act", lambda e, z=z, gi=gi, g_t=g_t: e.activation(out=g_t[:, gi * 512:(gi + 1) * 512], in_=z[:], func=AF.Copy), [zb], [g_b])
            P.op("pool", lambda e, g_t=g_t: e.tensor_tensor(out=g_t[:], in0=g_t[:], in1=brep[:], op=ALU.add), [g_b, br], [g_b])
            P.op("act", lambda e, g_t=g_t: e.activation(out=g_t[:], in_=g_t[:], func=AF.Sigmoid), [g_b], [g_b])
            a_t, a_b = ya.next()
            P.op("dve", lambda e, a_t=a_t, yg_t=yg_t, g_t=g_t: e.tensor_tensor(out=a_t[:], in0=yg_t[:], in1=g_t[:], op=ALU.mult), [yg_b, g_b], [a_b])
            P.dma("pool", scr.YA[rows, :], a_t[:], [a_b], [scr.bU], a_b)

    _phase(P, nc, body)


WNAMES = {
    "norm_g": [2, 1024], "final_g": [1024], "ple_w": [2, 256, 1024], "ple_gate_w": [2, 1024, 1024],
    "ab_w_in": [1, 1024, 3776], "ab_w_out": [1, 2048, 1024],
    "s5_a_re": [1, 2, 64, 64], "s5_a_im": [1, 2, 64, 64], "s5_log_dt": [1, 2, 64],
    "s5_b_re": [1, 2, 64, 64, 16], "s5_b_im": [1, 2, 64, 64, 16], "s5_c_re": [1, 2, 64, 16, 64], "s5_c_im": [1, 2, 64, 16, 64],
    "s5_d": [1, 1024], "s5_glu_w": [1, 1024, 1024], "s5_glu_b": [1, 1024],
    "mla_q_norm": [1, 384], "mla_w_q_up": [1, 384, 1536], "mla_kv_norm": [1, 256], "mla_w_kv_up": [1, 256, 2048],
    "hy_w_in": [1, 1024, 8192], "hy_w_out": [1, 2048, 1024], "hy_conv_w": [1, 3, 6144], "hy_conv_b": [1, 6144],
    "hy_f_w1": [1, 2, 33, 64], "hy_f_b1": [1, 2, 64], "hy_f_freq1": [1, 2, 64], "hy_f_w2": [1, 2, 64, 64], "hy_f_b2": [1, 2, 64],
    "hy_f_freq2": [1, 2, 64], "hy_f_w3": [1, 2, 64, 2048], "hy_bias": [1, 2048],
}


def build(Ls, opts):
    nc = bass.Bass("TRN2", target_bir_lowering=False)
    C = Ctx()
    LM = max(Ls)
    for n, shp in WNAMES.items():
        setattr(C, n, nc.dram_tensor(n, shp, F32, kind="ExternalInput").ap())
    C.ident = nc.dram_tensor("ident", [128, 128], F32, kind="ExternalInput").ap()
    C.rope_cs = nc.dram_tensor("rope_cs", [LM, 64], F32, kind="ExternalInput").ap()
    xs, ps_, ys = [], [], []
    for i, L in enumerate(Ls):
        xs.append(nc.dram_tensor(f"x{i}", [L, 1024], F32, kind="ExternalInput").ap())
        ps_.append(nc.dram_tensor(f"p{i}", [2, L, 256], F32, kind="ExternalInput").ap())
        ys.append(nc.dram_tensor(f"y{i}", [L, 1024], F32, kind="ExternalOutput").ap())
    dbg = opts.get("dbg", ())
    scr = Ctx()
    C.scr = scr

    def scratch(name, shape, dtype):
        kind = "ExternalOutput" if name in dbg else "Internal"
        return nc.dram_tensor("scr_" + name, shape, dtype, kind=kind).ap()

    scr.U = scratch("U", [LM, 1024], BF16)
    scr.G = scratch("G", [LM, 2048], BF16)
    scr.V = scratch("V", [LM, 1024], BF16)
    scr.QN = scratch("QN", [8, 128, LM], BF16)
    scr.QR = scratch("QR", [8, 64, LM], BF16)
    scr.KN = scratch("KN", [8, 128, LM], BF16)
    scr.KR = scratch("KR", [64, LM], BF16)
    scr.O = scratch("O", [LM, 1024], BF16)
    scr.YA = scratch("YA", [LM, 1024], BF16)
    scr.YH = scratch("YH", [LM, 2048], BF16)
    scr.H1 = scratch("H1", [LM, 1024], F32)
    scr.HT = scratch("HT", [8, 128, LM + 2], BF16)
    scr.MT = scratch("MT", [16, 128, LM], BF16)
    N1M = 2 * LM // 128
    scr.AT = scratch("AT", [2, N1M, 128, 128], BF16)
    scr.ZT = scratch("ZT", [2, 128, N1M, 128], BF16)
    scr.YHAT = scratch("YHAT", [16, 2, 128, N1M, 128], BF16)
    scr.VX = scratch("VX", [LM, 2048], BF16)
    scr.XG = scratch("XG", [LM, 2048], BF16)
    scr.CV = scratch("CV", [LM, 2048], BF16)
    C.hy_delta = nc.dram_tensor("hy_delta", [2048], F32, kind="ExternalInput").ap()
    C.s5_rowmask = nc.dram_tensor("s5_rowmask", [128, 4], F32, kind="ExternalInput").ap()
    C.s5_mf = nc.dram_tensor("s5_mf", [128, 128], F32, kind="ExternalInput").ap()
    C.s5_mb = nc.dram_tensor("s5_mb", [128, 128], F32, kind="ExternalInput").ap()
    C.s5_swap = nc.dram_tensor("s5_swap", [128, 128], F32, kind="ExternalInput").ap()
    NSM = s5_nstages(LM // 8)
    scr.SSC = scratch("SSC", [128, NSM * 3, 2, 128], F32)
    scr.SR = scratch("SR", [128, 128, NSM * 3, 128], BF16)
    scr.SB8 = scratch("SB8", [128, 128, 128], BF16)
    scr.SC8 = scratch("SC8", [128, 128, 128], BF16)
    scr.SD8 = scratch("SD8", [128, 128, 128], BF16)
    scr.YS = scratch("YS", [LM, 1024], BF16)
    fftc = {}
    for L in sorted(set(Ls)):
        tbs, N1, KL = fft_tables_shapes(L)
        d = {"tb": {k_: nc.dram_tensor(f"{k_}_{L}", list(shp), F32, kind="ExternalInput").ap() for k_, shp in tbs.items()}}
        d["zT"] = nc.dram_tensor(f"zT_{L}", [33, 2 * L], F32, kind="ExternalInput").ap()
        d["tl"] = nc.dram_tensor(f"tl_{L}", [2 * L], F32, kind="ExternalInput").ap()
        d["KT"] = scratch(f"KT_{L}", [2 * L, 2048], BF16)
        d["KH"] = scratch(f"KH_{L}", [16, 2, 128, N1, 128], F32)
        d["RS"] = scratch(f"RS_{L}", [1, 2048], F32)
        fftc[L] = d
    with ExitStack() as es:
        P = Prog(nc, es)
        for n in ["bU", "bG", "bV", "bQ", "bK", "bO", "bH", "bA"]:
            b = Buf(n, accum=True)
            setattr(scr, n, b)
        depth = opts.get("depth", 2)
        conv = opts.get("conv", True) and depth == 2
        if opts.get("s5", True):
            phase_S5setup(P, nc, C, NSM)
        if conv:
            for L in sorted(set(Ls)):
                d = fftc[L]
                phase_G(P, nc, C, L, d["zT"], d["tl"], d["KT"], d["RS"])
                phase_F(P, nc, C, L, d["KT"], 2 * L // 128, d["tb"], khat_dst=d["KH"])
        for i, L in enumerate(Ls):
            phs = opts.get("phases", "ABC")
            if "A" in phs:
                phase_A(P, nc, C, L, xs[i])
            if opts.get("s5", True):
                phase_S5main(P, nc, C, L, s5_nstages(L // 8))
                phase_S5post(P, nc, C, L)
            if "B" in phs:
                phase_B(P, nc, C, L)
            last = depth == 1
            if "C" in phs:
                phase_C(P, nc, C, L, 0, xs[i], ps_[i][0], C.ab_w_out[0], scr.H1, ys[i] if last else None, opts.get("s5", True))
            if depth == 2:
                phase_D1(P, nc, C, L)
                phase_D2(P, nc, C, L)
                d = fftc[L]
                phase_F(P, nc, C, L, scr.VX, L // 128, d["tb"], khat_src=d["KH"], yhat_dst=scr.YHAT)
                phase_I(P, nc, C, L, d["tb"], scr.YHAT)
                phase_C(P, nc, C, L, 1, scr.H1, ps_[i][1], C.hy_w_out[0], None, ys[i], False, rs_src=d["RS"])
        C.ninstr = P.ninstr
    return nc, C


def host_consts(LM):
    inv = 1.0 / (10000.0 ** (np.arange(0, 64, 2, dtype=np.float32) / 64.0))
    ang = np.arange(LM, dtype=np.float32)[:, None] * inv[None, :].astype(np.float32)
    cs = np.concatenate([np.cos(ang), np.sin(ang)], axis=1).astype(np.float32)
    out = {"ident": np.eye(128, dtype=np.float32), "rope_cs": cs}
    min_decay = math.log(1e-2) / 1.5
    max_decay = math.log(1e-2) / 0.3
    p_ = np.arange(128)
    rm = np.zeros((128, 4), np.float32)
    rm[:64, 0] = 1.0
    rm[64:, 1] = 1.0
    rm[:, 2] = np.where(p_ < 64, 1.0, -1.0)
    rm[64:, 3] = -1.0
    out["s5_rowmask"] = rm
    ii = p_ // 16
    out["s5_mf"] = (ii[None, :] >= ii[:, None]).astype(np.float32)
    out["s5_mb"] = (ii[None, :] <= ii[:, None]).astype(np.float32)
    sw = np.zeros((128, 128), np.float32)
    sw[p_, (p_ + 64) % 128] = 1.0
    out["s5_swap"] = sw
    out["hy_delta"] = np.abs(np.linspace(min_decay, max_decay, 2048, dtype=np.float32)).astype(np.float32)
    return out


def host_consts_L(L):
    out = {}
    tbs, N1, KL = fft_tables(L)
    for k_, v in tbs.items():
        out[f"{k_}_{L}"] = v
    t = np.linspace(0.0, 1.0, L, dtype=np.float32)[:, None]
    w = (2.0 * math.pi * np.arange(L, dtype=np.float32)[:, None] / L).astype(np.float32)
    bands = np.linspace(1e-4, 15, 16, dtype=np.float32)[None, :]
    z = np.concatenate([t, np.cos(bands * w), -np.sin(bands * w)], axis=-1).astype(np.float32)
    idx = np.concatenate([np.arange(L), np.array([0]), L - np.arange(1, L)])
    z2 = z[idx]
    tl = t[:, 0][idx].copy()
    tl[L] = 1.0e4
    out[f"zT_{L}"] = np.ascontiguousarray(z2.T.astype(np.float32))
    out[f"tl_{L}"] = np.ascontiguousarray(tl.astype(np.float32))
    return out


_CACHE = {}


def kernel(**inputs):
    Ls = [4096, 8192]
    if "nc" not in _CACHE:
        _CACHE["nc"] = build(Ls, {"depth": 2, "s5": True})
    nc, C = _CACHE["nc"]
    consts = host_consts(max(Ls))
    for L_ in Ls:
        consts.update(host_consts_L(L_))
    W = {n: np.ascontiguousarray(np.asarray(inputs[n], dtype=np.float32)) for n in WNAMES}
    xs, xp = np.asarray(inputs["x_sample"]), np.asarray(inputs["x_prompt"])
    psm, ppr = np.asarray(inputs["p_sample"]), np.asarray(inputs["p_prompt"])
    in_maps = []
    for c in range(8):
        m = dict(W)
        m.update(consts)
        m["x0"] = np.ascontiguousarray(xs[c])
        m["p0"] = np.ascontiguousarray(psm[:, c])
        m["x1"] = np.ascontiguousarray(xp[c % 2])
        m["p1"] = np.ascontiguousarray(ppr[:, c % 2])
        in_maps.append(m)
    res = run_bass_kernel_spmd(nc, in_maps, core_ids=list(range(8)))
    y_sample = np.stack([np.asarray(res.results[c]["y0"], dtype=np.float32) for c in range(8)], axis=0)
    y_prompt = np.stack([np.asarray(res.results[c]["y1"], dtype=np.float32) for c in range(2)], axis=0)
    return (y_prompt, y_sample)
```

```python
import math
from contextlib import ExitStack
import numpy as np
import concourse.bass as bass
import concourse.mybir as mybir
from concourse.bass_utils import run_bass_kernel_spmd

F32 = mybir.dt.float32
BF16 = mybir.dt.bfloat16
ALU = mybir.AluOpType
AF = mybir.ActivationFunctionType
AX = mybir.AxisListType

D = 1024
EPS = 1e-6
import os
CUT = float(os.environ.get('CUT', '99'))
CW = int(os.environ.get('CW', '640'))


class Buf:
    def __init__(self, name, accum=False):
        self.name = name
        self.w = {}
        self.r = {}
        self.accum = accum


class Prog:
    ENG = ["pe", "act", "dve", "pool", "sp"]

    def __init__(self, nc, es):
        self.nc = nc
        self.es = es
        self.q = {e: [] for e in self.ENG}
        self.sems = {}
        self.cnt = {}
        self.seen = {e: {} for e in self.ENG}
        ss = os.environ.get("SELF", "act,dve,pool").split(",")
        self.selfsync = {"pe": False, "act": "act" in ss, "dve": "dve" in ss, "pool": "pool" in ss, "sp": False}
        self.bufs = []
        self.ninstr = 0
        self.dma_map = {}
        for e in self.ENG:
            self.newsem(e)

    def newsem(self, key):
        self.sems[key] = self.es.enter_context(self.nc.semaphore("s_" + str(key)))
        self.cnt[key] = 0

    def buf(self, name, accum=False):
        b = Buf(name, accum)
        self.bufs.append(b)
        return b

    def op(self, eng, fn, reads=(), writes=(), dma=None):
        waits = {}

        def need(tok):
            for k, v in tok.items():
                if v > waits.get(k, 0):
                    waits[k] = v

        for b in reads:
            need(b.w)
        raw_self = waits.get(eng, 0)
        for b in writes:
            if not b.accum:
                need(b.w)
            need(b.r)
        wl = []
        for k, v in waits.items():
            if k == eng:
                if not self.selfsync[eng]:
                    continue
            if self.seen[eng].get(k, 0) >= v:
                continue
            self.seen[eng][k] = v
            wl.append((k, v))
        if dma is None:
            key, inc = eng, 1
        else:
            key, inc = dma, 16
            if key not in self.sems:
                self.newsem(key)
        self.cnt[key] += inc
        val = self.cnt[key]
        sems = self.sems

        def emit(e, wl=wl, fn=fn, key=key, inc=inc):
            for k, v in wl:
                e.wait_ge(sems[k], v)
            fn(e).then_inc(sems[key], inc)

        self.q[eng].append(emit)
        self.ninstr += 1
        if os.environ.get("KTRACE"):
            print("OP", self.ninstr, eng, "tok", key, val, "waits", wl, "R", [b.name for b in reads], "W", [b.name for b in writes])
        for b in reads:
            b.r[key] = max(b.r.get(key, 0), val)
        for b in writes:
            if b.accum:
                b.w[key] = max(b.w.get(key, 0), val)
            else:
                b.w = {key: val}
                b.r = {}

    def dma(self, eng, out, in_, reads, writes, key, slow=False):
        if isinstance(key, Buf):
            key = key.name
        key = (eng == "pool", key)
        if key not in self.dma_map:
            n = sum(1 for kk in self.dma_map if kk[0] == key[0])
            self.dma_map[key] = ("dmaG%d" if key[0] else "dmaS%d") % n
        if slow:
            self.op(eng, lambda e: e.dma_start(out=out, in_=in_, allow_slow_non_contiguous=True), reads, writes, dma=self.dma_map[key])
        else:
            self.op(eng, lambda e: e.dma_start(out=out, in_=in_), reads, writes, dma=self.dma_map[key])

    def sync_dram(self, b):
        return b

    def barrier(self):
        snap = dict(self.cnt)
        sems = self.sems
        for e in self.ENG:
            wl = []
            for k, v in snap.items():
                if v > 0 and self.seen[e].get(k, 0) < v:
                    self.seen[e][k] = v
                    wl.append((k, v))

            def emit(eh, wl=wl):
                for k, v in wl:
                    eh.wait_ge(sems[k], v)

            self.q[e].append(emit)
        self.dma_map = {}
        for b in self.bufs:
            b.w = {}
            b.r = {}
        self.bufs = [b for b in self.bufs if getattr(b, "persist", False)]

    def flush(self, block):
        q = self.q

        @block.tensor
        def _(e):
            for f in q["pe"]:
                f(e)

        @block.scalar
        def _(e):
            for f in q["act"]:
                f(e)

        @block.vector
        def _(e):
            for f in q["dve"]:
                f(e)

        @block.gpsimd
        def _(e):
            for f in q["pool"]:
                f(e)

        @block.sync
        def _(e):
            for f in q["sp"]:
                f(e)

        self.q = {e: [] for e in self.ENG}


class Rot:
    def __init__(self, P, alloc, name, shape, dtype, n):
        self.slots = []
        for i in range(n):
            t = alloc(f"{name}{i}", shape, dtype)
            self.slots.append((t, P.buf(f"{name}{i}")))
        self.i = 0

    def next(self):
        s = self.slots[self.i % len(self.slots)]
        self.i += 1
        return s


class Ctx:
    pass


_PH = [0]


_ONLY = [None]


def _phase(P, nc, body):
    if _ONLY[0] is not None and body.__qualname__.split(".")[0] not in _ONLY[0]:
        return
    _PH[0] += 1
    sfx = "_%d" % _PH[0]
    with ExitStack() as ph, nc.Block() as block:
        tot = [0]

        def sb(name, shape, dtype):
            n = 1
            for d in shape[1:]:
                n *= d
            tot[0] += ((n * (2 if dtype == BF16 else 4) + 31) // 32) * 32
            return ph.enter_context(nc.sbuf_tensor(name + sfx, shape, dtype))

        def ps(name, shape, dtype):
            return ph.enter_context(nc.psum_tensor(name + sfx, shape, dtype))

        body(sb, ps)
        if os.environ.get("KDEBUG"):
            print("phase", sfx, body.__qualname__, "sbuf bytes/partition", tot[0], "instr", P.ninstr)
        assert tot[0] <= 206 * 1024, tot[0]
        P.barrier()
        P.flush(block)


def load_weight_bf16(P, sb, wdst, wbuf, wsrc, K, N, stage_rot, rowscale=None, chunk=1024):
    i = 0
    for k in range(K):
        for c0 in range(0, N, chunk):
            c1 = min(N, c0 + chunk)
            st, stb = stage_rot.next()
            P.dma("sp", st[:, 0:c1 - c0], wsrc[k * 128:(k + 1) * 128, c0:c1], [], [stb], stb)
            eng = "dve" if i % 2 == 0 else "pool"
            if rowscale is None:
                P.op(eng, lambda e, st=st, k=k, c0=c0, c1=c1: e.tensor_copy(out=wdst[:, k, c0:c1], in_=st[:, 0:c1 - c0]), [stb], [wbuf])
            else:
                rs, rsb = rowscale
                P.op(eng, lambda e, st=st, k=k, c0=c0, c1=c1, rs=rs: e.tensor_scalar(out=wdst[:, k, c0:c1], in0=st[:, 0:c1 - c0], scalar1=rs[:, k:k + 1], scalar2=None, op0=ALU.mult), [stb, rsb], [wbuf])
            i += 1


def rstd_from_ssq(P, eng, rstd, ssq, n, R, W):
    P.op(eng, lambda e: e.tensor_scalar(out=rstd, in0=ssq, scalar1=1.0 / n, scalar2=EPS, op0=ALU.mult, op1=ALU.add), R, W)
    P.op("act", lambda e: e.activation(out=rstd, in_=rstd, func=AF.Sqrt), W, W)
    P.op(eng, lambda e: e.reciprocal(out=rstd, in_=rstd), W, W)


def phase_A(P, nc, C, L, x_ap):
    NT = L // 128
    scr = C.scr

    def body(sb, ps):
        ident = sb("identA", [128, 128], BF16)
        identf = sb("identAf", [128, 128], F32)
        bid = P.buf("ident")
        P.dma("sp", identf[:], C.ident[:, :], [], [bid], bid)
        P.op("dve", lambda e: e.tensor_copy(out=ident[:], in_=identf[:]), [bid], [bid])
        ng = sb("ngA", [128, 8], F32)
        qn = sb("qnA", [128, 3], F32)
        kvn = sb("kvnA", [128, 2], F32)
        bsm = P.buf("smallA")
        P.dma("sp", ng[:], C.norm_g[0].rearrange("(k p) -> p k", p=128), [], [bsm], bsm, slow=True)
        P.dma("sp", qn[:], C.mla_q_norm[0].rearrange("(k p) -> p k", p=128), [], [bsm], bsm, slow=True)
        P.dma("sp", kvn[:], C.mla_kv_norm[0].rearrange("(k p) -> p k", p=128), [], [bsm], bsm, slow=True)
        stage = Rot(P, sb, "wstA", [128, 1024], F32, 2)
        w_in = sb("w_inA", [128, 8, 3776], BF16)
        w_q = sb("w_qA", [128, 3, 1536], BF16)
        w_kv = sb("w_kvA", [128, 2, 2048], BF16)
        bw_in, bw_q, bw_kv = P.buf("w_in"), P.buf("w_q"), P.buf("w_kv")
        load_weight_bf16(P, sb, w_in, bw_in, C.ab_w_in[0], 8, 3776, stage, rowscale=(ng, bsm))
        load_weight_bf16(P, sb, w_q, bw_q, C.mla_w_q_up[0], 3, 1536, stage, rowscale=(qn, bsm))
        load_weight_bf16(P, sb, w_kv, bw_kv, C.mla_w_kv_up[0], 2, 2048, stage, rowscale=(kvn, bsm))

        xt = Rot(P, sb, "xtA", [128, 1024], F32, 2)
        cst = Rot(P, sb, "csA", [128, 64], F32, 2)
        junk = sb("junkA", [128, 1024], BF16)
        bjunk = P.buf("junkA")
        stat = Rot(P, sb, "statA", [128, 8], F32, 2)
        hn = Rot(P, sb, "hnA", [128, 1024], BF16, 2)
        hnT = Rot(P, sb, "hnTA", [128, 8, 128], BF16, 2)
        u_sb = Rot(P, sb, "uA", [128, 1024], BF16, 2)
        g_sb = Rot(P, sb, "gA", [128, 2048], BF16, 2)
        lat = Rot(P, sb, "latA", [128, 640], BF16, 2)
        latT = Rot(P, sb, "latTA", [128, int(os.environ.get("LATK", "5")), 128], BF16, 2)
        kr32 = Rot(P, sb, "kr32A", [128, 64], F32, 2)
        q_sb = Rot(P, sb, "qA", [128, 1536], BF16, 2)
        qtmp = Rot(P, sb, "qtmpA", [128, 4, 256], F32, 2)
        kn_sb = Rot(P, sb, "knA", [128, 1024], BF16, 2)
        v_sb = Rot(P, sb, "vA", [128, 1024], BF16, 2)
        kr_sb = Rot(P, sb, "krA", [128, 64], BF16, 2)
        qnT = Rot(P, sb, "qnTA", [128, 8, 256], BF16, 2)
        qrT = Rot(P, sb, "qrTA", [64, 8, 256], BF16, 2)
        knT = Rot(P, sb, "knTA", [128, 8, 256], BF16, 2)
        krT = Rot(P, sb, "krTA", [64, 256], BF16, 2)
        zp = Rot(P, ps, "zpA", [128, 512], F32, 2)
        tp = Rot(P, ps, "tpA", [128, 1024], BF16, 2)
        tq = Rot(P, ps, "tqA", [128, 1024], BF16, 2)
        qp = Rot(P, ps, "qpA", [128, 512], F32, 2)

        def in_proj_group(hT, hTb, c0, c1):
            z, zb = zp.next()
            for k in range(8):
                P.op("pe", lambda e, z=z, k=k: e.matmul(z[:, 0:c1 - c0], lhsT=hT[:, k, :], rhs=w_in[:, k, c0:c1], start=(k == 0), stop=(k == 7)), [hTb, bw_in], [zb])
            return z, zb

        cur = None
        for t in range(NT if CUT > 1 else 0):
            t4 = t % 2
            if t4 == 0:
                cur = (qnT.next(), qrT.next(), knT.next(), krT.next())
            (qnT_t, qnT_b), (qrT_t, qrT_b), (knT_t, knT_b), (krT_t, krT_b) = cur
            x, xb = xt.next()
            cs, csb = cst.next()
            P.dma("sp", x[:], x_ap[t * 128:(t + 1) * 128, :], [], [xb], xb)
            P.dma("sp", cs[:], C.rope_cs[t * 128:(t + 1) * 128, :], [], [csb], csb)
            st, stb = stat.next()
            P.op("act", lambda e, x=x, st=st: e.activation(out=junk[:], in_=x[:], func=AF.Square, accum_out=st[:, 0:1]), [xb], [bjunk, stb])
            rstd_from_ssq(P, "dve", st[:, 1:2], st[:, 0:1], D, [stb], [stb])
            h, hb = hn.next()
            P.op("act", lambda e, x=x, st=st, h=h: e.activation(out=h[:], in_=x[:], func=AF.Copy, scale=st[:, 1:2]), [xb, stb], [hb])
            tpp, tpb = tp.next()
            for k in range(8):
                P.op("pe", lambda e, k=k, tpp=tpp, h=h: e.transpose(out=tpp[:, k * 128:(k + 1) * 128], in_=h[:, k * 128:(k + 1) * 128], identity=ident[:]), [hb, bid], [tpb])
            hT, hTb = hnT.next()
            P.op("act", lambda e, hT=hT, tpp=tpp: e.activation(out=hT[:].rearrange("p k t -> p (k t)"), in_=tpp[:], func=AF.Copy), [tpb], [hTb])
            u, ub = u_sb.next()
            for gi in range(2):
                z, zb = in_proj_group(hT, hTb, gi * 512, gi * 512 + 512)
                P.op("act", lambda e, z=z, u=u, gi=gi: e.activation(out=u[:, gi * 512:(gi + 1) * 512], in_=z[:], func=AF.Copy), [zb], [ub])
            P.dma("pool", scr.U[t * 128:(t + 1) * 128, :], u[:], [ub], [scr.bU], ub)
            if CUT <= 2:
                continue
            la, lab = lat.next()
            z, zb = in_proj_group(hT, hTb, 1024, 1408)
            P.op("act", lambda e, z=z, st=st: e.activation(out=junk[:, 0:384], in_=z[:, 0:384], func=AF.Square, accum_out=st[:, 2:3]), [zb], [bjunk, stb])
            rstd_from_ssq(P, "dve", st[:, 3:4], st[:, 2:3], 384, [stb], [stb])
            P.op("act", lambda e, z=z, st=st, la=la: e.activation(out=la[:, 0:384], in_=z[:, 0:384], func=AF.Copy, scale=st[:, 3:4]), [zb, stb], [lab])
            z, zb = in_proj_group(hT, hTb, 1408, 1728)
            P.op("act", lambda e, z=z, st=st: e.activation(out=junk[:, 0:256], in_=z[:, 0:256], func=AF.Square, accum_out=st[:, 4:5]), [zb], [bjunk, stb])
            rstd_from_ssq(P, "dve", st[:, 5:6], st[:, 4:5], 256, [stb], [stb])
            P.op("act", lambda e, z=z, st=st, la=la: e.activation(out=la[:, 384:640], in_=z[:, 0:256], func=AF.Copy, scale=st[:, 5:6]), [zb, stb], [lab])
            k32, k32b = kr32.next()
            P.op("act", lambda e, z=z, k32=k32: e.activation(out=k32[:], in_=z[:, 256:320], func=AF.Copy), [zb], [k32b])
            kr, krb = kr_sb.next()
            qt, qtb = qtmp.next()
            P.op("pool", lambda e, k32=k32, cs=cs, qt=qt: e.tensor_tensor(out=qt[:, 0, 0:32], in0=k32[:, 0:32], in1=cs[:, 0:32], op=ALU.mult), [k32b, csb], [qtb])
            P.op("pool", lambda e, k32=k32, cs=cs, qt=qt: e.tensor_tensor(out=qt[:, 0, 32:64], in0=k32[:, 32:64], in1=cs[:, 32:64], op=ALU.mult), [k32b, csb], [qtb])
            P.op("pool", lambda e, kr=kr, qt=qt: e.tensor_tensor(out=kr[:, 0:32], in0=qt[:, 0, 0:32], in1=qt[:, 0, 32:64], op=ALU.subtract), [qtb], [krb])
            P.op("pool", lambda e, k32=k32, cs=cs, qt=qt: e.tensor_tensor(out=qt[:, 0, 0:32], in0=k32[:, 0:32], in1=cs[:, 32:64], op=ALU.mult), [k32b, csb, krb], [qtb])
            P.op("pool", lambda e, k32=k32, cs=cs, qt=qt: e.tensor_tensor(out=qt[:, 0, 32:64], in0=k32[:, 32:64], in1=cs[:, 0:32], op=ALU.mult), [k32b, csb], [qtb])
            P.op("pool", lambda e, kr=kr, qt=qt: e.tensor_tensor(out=kr[:, 32:64], in0=qt[:, 0, 0:32], in1=qt[:, 0, 32:64], op=ALU.add), [qtb], [krb])
            if CUT <= 3:
                continue
            g, gb = g_sb.next()
            for gi in range(4):
                z, zb = in_proj_group(hT, hTb, 1728 + gi * 512, 1728 + gi * 512 + 512)
                P.op("act", lambda e, z=z, g=g, gi=gi: e.activation(out=g[:, gi * 512:(gi + 1) * 512], in_=z[:], func=AF.Silu), [zb], [gb])
            P.dma("pool", scr.G[t * 128:(t + 1) * 128, :], g[:], [gb], [scr.bG], gb)
            if CUT <= 4:
                continue
            tqq, tqb = tq.next()
            for k in range(int(os.environ.get("NK", "5"))):
                P.op("pe", lambda e, k=k, tqq=tqq, la=la: e.transpose(out=tqq[:, k * 128:(k + 1) * 128], in_=(h if os.environ.get("SRCH") else la)[:, k * 128:(k + 1) * 128], identity=ident[:]), [lab, bid, hb], [tqb])
            lT, lTb = latT.next()
            if not os.environ.get("NOCOPY"):
                if True:
                    P.op("act", lambda e, lT=lT, tqq=tqq: e.activation(out=lT[:].rearrange("p k t -> p (k t)"), in_=tqq[:, 0:640], func=AF.Copy), [tqb], [lTb])
                else:
                    if os.environ.get("JUNKDST"):
                        P.op("dve", lambda e, lT=lT, tqq=tqq: e.tensor_copy(out=junk[:, 0:CW], in_=tqq[:, 0:CW]), [tqb], [bjunk])
                    elif os.environ.get("TSCOPY"):
                        P.op("dve", lambda e, lT=lT, tqq=tqq: e.tensor_scalar(out=lT[:, 0:CW // 128, :].rearrange("p k t -> p (k t)"), in0=tqq[:, 0:CW], scalar1=1.0, scalar2=None, op0=ALU.mult), [tqb], [lTb])
                    else:
                        P.op("dve", lambda e, lT=lT, tqq=tqq: e.tensor_copy(out=lT[:, 0:CW // 128, :].rearrange("p k t -> p (k t)"), in_=tqq[:, 0:CW]), [tqb], [lTb])
            if CUT <= 4.1:
                continue
            q, qb = q_sb.next()
            for gi in range(3):
                pq, pqb = qp.next()
                for k in range(3):
                    P.op("pe", lambda e, pq=pq, k=k, gi=gi, lT=lT: e.matmul(pq[:], lhsT=lT[:, k, :], rhs=w_q[:, k, gi * 512:(gi + 1) * 512], start=(k == 0), stop=(k == 2)), [lTb, bw_q], [pqb])
                P.op("act", lambda e, pq=pq, q=q, gi=gi: e.activation(out=q[:, gi * 512:(gi + 1) * 512], in_=pq[:], func=AF.Copy), [pqb], [qb])
            if CUT <= 4.3:
                continue
            qv = q[:].rearrange("p (h d) -> p h d", h=8)
            x1 = qv[:, :, 128:160]
            x2 = qv[:, :, 160:192]
            cosb = cs[:, 0:32].unsqueeze(1).to_broadcast([128, 8, 32])
            sinb = cs[:, 32:64].unsqueeze(1).to_broadcast([128, 8, 32])
            qt, qtb = qtmp.next()
            a_ = qt[:, 0, :].rearrange("p (h d) -> p h d", h=8)
            b_ = qt[:, 1, :].rearrange("p (h d) -> p h d", h=8)
            c_ = qt[:, 2, :].rearrange("p (h d) -> p h d", h=8)
            d_ = qt[:, 3, :].rearrange("p (h d) -> p h d", h=8)
            P.op("pool", lambda e, a_=a_, x1=x1, cosb=cosb: e.tensor_tensor(out=a_, in0=x1, in1=cosb, op=ALU.mult), [qb, csb], [qtb])
            P.op("pool", lambda e, b_=b_, x2=x2, sinb=sinb: e.tensor_tensor(out=b_, in0=x2, in1=sinb, op=ALU.mult), [qb, csb], [qtb])
            P.op("pool", lambda e, c_=c_, x1=x1, sinb=sinb: e.tensor_tensor(out=c_, in0=x1, in1=sinb, op=ALU.mult), [qb, csb], [qtb])
            P.op("pool", lambda e, d_=d_, x2=x2, cosb=cosb: e.tensor_tensor(out=d_, in0=x2, in1=cosb, op=ALU.mult), [qb, csb], [qtb])
            P.op("pool", lambda e, a_=a_, b_=b_, x1=x1: e.tensor_tensor(out=x1, in0=a_, in1=b_, op=ALU.subtract), [qtb], [qb])
            P.op("pool", lambda e, c_=c_, d_=d_, x2=x2: e.tensor_tensor(out=x2, in0=c_, in1=d_, op=ALU.add), [qtb], [qb])
            if CUT <= 4.6:
                continue
            kn, knb = kn_sb.next()
            v, vb = v_sb.next()
            for gi in range(4):
                pq, pqb = qp.next()
                for k in range(2):
                    P.op("pe", lambda e, pq=pq, k=k, gi=gi, lT=lT: e.matmul(pq[:], lhsT=lT[:, 3 + k, :], rhs=w_kv[:, k, gi * 512:(gi + 1) * 512], start=(k == 0), stop=(k == 1)), [lTb, bw_kv], [pqb])
                pv = pq[:].rearrange("p (h d) -> p h d", h=2)
                P.op("act", lambda e, pv=pv, kn=kn, gi=gi: e.activation(out=kn[:, gi * 256:(gi + 1) * 256].rearrange("p (h d) -> p h d", h=2), in_=pv[:, :, 0:128], func=AF.Copy), [pqb], [knb])
                if os.environ.get("VMODE", "act") == "dve":
                    P.op("dve", lambda e, pv=pv, v=v, gi=gi: e.tensor_copy(out=v[:, gi * 256:(gi + 1) * 256].rearrange("p (h d) -> p h d", h=2), in_=pv[:, :, 128:256]), [pqb], [vb])
                else:
                    P.op("act", lambda e, pv=pv, v=v, gi=gi: e.activation(out=v[:, gi * 256:(gi + 1) * 256].rearrange("p (h d) -> p h d", h=2), in_=pv[:, :, 128:256], func=AF.Copy), [pqb], [vb])
            if os.environ.get("VMODE", "act") != "none":
                P.dma("pool", scr.V[t * 128:(t + 1) * 128, :], v[:], [vb], [scr.bV], vb)
            if CUT <= 5:
                continue
            tqq, tqb = tq.next()
            for hh in range(8):
                P.op("pe", lambda e, hh=hh, tqq=tqq, q=q: e.transpose(out=tqq[:, hh * 128:(hh + 1) * 128], in_=q[:, hh * 192:hh * 192 + 128], identity=ident[:]), [qb, bid], [tqb])
            P.op("act", lambda e, tqq=tqq, qnT_t=qnT_t, t4=t4: e.activation(out=qnT_t[:, :, t4 * 128:(t4 + 1) * 128], in_=tqq[:].rearrange("p (h t) -> p h t", h=8), func=AF.Copy), [tqb], [qnT_b])
            tqq, tqb = tq.next()
            for hh in range(8):
                P.op("pe", lambda e, hh=hh, tqq=tqq, q=q: e.transpose(out=tqq[0:64, hh * 128:(hh + 1) * 128], in_=q[:, hh * 192 + 128:hh * 192 + 192], identity=ident[:]), [qb, bid], [tqb])
            P.op("act", lambda e, tqq=tqq, qrT_t=qrT_t, t4=t4: e.activation(out=qrT_t[:, :, t4 * 128:(t4 + 1) * 128], in_=tqq[0:64, :].rearrange("p (h t) -> p h t", h=8), func=AF.Copy), [tqb], [qrT_b])
            tqq, tqb = tq.next()
            for hh in range(8):
                P.op("pe", lambda e, hh=hh, tqq=tqq, kn=kn: e.transpose(out=tqq[:, hh * 128:(hh + 1) * 128], in_=kn[:, hh * 128:(hh + 1) * 128], identity=ident[:]), [knb, bid], [tqb])
            P.op("act", lambda e, tqq=tqq, knT_t=knT_t, t4=t4: e.activation(out=knT_t[:, :, t4 * 128:(t4 + 1) * 128], in_=tqq[:].rearrange("p (h t) -> p h t", h=8), func=AF.Copy), [tqb], [knT_b])
            tqq, tqb = tq.next()
            P.op("pe", lambda e, tqq=tqq, kr=kr: e.transpose(out=tqq[0:64, 0:128], in_=kr[:, 0:64], identity=ident[:]), [krb, bid], [tqb])
            P.op("act", lambda e, tqq=tqq, krT_t=krT_t, t4=t4: e.activation(out=krT_t[:, t4 * 128:(t4 + 1) * 128], in_=tqq[0:64, 0:128], func=AF.Copy), [tqb], [krT_b])
            if t4 == 1 or t == NT - 1:
                nt = (t4 + 1) * 128
                t0 = (t - t4) * 128
                P.dma("pool", scr.QN[:, :, t0:t0 + nt].rearrange("h d t -> d h t"), qnT_t[:, :, 0:nt], [qnT_b], [scr.bQ], qnT_b)
                P.dma("pool", scr.QR[:, :, t0:t0 + nt].rearrange("h d t -> d h t"), qrT_t[:, :, 0:nt], [qrT_b], [scr.bQ], qrT_b)
                P.dma("pool", scr.KN[:, :, t0:t0 + nt].rearrange("h d t -> d h t"), knT_t[:, :, 0:nt], [knT_b], [scr.bK], knT_b)
                P.dma("pool", scr.KR[:, t0:t0 + nt], krT_t[:, 0:nt], [krT_b], [scr.bK], krT_b)

    _phase(P, nc, body)


def phase_B(P, nc, C, L):
    NKB = L // 128
    NQB = L // 512
    scr = C.scr
    scale = 192.0 ** -0.5

    def body(sb, ps):
        krt = sb("krB", [64, L], BF16)
        bkr = P.buf("krB")
        P.dma("sp", krt[:], scr.KR[:, 0:L], [], [bkr], bkr)
        qn = Rot(P, sb, "qnB", [128, L], BF16, 2)
        qr = Rot(P, sb, "qrB", [64, L], BF16, 2)
        kn = Rot(P, sb, "knB", [128, L], BF16, 2)
        vt = Rot(P, sb, "vtB", [128, NKB, 129], BF16, 2)
        for (v, vb) in vt.slots:
            P.op("pool", lambda e, v=v: e.memset(v[:, :, 128:129], 1.0), [], [vb])
        pT = Rot(P, sb, "pTB", [128, 1024], BF16, 3)
        osb = Rot(P, sb, "osbB", [128, 4, 128], BF16, 2)
        rc = Rot(P, sb, "rcB", [128, 4], F32, 2)
        sp_ = Rot(P, ps, "spB", [128, 1024], F32, 2)
        oa = Rot(P, ps, "oaB", [128, 512], F32, 2)
        ob = Rot(P, ps, "obB", [128, 512], F32, 2)
        for h in range(8):
            qn_t, qn_b = qn.next()
            qr_t, qr_b = qr.next()
            kn_t, kn_b = kn.next()
            v_t, v_b = vt.next()
            P.dma("sp", qn_t[:], scr.QN[h, :, 0:L], [], [qn_b], qn_b)
            P.dma("sp", qr_t[:], scr.QR[h, :, 0:L], [], [qr_b], qr_b)
            P.dma("sp", kn_t[:], scr.KN[h, :, 0:L], [], [kn_b], kn_b)
            P.dma("sp", v_t[:, :, 0:128], scr.V[0:L, h * 128:(h + 1) * 128].rearrange("(kb p) d -> p kb d", p=128), [], [v_b], v_b)
            for qb in range(NQB):
                oa_t, oa_b = oa.next()
                ob_t, ob_b = ob.next()
                def emit_qk(kp, qb=qb, kn_t=kn_t, qn_t=qn_t, qr_t=qr_t, kn_b=kn_b, qn_b=qn_b, qr_b=qr_b):
                    s_t, s_b = sp_.next()
                    for hh in range(2):
                        kb = kp * 2 + hh
                        P.op("pe", lambda e, s_t=s_t, kb=kb, hh=hh: e.matmul(s_t[:, hh * 512:(hh + 1) * 512], lhsT=kn_t[:, kb * 128:(kb + 1) * 128], rhs=qn_t[:, qb * 512:(qb + 1) * 512], start=True, stop=False), [kn_b, qn_b], [s_b])
                        P.op("pe", lambda e, s_t=s_t, kb=kb, hh=hh: e.matmul(s_t[:, hh * 512:(hh + 1) * 512], lhsT=krt[:, kb * 128:(kb + 1) * 128], rhs=qr_t[:, qb * 512:(qb + 1) * 512], start=False, stop=True), [bkr, qr_b], [s_b])
                    return s_t, s_b

                NKP = NKB // 2
                pend = [emit_qk(0)]
                for kp in range(NKP):
                    if kp + 1 < NKP:
                        pend.append(emit_qk(kp + 1))
                    s_t, s_b = pend.pop(0)
                    p_t, p_b = pT.next()
                    P.op("act", lambda e, s_t=s_t, p_t=p_t: e.activation(out=p_t[:], in_=s_t[:], func=AF.Exp, scale=scale), [s_b], [p_b])
                    for hh in range(2):
                        kb = kp * 2 + hh
                        for sub in range(4):
                            acc_t, acc_b = (oa_t, oa_b) if sub < 2 else (ob_t, ob_b)
                            c0 = (sub % 2) * 129
                            P.op("pe", lambda e, acc_t=acc_t, c0=c0, p_t=p_t, sub=sub, v_t=v_t, kb=kb, hh=hh: e.matmul(acc_t[:, c0:c0 + 129], lhsT=p_t[:, hh * 512 + sub * 128:hh * 512 + (sub + 1) * 128], rhs=v_t[:, kb, :], start=(kb == 0), stop=(kb == NKB - 1), skip_group_check=True), [p_b, v_b], [acc_b])
                o_t, o_b = osb.next()
                r_t, r_b = rc.next()
                for sub in range(4):
                    acc_t, acc_b = (oa_t, oa_b) if sub < 2 else (ob_t, ob_b)
                    c0 = (sub % 2) * 129
                    P.op("act", lambda e, r_t=r_t, acc_t=acc_t, c0=c0, sub=sub: e.activation(out=r_t[:, sub:sub + 1], in_=acc_t[:, c0 + 128:c0 + 129], func=AF.Copy), [acc_b], [r_b])
                    P.op("dve", lambda e, r_t=r_t, sub=sub: e.reciprocal(out=r_t[:, sub:sub + 1], in_=r_t[:, sub:sub + 1]), [r_b], [r_b])
                    P.op("act", lambda e, r_t=r_t, acc_t=acc_t, c0=c0, sub=sub, o_t=o_t: e.activation(out=o_t[:, sub, :], in_=acc_t[:, c0:c0 + 128], func=AF.Copy, scale=r_t[:, sub:sub + 1]), [acc_b, r_b], [o_b])
                P.dma("pool", scr.O[qb * 512:(qb + 1) * 512, h * 128:(h + 1) * 128].rearrange("(s p) d -> p s d", p=128), o_t[:], [o_b], [scr.bO], o_b)

    _phase(P, nc, body)


def phase_C(P, nc, C, L, layer, h_in, p_ap, w_out_ap, h_out, final_out, use_s5, rs_src=None):
    NT = L // 128
    scr = C.scr

    def body(sb, ps):
        ident = sb("identC", [128, 128], BF16)
        identf = sb("identCf", [128, 128], F32)
        bid = P.buf("identC")
        P.dma("sp", identf[:], C.ident[:, :], [], [bid], bid)
        P.op("dve", lambda e: e.tensor_copy(out=ident[:], in_=identf[:]), [bid], [bid])
        stage = Rot(P, sb, "wstC", [128, 1024], F32, 2)
        w_out = sb("w_outC", [128, 16, 1024], BF16)
        w_pg = sb("w_pgC", [128, 8, 1024], BF16)
        w_pl = sb("w_plC", [128, 2, 1024], BF16)
        bw_out, bw_pg, bw_pl = P.buf("w_outC"), P.buf("w_pgC"), P.buf("w_plC")
        load_weight_bf16(P, sb, w_out, bw_out, w_out_ap, 16, 1024, stage)
        load_weight_bf16(P, sb, w_pg, bw_pg, C.ple_gate_w[layer], 8, 1024, stage)
        load_weight_bf16(P, sb, w_pl, bw_pl, C.ple_w[layer], 2, 1024, stage)
        if final_out is not None:
            fg = sb("fgC", [128, 1024], F32)
            bfg = P.buf("fgC")
            P.dma("sp", fg[:], C.final_g.partition_broadcast(128), [], [bfg], bfg, slow=True)
        ht = Rot(P, sb, "htC", [128, 1024], F32, 2)
        if layer == 0:
            gt = Rot(P, sb, "gtC", [128, 2048], BF16, 2)
            yt = Rot(P, sb, "ytC", [128, 2048], BF16, 2)
        else:
            cvt = Rot(P, sb, "cvtC", [128, 2048], BF16, 2)
            vxt = Rot(P, sb, "vxtC", [128, 2048], BF16, 2)
            xgt = Rot(P, sb, "xgtC", [128, 2048], BF16, 2)
            rsr = sb("rsrC", [128, 2048], F32)
            bir = sb("birC", [128, 2048], F32)
            brr = P.buf("rsrC")
            P.dma("sp", rsr[:], rs_src[0].partition_broadcast(128), [], [brr], brr, slow=True)
            P.dma("sp", bir[:], C.hy_bias[0].partition_broadcast(128), [], [brr], brr, slow=True)
            tm1 = sb("tm1C", [128, 2048], F32)
            tm2 = sb("tm2C", [128, 2048], F32)
            btm1, btm2 = P.buf("tm1C"), P.buf("tm2C")
        pt = Rot(P, sb, "ptC", [128, 256], F32, 2)
        pb16 = Rot(P, sb, "pb16C", [128, 256], BF16, 2)
        mt = Rot(P, sb, "mtC", [128, 2048], BF16, 2)
        mT = Rot(P, sb, "mTC", [128, 16, 128], BF16, 2)
        h2 = Rot(P, sb, "h2C", [128, 1024], F32, 2)
        h2b = Rot(P, sb, "h2bC", [128, 1024], BF16, 2)
        h2T = Rot(P, sb, "h2TC", [128, 8, 128], BF16, 2)
        pT = Rot(P, sb, "pTC", [128, 2, 128], BF16, 2)
        sg = Rot(P, sb, "sgC", [128, 1024], F32, 2)
        h3 = Rot(P, sb, "h3C", [128, 1024], F32, 2)
        stat = Rot(P, sb, "statC", [128, 4], F32, 2)
        junk = sb("junkC", [128, 1024], BF16)
        bjunk = P.buf("junkC")
        tp = Rot(P, ps, "tpC", [128, 1024], BF16, 2)
        zp = Rot(P, ps, "zpC", [128, 512], F32, 4)
        def part1(t):
            rows = slice(t * 128, (t + 1) * 128)
            h_t, h_b = ht.next()
            if layer == 0:
                g_t, g_b = gt.next()
                y_t, y_b = yt.next()
            p_t, p_b = pt.next()
            P.dma("sp", h_t[:], h_in[rows, :], [], [h_b], h_b)
            P.dma("sp", p_t[:], p_ap[rows, :], [], [p_b], p_b)
            if layer == 0:
                P.dma("sp", g_t[:], scr.G[rows, :], [], [g_b], g_b)
                if use_s5:
                    P.dma("sp", y_t[:, 0:1024], scr.YA[rows, :], [], [y_b], y_b)
                else:
                    P.op("pool", lambda e, y_t=y_t: e.memset(y_t[:, 0:1024], 0.0), [], [y_b])
                P.dma("sp", y_t[:, 1024:2048], scr.O[rows, :], [], [y_b], y_b)
            m_t, m_b = mt.next()
            mT_t, mT_b = mT.next()
            if layer == 0:
                P.op("dve", lambda e, m_t=m_t, y_t=y_t, g_t=g_t: e.tensor_tensor(out=m_t[:], in0=y_t[:], in1=g_t[:], op=ALU.mult), [y_b, g_b], [m_b])
            else:
                cv_t, cv_b = cvt.next()
                vx_t, vx_b = vxt.next()
                xg_t, xg_b = xgt.next()
                P.dma("sp", cv_t[:], scr.CV[rows, :], [], [cv_b], cv_b)
                P.dma("sp", vx_t[:], scr.VX[rows, :], [], [vx_b], vx_b)
                P.dma("sp", xg_t[:], scr.XG[rows, :], [], [xg_b], xg_b)
                P.op("pool", lambda e, cv_t=cv_t: e.tensor_tensor(out=tm1[:], in0=cv_t[:], in1=rsr[:], op=ALU.mult), [cv_b, brr], [btm1])
                P.op("dve", lambda e, vx_t=vx_t: e.tensor_tensor(out=tm2[:], in0=vx_t[:], in1=bir[:], op=ALU.mult), [vx_b, brr], [btm2])
                P.op("pool", lambda e: e.tensor_tensor(out=tm1[:], in0=tm1[:], in1=tm2[:], op=ALU.add), [btm1, btm2], [btm1])
                P.op("dve", lambda e, m_t=m_t, xg_t=xg_t: e.tensor_tensor(out=m_t[:], in0=tm1[:], in1=xg_t[:], op=ALU.mult), [btm1, xg_b], [m_b])
            for half in range(2):
                tpp, tpb = tp.next()
                for k in range(8):
                    kk = half * 8 + k
                    P.op("pe", lambda e, tpp=tpp, k=k, kk=kk, m_t=m_t: e.transpose(out=tpp[:, k * 128:(k + 1) * 128], in_=m_t[:, kk * 128:(kk + 1) * 128], identity=ident[:]), [m_b, bid], [tpb])
                P.op("act", lambda e, tpp=tpp, mT_t=mT_t, half=half: e.activation(out=mT_t[:, half * 8:(half + 1) * 8, :].rearrange("p k t -> p (k t)"), in_=tpp[:], func=AF.Copy), [tpb], [mT_b])
            h2_t, h2_b = h2.next()
            for gi in range(2):
                z, zb = zp.next()
                for k in range(16):
                    P.op("pe", lambda e, z=z, k=k, gi=gi, mT_t=mT_t: e.matmul(z[:], lhsT=mT_t[:, k, :], rhs=w_out[:, k, gi * 512:(gi + 1) * 512], start=(k == 0), stop=(k == 15)), [mT_b, bw_out], [zb])
                P.op("act", lambda e, z=z, gi=gi, h2_t=h2_t: e.activation(out=h2_t[:, gi * 512:(gi + 1) * 512], in_=z[:], func=AF.Copy), [zb], [h2_b])
                P.op("pool", lambda e, gi=gi, h2_t=h2_t, h_t=h_t: e.tensor_tensor(out=h2_t[:, gi * 512:(gi + 1) * 512], in0=h2_t[:, gi * 512:(gi + 1) * 512], in1=h_t[:, gi * 512:(gi + 1) * 512], op=ALU.add), [h2_b, h_b], [h2_b])
            return dict(rows=rows, h2_t=h2_t, h2_b=h2_b, p_t=p_t, p_b=p_b)

        def part2(t, st_):
            rows, h2_t, h2_b, p_t, p_b = st_["rows"], st_["h2_t"], st_["h2_b"], st_["p_t"], st_["p_b"]
            hb_t, hb_b = h2b.next()
            P.op("act", lambda e, hb_t=hb_t, h2_t=h2_t: e.activation(out=hb_t[:], in_=h2_t[:], func=AF.Copy), [h2_b], [hb_b])
            tpp, tpb = tp.next()
            for k in range(8):
                P.op("pe", lambda e, tpp=tpp, k=k, hb_t=hb_t: e.transpose(out=tpp[:, k * 128:(k + 1) * 128], in_=hb_t[:, k * 128:(k + 1) * 128], identity=ident[:]), [hb_b, bid], [tpb])
            hT_t, hT_b = h2T.next()
            P.op("act", lambda e, tpp=tpp, hT_t=hT_t: e.activation(out=hT_t[:].rearrange("p k t -> p (k t)"), in_=tpp[:], func=AF.Copy), [tpb], [hT_b])
            pb_t, pb_b = pb16.next()
            P.op("act", lambda e, pb_t=pb_t, p_t=p_t: e.activation(out=pb_t[:], in_=p_t[:], func=AF.Copy), [p_b], [pb_b])
            tpp, tpb = tp.next()
            for k in range(2):
                P.op("pe", lambda e, tpp=tpp, k=k, pb_t=pb_t: e.transpose(out=tpp[:, k * 128:(k + 1) * 128], in_=pb_t[:, k * 128:(k + 1) * 128], identity=ident[:]), [pb_b, bid], [tpb])
            pT_t, pT_b = pT.next()
            P.op("act", lambda e, tpp=tpp, pT_t=pT_t: e.activation(out=pT_t[:].rearrange("p k t -> p (k t)"), in_=tpp[:, 0:256], func=AF.Copy), [tpb], [pT_b])
            sg_t, sg_b = sg.next()
            h3_t, h3_b = h3.next()
            for gi in range(2):
                z, zb = zp.next()
                for k in range(8):
                    P.op("pe", lambda e, z=z, k=k, gi=gi, hT_t=hT_t: e.matmul(z[:], lhsT=hT_t[:, k, :], rhs=w_pg[:, k, gi * 512:(gi + 1) * 512], start=(k == 0), stop=(k == 7)), [hT_b, bw_pg], [zb])
                P.op("act", lambda e, z=z, gi=gi, sg_t=sg_t: e.activation(out=sg_t[:, gi * 512:(gi + 1) * 512], in_=z[:], func=AF.Sigmoid), [zb], [sg_b])
                z2, z2b = zp.next()
                for k in range(2):
                    P.op("pe", lambda e, z2=z2, k=k, gi=gi, pT_t=pT_t: e.matmul(z2[:], lhsT=pT_t[:, k, :], rhs=w_pl[:, k, gi * 512:(gi + 1) * 512], start=(k == 0), stop=(k == 1)), [pT_b, bw_pl], [z2b])
                P.op("act", lambda e, z2=z2, gi=gi, h3_t=h3_t: e.activation(out=h3_t[:, gi * 512:(gi + 1) * 512], in_=z2[:], func=AF.Copy), [z2b], [h3_b])
                P.op("dve", lambda e, gi=gi, sg_t=sg_t, h3_t=h3_t: e.tensor_tensor(out=sg_t[:, gi * 512:(gi + 1) * 512], in0=sg_t[:, gi * 512:(gi + 1) * 512], in1=h3_t[:, gi * 512:(gi + 1) * 512], op=ALU.mult), [h3_b, sg_b], [sg_b])
            P.op("pool", lambda e, h3_t=h3_t, sg_t=sg_t, h2_t=h2_t: e.tensor_tensor(out=h3_t[:], in0=sg_t[:], in1=h2_t[:], op=ALU.add), [sg_b, h2_b], [h3_b])
            if final_out is None:
                P.dma("pool", h_out[rows, :], h3_t[:], [h3_b], [scr.bH], h3_b)
            else:
                st, stb = stat.next()
                P.op("act", lambda e, h3_t=h3_t, st=st: e.activation(out=junk[:], in_=h3_t[:], func=AF.Square, accum_out=st[:, 0:1]), [h3_b], [bjunk, stb])
                rstd_from_ssq(P, "dve", st[:, 1:2], st[:, 0:1], D, [stb], [stb])
                P.op("dve", lambda e, h3_t=h3_t, st=st, sg_t=sg_t: e.scalar_tensor_tensor(out=sg_t[:], in0=h3_t[:], scalar=st[:, 1:2], in1=fg[:], op0=ALU.mult, op1=ALU.mult), [h3_b, stb, bfg], [sg_b])
                P.dma("pool", final_out[rows, :], sg_t[:], [sg_b], [scr.bH], sg_b)

        prev = None
        for t in range(NT):
            cur = part1(t)
            if prev is not None:
                part2(t - 1, prev)
            prev = cur
        part2(NT - 1, prev)

    _phase(P, nc, body)


def phase_D1(P, nc, C, L):
    NT = L // 128
    scr = C.scr

    def body(sb, ps):
        ident = sb("identD", [128, 128], BF16)
        identf = sb("identDf", [128, 128], F32)
        bid = P.buf("identD")
        P.dma("sp", identf[:], C.ident[:, :], [], [bid], bid)
        P.op("dve", lambda e: e.tensor_copy(out=ident[:], in_=identf[:]), [bid], [bid])
        zc = sb("zcD", [128, 8, 2], BF16)
        bzc = P.buf("zcD")
        P.op("pool", lambda e: e.memset(zc[:], 0.0), [], [bzc])
        P.dma("pool", scr.HT[:, :, 0:1].rearrange("k f t -> f k t"), zc[:, :, 0:1], [bzc], [scr.bH], bzc, slow=True)
        P.dma("pool", scr.HT[:, :, L + 1:L + 2].rearrange("k f t -> f k t"), zc[:, :, 1:2], [bzc], [scr.bH], bzc, slow=True)
        xt = Rot(P, sb, "xtD", [128, 1024], F32, 2)
        junk = sb("junkD", [128, 1024], BF16)
        bjunk = P.buf("junkD")
        stat = Rot(P, sb, "statD", [128, 4], F32, 2)
        hn = Rot(P, sb, "hnD", [128, 1024], BF16, 2)
        hT4 = Rot(P, sb, "hT4D", [128, 8, 512], BF16, 2)
        tp = Rot(P, ps, "tpD", [128, 1024], BF16, 2)
        cur = None
        for t in range(NT):
            t4 = t % 4
            if t4 == 0:
                cur = hT4.next()
            h4, h4b = cur
            x, xb = xt.next()
            P.dma("sp", x[:], scr.H1[t * 128:(t + 1) * 128, :], [], [xb], xb)
            st, stb = stat.next()
            P.op("act", lambda e, x=x, st=st: e.activation(out=junk[:], in_=x[:], func=AF.Square, accum_out=st[:, 0:1]), [xb], [bjunk, stb])
            rstd_from_ssq(P, "dve", st[:, 1:2], st[:, 0:1], D, [stb], [stb])
            h, hb = hn.next()
            P.op("act", lambda e, x=x, st=st, h=h: e.activation(out=h[:], in_=x[:], func=AF.Copy, scale=st[:, 1:2]), [xb, stb], [hb])
            tpp, tpb = tp.next()
            for k in range(8):
                P.op("pe", lambda e, k=k, tpp=tpp, h=h: e.transpose(out=tpp[:, k * 128:(k + 1) * 128], in_=h[:, k * 128:(k + 1) * 128], identity=ident[:]), [hb, bid], [tpb])
            P.op("act", lambda e, tpp=tpp, h4=h4, t4=t4: e.activation(out=h4[:, :, t4 * 128:(t4 + 1) * 128], in_=tpp[:].rearrange("p (k t) -> p k t", k=8), func=AF.Copy), [tpb], [h4b])
            if t4 == 3 or t == NT - 1:
                nt = (t4 + 1) * 128
                t0 = (t - t4) * 128
                P.dma("pool", scr.HT[:, :, 1 + t0:1 + t0 + nt].rearrange("k f t -> f k t"), h4[:, :, 0:nt], [h4b], [scr.bH], h4b)

    _phase(P, nc, body)


def phase_D2(P, nc, C, L):
    scr = C.scr
    TB = 256
    NB = L // TB

    def body(sb, ps):
        ng = sb("ngD", [128, 8], F32)
        bsm = P.buf("smallD")
        P.dma("sp", ng[:], C.norm_g[1].rearrange("(k p) -> p k", p=128), [], [bsm], bsm, slow=True)
        cw = sb("cwD", [128, 3, 48], F32)
        cb = sb("cbD", [128, 48], F32)
        hb_ = sb("hbD", [128, 16], F32)
        P.dma("sp", cw[:], C.hy_conv_w[0].rearrange("j (i p) -> p j i", p=128), [], [bsm], bsm, slow=True)
        P.dma("sp", cb[:], C.hy_conv_b[0].rearrange("(i p) -> p i", p=128), [], [bsm], bsm, slow=True)
        P.dma("sp", hb_[:], C.hy_bias[0].rearrange("(i p) -> p i", p=128), [], [bsm], bsm, slow=True)
        stage = Rot(P, sb, "wstD", [128, 1024], F32, 2)
        w_in = sb("w_inD", [128, 8, 8192], BF16)
        bw_in = P.buf("w_inD")
        load_weight_bf16(P, sb, w_in, bw_in, C.hy_w_in[0], 8, 8192, stage, rowscale=(ng, bsm))
        hT = Rot(P, sb, "hTD", [128, 8, 258], BF16, 2)
        zs = Rot(P, sb, "zsD", [128, 258], F32, 4)
        uc = Rot(P, sb, "ucD", [128, 3, 256], F32, 2)
        sg = Rot(P, sb, "sgD", [128, 256], F32, 2)
        ident = sb("identD2", [128, 128], BF16)
        identf = sb("identD2f", [128, 128], F32)
        bid = P.buf("identD2")
        P.dma("sp", identf[:], C.ident[:, :], [], [bid], bid)
        P.op("dve", lambda e: e.tensor_copy(out=ident[:], in_=identf[:]), [bid], [bid])
        mo = Rot(P, sb, "moD", [128, 2, 256], BF16, 4)
        stg2 = Rot(P, sb, "stg2D", [128, 2, 2, 2048], BF16, 1)
        zp = Rot(P, ps, "zpD", [128, 512], F32, 4)
        tp = Rot(P, ps, "tpD2", [128, 1024], BF16, 2)
        def do_block(b):
            s0 = b * TB
            n_out = min(TB, L - s0)
            n_in = n_out + 2
            h_t, h_b = hT.next()
            st2, st2b = stg2.next()
            pending = []
            P.dma("sp", h_t[:, :, 0:n_in], scr.HT[:, :, s0:s0 + n_in].rearrange("k f t -> f k t"), [], [h_b], h_b)
            for i in range(16):
                u_t, u_b = uc.next()
                for part in range(4):
                    ch = part * 16 + i
                    z, zb = zp.next()
                    for k in range(8):
                        P.op("pe", lambda e, z=z, k=k, ch=ch, h_t=h_t: e.matmul(z[:, 0:n_in], lhsT=w_in[:, k, ch * 128:(ch + 1) * 128], rhs=h_t[:, k, 0:n_in], start=(k == 0), stop=(k == 7)), [h_b, bw_in], [zb])
                    if part < 3:
                        zs_t, zs_b = zs.next()
                        P.op("act", lambda e, z=z, zs_t=zs_t: e.activation(out=zs_t[:, 0:n_in], in_=z[:, 0:n_in], func=AF.Copy), [zb], [zs_b])
                        eng = "pool" if part != 1 else "dve"
                        P.op(eng, lambda e, zs_t=zs_t, u_t=u_t, part=part, ch=ch: e.tensor_scalar(out=u_t[:, part, 0:n_out], in0=zs_t[:, 0:n_out], scalar1=cw[:, 0, ch:ch + 1], scalar2=cb[:, ch:ch + 1], op0=ALU.mult, op1=ALU.add), [zs_b, bsm], [u_b])
                        P.op("dve", lambda e, zs_t=zs_t, u_t=u_t, part=part, ch=ch: e.scalar_tensor_tensor(out=u_t[:, part, 0:n_out], in0=zs_t[:, 1:1 + n_out], scalar=cw[:, 1, ch:ch + 1], in1=u_t[:, part, 0:n_out], op0=ALU.mult, op1=ALU.add), [zs_b, bsm, u_b], [u_b])
                        P.op("dve", lambda e, zs_t=zs_t, u_t=u_t, part=part, ch=ch: e.scalar_tensor_tensor(out=u_t[:, part, 0:n_out], in0=zs_t[:, 2:2 + n_out], scalar=cw[:, 2, ch:ch + 1], in1=u_t[:, part, 0:n_out], op0=ALU.mult, op1=ALU.add), [zs_b, bsm, u_b], [u_b])
                    else:
                        sg_t, sg_b = sg.next()
                        P.op("act", lambda e, z=z, sg_t=sg_t: e.activation(out=sg_t[:, 0:n_out], in_=z[:, 1:1 + n_out], func=AF.Silu), [zb], [sg_b])
                m_t, m_b = mo.next()
                P.op("pool", lambda e, u_t=u_t, m_t=m_t: e.tensor_tensor(out=m_t[:, 0, :], in0=u_t[:, 2, 0:n_out], in1=u_t[:, 1, 0:n_out], op=ALU.mult), [u_b], [m_b])
                P.op("pool", lambda e, u_t=u_t, sg_t=sg_t, m_t=m_t: e.tensor_tensor(out=m_t[:, 1, :], in0=u_t[:, 0, 0:n_out], in1=sg_t[:, 0:n_out], op=ALU.mult), [u_b, sg_b], [m_b])
                def finish(i=i, m_t=m_t, m_b=m_b):
                    tpp, tpb = tp.next()
                    for q in range(2):
                        for tl_ in range(2):
                            P.op("pe", lambda e, tpp=tpp, q=q, tl_=tl_: e.transpose(out=tpp[:, (q * 2 + tl_) * 128:(q * 2 + tl_ + 1) * 128], in_=m_t[:, q, tl_ * 128:(tl_ + 1) * 128], identity=ident[:]), [m_b, bid], [tpb])
                    P.op("act", lambda e, tpp=tpp: e.activation(out=st2[:, :, :, i * 128:(i + 1) * 128], in_=tpp[:, 0:512].rearrange("p (q t c) -> p q t c", q=2, t=2), func=AF.Copy), [tpb], [st2b])

                pending.append(finish)
                if len(pending) > 1:
                    pending.pop(0)()
            while pending:
                pending.pop(0)()
            for tl_ in range(2):
                r0 = s0 + tl_ * 128
                P.dma("sp", scr.VX[r0:r0 + 128, :], st2[:, 0, tl_, :], [st2b], [scr.bV], st2b)
                P.dma("sp", scr.XG[r0:r0 + 128, :], st2[:, 1, tl_, :], [st2b], [scr.bG], st2b)

        for b in range(NB):
            do_block(b)

    _phase(P, nc, body)


def fft_tables(L):
    N = 2 * L
    N1 = N // 128
    KL = L // 128
    k = np.arange(N1)[:, None].astype(np.float64)
    f1 = np.arange(N1)[None, :].astype(np.float64)
    ang1 = 2 * np.pi * k * f1 / N1
    p = np.arange(128).astype(np.float64)
    f2 = np.arange(128).astype(np.float64)
    f1v = np.arange(N1).astype(np.float64)
    angE = 2 * np.pi * p[:, None, None] * (f1v[None, :, None] + N1 * f2[None, None, :]) / N
    angE2 = 2 * np.pi * p[None, None, :] * (f1v[None, :, None] + N1 * f2[:, None, None]) / N
    t = {}
    t["s1c"] = np.cos(ang1)
    t["s1s"] = -np.sin(ang1)
    t["ec"] = np.cos(angE).reshape(128, N1 * 128)
    t["es"] = np.sin(angE).reshape(128, N1 * 128)
    t["e2c"] = np.cos(angE2).reshape(128, N1 * 128)
    t["e2s"] = np.sin(angE2).reshape(128, N1 * 128)
    t["i1c"] = np.cos(ang1).T[:, :KL] / N
    t["i1s"] = -np.sin(ang1).T[:, :KL] / N
    return {k_: np.ascontiguousarray(v.astype(np.float32)) for k_, v in t.items()}, N1, KL


def fft_tables_shapes(L):
    N1 = 2 * L // 128
    KL = L // 128
    return {"s1c": (N1, N1), "s1s": (N1, N1), "ec": (128, N1 * 128), "es": (128, N1 * 128),
            "e2c": (128, N1 * 128), "e2s": (128, N1 * 128), "i1c": (N1, KL), "i1s": (N1, KL)}, N1, KL


def load_table_bf16(P, sb, name, src, rows, cols, stage):
    t = sb(name, [rows, cols], BF16)
    b = P.buf(name)
    for c0 in range(0, cols, 1024):
        c1 = min(cols, c0 + 1024)
        st, stb = stage.next()
        P.dma("sp", st[0:rows, 0:c1 - c0], src[0:rows, c0:c1], [], [stb], stb)
        P.op("pool", lambda e, st=st, c0=c0, c1=c1: e.tensor_copy(out=t[:, c0:c1], in_=st[0:rows, 0:c1 - c0]), [stb], [b])
    return t, b


def phase_F(P, nc, C, L, src, KS, tb, khat_dst=None, khat_src=None, yhat_dst=None):
    N1 = 2 * L // 128
    scr = C.scr
    FC = min(4, N1)

    def body(sb, ps):
        stage = Rot(P, sb, "wstF", [128, 1024], F32, 2)
        s1c, bs1c = load_table_bf16(P, sb, "s1cF", tb["s1c"], KS, N1, stage)
        s1s, bs1s = load_table_bf16(P, sb, "s1sF", tb["s1s"], KS, N1, stage)
        ec, bec = load_table_bf16(P, sb, "ecF", tb["ec"], 128, N1 * 128, stage)
        es, bes = load_table_bf16(P, sb, "esF", tb["es"], 128, N1 * 128, stage)
        X = Rot(P, sb, "XF", [KS, 128 * 128], BF16, 2)
        Ast = Rot(P, sb, "AstF", [N1, 2, 512], BF16, 3)
        Bt = Rot(P, sb, "BtF", [128, 3, FC, 128], BF16, 3)
        Xs = Rot(P, sb, "XsF", [128, 2, FC * 128], F32, 2)
        Kh = Rot(P, sb, "KhF", [128, 2, FC * 128], F32, 3)
        Tm = Rot(P, sb, "TmF", [128, 4, FC * 128], F32, 2)
        Yo = Rot(P, sb, "YoF", [128, 2, FC * 128], BF16, 2)
        pa = Rot(P, ps, "paF", [128, 512], F32, 4)
        px = Rot(P, ps, "pxF", [128, 512], F32, 4)
        def load_x(s_):
            x_t, x_b = X.next()
            P.dma("sp", x_t[:].rearrange("k (p c) -> k p c", c=128), src[0:KS * 128, s_ * 128:(s_ + 1) * 128].rearrange("(k p) c -> k p c", p=128), [], [x_b], x_b)
            return x_t, x_b

        NFC = N1 // FC
        W = FC * 128

        def load_chunk(s_, fc):
            b_t, b_b = Bt.next()
            for r_ in range(2):
                P.dma("sp", b_t[:, r_, :, :], scr.AT[r_, fc * FC:(fc + 1) * FC, :, :].rearrange("f p c -> p f c"), [scr.bA], [b_b], b_b)
            kh = None
            if khat_dst is None:
                kh_t, kh_b = Kh.next()
                P.dma("sp", kh_t[:].rearrange("f r (g c) -> f r g c", c=128), khat_src[s_, :, :, fc * FC:(fc + 1) * FC, :].rearrange("r f g c -> f r g c"), [], [kh_b], kh_b)
                kh = (kh_t, kh_b)
            return (b_t, b_b, kh)

        nxt_x = load_x(0)
        for s in range(16):
            x_t, x_b = nxt_x
            if s + 1 < 16:
                nxt_x = load_x(s + 1)
            for cb in range(32):
                a_t, a_b = Ast.next()
                for ri, (tab, tabb) in enumerate(((s1c, bs1c), (s1s, bs1s))):
                    z, zb = pa.next()
                    P.op("pe", lambda e, z=z, tab=tab, x_t=x_t, cb=cb: e.matmul(z[0:N1, :], lhsT=tab[:, :], rhs=x_t[:, cb * 512:(cb + 1) * 512], start=True, stop=True), [x_b, tabb], [zb])
                    P.op("act", lambda e, z=z, a_t=a_t, ri=ri: e.activation(out=a_t[:, ri, :], in_=z[0:N1, :], func=AF.Copy), [zb], [a_b])
                P.dma("sp", scr.AT[:, 0:N1, cb * 4:(cb + 1) * 4, :].rearrange("r f p c -> f r p c"), a_t[:].rearrange("f r (p c) -> f r p c", c=128), [a_b], [scr.bA], a_b)
            nxt_c = load_chunk(s, 0)
            for fc in range(NFC):
                b_t, b_b, kh = nxt_c
                if fc + 1 < NFC:
                    nxt_c = load_chunk(s, fc + 1)
                P.op("dve", lambda e, b_t=b_t: e.tensor_scalar(out=b_t[:, 2, :, :], in0=b_t[:, 0, :, :], scalar1=-1.0, scalar2=None, op0=ALU.mult), [b_b], [b_b])
                zr, zrb = px.next()
                zi, zib = px.next()
                for j in range(FC):
                    f1 = fc * FC + j
                    P.op("pe", lambda e, zr=zr, j=j, f1=f1, b_t=b_t: e.matmul(zr[:, j * 128:(j + 1) * 128], lhsT=ec[:, f1 * 128:(f1 + 1) * 128], rhs=b_t[:, 0, j, :], start=True, stop=False, skip_group_check=True), [b_b, bec], [zrb])
                    P.op("pe", lambda e, zr=zr, j=j, f1=f1, b_t=b_t: e.matmul(zr[:, j * 128:(j + 1) * 128], lhsT=es[:, f1 * 128:(f1 + 1) * 128], rhs=b_t[:, 1, j, :], start=False, stop=True, skip_group_check=True), [b_b, bes], [zrb])
                    P.op("pe", lambda e, zi=zi, j=j, f1=f1, b_t=b_t: e.matmul(zi[:, j * 128:(j + 1) * 128], lhsT=ec[:, f1 * 128:(f1 + 1) * 128], rhs=b_t[:, 1, j, :], start=True, stop=False, skip_group_check=True), [b_b, bec], [zib])
                    P.op("pe", lambda e, zi=zi, j=j, f1=f1, b_t=b_t: e.matmul(zi[:, j * 128:(j + 1) * 128], lhsT=es[:, f1 * 128:(f1 + 1) * 128], rhs=b_t[:, 2, j, :], start=False, stop=True, skip_group_check=True), [b_b, bes], [zib])
                xs_t, xs_b = Xs.next()
                P.op("act", lambda e, zr=zr, xs_t=xs_t: e.activation(out=xs_t[:, 0, :], in_=zr[:, 0:W], func=AF.Copy), [zrb], [xs_b])
                P.op("act", lambda e, zi=zi, xs_t=xs_t: e.activation(out=xs_t[:, 1, :], in_=zi[:, 0:W], func=AF.Copy), [zib], [xs_b])
                if khat_dst is not None:
                    P.dma("sp", khat_dst[s, :, :, fc * FC:(fc + 1) * FC, :].rearrange("r f g c -> f r g c"), xs_t[:].rearrange("f r (g c) -> f r g c", c=128), [xs_b], [scr.bK], xs_b)
                else:
                    kh_t, kh_b = kh
                    tm, tmb = Tm.next()
                    yo, yob = Yo.next()
                    P.op("dve", lambda e, tm=tm, xs_t=xs_t, kh_t=kh_t: e.tensor_tensor(out=tm[:, 0, :], in0=xs_t[:, 0, :], in1=kh_t[:, 0, :], op=ALU.mult), [xs_b, kh_b], [tmb])
                    P.op("dve", lambda e, tm=tm, xs_t=xs_t, kh_t=kh_t: e.tensor_tensor(out=tm[:, 1, :], in0=xs_t[:, 1, :], in1=kh_t[:, 1, :], op=ALU.mult), [xs_b, kh_b], [tmb])
                    P.op("pool", lambda e, tm=tm, xs_t=xs_t, kh_t=kh_t: e.tensor_tensor(out=tm[:, 2, :], in0=xs_t[:, 0, :], in1=kh_t[:, 1, :], op=ALU.mult), [xs_b, kh_b], [tmb])
                    P.op("dve", lambda e, tm=tm, xs_t=xs_t, kh_t=kh_t: e.tensor_tensor(out=tm[:, 3, :], in0=xs_t[:, 1, :], in1=kh_t[:, 0, :], op=ALU.mult), [xs_b, kh_b], [tmb])
                    P.op("dve", lambda e, tm=tm, yo=yo: e.tensor_tensor(out=yo[:, 0, :], in0=tm[:, 0, :], in1=tm[:, 1, :], op=ALU.subtract), [tmb], [yob])
                    P.op("pool", lambda e, tm=tm, yo=yo: e.tensor_tensor(out=yo[:, 1, :], in0=tm[:, 2, :], in1=tm[:, 3, :], op=ALU.add), [tmb], [yob])
                    P.dma("sp", yhat_dst[s, :, :, fc * FC:(fc + 1) * FC, :].rearrange("r f g c -> f r g c"), yo[:].rearrange("f r (g c) -> f r g c", c=128), [yob], [scr.bQ], yob)

    _phase(P, nc, body)


def phase_I(P, nc, C, L, tb, yhat_src):
    N1 = 2 * L // 128
    KL = L // 128
    scr = C.scr
    FC = min(4, N1)
    PC = 4

    def body(sb, ps):
        stage = Rot(P, sb, "wstI", [128, 1024], F32, 2)
        i1c, bi1c = load_table_bf16(P, sb, "i1cI", tb["i1c"], N1, KL, stage)
        i1s, bi1s = load_table_bf16(P, sb, "i1sI", tb["i1s"], N1, KL, stage)
        e2c, be2c = load_table_bf16(P, sb, "e2cI", tb["e2c"], 128, N1 * 128, stage)
        e2s, be2s = load_table_bf16(P, sb, "e2sI", tb["e2s"], 128, N1 * 128, stage)
        Yt = Rot(P, sb, "YtI", [128, 3, FC, 128], BF16, 3)
        Zst = Rot(P, sb, "ZstI", [128, 2, FC * 128], BF16, 3)
        Zt = Rot(P, sb, "ZtI", [N1, 2, PC * 128], BF16, 4)
        Ot = Rot(P, sb, "OtI", [KL, 16 * 512], BF16, 2)
        pz = Rot(P, ps, "pzI", [128, 512], F32, 4)
        po = Rot(P, ps, "poI", [128, 512], F32, 2)
        W = FC * 128
        NFC = N1 // FC
        NPC = 128 // PC

        def load_y(s_, fc):
            y_t, y_b = Yt.next()
            P.dma("sp", y_t[:, 0:2, :, :], yhat_src[s_, :, :, fc * FC:(fc + 1) * FC, :].rearrange("r f g c -> f r g c"), [], [y_b], y_b)
            return y_t, y_b

        def load_z(pc):
            zz, zzb = Zt.next()
            for r_ in range(2):
                P.dma("sp", zz[:, r_, :].rearrange("f (p c) -> f p c", c=128), scr.ZT[r_, pc * PC:(pc + 1) * PC, 0:N1, :].rearrange("p f c -> f p c"), [scr.bA], [zzb], zzb)
            return zz, zzb

        for s in range(16):
            nxt_y = load_y(s, 0)
            for fc in range(NFC):
                y_t, y_b = nxt_y
                if fc + 1 < NFC:
                    nxt_y = load_y(s, fc + 1)
                P.op("dve", lambda e, y_t=y_t: e.tensor_scalar(out=y_t[:, 2, :, :], in0=y_t[:, 1, :, :], scalar1=-1.0, scalar2=None, op0=ALU.mult), [y_b], [y_b])
                zr, zrb = pz.next()
                zi, zib = pz.next()
                for j in range(FC):
                    f1 = fc * FC + j
                    P.op("pe", lambda e, zr=zr, j=j, f1=f1, y_t=y_t: e.matmul(zr[:, j * 128:(j + 1) * 128], lhsT=e2c[:, f1 * 128:(f1 + 1) * 128], rhs=y_t[:, 0, j, :], start=True, stop=False, skip_group_check=True), [y_b, be2c], [zrb])
                    P.op("pe", lambda e, zr=zr, j=j, f1=f1, y_t=y_t: e.matmul(zr[:, j * 128:(j + 1) * 128], lhsT=e2s[:, f1 * 128:(f1 + 1) * 128], rhs=y_t[:, 2, j, :], start=False, stop=True, skip_group_check=True), [y_b, be2s], [zrb])
                    P.op("pe", lambda e, zi=zi, j=j, f1=f1, y_t=y_t: e.matmul(zi[:, j * 128:(j + 1) * 128], lhsT=e2s[:, f1 * 128:(f1 + 1) * 128], rhs=y_t[:, 0, j, :], start=True, stop=False, skip_group_check=True), [y_b, be2s], [zib])
                    P.op("pe", lambda e, zi=zi, j=j, f1=f1, y_t=y_t: e.matmul(zi[:, j * 128:(j + 1) * 128], lhsT=e2c[:, f1 * 128:(f1 + 1) * 128], rhs=y_t[:, 1, j, :], start=False, stop=True, skip_group_check=True), [y_b, be2c], [zib])
                z_t, z_b = Zst.next()
                P.op("act", lambda e, zr=zr, z_t=z_t: e.activation(out=z_t[:, 0, :], in_=zr[:, 0:W], func=AF.Copy), [zrb], [z_b])
                P.op("act", lambda e, zi=zi, z_t=z_t: e.activation(out=z_t[:, 1, :], in_=zi[:, 0:W], func=AF.Copy), [zib], [z_b])
                P.dma("sp", scr.ZT[:, :, fc * FC:(fc + 1) * FC, :].rearrange("r p f c -> p r f c"), z_t[:].rearrange("p r (f c) -> p r f c", c=128), [z_b], [scr.bA], z_b)
            o_t, o_b = Ot.next()
            nxt_z = load_z(0)
            for pc in range(NPC):
                zz, zzb = nxt_z
                if pc + 1 < NPC:
                    nxt_z = load_z(pc + 1)
                o, ob = po.next()
                P.op("pe", lambda e, o=o, zz=zz: e.matmul(o[0:KL, :], lhsT=i1c[:, :], rhs=zz[:, 0, :], start=True, stop=False), [zzb, bi1c], [ob])
                P.op("pe", lambda e, o=o, zz=zz: e.matmul(o[0:KL, :], lhsT=i1s[:, :], rhs=zz[:, 1, :], start=False, stop=True), [zzb, bi1s], [ob])
                half = pc % 16
                P.op("act", lambda e, o=o, o_t=o_t, half=half: e.activation(out=o_t[:, half * 512:(half + 1) * 512], in_=o[0:KL, :], func=AF.Copy), [ob], [o_b])
                if half == 15:
                    p0 = (pc - 15) * PC
                    P.dma("sp", scr.CV[0:L, s * 128:(s + 1) * 128].rearrange("(k p) c -> k p c", p=128)[:, p0:p0 + 64, :], o_t[:].rearrange("k (p c) -> k p c", c=128), [o_b], [scr.bO], o_b)
                    if pc != NPC - 1:
                        o_t, o_b = Ot.next()

    _phase(P, nc, body)


def phase_G(P, nc, C, L, zT, tl, ktwo_dst, rs_dst):
    NT2 = 2 * L // 128
    scr = C.scr
    PI = math.pi

    def body(sb, ps):
        w1 = sb("w1G", [33, 2, 64], BF16)
        w2 = sb("w2G", [64, 2, 64], BF16)
        w3 = sb("w3G", [64, 2, 2048], BF16)
        stg = sb("stgG", [64, 2, 2048], F32)
        sm = sb("smG", [64, 2, 8], F32)
        bw = P.buf("wG")
        for d in range(2):
            P.dma("sp", stg[0:33, d, 0:64], C.hy_f_w1[0, d], [], [bw], bw)
        P.op("pool", lambda e: e.tensor_copy(out=w1[:], in_=stg[0:33, :, 0:64]), [bw], [bw])
        for d in range(2):
            P.dma("sp", stg[0:64, d, 64:128], C.hy_f_w2[0, d], [], [bw], bw)
        P.op("pool", lambda e: e.tensor_copy(out=w2[:], in_=stg[0:64, :, 64:128]), [bw], [bw])
        for i, src in enumerate((C.hy_f_b1, C.hy_f_freq1, C.hy_f_b2, C.hy_f_freq2)):
            for d in range(2):
                P.dma("sp", sm[:, d, i:i + 1], src[0, d].rearrange("(o u) -> o u", u=1), [], [bw], bw, slow=True)
        for (bi, fi, oi) in ((0, 1, 4), (2, 3, 5)):
            P.op("pool", lambda e, bi=bi, fi=fi, oi=oi: e.tensor_tensor(out=sm[:, :, oi:oi + 1], in0=sm[:, :, bi:bi + 1], in1=sm[:, :, fi:fi + 1], op=ALU.mult), [bw], [bw])
        bw3 = P.buf("w3G")
        for d in range(2):
            P.dma("sp", stg[:, d, :], C.hy_f_w3[0, d], [bw], [bw3], bw3)
        P.op("pool", lambda e: e.tensor_copy(out=w3[:], in_=stg[:]), [bw3], [bw3])
        negpi = sb("negpiG", [128, 1], F32)
        P.op("pool", lambda e: e.memset(negpi[:], -PI), [], [bw])
        ones = sb("onesG", [128, 128], BF16)
        P.op("pool", lambda e: e.memset(ones[:], 1.0), [], [bw])
        dl = sb("dlG", [128, 2048], F32)
        bdl = P.buf("dlG")
        P.dma("sp", dl[:], C.hy_delta.partition_broadcast(128), [], [bdl], bdl, slow=True)
        tt = sb("ttG", [128, NT2], F32)
        P.dma("sp", tt[:], tl.rearrange("(k p) -> p k", p=128), [], [bdl], bdl, slow=True)
        P.op("pool", lambda e: e.tensor_scalar(out=tt[:], in0=tt[:], scalar1=-1.0, scalar2=None, op0=ALU.mult), [bdl], [bdl])
        zt = Rot(P, sb, "ztG", [33, 512], F32, 2)
        ztb = Rot(P, sb, "ztbG", [33, 512], BF16, 2)
        v1 = Rot(P, sb, "v1G", [64, 512], F32, 2)
        ni = Rot(P, sb, "niG", [64, 512], mybir.dt.int32, 2)
        nf = Rot(P, sb, "nfG", [64, 512], F32, 2)

        def range_reduce(P, v, vb, ni_s, nf_s):
            n_i, nib = ni_s
            n_f, nfb = nf_s
            P.op("dve", lambda e: e.tensor_scalar(out=n_f[:], in0=v[:], scalar1=1.0 / (2 * PI), scalar2=None, op0=ALU.mult), [vb], [nfb])
            P.op("dve", lambda e: e.tensor_copy(out=n_i[:], in_=n_f[:]), [nfb], [nib])
            P.op("dve", lambda e: e.tensor_copy(out=n_f[:], in_=n_i[:]), [nib], [nfb])
            P.op("dve", lambda e: e.scalar_tensor_tensor(out=v[:], in0=n_f[:], scalar=-2 * PI, in1=v[:], op0=ALU.mult, op1=ALU.add), [nfb, vb], [vb])
            P.op("dve", lambda e: e.tensor_scalar(out=n_f[:], in0=v[:], scalar1=PI, scalar2=None, op0=ALU.is_gt), [vb], [nfb])
            P.op("dve", lambda e: e.scalar_tensor_tensor(out=v[:], in0=n_f[:], scalar=-2 * PI, in1=v[:], op0=ALU.mult, op1=ALU.add), [nfb, vb], [vb])
            P.op("dve", lambda e: e.tensor_scalar(out=n_f[:], in0=v[:], scalar1=-PI, scalar2=None, op0=ALU.is_lt), [vb], [nfb])
            P.op("dve", lambda e: e.scalar_tensor_tensor(out=v[:], in0=n_f[:], scalar=2 * PI, in1=v[:], op0=ALU.mult, op1=ALU.add), [nfb, vb], [vb])
        h1 = Rot(P, sb, "h1G", [64, 512], BF16, 2)
        h2 = Rot(P, sb, "h2G", [64, 512], BF16, 2)
        dec = Rot(P, sb, "decG", [128, 2048], F32, 2)
        kf = Rot(P, sb, "kfG", [128, 2048], F32, 2)
        kb16 = Rot(P, sb, "kbG", [128, 2048], BF16, 2)
        sq = Rot(P, sb, "sqG", [128, 2048], BF16, 2)
        pm = Rot(P, ps, "pmG", [128, 512], F32, 2)
        pk = Rot(P, ps, "pkG", [128, 512], F32, 2)
        pss = [ps("pssG%d" % i, [128, 512], F32) for i in range(4)]
        bss = P.buf("pssG")
        for blk in range(2 * L // 512):
            d = 0 if blk * 512 < L else 1
            z_t, z_b = zt.next()
            P.dma("sp", z_t[:], zT[:, blk * 512:(blk + 1) * 512], [], [z_b], z_b)
            zb_t, zb_b = ztb.next()
            P.op("pool", lambda e, z_t=z_t, zb_t=zb_t: e.tensor_copy(out=zb_t[:], in_=z_t[:]), [z_b], [zb_b])
            m, mb = pm.next()
            P.op("pe", lambda e, m=m, zb_t=zb_t, d=d: e.matmul(m[0:64, :], lhsT=w1[:, d, :], rhs=zb_t[:], start=True, stop=True), [zb_b, bw], [mb])
            v, vb = v1.next()
            P.op("act", lambda e, m=m, v=v, d=d: e.activation(out=v[:], in_=m[0:64, :], func=AF.Identity, scale=sm[:, d, 1:2], bias=sm[:, d, 4:5]), [mb, bw], [vb])
            range_reduce(P, v, vb, ni.next(), nf.next())
            h_1, h1b = h1.next()
            P.op("act", lambda e, v=v, h_1=h_1: e.activation(out=h_1[:], in_=v[:], func=AF.Sin), [vb], [h1b])
            m, mb = pm.next()
            P.op("pe", lambda e, m=m, h_1=h_1, d=d: e.matmul(m[0:64, :], lhsT=w2[:, d, :], rhs=h_1[:], start=True, stop=True), [h1b, bw], [mb])
            v, vb = v1.next()
            P.op("act", lambda e, m=m, v=v, d=d: e.activation(out=v[:], in_=m[0:64, :], func=AF.Identity, scale=sm[:, d, 3:4], bias=sm[:, d, 5:6]), [mb, bw], [vb])
            range_reduce(P, v, vb, ni.next(), nf.next())
            h_2, h2b = h2.next()
            P.op("act", lambda e, v=v, h_2=h_2: e.activation(out=h_2[:], in_=v[:], func=AF.Sin), [vb], [h2b])
            for ti in range(4):
                tile_i = blk * 4 + ti
                dc, dcb = dec.next()
                P.op("act", lambda e, dc=dc, tile_i=tile_i: e.activation(out=dc[:], in_=dl[:], func=AF.Exp, scale=tt[:, tile_i:tile_i + 1]), [bdl], [dcb])
                k_t, k_b = kf.next()
                for gi in range(4):
                    kk, kkb = pk.next()
                    P.op("pe", lambda e, kk=kk, h_2=h_2, ti=ti, gi=gi, d=d: e.matmul(kk[:], lhsT=h_2[:, ti * 128:(ti + 1) * 128], rhs=w3[:, d, gi * 512:(gi + 1) * 512], start=True, stop=True), [h2b, bw3], [kkb])
                    P.op("act", lambda e, kk=kk, k_t=k_t, gi=gi: e.activation(out=k_t[:, gi * 512:(gi + 1) * 512], in_=kk[:], func=AF.Copy), [kkb], [k_b])
                P.op("dve", lambda e, k_t=k_t, dc=dc: e.tensor_tensor(out=k_t[:], in0=k_t[:], in1=dc[:], op=ALU.mult), [k_b, dcb], [k_b])
                kb_t, kb_b = kb16.next()
                P.op("pool", lambda e, k_t=k_t, kb_t=kb_t: e.tensor_copy(out=kb_t[:], in_=k_t[:]), [k_b], [kb_b])
                P.dma("sp", ktwo_dst[tile_i * 128:(tile_i + 1) * 128, :], kb_t[:], [kb_b], [scr.bK], kb_b)
                sq_t, sq_b = sq.next()
                P.op("dve", lambda e, k_t=k_t, sq_t=sq_t: e.tensor_tensor(out=sq_t[:], in0=k_t[:], in1=k_t[:], op=ALU.mult), [k_b], [sq_b])
                for gi in range(4):
                    P.op("pe", lambda e, gi=gi, sq_t=sq_t, tile_i=tile_i: e.matmul(pss[gi][:], lhsT=ones[:], rhs=sq_t[:, gi * 512:(gi + 1) * 512], start=(tile_i == 0), stop=(tile_i == NT2 - 1)), [sq_b, bw], [bss])
        rs = sb("rsG", [128, 2048], F32)
        brs = P.buf("rsG")
        for gi in range(4):
            P.op("act", lambda e, gi=gi: e.activation(out=rs[:, gi * 512:(gi + 1) * 512], in_=pss[gi][:], func=AF.Copy), [bss], [brs])
        P.op("pool", lambda e: e.tensor_scalar(out=rs[:], in0=rs[:], scalar1=EPS, scalar2=None, op0=ALU.add), [brs], [brs])
        P.op("act", lambda e: e.activation(out=rs[:], in_=rs[:], func=AF.Sqrt), [brs], [brs])
        P.op("dve", lambda e: e.reciprocal(out=rs[:], in_=rs[:]), [brs], [brs])
        P.dma("sp", rs_dst[0:1, :], rs[0:1, :], [brs], [scr.bK], brs)

    _phase(P, nc, body)


def s5_cmul(P, eng2, out_r, out_i, xr, xi, yr, yi, t1, t2, R, W, TB):
    P.op("pool", lambda e: e.tensor_tensor(out=t1, in0=xr, in1=yr, op=ALU.mult), R, [TB])
    P.op("dve", lambda e: e.tensor_tensor(out=t2, in0=xi, in1=yi, op=ALU.mult), R, [TB])
    P.op("pool", lambda e: e.tensor_tensor(out=out_r, in0=t1, in1=t2, op=ALU.subtract), [TB] + R, W)
    P.op("pool", lambda e: e.tensor_tensor(out=t1, in0=xr, in1=yi, op=ALU.mult), R + W, [TB])
    P.op("dve", lambda e: e.tensor_tensor(out=t2, in0=xi, in1=yr, op=ALU.mult), R + W, [TB])
    P.op("pool", lambda e: e.tensor_tensor(out=out_i, in0=t1, in1=t2, op=ALU.add), [TB] + R, W)


def s5_nstages(T):
    n, span = 0, 1
    while span < T:
        span *= 4
        n += 1
    return n


def phase_S5setup(P, nc, C, NS):
    scr = C.scr
    PI = math.pi
    G2 = 128

    def body(sb, ps):
        ident = sb("identS", [128, 128], BF16)
        identf = sb("identSf", [128, 128], F32)
        bid = P.buf("identS")
        P.dma("sp", identf[:], C.ident[:, :], [], [bid], bid)
        P.op("dve", lambda e: e.tensor_copy(out=ident[:], in_=identf[:]), [bid], [bid])
        msk = sb("mskS", [128, 4], F32)
        bm = P.buf("mskS")
        P.dma("sp", msk[:], C.s5_rowmask[:, :], [], [bm], bm)
        mfb = sb("mfbS", [128, 2, 128], F32)
        P.dma("sp", mfb[:, 0, :], C.s5_mf[:, :], [], [bm], bm)
        P.dma("sp", mfb[:, 1, :], C.s5_mb[:, :], [], [bm], bm)
        ar = sb("arS", [128, G2], F32)
        ai = sb("aiS", [128, G2], F32)
        dt = sb("dtS", [128, G2], F32)
        ba = P.buf("aS")
        for half in range(2):
            P.dma("sp", ar[half * 64:(half + 1) * 64, :].rearrange("n (d g) -> n d g", d=2), C.s5_a_re[0].rearrange("d g n -> n d g"), [], [ba], ba, slow=True)
            P.dma("sp", ai[half * 64:(half + 1) * 64, :].rearrange("n (d g) -> n d g", d=2), C.s5_a_im[0].rearrange("d g n -> n d g"), [], [ba], ba, slow=True)
        P.dma("sp", dt[:], C.s5_log_dt[0].rearrange("d g -> (d g)").partition_broadcast(128), [], [ba], ba, slow=True)
        P.op("act", lambda e: e.activation(out=dt[:], in_=dt[:], func=AF.Exp), [ba], [ba])
        NT_ = 12
        tmp = [sb("tmpS%d" % i, [128, G2], F32) for i in range(NT_)]
        btmp = [P.buf("tmpS%d" % i) for i in range(NT_)]
        ni = sb("niS", [128, G2], mybir.dt.int32)
        lr, li, mag, pr, pi_, cs_arg = tmp[0], tmp[1], tmp[2], tmp[3], tmp[4], tmp[5]
        bl = P.buf("lS")
        P.op("pool", lambda e: e.tensor_tensor(out=lr[:], in0=ar[:], in1=dt[:], op=ALU.mult), [ba], [bl])
        P.op("pool", lambda e: e.tensor_tensor(out=li[:], in0=ai[:], in1=dt[:], op=ALU.mult), [ba], [bl])
        P.op("act", lambda e: e.activation(out=mag[:], in_=lr[:], func=AF.Exp), [bl], [bl])

        def rr(v):
            nf = tmp[6]
            P.op("dve", lambda e: e.tensor_scalar(out=nf[:], in0=v[:], scalar1=1.0 / (2 * PI), scalar2=None, op0=ALU.mult), [bl], [bl])
            P.op("dve", lambda e: e.tensor_copy(out=ni[:], in_=nf[:]), [bl], [bl])
            P.op("dve", lambda e: e.tensor_copy(out=nf[:], in_=ni[:]), [bl], [bl])
            P.op("dve", lambda e: e.scalar_tensor_tensor(out=v[:], in0=nf[:], scalar=-2 * PI, in1=v[:], op0=ALU.mult, op1=ALU.add), [bl], [bl])
            P.op("dve", lambda e: e.tensor_scalar(out=nf[:], in0=v[:], scalar1=PI, scalar2=None, op0=ALU.is_gt), [bl], [bl])
            P.op("dve", lambda e: e.scalar_tensor_tensor(out=v[:], in0=nf[:], scalar=-2 * PI, in1=v[:], op0=ALU.mult, op1=ALU.add), [bl], [bl])
            P.op("dve", lambda e: e.tensor_scalar(out=nf[:], in0=v[:], scalar1=-PI, scalar2=None, op0=ALU.is_lt), [bl], [bl])
            P.op("dve", lambda e: e.scalar_tensor_tensor(out=v[:], in0=nf[:], scalar=2 * PI, in1=v[:], op0=ALU.mult, op1=ALU.add), [bl], [bl])

        P.op("pool", lambda e: e.tensor_scalar(out=cs_arg[:], in0=li[:], scalar1=PI / 2, scalar2=None, op0=ALU.add), [bl], [bl])
        rr(li)
        rr(cs_arg)
        P.op("act", lambda e: e.activation(out=pi_[:], in_=li[:], func=AF.Sin), [bl], [bl])
        P.op("act", lambda e: e.activation(out=pr[:], in_=cs_arg[:], func=AF.Sin), [bl], [bl])
        P.op("pool", lambda e: e.tensor_tensor(out=pr[:], in0=pr[:], in1=mag[:], op=ALU.mult), [bl], [bl])
        P.op("pool", lambda e: e.tensor_tensor(out=pi_[:], in0=pi_[:], in1=mag[:], op=ALU.mult), [bl], [bl])
        PW = sb("PWS", [128, 9, 2, G2], F32)
        bpw = P.buf("PWS")
        P.op("pool", lambda e: e.memset(PW[:, 0, 0, :], 1.0), [], [bpw])
        P.op("pool", lambda e: e.memset(PW[:, 0, 1, :], 0.0), [], [bpw])
        P.op("pool", lambda e: e.tensor_copy(out=PW[:, 1, 0, :], in_=pr[:]), [bl], [bpw])
        P.op("pool", lambda e: e.tensor_copy(out=PW[:, 1, 1, :], in_=pi_[:]), [bl], [bpw])
        t1, t2 = tmp[7], tmp[8]
        btt = P.buf("ttS")
        for e_ in range(2, 9):
            s5_cmul(P, None, PW[:, e_, 0, :], PW[:, e_, 1, :], PW[:, e_ - 1, 0, :], PW[:, e_ - 1, 1, :], PW[:, 1, 0, :], PW[:, 1, 1, :], t1[:], t2[:], [bpw], [bpw], btt)
        inv8 = sb("inv8S", [128, 2, G2], F32)
        binv = P.buf("inv8S")
        P.op("pool", lambda e: e.tensor_tensor(out=t1[:], in0=PW[:, 8, 0, :], in1=PW[:, 8, 0, :], op=ALU.mult), [bpw], [btt])
        P.op("pool", lambda e: e.tensor_tensor(out=t2[:], in0=PW[:, 8, 1, :], in1=PW[:, 8, 1, :], op=ALU.mult), [bpw], [btt])
        P.op("pool", lambda e: e.tensor_tensor(out=t1[:], in0=t1[:], in1=t2[:], op=ALU.add), [btt], [btt])
        P.op("dve", lambda e: e.reciprocal(out=t1[:], in_=t1[:]), [btt], [btt])
        P.op("pool", lambda e: e.tensor_tensor(out=inv8[:, 0, :], in0=PW[:, 8, 0, :], in1=t1[:], op=ALU.mult), [bpw, btt], [binv])
        P.op("pool", lambda e: e.tensor_tensor(out=inv8[:, 1, :], in0=PW[:, 8, 1, :], in1=t1[:], op=ALU.mult), [bpw, btt], [binv])
        P.op("pool", lambda e: e.tensor_scalar(out=inv8[:, 1, :], in0=inv8[:, 1, :], scalar1=-1.0, scalar2=None, op0=ALU.mult), [binv], [binv])
        SC = sb("SCS", [128, NS * 3, 2, G2], F32)
        bsc = P.buf("SCS")
        q = sb("qS", [128, 2, G2], F32)
        bq = P.buf("qS")
        P.op("pool", lambda e: e.tensor_copy(out=q[:], in_=PW[:, 8, :, :]), [bpw], [bq])
        for m in range(NS):
            P.op("pool", lambda e, m=m: e.tensor_copy(out=SC[:, m * 3, :, :], in_=q[:]), [bq], [bsc])
            s5_cmul(P, None, SC[:, m * 3 + 1, 0, :], SC[:, m * 3 + 1, 1, :], q[:, 0, :], q[:, 1, :], q[:, 0, :], q[:, 1, :], t1[:], t2[:], [bq, bsc], [bsc], btt)
            s5_cmul(P, None, SC[:, m * 3 + 2, 0, :], SC[:, m * 3 + 2, 1, :], SC[:, m * 3 + 1, 0, :], SC[:, m * 3 + 1, 1, :], q[:, 0, :], q[:, 1, :], t1[:], t2[:], [bq, bsc], [bsc], btt)
            if m < NS - 1:
                s5_cmul(P, None, q[:, 0, :], q[:, 1, :], SC[:, m * 3 + 1, 0, :], SC[:, m * 3 + 1, 1, :], SC[:, m * 3 + 1, 0, :], SC[:, m * 3 + 1, 1, :], t1[:], t2[:], [bsc], [bq], btt)
        P.op("pool", lambda e: e.tensor_scalar(out=SC[:, :, 1, :], in0=SC[:, :, 1, :], scalar1=msk[:, 2:3], scalar2=None, op0=ALU.mult), [bsc, bm], [bsc])
        P.dma("sp", scr.SSC[:, 0:NS * 3, :, :], SC[:], [bsc], [scr.bK], bsc)
        swapf = sb("swapSf", [128, 128], F32)
        P.dma("sp", swapf[:], C.s5_swap[:, :], [], [bid], bid)
        NC3 = NS * 3
        r1s = Rot(P, sb, "r1S", [128, NC3, 128], F32, 2)
        r2s = Rot(P, sb, "r2S", [128, NC3, 128], F32, 2)
        rst = Rot(P, sb, "rstS", [128, NC3, 128], BF16, 3)
        idb = identf[:].unsqueeze(1).to_broadcast([128, NC3, 128])
        swb = swapf[:].unsqueeze(1).to_broadcast([128, NC3, 128])
        for dg in range(G2):
            rs_t, rs_b = rst.next()
            r1, r1b = r1s.next()
            r2, r2b = r2s.next()
            sc1 = SC[:, :, 0, dg:dg + 1].to_broadcast([128, NC3, 128])
            sc2 = SC[:, :, 1, dg:dg + 1].to_broadcast([128, NC3, 128])
            P.op("pool", lambda e, r1=r1, sc1=sc1: e.tensor_tensor(out=r1[:], in0=idb, in1=sc1, op=ALU.mult), [bid, bsc], [r1b])
            P.op("dve", lambda e, r2=r2, sc2=sc2: e.tensor_tensor(out=r2[:], in0=swb, in1=sc2, op=ALU.mult), [bid, bsc], [r2b])
            P.op("pool", lambda e, r1=r1, r2=r2, rs_t=rs_t: e.tensor_tensor(out=rs_t[:], in0=r1[:], in1=r2[:], op=ALU.add), [r1b, r2b], [rs_b])
            P.dma("sp", scr.SR[dg, :, 0:NS * 3, :], rs_t[:], [rs_b], [scr.bK], rs_b)
        cr, ci, den = tmp[9], tmp[10], tmp[11]
        bc = P.buf("cS")
        xr = tmp[2]
        P.op("pool", lambda e: e.tensor_scalar(out=xr[:], in0=PW[:, 1, 0, :], scalar1=-1.0, scalar2=None, op0=ALU.add), [bpw, bl], [bl])
        P.op("pool", lambda e: e.tensor_tensor(out=den[:], in0=ar[:], in1=ar[:], op=ALU.mult), [ba], [bc])
        P.op("pool", lambda e: e.tensor_tensor(out=t1[:], in0=ai[:], in1=ai[:], op=ALU.mult), [ba, binv], [btt])
        P.op("pool", lambda e: e.tensor_tensor(out=den[:], in0=den[:], in1=t1[:], op=ALU.add), [btt], [bc])
        P.op("dve", lambda e: e.reciprocal(out=den[:], in_=den[:]), [bc], [bc])
        P.op("pool", lambda e: e.tensor_tensor(out=t1[:], in0=xr[:], in1=ar[:], op=ALU.mult), [bl, ba], [btt])
        P.op("pool", lambda e: e.tensor_tensor(out=t2[:], in0=PW[:, 1, 1, :], in1=ai[:], op=ALU.mult), [bpw, ba], [btt])
        P.op("pool", lambda e: e.tensor_tensor(out=cr[:], in0=t1[:], in1=t2[:], op=ALU.add), [btt], [bc])
        P.op("pool", lambda e: e.tensor_tensor(out=cr[:], in0=cr[:], in1=den[:], op=ALU.mult), [bc], [bc])
        P.op("pool", lambda e: e.tensor_tensor(out=t1[:], in0=PW[:, 1, 1, :], in1=ar[:], op=ALU.mult), [bpw, ba, bc], [btt])
        P.op("pool", lambda e: e.tensor_tensor(out=t2[:], in0=xr[:], in1=ai[:], op=ALU.mult), [bl, ba], [btt])
        P.op("pool", lambda e: e.tensor_tensor(out=ci[:], in0=t1[:], in1=t2[:], op=ALU.subtract), [btt], [bc])
        P.op("pool", lambda e: e.tensor_tensor(out=ci[:], in0=ci[:], in1=den[:], op=ALU.mult), [bc], [bc])
        Bri = sb("BriS", [128, 2, G2, 16], F32)
        bB = P.buf("BriS")
        for half in range(2):
            for d in range(2):
                P.dma("sp", Bri[half * 64:(half + 1) * 64, 0, d * 64:(d + 1) * 64, :], C.s5_b_re[0, d].rearrange("g n c -> n g c"), [], [bB], bB)
                P.dma("sp", Bri[half * 64:(half + 1) * 64, 1, d * 64:(d + 1) * 64, :], C.s5_b_im[0, d].rearrange("g n c -> n g c"), [], [bB], bB)
        Bb = sb("BbS", [128, 2, G2, 16], F32)
        bBb = P.buf("BbS")
        T1 = sb("T1S", [128, 64, 16], F32)
        T2 = sb("T2S", [128, 64, 16], F32)
        bT = P.buf("TS")
        for d in range(2):
            gs = slice(d * 64, (d + 1) * 64)
            crb = cr[:, gs].unsqueeze(2).to_broadcast([128, 64, 16])
            cib = ci[:, gs].unsqueeze(2).to_broadcast([128, 64, 16])
            s5_cmul(P, None, Bb[:, 0, gs, :], Bb[:, 1, gs, :], Bri[:, 0, gs, :], Bri[:, 1, gs, :], crb, cib, T1[:], T2[:], [bB, bc], [bBb], bT)
        Cri = sb("CriS", [128, 2, G2, 16], F32)
        bC = P.buf("CriS")
        cl = Rot(P, sb, "clS", [128, 128], F32, 2)
        tpc = Rot(P, ps, "tpcS", [128, 512], F32, 2)
        for ri, src in enumerate((C.s5_c_re, C.s5_c_im)):
            for d in range(2):
                for o in range(8):
                    c_t, c_b = cl.next()
                    for dup in range(2):
                        P.dma("sp", c_t[:, dup * 64:(dup + 1) * 64], src[0, d, o * 8:(o + 1) * 8].rearrange("g c n -> (g c) n"), [], [c_b], c_b)
                    tp_, tpb_ = tpc.next()
                    P.op("pe", lambda e, tp_=tp_, c_t=c_t: e.transpose(out=tp_[:, 0:128], in_=c_t[:], identity=identf[:]), [c_b, bid], [tpb_])
                    P.op("act", lambda e, tp_=tp_, ri=ri, d=d, o=o: e.activation(out=Cri[:, ri, d * 64 + o * 8:d * 64 + (o + 1) * 8, :], in_=tp_[:, 0:128].rearrange("p (g c) -> p g c", c=16), func=AF.Copy), [tpb_], [bC])
        GC = 32
        W8 = sb("W8S", [128, 2, GC, 8, 16], BF16)
        W8s = sb("W8sS", [128, GC, 8, 16], BF16)
        C8 = sb("C8S", [128, GC, 8, 16], BF16)
        bW8, bW8s, bC8 = P.buf("W8S"), P.buf("W8sS"), P.buf("C8S")
        O1 = sb("O1S", [128, GC, 16], F32)
        O2 = sb("O2S", [128, GC, 16], F32)
        O3 = sb("O3S", [128, GC, 16], F32)
        O4 = sb("O4S", [128, GC, 16], F32)
        bO = P.buf("OS")
        tpb16 = Rot(P, ps, "tpb16S", [128, 1024], BF16, 2)
        pd = Rot(P, ps, "pdS", [128, 512], F32, 2)
        b8st = Rot(P, sb, "b8stS", [128, 8, 128], BF16, 2)
        d8f = Rot(P, sb, "d8fS", [128, 4, 128], F32, 2)
        d8st = Rot(P, sb, "d8stS", [128, 4, 128], BF16, 2)
        TT1 = T1[:, 0:GC, :]
        TT2 = T2[:, 0:GC, :]
        for ch in range(G2 // GC):
            d = (ch * GC) // 64
            gs = slice(ch * GC, (ch + 1) * GC)
            for i in range(8):
                eb = (7 - i) if d == 0 else i
                prb = PW[:, eb, 0, gs].unsqueeze(2).to_broadcast([128, GC, 16])
                pib = PW[:, eb, 1, gs].unsqueeze(2).to_broadcast([128, GC, 16])
                s5_cmul(P, None, O1[:], O2[:], Bb[:, 0, gs, :], Bb[:, 1, gs, :], prb, pib, TT1, TT2, [bBb, bpw], [bO], bT)
                P.op("pool", lambda e, i=i: e.tensor_copy(out=W8[:, 0, :, i, :], in_=O1[:]), [bO], [bW8])
                P.op("pool", lambda e, i=i: e.tensor_copy(out=W8[:, 1, :, i, :], in_=O2[:]), [bO], [bW8])
                i8r = inv8[:, 0, gs].unsqueeze(2).to_broadcast([128, GC, 16])
                i8i = inv8[:, 1, gs].unsqueeze(2).to_broadcast([128, GC, 16])
                s5_cmul(P, None, O3[:], O4[:], O1[:], O2[:], i8r, i8i, TT1, TT2, [bO, binv], [bO], bT)
                P.op("pool", lambda e: e.tensor_scalar(out=O3[:], in0=O3[:], scalar1=msk[:, 0:1], scalar2=None, op0=ALU.mult), [bO, bm], [bO])
                P.op("dve", lambda e, i=i: e.scalar_tensor_tensor(out=W8s[:, :, i, :], in0=O4[:], scalar=msk[:, 1:2], in1=O3[:], op0=ALU.mult, op1=ALU.add), [bO, bm], [bW8s])
                ec_ = (i + 1) if d == 0 else (8 - i)
                prc = PW[:, ec_, 0, gs].unsqueeze(2).to_broadcast([128, GC, 16])
                pic = PW[:, ec_, 1, gs].unsqueeze(2).to_broadcast([128, GC, 16])
                s5_cmul(P, None, O1[:], O2[:], Cri[:, 0, gs, :], Cri[:, 1, gs, :], prc, pic, TT1, TT2, [bC, bpw, bW8, bW8s], [bO], bT)
                P.op("pool", lambda e: e.tensor_scalar(out=O1[:], in0=O1[:], scalar1=msk[:, 0:1], scalar2=None, op0=ALU.mult), [bO, bm], [bO])
                P.op("dve", lambda e, i=i: e.scalar_tensor_tensor(out=C8[:, :, i, :], in0=O2[:], scalar=msk[:, 3:4], in1=O1[:], op0=ALU.mult, op1=ALU.add), [bO, bm], [bC8])
            P.dma("sp", scr.SC8[:, ch * GC:(ch + 1) * GC, :], C8[:].rearrange("p g i c -> p g (i c)"), [bC8], [scr.bK], bC8)
            for j0 in range(0, GC, 8):
                tp_, tpb_ = tpb16.next()
                for j in range(8):
                    for ri in range(2):
                        P.op("pe", lambda e, tp_=tp_, j=j, j0=j0, ri=ri: e.transpose(out=tp_[:, j * 128 + ri * 64:j * 128 + (ri + 1) * 64], in_=W8[0:64, ri, j0 + j, :, :].rearrange("n i c -> n (i c)"), identity=ident[0:64, 0:64]), [bW8, bid], [tpb_])
                st_, stb_ = b8st.next()
                P.op("act", lambda e, tp_=tp_, st_=st_: e.activation(out=st_[:].rearrange("p j n -> p (j n)"), in_=tp_[:], func=AF.Copy), [tpb_], [stb_])
                dg0 = ch * GC + j0
                P.dma("sp", scr.SB8[dg0:dg0 + 8, :, :].rearrange("j p n -> p j n"), st_[:], [stb_], [scr.bK], stb_)
            for j0 in range(0, GC, 4):
                pp_, ppb_ = pd.next()
                for j in range(4):
                    P.op("pe", lambda e, pp_=pp_, j=j, j0=j0: e.matmul(pp_[:, j * 128:(j + 1) * 128], lhsT=W8s[:, j0 + j, :, :].rearrange("n i c -> n (i c)"), rhs=C8[:, j0 + j, :, :].rearrange("n i c -> n (i c)"), start=True, stop=True, skip_group_check=True), [bW8s, bC8], [ppb_])
                f_, fb_ = d8f.next()
                P.op("act", lambda e, pp_=pp_, f_=f_: e.activation(out=f_[:].rearrange("p j c -> p (j c)"), in_=pp_[:], func=AF.Copy), [ppb_], [fb_])
                o_, ob_ = d8st.next()
                mk = mfb[:, d, :].unsqueeze(1).to_broadcast([128, 4, 128])
                P.op("pool", lambda e, f_=f_, o_=o_, mk=mk: e.tensor_tensor(out=o_[:], in0=f_[:], in1=mk, op=ALU.mult), [fb_, bm], [ob_])
                dg0 = ch * GC + j0
                P.dma("sp", scr.SD8[dg0:dg0 + 4, :, :].rearrange("j p n -> p j n"), o_[:], [ob_], [scr.bK], ob_)

    _phase(P, nc, body)


def phase_S5main(P, nc, C, L, NS):
    T = L // 8
    NTT = max(1, T // 128)
    TT = min(T, 128)
    scr = C.scr

    def body(sb, ps):
        ident = sb("identM", [128, 128], BF16)
        identf = sb("identMf", [128, 128], F32)
        swapf = sb("swapMf", [128, 128], F32)
        bid = P.buf("identM")
        P.dma("sp", identf[:], C.ident[:, :], [], [bid], bid)
        P.dma("sp", swapf[:], C.s5_swap[:, :], [], [bid], bid)
        P.op("dve", lambda e: e.tensor_copy(out=ident[:], in_=identf[:]), [bid], [bid])
        SC = sb("SCM", [128, NS * 3, 2, 128], F32)
        bsc = P.buf("SCM")
        P.dma("sp", SC[:], scr.SSC[:, 0:NS * 3, :, :], [], [bsc], bsc)
        uo = Rot(P, sb, "uoM", [128, NTT, 8, 128], BF16, 2)
        uo2 = Rot(P, sb, "uo2M", [128, NTT, 8, 8, 16], BF16, 2)
        yo = Rot(P, sb, "yoM", [128, NTT, 8, 128], BF16, 2)
        wg = Rot(P, sb, "wgM", [128, 6, 128], BF16, 4)
        us = Rot(P, sb, "usM", [128, T], BF16, 4)
        sbuf_ = Rot(P, sb, "sM", [128, T], BF16, 12)
        ys = Rot(P, sb, "ysM", [128, T], BF16, 3)
        Ra = Rot(P, sb, "RaM", [128, NS * 3, 128], BF16, 8)
        Rt = Rot(P, sb, "RtM", [128, 128], F32, 8)
        Rb = Rot(P, sb, "RbM", [128, 128], BF16, 26)
        tp = Rot(P, ps, "tpM", [128, 1024], BF16, 2)
        pp = Rot(P, ps, "ppM", [128, 512], F32, 6)
        chunks = [(c0, min(512, T - c0)) for c0 in range(0, T, 512)]

        def do_groups(o, g8s, uo_t, uo_b, yo_t, yo_b):
            G = []
            for g8 in g8s:
                g = o * 8 + g8
                w_t, w_b = wg.next()
                for d in range(2):
                    P.dma("sp", w_t[:, d, :], scr.SB8[d * 64 + g, :, :], [], [w_b], w_b)
                    P.dma("sp", w_t[:, 2 + d, :], scr.SC8[:, d * 64 + g, :], [], [w_b], w_b)
                    P.dma("sp", w_t[:, 4 + d, :], scr.SD8[d * 64 + g, :, :], [], [w_b], w_b)
                tpp, tpb = tp.next()
                for tt in range(NTT):
                    P.op("pe", lambda e, tt=tt, tpp=tpp, g8=g8: e.transpose(out=tpp[:, tt * TT:(tt + 1) * TT], in_=uo_t[0:TT, tt, g8, :, :].rearrange("p i c -> p (i c)"), identity=ident[0:TT, 0:TT]), [uo_b, bid], [tpb])
                us_t, us_b = us.next()
                P.op("act", lambda e, us_t=us_t, tpp=tpp: e.activation(out=us_t[:], in_=tpp[:, 0:T], func=AF.Copy), [tpb], [us_b])
                G.append(dict(g=g, g8=g8, w_t=w_t, w_b=w_b, us_t=us_t, us_b=us_b))
            lanes = []
            for gi_, gd in enumerate(G):
                for d in range(2):
                    s_t, s_b = sbuf_.next()
                    for (c0, w) in chunks:
                        z, zb = pp.next()
                        P.op("pe", lambda e, z=z, c0=c0, w=w, d=d, gd=gd: e.matmul(z[:, 0:w], lhsT=gd["w_t"][:, d, :], rhs=gd["us_t"][:, c0:c0 + w], start=True, stop=True), [gd["us_b"], gd["w_b"]], [zb])
                        P.op("act", lambda e, z=z, c0=c0, w=w, s_t=s_t: e.activation(out=s_t[:, c0:c0 + w], in_=z[:, 0:w], func=AF.Copy), [zb], [s_b])
                    ra_t, ra_b = Ra.next()
                    P.dma("sp", ra_t[:], scr.SR[d * 64 + gd["g"], :, 0:NS * 3, :], [], [ra_b], ra_b)
                    lanes.append(dict(d=d, dg=d * 64 + gd["g"], cur=(s_t, s_b), ra=(ra_t, ra_b)))
            for m in range(NS):
                S = 4 ** m
                for ln in lanes:
                    ra_t, ra_b = ln["ra"]
                    ln["Rs"] = [(J * S, ra_t[:, m * 3 + J - 1, :], ra_b) for J in range(1, 4) if J * S < T]
                for ln in lanes:
                    d = ln["d"]
                    s_t, s_b = ln["cur"]
                    n_t, n_b = sbuf_.next()
                    for (c0, w) in chunks:
                        z, zb = pp.next()
                        mms = [(0, w, c0, ident[:], bid)]
                        for (sh, r2, r2b) in ln["Rs"]:
                            if d == 0:
                                a = max(c0, sh)
                                if a < c0 + w:
                                    mms.append((a - c0, w, a - sh, r2, r2b))
                            else:
                                bnd = min(c0 + w, T - sh)
                                if bnd > c0:
                                    mms.append((0, bnd - c0, c0 + sh, r2, r2b))
                        for k_, (o0, o1, src0, lt, ltb) in enumerate(mms):
                            P.op("pe", lambda e, z=z, o0=o0, o1=o1, src0=src0, lt=lt, s_t=s_t, k_=k_, nm=len(mms): e.matmul(z[:, o0:o1], lhsT=lt, rhs=s_t[:, src0:src0 + (o1 - o0)], start=(k_ == 0), stop=(k_ == nm - 1), skip_group_check=True), [s_b, ltb], [zb])
                        P.op("act", lambda e, z=z, c0=c0, w=w, n_t=n_t: e.activation(out=n_t[:, c0:c0 + w], in_=z[:, 0:w], func=AF.Copy), [zb], [n_b])
                    ln["cur"] = (n_t, n_b)
            for gi_, gd in enumerate(G):
                fin = [lanes[gi_ * 2]["cur"], lanes[gi_ * 2 + 1]["cur"]]
                w_t, w_b, us_t, us_b, g8 = gd["w_t"], gd["w_b"], gd["us_t"], gd["us_b"], gd["g8"]
                y_t, y_b = ys.next()
                for (c0, w) in chunks:
                    z, zb = pp.next()
                    mms = [(0, w, us_t, us_b, c0, 4), (0, w, us_t, us_b, c0, 5)]
                    a = max(c0, 1)
                    if a < c0 + w:
                        mms.append((a - c0, w, fin[0][0], fin[0][1], a - 1, 2))
                    bnd = min(c0 + w, T - 1)
                    if bnd > c0:
                        mms.append((0, bnd - c0, fin[1][0], fin[1][1], c0 + 1, 3))
                    for k_, (o0, o1, src, srcb, src0, wi) in enumerate(mms):
                        P.op("pe", lambda e, z=z, o0=o0, o1=o1, src=src, src0=src0, wi=wi, k_=k_, nm=len(mms), w_t=w_t: e.matmul(z[:, o0:o1], lhsT=w_t[:, wi, :], rhs=src[:, src0:src0 + (o1 - o0)], start=(k_ == 0), stop=(k_ == nm - 1), skip_group_check=True), [srcb, w_b], [zb])
                    P.op("act", lambda e, z=z, c0=c0, w=w, y_t=y_t: e.activation(out=y_t[:, c0:c0 + w], in_=z[:, 0:w], func=AF.Copy), [zb], [y_b])
                tpp2, tpb2 = tp.next()
                for tt in range(NTT):
                    P.op("pe", lambda e, tt=tt, tpp2=tpp2, y_t=y_t: e.transpose(out=tpp2[0:TT, tt * 128:(tt + 1) * 128], in_=y_t[:, tt * TT:(tt + 1) * TT], identity=ident[:]), [y_b, bid], [tpb2])
                P.op("act", lambda e, tpp2=tpp2, g8=g8: e.activation(out=yo_t[0:TT, :, :, g8 * 16:(g8 + 1) * 16], in_=tpp2[0:TT, 0:NTT * 128].rearrange("p (t i c) -> p t i c", t=NTT, i=8), func=AF.Copy), [tpb2], [yo_b])

        for o in range(8):
            uo_t, uo_b = uo.next()
            yo_t, yo_b = yo.next()
            for tt in range(NTT):
                P.dma("sp", uo_t[0:TT, tt, :, :], scr.U[tt * TT * 8:(tt + 1) * TT * 8, o * 128:(o + 1) * 128].rearrange("(p i) c -> p i c", i=8), [], [uo_b], uo_b)
            u2_t, u2_b = uo2.next()
            for tt in range(NTT):
                P.op("pool", lambda e, tt=tt, u2_t=u2_t, uo_t=uo_t: e.tensor_copy(out=u2_t[0:TT, tt, :, :, :], in_=uo_t[0:TT, tt, :, :].rearrange("p i (g c) -> p g i c", c=16)), [uo_b], [u2_b])
            for g8 in range(0, 8, 2):
                do_groups(o, [g8, g8 + 1], u2_t, u2_b, yo_t, yo_b)
            for tt in range(NTT):
                P.dma("sp", scr.YS[tt * TT * 8:(tt + 1) * TT * 8, o * 128:(o + 1) * 128].rearrange("(p i) c -> p i c", i=8), yo_t[0:TT, tt, :, :], [yo_b], [scr.bV], yo_b)

    _phase(P, nc, body)


def phase_S5post(P, nc, C, L):
    NT = L // 128
    scr = C.scr

    def body(sb, ps):
        ident = sb("identP", [128, 128], BF16)
        identf = sb("identPf", [128, 128], F32)
        bid = P.buf("identP")
        P.dma("sp", identf[:], C.ident[:, :], [], [bid], bid)
        P.op("dve", lambda e: e.tensor_copy(out=ident[:], in_=identf[:]), [bid], [bid])
        stage = Rot(P, sb, "wstP", [128, 1024], F32, 2)
        w_g = sb("w_gP", [128, 8, 1024], BF16)
        bw = P.buf("w_gP")
        load_weight_bf16(P, sb, w_g, bw, C.s5_glu_w[0], 8, 1024, stage)
        drep = sb("drepP", [128, 1024], F32)
        brep = sb("brepP", [128, 1024], F32)
        br = P.buf("repP")
        P.dma("sp", drep[:], C.s5_d[0].partition_broadcast(128), [], [br], br, slow=True)
        P.dma("sp", brep[:], C.s5_glu_b[0].partition_broadcast(128), [], [br], br, slow=True)
        ut = Rot(P, sb, "utP", [128, 1024], BF16, 2)
        yst = Rot(P, sb, "ystP", [128, 1024], BF16, 2)
        y = Rot(P, sb, "yP", [128, 1024], F32, 2)
        w = Rot(P, sb, "wP", [128, 1024], F32, 2)
        sg = Rot(P, sb, "sgP", [128, 1024], F32, 2)
        yg = Rot(P, sb, "ygP", [128, 1024], BF16, 2)
        ygT = Rot(P, sb, "ygTP", [128, 8, 128], BF16, 2)
        ya = Rot(P, sb, "yaP", [128, 1024], BF16, 2)
        tp = Rot(P, ps, "tpP", [128, 1024], BF16, 2)
        zp = Rot(P, ps, "zpP", [128, 512], F32, 4)
        for t in range(NT):
            rows = slice(t * 128, (t + 1) * 128)
            u_t, u_b = ut.next()
            s_t, s_b = yst.next()
            P.dma("sp", u_t[:], scr.U[rows, :], [], [u_b], u_b)
            P.dma("sp", s_t[:], scr.YS[rows, :], [], [s_b], s_b)
            y_t, y_b = y.next()
            w_t, w_b = w.next()
            P.op("pool", lambda e, y_t=y_t, u_t=u_t: e.tensor_tensor(out=y_t[:], in0=u_t[:], in1=drep[:], op=ALU.mult), [u_b, br], [y_b])
            P.op("pool", lambda e, y_t=y_t, s_t=s_t: e.tensor_tensor(out=y_t[:], in0=y_t[:], in1=s_t[:], op=ALU.add), [s_b, y_b], [y_b])
            P.op("dve", lambda e, y_t=y_t, w_t=w_t: e.tensor_tensor(out=w_t[:], in0=y_t[:], in1=y_t[:], op=ALU.mult), [y_b], [w_b])
            P.op("pool", lambda e, w_t=w_t: e.tensor_scalar(out=w_t[:], in0=w_t[:], scalar1=0.044715, scalar2=1.0, op0=ALU.mult, op1=ALU.add), [w_b], [w_b])
            P.op("pool", lambda e, w_t=w_t, y_t=y_t: e.tensor_tensor(out=w_t[:], in0=w_t[:], in1=y_t[:], op=ALU.mult), [w_b, y_b], [w_b])
            g_t, g_b = sg.next()
            P.op("act", lambda e, w_t=w_t, g_t=g_t: e.activation(out=g_t[:], in_=w_t[:], func=AF.Sigmoid, scale=2.0 * math.sqrt(2.0 / math.pi)), [w_b], [g_b])
            yg_t, yg_b = yg.next()
            P.op("dve", lambda e, yg_t=yg_t, y_t=y_t, g_t=g_t: e.tensor_tensor(out=yg_t[:], in0=y_t[:], in1=g_t[:], op=ALU.mult), [y_b, g_b], [yg_b])
            tpp, tpb = tp.next()
            for k in range(8):
                P.op("pe", lambda e, k=k, tpp=tpp, yg_t=yg_t: e.transpose(out=tpp[:, k * 128:(k + 1) * 128], in_=yg_t[:, k * 128:(k + 1) * 128], identity=ident[:]), [yg_b, bid], [tpb])
            yT, yTb = ygT.next()
            P.op("act", lambda e, tpp=tpp, yT=yT: e.activation(out=yT[:].rearrange("p k t -> p (k t)"), in_=tpp[:], func=AF.Copy), [tpb], [yTb])
            for gi in range(2):
                z, zb = zp.next()
                for k in range(8):
                    P.op("pe", lambda e, z=z, k=k, gi=gi, yT=yT: e.matmul(z[:], lhsT=yT[:, k, :], rhs=w_g[:, k, gi * 512:(gi + 1) * 512], start=(k == 0), stop=(k == 7)), [yTb, bw], [zb])
                P.op("act", lambda e, z=z, gi=gi, g_t=g_t: e.activation(out=g_t[:, gi * 512:(gi + 1) * 512], in_=z[:], func=AF.Copy), [zb], [g_b])
            P.op("pool", lambda e, g_t=g_t: e.tensor_tensor(out=g_t[:], in0=g_t[:], in1=brep[:], op=ALU.add), [g_b, br], [g_b])
            P.op("act", lambda e, g_t=g_t: e.activation(out=g_t[:], in_=g_t[:], func=AF.Sigmoid), [g_b], [g_b])
            a_t, a_b = ya.next()
            P.op("dve", lambda e, a_t=a_t, yg_t=yg_t, g_t=g_t: e.tensor_tensor(out=a_t[:], in0=yg_t[:], in1=g_t[:], op=ALU.mult), [yg_b, g_b], [a_b])
            P.dma("pool", scr.YA[rows, :], a_t[:], [a_b], [scr.bU], a_b)

    _phase(P, nc, body)


WNAMES = {
    "norm_g": [2, 1024], "final_g": [1024], "ple_w": [2, 256, 1024], "ple_gate_w": [2, 1024, 1024],
    "ab_w_in": [1, 1024, 3776], "ab_w_out": [1, 2048, 1024],
    "s5_a_re": [1, 2, 64, 64], "s5_a_im": [1, 2, 64, 64], "s5_log_dt": [1, 2, 64],
    "s5_b_re": [1, 2, 64, 64, 16], "s5_b_im": [1, 2, 64, 64, 16], "s5_c_re": [1, 2, 64, 16, 64], "s5_c_im": [1, 2, 64, 16, 64],
    "s5_d": [1, 1024], "s5_glu_w": [1, 1024, 1024], "s5_glu_b": [1, 1024],
    "mla_q_norm": [1, 384], "mla_w_q_up": [1, 384, 1536], "mla_kv_norm": [1, 256], "mla_w_kv_up": [1, 256, 2048],
    "hy_w_in": [1, 1024, 8192], "hy_w_out": [1, 2048, 1024], "hy_conv_w": [1, 3, 6144], "hy_conv_b": [1, 6144],
    "hy_f_w1": [1, 2, 33, 64], "hy_f_b1": [1, 2, 64], "hy_f_freq1": [1, 2, 64], "hy_f_w2": [1, 2, 64, 64], "hy_f_b2": [1, 2, 64],
    "hy_f_freq2": [1, 2, 64], "hy_f_w3": [1, 2, 64, 2048], "hy_bias": [1, 2048],
}


def build(Ls, opts):
    nc = bass.Bass("TRN2", target_bir_lowering=False)
    C = Ctx()
    LM = max(Ls)
    for n, shp in WNAMES.items():
        setattr(C, n, nc.dram_tensor(n, shp, F32, kind="ExternalInput").ap())
    C.ident = nc.dram_tensor("ident", [128, 128], F32, kind="ExternalInput").ap()
    C.rope_cs = nc.dram_tensor("rope_cs", [LM, 64], F32, kind="ExternalInput").ap()
    xs, ps_, ys = [], [], []
    for i, L in enumerate(Ls):
        xs.append(nc.dram_tensor(f"x{i}", [L, 1024], F32, kind="ExternalInput").ap())
        ps_.append(nc.dram_tensor(f"p{i}", [2, L, 256], F32, kind="ExternalInput").ap())
        ys.append(nc.dram_tensor(f"y{i}", [L, 1024], F32, kind="ExternalOutput").ap())
    dbg = opts.get("dbg", ())
    scr = Ctx()
    C.scr = scr

    def scratch(name, shape, dtype):
        kind = "ExternalOutput" if name in dbg else "Internal"
        return nc.dram_tensor("scr_" + name, shape, dtype, kind=kind).ap()

    scr.U = scratch("U", [LM, 1024], BF16)
    scr.G = scratch("G", [LM, 2048], BF16)
    scr.V = scratch("V", [LM, 1024], BF16)
    scr.QN = scratch("QN", [8, 128, LM], BF16)
    scr.QR = scratch("QR", [8, 64, LM], BF16)
    scr.KN = scratch("KN", [8, 128, LM], BF16)
    scr.KR = scratch("KR", [64, LM], BF16)
    scr.O = scratch("O", [LM, 1024], BF16)
    scr.YA = scratch("YA", [LM, 1024], BF16)
    scr.YH = scratch("YH", [LM, 2048], BF16)
    scr.H1 = scratch("H1", [LM, 1024], F32)
    scr.HT = scratch("HT", [8, 128, LM + 2], BF16)
    scr.MT = scratch("MT", [16, 128, LM], BF16)
    N1M = 2 * LM // 128
    scr.AT = scratch("AT", [2, N1M, 128, 128], BF16)
    scr.ZT = scratch("ZT", [2, 128, N1M, 128], BF16)
    scr.YHAT = scratch("YHAT", [16, 2, 128, N1M, 128], BF16)
    scr.VX = scratch("VX", [LM, 2048], BF16)
    scr.XG = scratch("XG", [LM, 2048], BF16)
    scr.CV = scratch("CV", [LM, 2048], BF16)
    C.hy_delta = nc.dram_tensor("hy_delta", [2048], F32, kind="ExternalInput").ap()
    C.s5_rowmask = nc.dram_tensor("s5_rowmask", [128, 4], F32, kind="ExternalInput").ap()
    C.s5_mf = nc.dram_tensor("s5_mf", [128, 128], F32, kind="ExternalInput").ap()
    C.s5_mb = nc.dram_tensor("s5_mb", [128, 128], F32, kind="ExternalInput").ap()
    C.s5_swap = nc.dram_tensor("s5_swap", [128, 128], F32, kind="ExternalInput").ap()
    NSM = s5_nstages(LM // 8)
    scr.SSC = scratch("SSC", [128, NSM * 3, 2, 128], F32)
    scr.SR = scratch("SR", [128, 128, NSM * 3, 128], BF16)
    scr.SB8 = scratch("SB8", [128, 128, 128], BF16)
    scr.SC8 = scratch("SC8", [128, 128, 128], BF16)
    scr.SD8 = scratch("SD8", [128, 128, 128], BF16)
    scr.YS = scratch("YS", [LM, 1024], BF16)
    fftc = {}
    for L in sorted(set(Ls)):
        tbs, N1, KL = fft_tables_shapes(L)
        d = {"tb": {k_: nc.dram_tensor(f"{k_}_{L}", list(shp), F32, kind="ExternalInput").ap() for k_, shp in tbs.items()}}
        d["zT"] = nc.dram_tensor(f"zT_{L}", [33, 2 * L], F32, kind="ExternalInput").ap()
        d["tl"] = nc.dram_tensor(f"tl_{L}", [2 * L], F32, kind="ExternalInput").ap()
        d["KT"] = scratch(f"KT_{L}", [2 * L, 2048], BF16)
        d["KH"] = scratch(f"KH_{L}", [16, 2, 128, N1, 128], F32)
        d["RS"] = scratch(f"RS_{L}", [1, 2048], F32)
        fftc[L] = d
    with ExitStack() as es:
        P = Prog(nc, es)
        for n in ["bU", "bG", "bV", "bQ", "bK", "bO", "bH", "bA"]:
            b = Buf(n, accum=True)
            setattr(scr, n, b)
        depth = opts.get("depth", 2)
        conv = opts.get("conv", True) and depth == 2
        if opts.get("s5", True):
            phase_S5setup(P, nc, C, NSM)
        if conv:
            for L in sorted(set(Ls)):
                d = fftc[L]
                phase_G(P, nc, C, L, d["zT"], d["tl"], d["KT"], d["RS"])
                phase_F(P, nc, C, L, d["KT"], 2 * L // 128, d["tb"], khat_dst=d["KH"])
        for i, L in enumerate(Ls):
            phs = opts.get("phases", "ABC")
            if "A" in phs:
                phase_A(P, nc, C, L, xs[i])
            if opts.get("s5", True):
                phase_S5main(P, nc, C, L, s5_nstages(L // 8))
                phase_S5post(P, nc, C, L)
            if "B" in phs:
                phase_B(P, nc, C, L)
            last = depth == 1
            if "C" in phs:
                phase_C(P, nc, C, L, 0, xs[i], ps_[i][0], C.ab_w_out[0], scr.H1, ys[i] if last else None, opts.get("s5", True))
            if depth == 2:
                phase_D1(P, nc, C, L)
                phase_D2(P, nc, C, L)
                d = fftc[L]
                phase_F(P, nc, C, L, scr.VX, L // 128, d["tb"], khat_src=d["KH"], yhat_dst=scr.YHAT)
                phase_I(P, nc, C, L, d["tb"], scr.YHAT)
                phase_C(P, nc, C, L, 1, scr.H1, ps_[i][1], C.hy_w_out[0], None, ys[i], False, rs_src=d["RS"])
        C.ninstr = P.ninstr
    return nc, C


def host_consts(LM):
    inv = 1.0 / (10000.0 ** (np.arange(0, 64, 2, dtype=np.float32) / 64.0))
    ang = np.arange(LM, dtype=np.float32)[:, None] * inv[None, :].astype(np.float32)
    cs = np.concatenate([np.cos(ang), np.sin(ang)], axis=1).astype(np.float32)
    out = {"ident": np.eye(128, dtype=np.float32), "rope_cs": cs}
    min_decay = math.log(1e-2) / 1.5
    max_decay = math.log(1e-2) / 0.3
    p_ = np.arange(128)
    rm = np.zeros((128, 4), np.float32)
    rm[:64, 0] = 1.0
    rm[64:, 1] = 1.0
    rm[:, 2] = np.where(p_ < 64, 1.0, -1.0)
    rm[64:, 3] = -1.0
    out["s5_rowmask"] = rm
    ii = p_ // 16
    out["s5_mf"] = (ii[None, :] >= ii[:, None]).astype(np.float32)
    out["s5_mb"] = (ii[None, :] <= ii[:, None]).astype(np.float32)
    sw = np.zeros((128, 128), np.float32)
    sw[p_, (p_ + 64) % 128] = 1.0
    out["s5_swap"] = sw
    out["hy_delta"] = np.abs(np.linspace(min_decay, max_decay, 2048, dtype=np.float32)).astype(np.float32)
    return out


def host_consts_L(L):
    out = {}
    tbs, N1, KL = fft_tables(L)
    for k_, v in tbs.items():
        out[f"{k_}_{L}"] = v
    t = np.linspace(0.0, 1.0, L, dtype=np.float32)[:, None]
    w = (2.0 * math.pi * np.arange(L, dtype=np.float32)[:, None] / L).astype(np.float32)
    bands = np.linspace(1e-4, 15, 16, dtype=np.float32)[None, :]
    z = np.concatenate([t, np.cos(bands * w), -np.sin(bands * w)], axis=-1).astype(np.float32)
    idx = np.concatenate([np.arange(L), np.array([0]), L - np.arange(1, L)])
    z2 = z[idx]
    tl = t[:, 0][idx].copy()
    tl[L] = 1.0e4
    out[f"zT_{L}"] = np.ascontiguousarray(z2.T.astype(np.float32))
    out[f"tl_{L}"] = np.ascontiguousarray(tl.astype(np.float32))
    return out


_CACHE = {}


def kernel(**inputs):
    Ls = [4096, 8192]
    if "nc" not in _CACHE:
        _CACHE["nc"] = build(Ls, {"depth": 2, "s5": True})
    nc, C = _CACHE["nc"]
    consts = host_consts(max(Ls))
    for L_ in Ls:
        consts.update(host_consts_L(L_))
    W = {n: np.ascontiguousarray(np.asarray(inputs[n], dtype=np.float32)) for n in WNAMES}
    xs, xp = np.asarray(inputs["x_sample"]), np.asarray(inputs["x_prompt"])
    psm, ppr = np.asarray(inputs["p_sample"]), np.asarray(inputs["p_prompt"])
    in_maps = []
    for c in range(8):
        m = dict(W)
        m.update(consts)
        m["x0"] = np.ascontiguousarray(xs[c])
        m["p0"] = np.ascontiguousarray(psm[:, c])
        m["x1"] = np.ascontiguousarray(xp[c % 2])
        m["p1"] = np.ascontiguousarray(ppr[:, c % 2])
        in_maps.append(m)
    res = run_bass_kernel_spmd(nc, in_maps, core_ids=list(range(8)))
    y_sample = np.stack([np.asarray(res.results[c]["y0"], dtype=np.float32) for c in range(8)], axis=0)
    y_prompt = np.stack([np.asarray(res.results[c]["y1"], dtype=np.float32) for c in range(2)], axis=0)
    return (y_prompt, y_sample)
```
